# Optimizing a Trainium2 kernel written in Bass

```python
import jax, jax.numpy as jnp
from jax import lax
import numpy as np

D_MODEL = 1024
BATCH = 16
SEQ = 256
DEPTH = 1
DEC_BATCH = 8
DEC_SEQ = 4096
PAST_LEN = 256

GRID_W = 64
HEAD_DIM = 64
ATTN_WIDTH = D_MODEL // 2
N_Q_HEADS = ATTN_WIDTH // HEAD_DIM
N_KV_HEADS = N_Q_HEADS // 4
KV_GROUP = N_Q_HEADS // N_KV_HEADS
AXIS_DIM = HEAD_DIM // 2
ROPE_THETA = 10000.0
ATTN_SCALE = HEAD_DIM ** -0.5
Q_BLOCK = 128
MLSTM_WIDTH = D_MODEL - ATTN_WIDTH
MLSTM_HEADS = 4
MLSTM_DK = MLSTM_WIDTH // MLSTM_HEADS
MLSTM_DV = MLSTM_DK
CHUNK = 128
D_FF = -(-8 * D_MODEL // (3 * 256)) * 256
EPS = 1e-6
F32 = jnp.float32

SPLIT_WIDTHS = [N_Q_HEADS * HEAD_DIM, N_KV_HEADS * HEAD_DIM, N_KV_HEADS * HEAD_DIM,
                MLSTM_WIDTH, MLSTM_WIDTH, MLSTM_WIDTH, MLSTM_WIDTH, 4 * MLSTM_HEADS]
N_IN = sum(SPLIT_WIDTHS)
SPLIT_POINTS = [int(v) for v in np.cumsum(SPLIT_WIDTHS)[:-1]]

kernel_name = 'hybrid_gqa_mlstm_diffusion_step'


def rmsnorm(x, w):
    xf = x.astype(F32)
    y = xf * lax.rsqrt(jnp.mean(xf * xf, axis=-1, keepdims=True) + EPS)
    return (y * w.astype(F32)).astype(x.dtype)


def ada_mods(cond, w_ada_l, b_ada_l):
    m = jax.nn.silu(cond) @ w_ada_l + b_ada_l
    m = m.reshape(-1, 1, 6 * D_MODEL)
    return jnp.split(m, 6, axis=-1)


def axial_rope(n_tok):
    rows = n_tok // GRID_W
    row_ids = jnp.repeat(jnp.arange(rows, dtype=F32), GRID_W)
    col_ids = jnp.tile(jnp.arange(GRID_W, dtype=F32), rows)
    inv = ROPE_THETA ** (-jnp.arange(0, AXIS_DIM, 2, dtype=F32) / AXIS_DIM)
    ang = jnp.concatenate([row_ids[:, None] * inv, col_ids[:, None] * inv], axis=-1)
    return jnp.cos(ang), jnp.sin(ang)


def apply_rope(x, cos, sin):
    B, T, H, _ = x.shape
    xr = x.astype(F32).reshape(B, T, H, 2, 2, AXIS_DIM // 2)
    x1, x2 = xr[..., 0, :], xr[..., 1, :]
    c = cos.reshape(T, 1, 2, AXIS_DIM // 2)
    s = sin.reshape(T, 1, 2, AXIS_DIM // 2)
    out = jnp.stack([x1 * c - x2 * s, x2 * c + x1 * s], axis=-2)
    return out.reshape(x.shape).astype(x.dtype)


def attention(q, k, v):
    B, Tq, _, _ = q.shape
    nb = Tq // Q_BLOCK
    qb = q.reshape(B, nb, Q_BLOCK, N_KV_HEADS, KV_GROUP, HEAD_DIM).transpose(1, 0, 2, 3, 4, 5)

    def one_block(qi):
        s = jnp.einsum('bqhgd,bkhd->bhgqk', qi, k).astype(F32) * ATTN_SCALE
        p = jax.nn.softmax(s, axis=-1).astype(v.dtype)
        return jnp.einsum('bhgqk,bkhd->bqhgd', p, v)

    o = lax.map(one_block, qb)
    return o.transpose(1, 0, 2, 3, 4, 5).reshape(B, Tq, N_Q_HEADS * HEAD_DIM)


def mlstm_scan(q, k, v, ig, lf, C0, n0, m0):
    B, H, T, _ = q.shape
    nc = T // CHUNK

    def to_chunks(a):
        return jnp.moveaxis(a.reshape(B, H, nc, CHUNK, *a.shape[3:]), 2, 0)

    mask = jnp.tril(jnp.ones((CHUNK, CHUNK), dtype=bool))

    def step(carry, inp):
        C, n, m = carry
        qc, kc, vc, ic, fc = inp
        b = jnp.cumsum(fc, axis=-1)
        log_d = jnp.where(mask, b[..., :, None] - b[..., None, :] + ic[..., None, :], -jnp.inf)
        m_inter = b + m[..., None]
        m_t = jnp.maximum(jnp.max(log_d, axis=-1), m_inter)
        s = jnp.einsum('bhld,bhsd->bhls', qc, kc) * jnp.exp(log_d - m_t[..., None])
        inter = jnp.exp(m_inter - m_t)
        num = jnp.einsum('bhls,bhse->bhle', s, vc) + inter[..., None] * jnp.einsum('bhld,bhde->bhle', qc, C)
        den = jnp.sum(s, axis=-1) + inter * jnp.einsum('bhld,bhd->bhl', qc, n)
        h = num / jnp.maximum(jnp.abs(den), jnp.exp(-m_t))[..., None]
        m_new = m_t[..., -1]
        w = jnp.exp(b[..., -1:] - b + ic - m_new[..., None])
        decay = jnp.exp(b[..., -1] + m - m_new)
        C_new = decay[..., None, None] * C + jnp.einsum('bhs,bhsd,bhse->bhde', w, kc, vc)
        n_new = decay[..., None] * n + jnp.einsum('bhs,bhsd->bhd', w, kc)
        return (C_new, n_new, m_new), h

    inputs = (to_chunks(q), to_chunks(k), to_chunks(v), to_chunks(ig), to_chunks(lf))
    (C, n, m), hs = lax.scan(step, (C0, n0, m0), inputs)
    h = jnp.moveaxis(hs, 0, 2).reshape(B, H, T, MLSTM_DV)
    return h, (C, n, m)


def mlstm_bidir(q, k, v, gates, init_f, init_b):
    h_f, st_f = mlstm_scan(q, k, v, gates[0], jax.nn.log_sigmoid(gates[1]), *init_f)
    fl = lambda a: jnp.flip(a, axis=2)
    h_b, st_b = mlstm_scan(fl(q), fl(k), fl(v), fl(gates[2]), fl(jax.nn.log_sigmoid(gates[3])), *init_b)
    return h_f + fl(h_b), st_f, st_b


def swiglu(h, w_gu, w_down):
    g, u = jnp.split(h @ w_gu, 2, axis=-1)
    return (jax.nn.silu(g) * u) @ w_down


def trunk_layer(x, mods, norm1_w, w_in, gate_bias, q_norm_w, k_norm_w, mlstm_norm_w, w_out,
                norm2_w, w_gu, w_down, rope, ctx_kv, init_f, init_b):
    sh1, sc1, g1, sh2, sc2, g2 = mods
    B, T, _ = x.shape
    h = rmsnorm(x, norm1_w) * (1 + sc1) + sh1
    aq, ak, av, mq, mk, mv, mo, mg = jnp.split(h @ w_in, SPLIT_POINTS, axis=-1)
    aq = rmsnorm(aq.reshape(B, T, N_Q_HEADS, HEAD_DIM), q_norm_w)
    ak = rmsnorm(ak.reshape(B, T, N_KV_HEADS, HEAD_DIM), k_norm_w)
    av = av.reshape(B, T, N_KV_HEADS, HEAD_DIM)
    if rope is None:
        keys, vals = ak, av
    else:
        aq = apply_rope(aq, *rope)
        keys = jnp.concatenate([apply_rope(ak, *rope), ctx_kv[0].astype(ak.dtype)], axis=1)
        vals = jnp.concatenate([av, ctx_kv[1].astype(av.dtype)], axis=1)
    attn_o = attention(aq, keys, vals)
    heads = lambda a: a.reshape(B, T, MLSTM_HEADS, -1).transpose(0, 2, 1, 3).astype(F32)
    gates = (mg.reshape(B, T, 4, MLSTM_HEADS).astype(F32) + gate_bias.astype(F32)).transpose(2, 0, 3, 1)
    hm, st_f, st_b = mlstm_bidir(heads(mq), heads(mk) * (MLSTM_DK ** -0.5), heads(mv), gates, init_f, init_b)
    hm = rmsnorm(hm.transpose(0, 2, 1, 3), mlstm_norm_w.reshape(MLSTM_HEADS, MLSTM_DV))
    hm = (hm.reshape(B, T, MLSTM_WIDTH) * jax.nn.sigmoid(mo.astype(F32))).astype(x.dtype)
    mix = jnp.concatenate([attn_o, hm], axis=-1) @ w_out
    x = x + g1 * mix
    h2 = rmsnorm(x, norm2_w) * (1 + sc2) + sh2
    x = x + g2 * swiglu(h2, w_gu, w_down)
    return x, ak, av, st_f, st_b


def setup_inputs(seed: int = 0) -> dict:
    key = jax.random.key(seed)
    ks = jax.random.split(key, 24)
    nrm = lambda k, shape, s=1.0: jax.random.normal(k, shape, F32) * s
    base_gate = jnp.array([0.0, 1.0, 0.0, 1.0], F32)[:, None] * jnp.linspace(3.0, 6.0, MLSTM_HEADS, dtype=F32)[None, :]
    return {
        'x_prompt': nrm(ks[0], (BATCH, SEQ, D_MODEL)),
        'x_sample': nrm(ks[1], (DEC_BATCH, DEC_SEQ, D_MODEL)),
        'cache_k': nrm(ks[2], (DEC_BATCH, DEPTH, PAST_LEN, N_KV_HEADS, HEAD_DIM)),
        'cache_v': nrm(ks[3], (DEC_BATCH, DEPTH, PAST_LEN, N_KV_HEADS, HEAD_DIM)),
        'state_C': nrm(ks[4], (DEC_BATCH, DEPTH, 2, MLSTM_HEADS, MLSTM_DK, MLSTM_DV), 0.1),
        'state_n': nrm(ks[5], (DEC_BATCH, DEPTH, 2, MLSTM_HEADS, MLSTM_DK), 0.1),
        'state_m': nrm(ks[6], (DEC_BATCH, DEPTH, 2, MLSTM_HEADS), 0.5),
        'c': nrm(ks[7], (DEC_BATCH, D_MODEL)),
        'c_ctx': nrm(ks[8], (D_MODEL,)),
        'w_ada': nrm(ks[9], (DEPTH, D_MODEL, 6 * D_MODEL), 0.5 * D_MODEL ** -0.5),
        'b_ada': nrm(ks[10], (DEPTH, 6 * D_MODEL), 0.02),
        'norm1_w': 1.0 + nrm(ks[11], (DEPTH, D_MODEL), 0.05),
        'w_in': nrm(ks[12], (DEPTH, D_MODEL, N_IN), D_MODEL ** -0.5),
        'gate_bias': base_gate + nrm(ks[13], (DEPTH, 4, MLSTM_HEADS), 0.1),
        'q_norm_w': 1.0 + nrm(ks[14], (DEPTH, HEAD_DIM), 0.05),
        'k_norm_w': 1.0 + nrm(ks[15], (DEPTH, HEAD_DIM), 0.05),
        'mlstm_norm_w': 1.0 + nrm(ks[16], (DEPTH, MLSTM_WIDTH), 0.05),
        'w_out': nrm(ks[17], (DEPTH, D_MODEL, D_MODEL), D_MODEL ** -0.5),
        'norm2_w': 1.0 + nrm(ks[18], (DEPTH, D_MODEL), 0.05),
        'w_gu': nrm(ks[19], (DEPTH, D_MODEL, 2 * D_FF), D_MODEL ** -0.5),
        'w_down': nrm(ks[20], (DEPTH, D_FF, D_MODEL), D_FF ** -0.5),
        'final_norm_w': 1.0 + nrm(ks[21], (D_MODEL,), 0.05),
    }


def reference(x_prompt, x_sample, cache_k, cache_v, state_C, state_n, state_m, c, c_ctx,
              w_ada, b_ada, norm1_w, w_in, gate_bias, q_norm_w, k_norm_w, mlstm_norm_w, w_out,
              norm2_w, w_gu, w_down, final_norm_w):
    xp = x_prompt
    B = xp.shape[0]
    zero_state = (jnp.zeros((B, MLSTM_HEADS, MLSTM_DK, MLSTM_DV), F32),
                  jnp.zeros((B, MLSTM_HEADS, MLSTM_DK), F32),
                  jnp.zeros((B, MLSTM_HEADS), F32))
    ks_, vs_, Cs_, ns_, ms_ = [], [], [], [], []
    for l in range(DEPTH):
        mods = ada_mods(c_ctx, w_ada[l], b_ada[l])
        xp, k_l, v_l, st_f, st_b = trunk_layer(
            xp, mods, norm1_w[l], w_in[l], gate_bias[l], q_norm_w[l], k_norm_w[l], mlstm_norm_w[l],
            w_out[l], norm2_w[l], w_gu[l], w_down[l], None, None, zero_state, zero_state)
        ks_.append(k_l)
        vs_.append(v_l)
        Cs_.append(jnp.stack([st_f[0], st_b[0]], axis=1))
        ns_.append(jnp.stack([st_f[1], st_b[1]], axis=1))
        ms_.append(jnp.stack([st_f[2], st_b[2]], axis=1))
    y_prompt = rmsnorm(xp, final_norm_w)
    new_cache_k = jnp.stack(ks_, axis=1)
    new_cache_v = jnp.stack(vs_, axis=1)
    new_state_C = jnp.stack(Cs_, axis=1)
    new_state_n = jnp.stack(ns_, axis=1)
    new_state_m = jnp.stack(ms_, axis=1)

    xs = x_sample
    rope = axial_rope(xs.shape[1])
    for l in range(DEPTH):
        mods = ada_mods(c, w_ada[l], b_ada[l])
        init_f = (state_C[:, l, 0].astype(F32), state_n[:, l, 0].astype(F32), state_m[:, l, 0].astype(F32))
        init_b = (state_C[:, l, 1].astype(F32), state_n[:, l, 1].astype(F32), state_m[:, l, 1].astype(F32))
        xs, _, _, _, _ = trunk_layer(
            xs, mods, norm1_w[l], w_in[l], gate_bias[l], q_norm_w[l], k_norm_w[l], mlstm_norm_w[l],
            w_out[l], norm2_w[l], w_gu[l], w_down[l], rope, (cache_k[:, l], cache_v[:, l]), init_f, init_b)
    y_sample = rmsnorm(xs, final_norm_w)
    return (y_prompt, y_sample, new_cache_k, new_cache_v, new_state_C, new_state_n, new_state_m)
```

```python
import numpy as np
import concourse.bass as bass
import concourse.mybir as mybir
from concourse.bass_utils import run_bass_kernel_spmd

F32 = mybir.dt.float32
BF16 = mybir.dt.bfloat16
AF = mybir.ActivationFunctionType
ALU = mybir.AluOpType
AX = mybir.AxisListType

D = 1024
NT = 36
TT = NT * 128
NIN = 2832
DFF = 2816
NF = 22
EPS = 1e-6
KSC = 128.0 ** -0.5
NKEY = 38
import os as _os
NDS = 64
PRE_IN_ATT = int(_os.environ.get('PRE_IN_ATT', '1'))
STRICT = bool(int(_os.environ.get('KSTRICT', '1')))


class Tl:
    def __init__(s, h):
        s.h = h
        s.w = []
        s.r = []

    def __getitem__(s, i):
        return s.h[i]


class K:
    ENG = ("pe", "act", "dve", "pool", "sp")

    def __init__(s, nc):
        s.nc = nc
        s.ops = {e: [] for e in s.ENG}
        s.cnt = {e: 0 for e in s.ENG}
        s.seen = {e: {} for e in s.ENG}
        s.sem = {}
        s.dsems = [nc.alloc_semaphore(f"d_{i}") for i in range(NDS)]
        s.dcnt = [0] * NDS
        s.dptr = 0
        for e in s.ENG:
            s.sem[e] = nc.alloc_semaphore(f"s_{e}")
        s.sb_ptr = 0
        s.sb_phase = 0
        s.nalloc = 0

    def sb(s, shape, dt, name=None):
        esz = 4 if dt == F32 else 2
        n = 1
        for d_ in shape[1:]:
            n *= d_
        nbytes = (n * esz + 63) // 64 * 64
        off = s.sb_ptr
        s.sb_ptr += nbytes
        assert s.sb_ptr <= s.sb_top, (name, s.sb_ptr, s.sb_top)
        s.nalloc += 1
        h = s.nc.alloc_sbuf_tensor_at(f"{name or 't'}_{s.nalloc}", list(shape), dt, offset=off)
        return Tl(h)

    def phase_mem(s):
        s.sb_ptr = s.sb_phase

    def op(s, e, fn, reads=(), writes=(), inc=True):
        deps = []
        for t in reads:
            for wt in t.w:
                if wt[0] != e or e != "pe":
                    deps.append(wt)
        for t in writes:
            for wt in t.w:
                if wt[0] != e or (STRICT and e != "pe"):
                    deps.append(wt)
            for rt in t.r:
                if rt[0] != e or (STRICT and e != "pe"):
                    deps.append(rt)
        need = {}
        for key, val in deps:
            if val > need.get(key, 0):
                need[key] = val
        waits = []
        for key, val in need.items():
            if s.seen[e].get(key, 0) >= val:
                continue
            s.seen[e][key] = val
            waits.append((key, val))
        tok = (e, s.cnt[e] + 1)
        if inc:
            s.cnt[e] += 1
        s.ops[e].append((waits, fn, inc, None))
        for t in writes:
            t.w = [tok]
            t.r = []
        for t in reads:
            if t not in writes:
                t.r.append(tok)
                if len(t.r) > 24:
                    t.r = s._compact(t.r)
        return tok

    @staticmethod
    def _compact(r):
        best = {}
        for key, val in r:
            if val > best.get(key, 0):
                best[key] = val
        return list(best.items())

    def dma(s, out, in_, reads=(), writes=(), q="sp", slow=False):
        deps = []
        for t in reads:
            deps.extend(t.w)
        for t in writes:
            for wt in t.w:
                if not isinstance(wt[0], int):
                    deps.append(wt)
            deps.extend(t.r)
        need = {}
        for key, val in deps:
            if val > need.get(key, 0):
                need[key] = val
        waits = []
        for key, val in need.items():
            if s.seen[q].get(key, 0) >= val:
                continue
            s.seen[q][key] = val
            waits.append((key, val))
        si = s.dptr
        s.dptr = (s.dptr + 1) % len(s.dsems)
        if s.dcnt[si] > s.seen[q].get(si, 0):
            s.seen[q][si] = s.dcnt[si]
            waits.append((si, s.dcnt[si]))
        s.dcnt[si] += 16
        tok = (si, s.dcnt[si])
        s.ops[q].append((waits, (out, in_, slow), False, si))
        for t in writes:
            if t.w and all(isinstance(wt[0], int) for wt in t.w):
                t.w = t.w + [tok]
            else:
                t.w = [tok]
            t.r = []
        for t in reads:
            t.r.append(tok)
        return tok

    def barrier(s):
        waits = []
        for e in s.ENG:
            if e != "sp" and s.cnt[e] > s.seen["sp"].get(e, 0):
                s.seen["sp"][e] = s.cnt[e]
                waits.append((e, s.cnt[e]))
        for si in range(len(s.dsems)):
            if s.dcnt[si] > s.seen["sp"].get(si, 0):
                s.seen["sp"][si] = s.dcnt[si]
                waits.append((si, s.dcnt[si]))
        s.cnt["sp"] += 1
        v = s.cnt["sp"]
        s.ops["sp"].append((waits, "inc", True, None))
        for e in s.ENG:
            if e != "sp":
                s.seen[e]["sp"] = v
                s.ops[e].append(([("sp", v)], None, False, None))
                for si in range(len(s.dsems)):
                    s.seen[e][si] = s.dcnt[si]
                for e2 in s.ENG:
                    s.seen[e][e2] = max(s.seen[e].get(e2, 0), s.cnt[e2])

    def semof(s, key):
        return s.dsems[key] if isinstance(key, int) else s.sem[key]

    def emit(s, e, eng):
        for waits, fn, inc, si in s.ops[e]:
            for key, val in waits:
                eng.wait_ge(s.semof(key), val)
            if fn is None:
                continue
            if fn == "inc":
                eng.sem_inc(s.sem[e], 1)
                continue
            if si is not None:
                out, in_, slow = fn
                if slow:
                    ins = eng.dma_start(out=out, in_=in_, allow_slow_non_contiguous=True)
                else:
                    ins = eng.dma_start(out=out, in_=in_)
                ins.then_inc(s.dsems[si], 16)
                continue
            ins = fn(eng)
            if inc:
                ins.then_inc(s.sem[e], 1)


def build_nc(debug=False, stop=99):
    import os
    stop = int(os.environ.get('KSTOP', stop))
    nc = bass.Bass("TRN2", target_bir_lowering=False)
    k = K(nc)
    k.sb_ptr = (nc.sbuf_base + 63) // 64 * 64
    k.sb_top = nc.sbuf_top

    def din(name, shape, dt=F32):
        return nc.dram_tensor(name, list(shape), dt, kind="ExternalInput").ap()

    def dout(name, shape, dt=F32):
        return nc.dram_tensor(name, list(shape), dt, kind="ExternalOutput").ap()

    def dscr(name, shape, dt=BF16):
        return nc.dram_tensor(name, list(shape), dt, kind="ExternalOutput" if debug else "Internal").ap()

    X = din("x", [TT, D])
    CK = din("cache_k", [256, 128])
    CV = din("cache_v", [256, 128])
    SC = din("state_C", [2, 4, 128, 128])
    SN = din("state_n", [2, 4, 128])
    SM = din("state_m", [2, 4])
    COND = din("cond", [2, D])
    WADA = din("w_ada", [D, 6 * D])
    BADA = din("b_ada", [6 * D])
    N1W = din("norm1_w", [D])
    WIN = din("w_in", [D, NIN])
    GB = din("gate_bias", [4, 4])
    QNW = din("q_norm_w", [64])
    KNW = din("k_norm_w", [64])
    MNW = din("mlstm_norm_w", [512])
    WOUT = din("w_out", [D, D])
    N2W = din("norm2_w", [D])
    WGU = din("w_gu", [D, 2 * DFF])
    WDN = din("w_down", [DFF, D])
    FNW = din("final_norm_w", [D])
    ROPE = din("rope", [128, 32, 2, 64])
    IDENT = din("ident", [128, 128])
    MASKS = din("masks", [128, 2, 512])

    Y = dout("y", [TT, D])
    NK = dout("nk", [512, 128])
    NV = dout("nv", [512, 128])
    NC_ = dout("nC", [2, 2, 4, 128, 128])
    NN = dout("nn", [2, 2, 4, 128])
    NM = dout("nm", [2, 2, 4])

    MODS = dscr("mods", [2, 6 * D], F32)
    QTd = dscr("QTd", [4, 128, TT])
    MQTd = dscr("MQTd", [4, 128, TT])
    MKTd = dscr("MKTd", [4, 128, TT])
    MKd = dscr("MKd", [TT, 512])
    MVd = dscr("MVd", [TT, 520])
    MOd = dscr("MOd", [TT, 512])
    GATd = dscr("GATd", [4, 4, TT], F32)
    AOTd = dscr("AOTd", [4, 128, TT])
    HMTd = dscr("HMTd", [4, 128, TT])
    H2Td = dscr("H2Td", [8, 128, TT])
    X1d = dscr("X1d", [TT, D], F32)

    PSALL = nc.alloc_psum_tensor("psall", [128, 4096], F32)
    PS = [Tl(PSALL[:, i * 512:(i + 1) * 512]) for i in range(8)]

    ident = k.sb([128, 128], F32, "ident")
    ones4 = k.sb([4, 128], F32, "ones4")
    eye4 = k.sb([4, 4], F32, "eye4")
    modc = k.sb([128, 2, 6, 8], F32, "modc")
    G1 = k.sb([128, 2, 8], F32, "G1")
    G2 = k.sb([128, 2, 8], F32, "G2")
    n1c = k.sb([128, 8], F32, "n1c")
    n2c = k.sb([128, 8], F32, "n2c")
    k.mhalf = k.sb([128, 8], F32, "mhalf")
    k.op("dve", lambda e: e.memset(k.mhalf[:, :], -0.5), writes=[k.mhalf])
    sb_phase0 = k.sb_ptr
    KT2 = k.sb([128, 2, TT + 256], BF16, "KT2")
    VA = k.sb([128, NKEY, 2, 192], BF16, "VA")
    sb_phase0_a = k.sb_ptr
    Win = k.sb([128, 8, NIN], BF16, "Win")
    Wg = k.sb([128, 8, 128], BF16, "Wg")
    stgw = [k.sb([128, NIN // 2], F32, f"stgw{i}") for i in range(2)]
    k.sb_phase = k.sb_ptr
    k.op("pool", lambda e: e.memset(Wg[:, :, :], 0.0), writes=[Wg])

    k.dma(ident[:, :], IDENT[:, :], writes=[ident])
    k.op("dve", lambda e: e.memset(ones4[:, :], 1.0), writes=[ones4])
    k.dma(eye4[:, :], IDENT[0:4, 0:4], writes=[eye4])

    condT = k.sb([128, 8, 2], F32, "condT")
    sil = k.sb([128, 8, 2], F32, "sil")
    tmpc = k.sb([128, 8, 2], F32, "tmpc")
    mods_sb = k.sb([2, 6 * D], F32, "mods_sb")
    bada = k.sb([2, 6 * D], F32, "bada")
    wa = [k.sb([128, 512], F32, f"wa{i}") for i in range(4)]
    for j in range(2):
        k.dma(condT[:, :, j], COND[j].rearrange("(c p) -> p c", p=128), writes=[condT], slow=True)
    for j in range(2):
        k.dma(bada[j:j + 1, :], BADA.rearrange("(o n) -> o n", o=1), writes=[bada])
    k.op("act", lambda e: e.activation(out=tmpc[:, :, :], in_=condT[:, :, :], func=AF.Exp, scale=-1.0),
         reads=[condT], writes=[tmpc])
    k.op("dve", lambda e: e.tensor_scalar_add(out=tmpc[:, :, :], in0=tmpc[:, :, :], scalar1=1.0),
         reads=[tmpc], writes=[tmpc])
    k.op("dve", lambda e: e.reciprocal(out=tmpc[:, :, :], in_=tmpc[:, :, :]), reads=[tmpc], writes=[tmpc])
    k.op("dve", lambda e: e.tensor_tensor(out=sil[:, :, :], in0=condT[:, :, :], in1=tmpc[:, :, :], op=ALU.mult),
         reads=[condT, tmpc], writes=[sil])
    win_steps = []
    HW = NIN // 2
    for kc in range(8):
        for hf in range(2):
            def step(kc=kc, hf=hf):
                st = stgw[hf]
                k.dma(st[:, :], WIN[kc * 128:(kc + 1) * 128, hf * HW:(hf + 1) * HW], writes=[st])
                k.op("pool", lambda e: e.tensor_copy(out=Win[:, kc, hf * HW:(hf + 1) * HW], in_=st[:, :]),
                     reads=[st], writes=[Win])
                if hf == 1:
                    k.op("pool", lambda e: e.tensor_copy(
                        out=Wg[:, kc, :].rearrange("p (j w) -> p j w", w=32)[:, :, 0:4],
                        in_=st[:, HW - 16:HW].rearrange("p (j w) -> p j w", w=4)), reads=[st], writes=[Wg])
            win_steps.append(step)
    it = 0
    for n in range(12):
        pb = PS[n % 2]
        for kc in range(8):
            w_t = wa[it % 4]
            if it % 6 == 0 and win_steps:
                win_steps.pop(0)()
            it += 1
            k.dma(w_t[:, :], WADA[kc * 128:(kc + 1) * 128, n * 512:(n + 1) * 512], writes=[w_t])
            k.op("pe", lambda e, w_t=w_t, kc=kc, pb=pb: e.matmul(pb[0:2, :], lhsT=sil[:, kc, :], rhs=w_t[:, :],
                                                                start=(kc == 0), stop=(kc == 7)),
                 reads=[w_t, sil], writes=[pb], inc=True)
        k.op("dve", lambda e, n=n, pb=pb: e.tensor_tensor(out=mods_sb[:, n * 512:(n + 1) * 512], in0=pb[0:2, :],
                                                          in1=bada[:, n * 512:(n + 1) * 512], op=ALU.add),
             reads=[pb, bada], writes=[mods_sb])
    while win_steps:
        win_steps.pop(0)()
    k.dma(MODS[:, :], mods_sb[:, :], reads=[mods_sb])
    k.barrier()
    for j in range(2):
        for s6 in range(6):
            k.dma(modc[:, j, s6, :], MODS[j, s6 * D:(s6 + 1) * D].rearrange("(c p) -> p c", p=128),
                  writes=[modc], slow=True)
    k.dma(n1c[:, :], N1W.rearrange("(c p) -> p c", p=128), writes=[n1c], slow=True)
    k.dma(n2c[:, :], N2W.rearrange("(c p) -> p c", p=128), writes=[n2c], slow=True)
    for j in range(2):
        k.op("dve", lambda e, j=j: e.scalar_tensor_tensor(out=G1[:, j, :], in0=modc[:, j, 1, :], scalar=1.0,
                                                          in1=n1c[:, :], op0=ALU.add, op1=ALU.mult),
             reads=[modc, n1c], writes=[G1])
        k.op("dve", lambda e, j=j: e.scalar_tensor_tensor(out=G2[:, j, :], in0=modc[:, j, 4, :], scalar=1.0,
                                                          in1=n2c[:, :], op0=ALU.add, op1=ALU.mult),
             reads=[modc, n2c], writes=[G2])
    k.barrier()
    k.phase_mem()
    ctx = dict(locals())
    if stop >= 1:
        phase_a(ctx)
    if stop >= 2:
        phase_b1(ctx)
    if stop >= 3:
        phase_b2(ctx)
    if stop >= 4:
        phase_c1(ctx)
    if stop >= 5:
        phase_c2(ctx)
    k.barrier()

    with nc.allow_low_precision(reason="bf16 matmul operands by design"), nc.Block() as block:
        names = {"pe": "tensor", "act": "scalar", "dve": "vector", "pool": "gpsimd", "sp": "sync"}
        for e in K.ENG:
            getattr(block, names[e])(lambda eng, e=e: k.emit(e, eng))
    return nc


class NS:
    def __init__(s, d):
        s.__dict__.update(d)


def rstd_from_ss(k, ss, out, n_inv, width):
    k.op("act", lambda e: e.activation(out=out[:, 0:width], in_=ss[:, 0:width], func=AF.Ln, scale=n_inv, bias=EPS),
         reads=[ss], writes=[out])
    k.op("act", lambda e: e.activation(out=out[:, 0:width], in_=out[:, 0:width], func=AF.Exp, scale=-0.5),
         reads=[out], writes=[out])


def norm_to_hT(k, c, xt, hT, col0, Gc, SHc, bufs, pA, pB, defer=None):
    junk, ss, rstd, xs = bufs
    k.op("dve", lambda e: e.scalar_tensor_tensor(out=junk[:, :], in0=xt[:, :], scalar=1.0, in1=xt[:, :],
                                                 op0=ALU.mult, op1=ALU.mult, accum_out=ss[:, 0:1]),
         reads=[xt], writes=[junk, ss])
    rstd_from_ss(k, ss, rstd, 1.0 / D, 1)
    k.op("pool", lambda e: e.tensor_scalar(out=xs[:, :], in0=xt[:, :], scalar1=rstd[:, 0:1], scalar2=1.0,
                                           op0=ALU.mult, op1=ALU.mult),
         reads=[xt, rstd], writes=[xs])

    def pe_part():
        for half, pb in ((0, pA), (1, pB)):
            for cc in range(4):
                ch = half * 4 + cc
                k.op("pe", lambda e, ch=ch, cc=cc, pb=pb: e.transpose(pb[:, cc * 128:(cc + 1) * 128],
                                                                      xs[:, ch * 128:(ch + 1) * 128], c.ident[:, :]),
                     reads=[xs, c.ident], writes=[pb], inc=(cc == 3))
            for cc in range(4):
                ch = half * 4 + cc
                k.op("act", lambda e, ch=ch, cc=cc, pb=pb: e.activation(
                    out=hT[:, ch, col0:col0 + 128], in_=pb[:, cc * 128:(cc + 1) * 128], func=AF.Identity,
                    scale=Gc(ch), bias=SHc(ch)), reads=[pb, c.G1, c.G2, c.modc], writes=[hT])

    if defer is None:
        pe_part()
    else:
        defer.append(pe_part)


def phase_a(ctx):
    c = NS(ctx)
    k = c.k
    PS = c.PS
    KT2, VA, Win, Wg = c.KT2, c.VA, c.Win, c.Wg
    k.op("pool", lambda e: e.memset(VA[:, :, :, :], 1.0), writes=[VA])
    xb = [k.sb([128, D], F32, f"xb{i}") for i in range(3)]
    junk = k.sb([128, D], BF16, "junk")
    xs = [k.sb([128, D], F32, f"xs{i}") for i in range(3)]
    ss = k.sb([128, 8], F32, "ss")
    rstd = k.sb([128, 8], F32, "rstd")
    hT = [k.sb([128, 8, 512], BF16, f"hT{i}") for i in range(2)]
    ropet = [k.sb([128, 4, 2, 64], F32, f"rope{i}") for i in range(2)]
    wq_bc = k.sb([128, 64], F32, "wq_bc")
    wk_bc = k.sb([128, 64], F32, "wk_bc")
    gbcol = k.sb([128, 1], F32, "gbcol")
    qf = k.sb([128, 512], F32, "qf")
    sq = k.sb([128, 512], F32, "sq")
    qn2 = [k.sb([128, 512], F32, f"qn{i}") for i in range(2)]
    t1 = k.sb([128, 512], F32, "t1")
    t2 = k.sb([128, 512], F32, "t2")
    qr2 = [k.sb([128, 512], F32, f"qr{i}") for i in range(2)]
    kvf = [k.sb([128, 256], F32, f"kvf{i}") for i in range(2)]
    kn = [k.sb([128, 128], F32, f"kn{i}") for i in range(2)]
    kt1 = k.sb([128, 128], F32, "kt1")
    kt2 = k.sb([128, 128], F32, "kt2")
    kr = k.sb([128, 128], F32, "kr")
    kdup2 = [k.sb([128, 2, 2, 64], F32, f"kdup{i}") for i in range(2)]
    ssq = k.sb([128, 8], F32, "ssq")
    rsq = k.sb([128, 8], F32, "rsq")
    ssk = k.sb([128, 8], F32, "ssk")
    rsk = k.sb([128, 8], F32, "rsk")
    mot = k.sb([128, 512], F32, "mot")
    gsb = k.sb([128, 512], F32, "gsb")
    gtmp = k.sb([128, 512], F32, "gtmp")
    cstg = k.sb([128, 2, 128], F32, "cstg")
    QTs = [k.sb([128, 4, 512], BF16, f"QTs{i}") for i in range(1)]
    MKs = [k.sb([128, 4, 512], BF16, f"MKs{i}") for i in range(1)]
    MVs = [k.sb([128, 4, 4, 130], BF16, f"MVs{i}") for i in range(1)]
    MOs = [k.sb([128, 4, 512], BF16, f"MOs{i}") for i in range(1)]
    MQTs = [k.sb([128, 4, 512], BF16, f"MQTs{i}") for i in range(1)]
    MKTs = [k.sb([128, 4, 512], BF16, f"MKTs{i}") for i in range(1)]

    k.dma(wq_bc[:, :], c.QNW.partition_broadcast(128), writes=[wq_bc])
    k.dma(wk_bc[:, :], c.KNW.partition_broadcast(128), writes=[wk_bc])
    k.op("dve", lambda e: e.tensor_scalar_mul(out=wq_bc[:, :], in0=wq_bc[:, :], scalar1=0.125),
         reads=[wq_bc], writes=[wq_bc])
    k.op("dve", lambda e: e.memset(gbcol[:, :], 0.0), writes=[gbcol])
    for j in range(4):
        k.dma(gbcol[32 * j:32 * j + 4, 0:1], c.GB[j].rearrange("(h o) -> h o", o=1), writes=[gbcol], slow=True)
    k.op("pool", lambda e: e.memset(MVs[0][:, :, :, :], 1.0), writes=[MVs[0]])
    kdup = kdup2[0]
    for blk in range(2):
        k.dma(cstg[:, 0, :], c.CK[blk * 128:(blk + 1) * 128, :], writes=[cstg])
        k.dma(cstg[:, 1, :], c.CV[blk * 128:(blk + 1) * 128, :], writes=[cstg])
        k.op("dve", lambda e: e.tensor_copy(out=kdup[:, :, :, :],
                                            in_=cstg[:, 0, :].rearrange("p (g o d) -> p g o d", g=2, o=1)
                                            .to_broadcast([128, 2, 2, 64])), reads=[cstg], writes=[kdup])
        pk = PS[2]
        for g in range(2):
            k.op("pe", lambda e, g=g: e.transpose(pk[:, g * 128:(g + 1) * 128],
                                                  kdup[:, g, :, :].rearrange("p a d -> p (a d)"), c.ident[:, :]),
                 reads=[kdup, c.ident], writes=[pk], inc=(g == 1))
        col = TT + blk * 128
        k.op("act", lambda e, col=col: e.activation(out=KT2[:, :, col:col + 128],
                                                    in_=pk[:, 0:256].rearrange("p (g t) -> p g t", g=2),
                                                    func=AF.Copy), reads=[pk], writes=[KT2])
        k.op("dve", lambda e, blk=blk: e.tensor_copy(out=VA[:, 36 + blk, :, 64:128],
                                                     in_=cstg[:, 1, :].rearrange("p (g d) -> p g d", g=2)),
             reads=[cstg], writes=[VA])

    class Deferred(list):
        cur = 0

        def append(self, fn):
            list.append(self, (self.cur, fn))

    deferred = Deferred()

    def flush(upto=10 ** 9):
        while deferred and deferred[0][0] <= upto:
            deferred.pop(0)[1]()

    def tile_a(s, i):
        cj = 0 if s < 8 else 1
        ti = s * 4 + i
        xt = xb[ti % 3]
        k.dma(xt[:, :], c.X[ti * 128:(ti + 1) * 128, :], writes=[xt])
        norm_to_hT(k, c, xt, hT[s % 2], i * 128,
                   lambda ch, cj=cj: c.G1[:, cj, ch:ch + 1], lambda ch, cj=cj: c.modc[:, cj, 0, ch:ch + 1],
                   (junk, ss, rstd, xs[ti % 3]), PS[0], PS[1], defer=deferred)

    def st_body(s):
        cj = 0 if s < 8 else 1
        smp = s < 8
        h = hT[s % 2]
        if smp:
            rp = ropet[s % 2]
            k.dma(rp[:, :, :, :], c.ROPE[:, s * 4:(s + 1) * 4, :, :], writes=[rp])
        QT_, MK_, MV_, MO_, MQT_, MKT_ = QTs[0], MKs[0], MVs[0], MOs[0], MQTs[0], MKTs[0]
        if s == 0:
            for i in range(4):
                tile_a(0, i)
                flush()

        def tile_b(i):
            ti = s * 4 + i
            tsl = slice(i * 128, (i + 1) * 128)
            pq, pkv, pmk, pmv, pmo = PS[2], PS[3], PS[4], PS[5], PS[6]
            for (pb, c0, c1) in ((pmk, 1280, 1792), (pmv, 1792, 2304), (pmo, 2304, 2816), (pq, 0, 512),
                                 (pkv, 512, 768)):
                for kc in range(8):
                    k.op("pe", lambda e, pb=pb, c0=c0, c1=c1, kc=kc, tsl=tsl: e.matmul(
                        pb[:, 0:c1 - c0], lhsT=h[:, kc, tsl], rhs=Win[:, kc, c0:c1], start=(kc == 0), stop=(kc == 7)),
                        reads=[h, Win], writes=[pb], inc=(kc == 7))
            flush(ti - 2)
            deferred.cur = ti
            qn, qr, kdup = qn2[ti % 2], qr2[ti % 2], kdup2[ti % 2]
            k.op("act", lambda e, i=i: e.activation(out=MK_[:, i, :], in_=pmk[:, :], func=AF.Copy, scale=KSC),
                 reads=[pmk], writes=[MK_])
            k.op("dve", lambda e, i=i: e.tensor_copy(out=MV_[:, i, :, 0:128],
                                                     in_=pmv[:, :].rearrange("p (h d) -> p h d", h=4)),
                 reads=[pmv], writes=[MV_])
            k.op("act", lambda e: e.activation(out=mot[:, :], in_=pmo[:, :], func=AF.Exp, scale=-1.0),
                 reads=[pmo], writes=[mot])
            k.op("dve", lambda e: e.tensor_scalar_add(out=mot[:, :], in0=mot[:, :], scalar1=1.0),
                 reads=[mot], writes=[mot])
            k.op("dve", lambda e, i=i: e.reciprocal(out=MO_[:, i, :], in_=mot[:, :]), reads=[mot], writes=[MO_])
            kv = kvf[ti % 2]
            kk = kn[ti % 2]
            k.op("act", lambda e: e.activation(out=qf[:, :], in_=pq[:, :], func=AF.Copy), reads=[pq], writes=[qf])
            k.op("dve", lambda e: e.tensor_tensor(out=sq[:, :], in0=qf[:, :], in1=qf[:, :], op=ALU.mult),
                 reads=[qf], writes=[sq])
            k.op("dve", lambda e: e.tensor_reduce(out=ssq[:, 0:8], in_=sq[:, :].rearrange("p (h d) -> p h d", d=64),
                                                  axis=AX.X, op=ALU.add), reads=[sq], writes=[ssq])
            rstd_from_ss(k, ssq, rsq, 1.0 / 64, 8)
            k.op("dve", lambda e: e.tensor_tensor(out=qn[:, :].rearrange("p (h d) -> p h d", d=64),
                                                  in0=qf[:, :].rearrange("p (h d) -> p h d", d=64),
                                                  in1=rsq[:, 0:8].unsqueeze(2).to_broadcast([128, 8, 64]),
                                                  op=ALU.mult), reads=[qf, rsq], writes=[qn])
            k.op("dve", lambda e: e.tensor_tensor(out=qn[:, :].rearrange("p (h d) -> p h d", d=64),
                                                  in0=qn[:, :].rearrange("p (h d) -> p h d", d=64),
                                                  in1=wq_bc[:, :].unsqueeze(1).to_broadcast([128, 8, 64]),
                                                  op=ALU.mult), reads=[qn, wq_bc], writes=[qn])
            if smp:
                k.op("pool", lambda e, i=i: e.tensor_tensor(
                    out=t1[:, :].rearrange("p (h d) -> p h d", d=64),
                    in0=qn[:, :].rearrange("p (h d) -> p h d", d=64),
                    in1=rp[:, i, 0, :].unsqueeze(1).to_broadcast([128, 8, 64]), op=ALU.mult),
                    reads=[qn, rp], writes=[t1])
                for jj in range(2):
                    k.op("pool", lambda e, i=i, jj=jj: e.tensor_tensor(
                        out=t2[:, :].rearrange("p (h a j w) -> p h a j w", h=8, a=2, j=2)[:, :, :, jj, :],
                        in0=qn[:, :].rearrange("p (h a j w) -> p h a j w", h=8, a=2, j=2)[:, :, :, 1 - jj, :],
                        in1=rp[:, i, 1, :].rearrange("p (a j w) -> p a j w", a=2, j=2)[:, :, jj, :].unsqueeze(1)
                        .to_broadcast([128, 8, 2, 16]), op=ALU.mult),
                        reads=[qn, rp], writes=[t2])
                k.op("pool", lambda e: e.tensor_tensor(out=qr[:, :], in0=t1[:, :], in1=t2[:, :], op=ALU.add),
                     reads=[t1, t2], writes=[qr])
                qsrc = qr
            else:
                qsrc = qn
            def q_tr(qsrc=qsrc, tsl=tsl):
                pt = PS[7]
                for j in range(4):
                    k.op("pe", lambda e, j=j: e.transpose(pt[:, j * 128:(j + 1) * 128],
                                                          qsrc[:, j * 128:(j + 1) * 128], c.ident[:, :]),
                         reads=[qsrc, c.ident], writes=[pt], inc=(j == 3))
                k.op("act", lambda e: e.activation(out=QT_[:, :, tsl],
                                                   in_=pt[:, :].rearrange("p (j t) -> p j t", j=4),
                                                   func=AF.Copy), reads=[pt], writes=[QT_])
            deferred.append(q_tr)
            k.op("act", lambda e, kv=kv: e.activation(out=kv[:, :], in_=pkv[:, 0:256], func=AF.Copy),
                 reads=[pkv], writes=[kv])
            k.op("dve", lambda e, kv=kv: e.tensor_tensor(out=kt1[:, :], in0=kv[:, 0:128], in1=kv[:, 0:128],
                                                         op=ALU.mult), reads=[kv], writes=[kt1])
            k.op("dve", lambda e: e.tensor_reduce(out=ssk[:, 0:2], in_=kt1[:, :].rearrange("p (h d) -> p h d", d=64),
                                                  axis=AX.X, op=ALU.add), reads=[kt1], writes=[ssk])
            rstd_from_ss(k, ssk, rsk, 1.0 / 64, 2)
            k.op("dve", lambda e, kv=kv, kk=kk: e.tensor_tensor(
                out=kk[:, :].rearrange("p (h d) -> p h d", d=64),
                in0=kv[:, 0:128].rearrange("p (h d) -> p h d", d=64),
                in1=rsk[:, 0:2].unsqueeze(2).to_broadcast([128, 2, 64]), op=ALU.mult),
                reads=[kv, rsk], writes=[kk])
            k.op("dve", lambda e, kk=kk: e.tensor_tensor(
                out=kk[:, :].rearrange("p (h d) -> p h d", d=64),
                in0=kk[:, :].rearrange("p (h d) -> p h d", d=64),
                in1=wk_bc[:, :].unsqueeze(1).to_broadcast([128, 2, 64]), op=ALU.mult),
                reads=[kk, wk_bc], writes=[kk])
            if smp:
                k.op("pool", lambda e, i=i, kk=kk: e.tensor_tensor(
                    out=kt1[:, :].rearrange("p (h d) -> p h d", d=64),
                    in0=kk[:, :].rearrange("p (h d) -> p h d", d=64),
                    in1=rp[:, i, 0, :].unsqueeze(1).to_broadcast([128, 2, 64]), op=ALU.mult),
                    reads=[kk, rp], writes=[kt1])
                for jj in range(2):
                    k.op("pool", lambda e, i=i, kk=kk, jj=jj: e.tensor_tensor(
                        out=kt2[:, :].rearrange("p (h a j w) -> p h a j w", h=2, a=2, j=2)[:, :, :, jj, :],
                        in0=kk[:, :].rearrange("p (h a j w) -> p h a j w", h=2, a=2, j=2)[:, :, :, 1 - jj, :],
                        in1=rp[:, i, 1, :].rearrange("p (a j w) -> p a j w", a=2, j=2)[:, :, jj, :].unsqueeze(1)
                        .to_broadcast([128, 2, 2, 16]), op=ALU.mult),
                        reads=[kk, rp], writes=[kt2])
                k.op("pool", lambda e: e.tensor_tensor(out=kr[:, :], in0=kt1[:, :], in1=kt2[:, :], op=ALU.add),
                     reads=[kt1, kt2], writes=[kr])
                ksrc = kr
            else:
                ksrc = kk
                pt0 = (ti - 32) * 128
                k.dma(c.NK[pt0:pt0 + 128, :], kk[:, :], reads=[kk])
                k.dma(c.NV[pt0:pt0 + 128, :], kv[:, 128:256], reads=[kv])
            k.op("dve", lambda e, ksrc=ksrc: e.tensor_copy(
                out=kdup[:, :, :, :], in_=ksrc[:, :].rearrange("p (g o d) -> p g o d", g=2, o=1)
                .to_broadcast([128, 2, 2, 64])), reads=[ksrc], writes=[kdup])
            def k_tr(ti=ti):
                pk = PS[7]
                for g in range(2):
                    k.op("pe", lambda e, g=g: e.transpose(pk[:, g * 128:(g + 1) * 128],
                                                          kdup[:, g, :, :].rearrange("p a d -> p (a d)"),
                                                          c.ident[:, :]),
                         reads=[kdup, c.ident], writes=[pk], inc=(g == 1))
                k.op("act", lambda e: e.activation(out=KT2[:, :, ti * 128:(ti + 1) * 128],
                                                   in_=pk[:, 0:256].rearrange("p (g t) -> p g t", g=2),
                                                   func=AF.Copy), reads=[pk], writes=[KT2])
            deferred.append(k_tr)
            k.op("dve", lambda e, ti=ti, kv=kv: e.tensor_copy(out=VA[:, ti, :, 64:128],
                                                              in_=kv[:, 128:256].rearrange("p (g d) -> p g d", g=2)),
                 reads=[kv], writes=[VA])
        for i in range(4):
            tile_b(i)
            if s + 1 < 9:
                tile_a(s + 1, i)
        for hh in range(8):
            pb = PS[2 + hh % 4]
            c0 = 768 + hh * 128
            for kc in range(8):
                k.op("pe", lambda e, pb=pb, c0=c0, kc=kc: e.matmul(
                    pb[:, :], lhsT=Win[:, kc, c0:c0 + 128], rhs=h[:, kc, :], start=(kc == 0), stop=(kc == 7)),
                    reads=[h, Win], writes=[pb], inc=(kc == 7))
            if hh == 3:
                flush()
            if hh < 4:
                k.op("dve", lambda e, pb=pb, hh=hh: e.tensor_copy(out=MQT_[:, hh, :], in_=pb[:, :]),
                     reads=[pb], writes=[MQT_])
            else:
                k.op("act", lambda e, pb=pb, hh=hh: e.activation(out=MKT_[:, hh - 4, :], in_=pb[:, :], func=AF.Copy,
                                                                 scale=KSC), reads=[pb], writes=[MKT_])
        pg = PS[6]
        for kc in range(8):
            k.op("pe", lambda e, kc=kc: e.matmul(pg[:, :], lhsT=Wg[:, kc, :], rhs=h[:, kc, :], start=(kc == 0),
                                                 stop=(kc == 7)), reads=[h, Wg], writes=[pg], inc=(kc == 7))
        k.op("act", lambda e: e.activation(out=gsb[:, :], in_=pg[:, :], func=AF.Identity, bias=gbcol[:, 0:1]),
             reads=[pg, gbcol], writes=[gsb])
        for r0 in (32, 96):
            k.op("act", lambda e, r0=r0: e.activation(out=gtmp[r0:r0 + 4, :], in_=gsb[r0:r0 + 4, :], func=AF.Exp,
                                                      scale=-1.0), reads=[gsb], writes=[gtmp])
            k.op("act", lambda e, r0=r0: e.activation(out=gtmp[r0:r0 + 4, :], in_=gtmp[r0:r0 + 4, :], func=AF.Ln,
                                                      bias=1.0), reads=[gtmp], writes=[gtmp])
            k.op("dve", lambda e, r0=r0: e.tensor_scalar_mul(out=gsb[r0:r0 + 4, :], in0=gtmp[r0:r0 + 4, :],
                                                             scalar1=-1.0), reads=[gtmp], writes=[gsb])
        flush()
        tok = slice(s * 512, (s + 1) * 512)
        k.dma(c.QTd[:, :, tok].rearrange("c p t -> p c t"), QT_[:, :, :], reads=[QT_])
        k.dma(c.MQTd[:, :, tok].rearrange("c p t -> p c t"), MQT_[:, :, :], reads=[MQT_])
        k.dma(c.MKTd[:, :, tok].rearrange("c p t -> p c t"), MKT_[:, :, :], reads=[MKT_])
        k.dma(c.MKd[tok, :].rearrange("(i p) f -> p i f", p=128), MK_[:, :, :], reads=[MK_])
        k.dma(c.MVd[tok, :].rearrange("(i p) f -> p i f", p=128), MV_[:, :, :, :].rearrange("p i h d -> p i (h d)"),
              reads=[MV_])
        k.dma(c.MOd[tok, :].rearrange("(i p) f -> p i f", p=128), MO_[:, :, :], reads=[MO_])
        for j in range(4):
            k.dma(c.GATd[j, :, tok], gsb[32 * j:32 * j + 4, :], reads=[gsb])
    for s in range(9):
        st_body(s)
    k.barrier()
    k.sb_phase = c.sb_phase0_a
    k.phase_mem()


def _consts():
    f = np.float32
    tok = np.arange(4096)
    row = (tok // 64).astype(f)
    col = (tok % 64).astype(f)
    inv = (f(10000.0) ** (-np.arange(0, 32, 2, dtype=f) / f(32))).astype(f)
    ang = np.stack([row[:, None] * inv[None, :], col[:, None] * inv[None, :]], axis=1).astype(f)
    cs, sn = np.cos(ang).astype(f), np.sin(ang).astype(f)
    Cf = np.stack([cs, cs], axis=2)
    Ss = np.stack([-sn, sn], axis=2)
    tab = np.stack([Cf.reshape(4096, 64), Ss.reshape(4096, 64)], axis=1)
    rope = np.ascontiguousarray(tab.reshape(32, 128, 2, 64).transpose(1, 0, 2, 3))
    ident = np.eye(128, dtype=f)
    sidx = np.arange(128)[:, None]
    lidx = np.arange(128)[None, :]
    mf = (lidx >= sidx).astype(f)
    mb = (lidx <= sidx).astype(f)
    masks = np.stack([np.tile(mf, (1, 4)), np.tile(mb, (1, 4))], axis=1)
    return rope, ident, np.ascontiguousarray(masks)


def make_in_maps(inp):
    rope, ident, masks = _consts()
    f = np.float32
    c = lambda a: np.ascontiguousarray(np.asarray(a), dtype=f)
    maps = []
    for b in range(8):
        x = np.concatenate([inp["x_sample"][b], inp["x_prompt"][2 * b], inp["x_prompt"][2 * b + 1]], axis=0)
        m = {
            "x": c(x),
            "cache_k": c(np.asarray(inp["cache_k"])[b, 0].reshape(256, 128)),
            "cache_v": c(np.asarray(inp["cache_v"])[b, 0].reshape(256, 128)),
            "state_C": c(np.asarray(inp["state_C"])[b, 0]),
            "state_n": c(np.asarray(inp["state_n"])[b, 0]),
            "state_m": c(np.asarray(inp["state_m"])[b, 0]),
            "cond": c(np.stack([np.asarray(inp["c"])[b], np.asarray(inp["c_ctx"])], axis=0)),
            "w_ada": c(np.asarray(inp["w_ada"])[0]), "b_ada": c(np.asarray(inp["b_ada"])[0]),
            "norm1_w": c(np.asarray(inp["norm1_w"])[0]), "w_in": c(np.asarray(inp["w_in"])[0]),
            "gate_bias": c(np.asarray(inp["gate_bias"])[0]), "q_norm_w": c(np.asarray(inp["q_norm_w"])[0]),
            "k_norm_w": c(np.asarray(inp["k_norm_w"])[0]), "mlstm_norm_w": c(np.asarray(inp["mlstm_norm_w"])[0]),
            "w_out": c(np.asarray(inp["w_out"])[0]), "norm2_w": c(np.asarray(inp["norm2_w"])[0]),
            "w_gu": c(np.asarray(inp["w_gu"])[0]), "w_down": c(np.asarray(inp["w_down"])[0]),
            "final_norm_w": c(inp["final_norm_w"]),
            "rope": rope, "ident": ident, "masks": masks,
        }
        maps.append(m)
    return maps


_NC = None


def kernel(**inp):
    global _NC
    if _NC is None:
        _NC = build_nc()
    maps = make_in_maps(inp)
    res = run_bass_kernel_spmd(_NC, maps, core_ids=list(range(8)))
    R = res.results
    f = np.float32
    y_s = np.stack([R[b]["y"][0:4096] for b in range(8)], axis=0).astype(f)
    y_p = np.stack([R[b]["y"][4096 + 256 * j:4096 + 256 * (j + 1)] for b in range(8) for j in range(2)], axis=0).astype(f)
    nk = np.stack([R[b]["nk"][256 * j:256 * (j + 1)].reshape(256, 2, 64) for b in range(8) for j in range(2)], axis=0)
    nv = np.stack([R[b]["nv"][256 * j:256 * (j + 1)].reshape(256, 2, 64) for b in range(8) for j in range(2)], axis=0)
    nC = np.stack([R[b]["nC"][j] for b in range(8) for j in range(2)], axis=0)
    nn = np.stack([R[b]["nn"][j] for b in range(8) for j in range(2)], axis=0)
    nm = np.stack([R[b]["nm"][j] for b in range(8) for j in range(2)], axis=0)
    return (y_p, y_s, nk[:, None].astype(f), nv[:, None].astype(f), nC[:, None].astype(f), nn[:, None].astype(f),
            nm[:, None].astype(f))


def phase_b1(ctx):
    c = NS(ctx)
    k = c.k
    PS = c.PS
    KT2, VA = c.KT2, c.VA
    pre_gen, b2_main = make_b2(ctx)
    ctx["b2_main"] = b2_main
    k.sb_phase = k.sb_ptr
    pre = pre_gen()
    SP = [Tl(c.PSALL[:, b * 1024:(b + 1) * 1024]) for b in range(2)]
    QTg = [k.sb([128, 4, 512], BF16, f"QTg{i}") for i in range(2)]
    PTP = [k.sb([128, 1024], BF16, f"PT{i}") for i in range(3)]
    AO = [k.sb([128, 4, 512], BF16, f"AO{i}") for i in range(2)]
    rden = [k.sb([128, 512], F32, f"rden{i}") for i in range(2)]
    obs = k.sb([128, 512], F32, "obs")
    groups = [(g * 512, 512, list(range(32)) + [36, 37]) for g in range(8)]
    groups += [(4096, 256, [32, 33]), (4352, 256, [34, 35])]
    cnt = [0]

    def group_body(gi, t0, nq, kbs):
        qt = QTg[gi % 2]
        ao = AO[gi % 2]
        k.dma(qt[:, :, 0:nq], c.QTd[:, :, t0:t0 + nq].rearrange("c p t -> p c t"), writes=[qt])
        iters = [(j, idx, kb) for j in range(4) for idx, kb in enumerate(kbs)]
        base = cnt[0]
        cnt[0] += len(iters)

        def emit_s(n):
            j, idx, kb = iters[n]
            g = j // 2
            kcol = kb * 128 if kb < 36 else TT + (kb - 36) * 128
            sp = SP[(base + n) % 2]
            k.op("pe", lambda e: e.matmul(sp[:, 0:nq], lhsT=KT2[0:64, g, kcol:kcol + 128], rhs=qt[0:64, j, 0:nq],
                                          start=True, stop=True), reads=[KT2, qt], writes=[sp], inc=False)
            k.op("pe", lambda e: e.matmul(sp[:, 512:512 + nq], lhsT=KT2[64:128, g, kcol:kcol + 128],
                                          rhs=qt[64:128, j, 0:nq], start=True, stop=True),
                 reads=[KT2, qt], writes=[sp])

        def emit_rest(n):
            j, idx, kb = iters[n]
            g = j // 2
            sp = SP[(base + n) % 2]
            pt = PTP[(base + n) % 3]
            oa, ob = (PS[4], PS[6])[j % 2], PS[5]
            k.op("act", lambda e: e.activation(out=pt[:, :].rearrange("p (a t) -> p a t", a=2)[:, :, 0:nq],
                                               in_=sp[:, :].rearrange("p (a t) -> p a t", a=2)[:, :, 0:nq],
                                               func=AF.Exp), reads=[sp], writes=[pt])
            first, last = idx == 0, idx == len(kbs) - 1
            k.op("pe", lambda e: e.matmul(oa[:, 0:nq], lhsT=VA[:, kb, g, 64:192], rhs=pt[:, 0:nq], start=first,
                                          stop=last), reads=[VA, pt], writes=[oa], inc=False)
            k.op("pe", lambda e: e.matmul(ob[:, 0:nq], lhsT=VA[:, kb, g, 0:128], rhs=pt[:, 512:512 + nq],
                                          start=first, stop=last), reads=[VA, pt], writes=[ob])
            if last:
                ra, rb = rden[0], rden[1]
                k.op("dve", lambda e: e.tensor_copy(out=obs[:, 0:nq], in_=ob[:, 0:nq]), reads=[ob], writes=[obs])
                k.op("dve", lambda e: e.reciprocal(out=rb[64:128, 0:nq], in_=obs[0:64, 0:nq]), reads=[obs],
                     writes=[rb])
                k.op("dve", lambda e: e.tensor_tensor(out=ao[64:128, j, 0:nq], in0=obs[64:128, 0:nq],
                                                      in1=rb[64:128, 0:nq], op=ALU.mult), reads=[obs, rb],
                     writes=[ao])
                k.op("dve", lambda e: e.reciprocal(out=ra[0:64, 0:nq], in_=oa[64:128, 0:nq]), reads=[oa], writes=[ra])
                k.op("dve", lambda e: e.tensor_tensor(out=ao[0:64, j, 0:nq], in0=oa[0:64, 0:nq], in1=ra[0:64, 0:nq],
                                                      op=ALU.mult), reads=[oa, ra], writes=[ao])

        emit_s(0)
        for n in range(len(iters)):
            if n + 1 < len(iters):
                emit_s(n + 1)
            emit_rest(n)
            if PRE_IN_ATT:
                next(pre, None)
        k.dma(c.AOTd[:, :, t0:t0 + nq].rearrange("c p t -> p c t"), ao[:, :, 0:nq], reads=[ao])

    for gi, (t0, nq, kbs) in enumerate(groups):
        group_body(gi, t0, nq, kbs)
    for _ in pre:
        pass
    k.barrier()
    k.phase_mem()


def make_b2(ctx):
    c = NS(ctx)
    k = c.k
    PS = c.PS
    pS = PS[0]
    pGd, pId, pTd = (PS[1], PS[4]), (PS[2], PS[5]), (PS[3], PS[6])
    pX = PS[7]
    Cst = [k.sb([128, 4, 130], F32, f"Cst{i}") for i in range(2)]
    Csnap = k.sb([128, NT, 4, 130], BF16, "Csnap")
    mst = [k.sb([4, 2], F32, f"mst{i}") for i in range(2)]
    mbprev = k.sb([4, NT + 4], F32, "mbprev")
    GT = [k.sb([4, 4, 128], F32, f"GT{i}") for i in range(2)]
    kk = [k.sb([128, 4, 128], BF16, f"kk{i}") for i in range(2)]
    va = [k.sb([128, 4, 130], BF16, f"va{i}") for i in range(2)]
    rows = [[k.sb([4, 128], F32, f"row{d}_{i}") for i in range(8)] for d in range(2)]
    dg = [k.sb([4, 4], F32, f"dg{d}") for d in range(2)]
    cols = [k.sb([128, 24], F32, f"cols{d}") for d in range(2)]
    kw = k.sb([128, 512], BF16, "kw")
    cols1p = [cols[1], k.sb([128, 24], F32, "cols1b")]
    ident, ones4, eye4 = c.ident, c.ones4, c.eye4
    maskneg = identb = None
    masks = mnw = Cbf = qT = kT = mo = blk = Dsb = swT = qI = hd = hm = sqh = ss4 = rs4 = hmT = None

    def late_alloc():
        nonlocal maskneg, identb
        nonlocal masks, mnw, Cbf, qT, kT, mo, blk, Dsb, swT, qI, hd, hm, sqh, ss4, rs4, hmT
        masks = k.sb([128, 2, 512], F32, "masks")
        mnw = k.sb([128, 512], F32, "mnw")
        Cbf = k.sb([128, 4, 130], BF16, "Cbf")
        qT = [k.sb([128, 4, 128], BF16, f"qT{i}") for i in range(2)]
        kT = [k.sb([128, 4, 128], BF16, f"kT{i}") for i in range(2)]
        mo = [k.sb([128, 512], BF16, f"mo{i}") for i in range(2)]
        blk = [[k.sb([4, 4, 128], F32, f"blk{d}_{i}") for i in range(2)] for d in range(2)]
        Dsb = [k.sb([128, 512], F32, f"Dsb{d}") for d in range(2)]
        swT = [k.sb([128, 512], BF16, f"swT{d}") for d in range(2)]
        qI = [k.sb([128, 512], BF16, f"qI{d}") for d in range(2)]
        hd = [k.sb([128, 512], F32, f"hd{d}") for d in range(2)]
        hm = k.sb([128, 512], F32, "hm")
        sqh = k.sb([128, 512], F32, "sqh")
        ss4 = k.sb([128, 8], F32, "ss4")
        rs4 = k.sb([128, 8], F32, "rs4")
        hmT = [k.sb([128, 4, 128], BF16, f"hmT{i}") for i in range(2)]
        maskneg = k.sb([128, 2, 512], BF16, "maskneg")
        identb = k.sb([128, 128], BF16, "identb")
        k.dma(masks[:, :, :], c.MASKS[:, :, :], writes=[masks])
        k.dma(mnw[:, :], c.MNW.partition_broadcast(128), writes=[mnw])
        k.op("dve", lambda e: e.tensor_scalar(out=maskneg[:, :, :], in0=masks[:, :, :], scalar1=-1.0, scalar2=30000.0,
                                              op0=ALU.add, op1=ALU.mult), reads=[masks], writes=[maskneg])
        k.op("dve", lambda e: e.tensor_copy(out=identb[:, :], in_=ident[:, :]), reads=[ident], writes=[identb])

    def load_chunk(ci, full):
        tok = slice(ci * 128, (ci + 1) * 128)
        i2 = ci % 2
        k.dma(GT[i2][:, :, :], c.GATd[:, :, tok].rearrange("t h n -> h t n"), writes=[GT[i2]])
        k.dma(kk[i2][:, :, :], c.MKd[tok, :].rearrange("p (h d) -> p h d", h=4), writes=[kk[i2]])
        k.dma(va[i2][:, :, :], c.MVd[tok, :].rearrange("p (h d) -> p h d", h=4), writes=[va[i2]])
        if full:
            k.dma(qT[i2][:, :, :], c.MQTd[:, :, tok].rearrange("h p t -> p h t"), writes=[qT[i2]])
            k.dma(kT[i2][:, :, :], c.MKTd[:, :, tok].rearrange("h p t -> p h t"), writes=[kT[i2]])
            k.dma(mo[i2][:, :], c.MOd[tok, :], writes=[mo[i2]])

    def gate_prep(d, gt, mprev, full, upd, cl=None, pT=None):
        mt, mc = mprev
        rb_, ra_, rg_, rng_, rin_, rgu_, rw_, rt_ = rows[d]
        pT = pT or pTd[d]
        cl = cl or cols[d]
        lf = lambda: gt[:, 1 + 2 * d, :]
        ig = lambda: gt[:, 2 * d, :]
        rv = (lambda ap: ap) if d == 0 else (lambda ap: ap[:, ::-1])
        last = 127 if d == 0 else 0
        k.op("dve", lambda e: e.tensor_tensor_scan(out=rv(rb_[:, :]), data0=rv(ones4[:, :]), data1=rv(lf()),
                                                   initial=0.0, op0=ALU.mult, op1=ALU.add),
             reads=[gt, ones4], writes=[rb_])
        yield
        k.op("dve", lambda e: e.tensor_tensor(out=ra_[:, :], in0=ig(), in1=rb_[:, :], op=ALU.subtract),
             reads=[gt, rb_], writes=[ra_])
        yield
        k.op("pe", lambda e: e.transpose(pT[:, 0:4], ra_[:, :], ident[0:4, 0:4]), reads=[ra_, ident], writes=[pT])
        k.op("dve", lambda e: e.tensor_tensor_scan(out=rv(rg_[:, :]), data0=rv(ra_[:, :]), data1=rv(ra_[:, :]),
                                                   initial=mt[:, mc:mc + 1], op0=ALU.max, op1=ALU.max),
             reads=[ra_, mt], writes=[rg_])
        yield
        k.op("dve", lambda e: e.tensor_scalar_mul(out=rng_[:, :], in0=rg_[:, :], scalar1=-1.0),
             reads=[rg_], writes=[rng_])
        if full:
            k.op("dve", lambda e: e.tensor_tensor(out=rt_[:, :], in0=rb_[:, :], in1=rg_[:, :], op=ALU.add),
                 reads=[rb_, rg_], writes=[rt_])
            yield
            bk0 = blk[d][0]
            k.op("dve", lambda e: e.tensor_tensor(
                out=bk0[:, :, :], in0=rng_[:, :].unsqueeze(1).to_broadcast([4, 4, 128]),
                in1=eye4[:, :].unsqueeze(2).to_broadcast([4, 4, 128]), op=ALU.mult),
                reads=[rng_, eye4], writes=[bk0])
            yield
            k.op("pe", lambda e: e.matmul(pGd[d][:, :], lhsT=ones4[:, :],
                                          rhs=bk0[:, :, :].rearrange("k h l -> k (h l)"),
                                          start=True, stop=False), reads=[ones4, bk0], writes=[pGd[d]], inc=False)
            k.op("pe", lambda e: e.matmul(pGd[d][:, :], lhsT=identb[:, :], rhs=maskneg[:, d, :],
                                          start=False, stop=True), reads=[identb, maskneg], writes=[pGd[d]])
        yield
        k.op("act", lambda e: e.activation(out=rin_[:, :], in_=rng_[:, :], func=AF.Exp, bias=mt[:, mc:mc + 1]),
             reads=[rng_, mt], writes=[rin_])
        if full:
            yield
            bk1 = blk[d][1]
            k.op("dve", lambda e: e.tensor_tensor(
                out=bk1[:, :, :], in0=rin_[:, :].unsqueeze(1).to_broadcast([4, 4, 128]),
                in1=eye4[:, :].unsqueeze(2).to_broadcast([4, 4, 128]), op=ALU.mult),
                reads=[rin_, eye4], writes=[bk1])
            yield
            k.op("pe", lambda e: e.matmul(pId[d][:, :], lhsT=ones4[:, :],
                                          rhs=bk1[:, :, :].rearrange("k h l -> k (h l)"),
                                          start=True, stop=True), reads=[ones4, bk1], writes=[pId[d]])
        if full:
            k.op("act", lambda e: e.activation(out=rgu_[:, :], in_=rt_[:, :], func=AF.Exp, scale=-1.0),
                 reads=[rt_], writes=[rgu_])
        if upd:
            k.op("act", lambda e: e.activation(out=rw_[:, :], in_=ra_[:, :], func=AF.Exp,
                                               bias=rng_[:, last:last + 1]), reads=[ra_, rng_], writes=[rw_])
        yield
        if full:
            k.op("pe", lambda e: e.transpose(pT[:, 4:8], rgu_[:, :], ident[0:4, 0:4]), reads=[rgu_, ident],
                 writes=[pT])
        if upd:
            k.op("pe", lambda e: e.transpose(pT[:, 8:12], rw_[:, :], ident[0:4, 0:4]), reads=[rw_, ident],
                 writes=[pT])
            k.op("dve", lambda e: e.tensor_scalar(out=dg[d][:, :], in0=eye4[:, :], scalar1=rin_[:, last:last + 1],
                                                  scalar2=None, op0=ALU.mult), reads=[eye4, rin_], writes=[dg[d]])
            yield
            k.op("pe", lambda e: e.matmul(pT[:, 12:16], lhsT=ones4[:, :], rhs=dg[d][:, :], start=True, stop=True),
                 reads=[ones4, dg[d]], writes=[pT])
        yield
        hi = 16 if upd else 8
        if not full and upd:
            k.op("dve", lambda e: e.tensor_copy(out=cl[:, 0:4], in_=pT[:, 0:4]), reads=[pT], writes=[cl])
            k.op("dve", lambda e: e.tensor_copy(out=cl[:, 8:16], in_=pT[:, 8:16]), reads=[pT], writes=[cl])
        else:
            k.op("dve", lambda e: e.tensor_copy(out=cl[:, 0:hi], in_=pT[:, 0:hi]), reads=[pT], writes=[cl])
        yield

    def new_m(d, mt_out, mc_out):
        rb_, rg_ = rows[d][0], rows[d][2]
        last = 127 if d == 0 else 0
        k.op("dve", lambda e: e.tensor_tensor(out=mt_out[:, mc_out:mc_out + 1], in0=rb_[:, last:last + 1],
                                              in1=rg_[:, last:last + 1], op=ALU.add),
             reads=[rb_, rg_], writes=[mt_out])

    def state_update(d, kk_, va_, banks=None, cl=None, off=0):
        banks = banks or (pX, pId[d])
        Cs = Cst[d]
        cl = cl or cols[d]
        k.op("dve", lambda e: e.tensor_tensor(out=kw[:, :].rearrange("p (h d) -> p h d", h=4), in0=kk_[:, :, :],
                                              in1=cl[:, 8:12].unsqueeze(2).to_broadcast([128, 4, 128]),
                                              op=ALU.mult), reads=[kk_, cl], writes=[kw])
        yield
        for h in range(4):
            pd = banks[h % 2]
            k.op("pe", lambda e, h=h, pd=pd: e.matmul(pd[:, off:off + 129], lhsT=kw[:, h * 128:(h + 1) * 128],
                                                      rhs=va_[:, h, 0:129], start=True, stop=True),
                 reads=[kw, va_], writes=[pd])
            yield
            k.op("dve", lambda e, h=h, pd=pd: e.scalar_tensor_tensor(
                out=Cs[:, h, 0:129], in0=Cs[:, h, 0:129], scalar=cl[:, 12 + h:13 + h], in1=pd[:, off:off + 129],
                op0=ALU.mult, op1=ALU.add), reads=[Cs, cl, pd], writes=[Cs])
            yield

    def run(*gens):
        gens = list(gens)
        while gens:
            for g in list(gens):
                try:
                    next(g)
                except StopIteration:
                    gens.remove(g)

    def init_dir(seq, d):
        Cs = Cst[d]
        if seq == 0:
            k.op("pool", lambda e: e.memset(Cs[:, :, :], 0.0), writes=[Cs])
            k.op("dve", lambda e: e.memset(mst[d][:, :], 0.0), writes=[mst[d]])
            k.dma(Cs[:, :, 0:128], c.SC[d].rearrange("h p e -> p h e"), writes=[Cs])
            k.dma(Cs[:, :, 128], c.SN[d].rearrange("h p -> p h"), writes=[Cs], slow=True)
            k.dma(mst[d][:, 0:1], c.SM[d].rearrange("(h o) -> h o", o=1), writes=[mst[d]], slow=True)
        else:
            k.op("pool", lambda e: e.memset(Cs[:, :, :], 0.0), writes=[Cs])
            k.op("dve", lambda e: e.memset(mst[d][:, :], 0.0), writes=[mst[d]])

    def store_state(seq, d):
        p = seq - 1
        Cs = Cst[d]
        k.dma(c.NC_[p, d].rearrange("h p e -> p h e"), Cs[:, :, 0:128], reads=[Cs])
        k.dma(c.NN[p, d].rearrange("h p -> p h"), Cs[:, :, 128], reads=[Cs], slow=True)
        k.dma(c.NM[p, d].rearrange("(h o) -> h o", o=1), mst[d][:, 0:1], reads=[mst[d]], slow=True)

    def pre_G(ci):
        i2 = ci % 2
        load_chunk(ci, False)
        yield
        k.op("dve", lambda e: e.tensor_copy(out=mbprev[:, ci:ci + 1], in_=mst[1][:, 0:1]), reads=[mst[1]],
             writes=[mbprev])
        yield
        yield from gate_prep(1, GT[i2], (mst[1], 0), False, True, cl=cols1p[i2], pT=pX)
        new_m(1, mst[1], 0)
        yield

    def pre_U(ci):
        i2 = ci % 2
        k.op("act", lambda e: e.activation(out=Csnap[:, ci, :, :], in_=Cst[1][:, :, :], func=AF.Copy),
             reads=[Cst[1]], writes=[Csnap])
        yield
        yield from state_update(1, kk[i2], va[i2], banks=(pX, pX), cl=cols1p[i2], off=128)

    def zip2(g1, g2):
        gens = [g for g in (g1, g2) if g is not None]
        while gens:
            for g in list(gens):
                try:
                    next(g)
                except StopIteration:
                    gens.remove(g)
            yield

    seqs = [(0, list(range(32))), (1, [32, 33]), (2, [34, 35])]

    def pre_gen():
        for seq, chunks in seqs:
            init_dir(seq, 1)
            yield
            order = list(reversed(chunks))
            yield from pre_G(order[0])
            for n, ci in enumerate(order):
                nxt = pre_G(order[n + 1]) if n + 1 < len(order) else None
                yield from zip2(nxt, pre_U(ci))
            if seq > 0:
                store_state(seq, 1)
                yield

    def dir_chain(d, ci, q_, kk_, va_, gt):
        mprev = (mst[0], 0) if d == 0 else (mbprev, ci)
        rng_, rin_ = rows[d][3], rows[d][4]
        pG, pI, pT, cl = pGd[d], pId[d], pTd[d], cols[d]
        yield from gate_prep(d, gt, mprev, True, d == 0)
        if d == 0:
            new_m(0, mst[0], 0)
        for h in range(4):
            k.op("act", lambda e, h=h: e.activation(out=Dsb[d][:, h * 128:(h + 1) * 128],
                                                    in_=pG[:, h * 128:(h + 1) * 128], func=AF.Exp,
                                                    bias=cl[:, h:h + 1]), reads=[pG, cl], writes=[Dsb[d]])
        k.op("dve", lambda e: e.tensor_tensor(out=qI[d][:, :], in0=pI[:, :],
                                              in1=q_[:, :, :].rearrange("p h t -> p (h t)"), op=ALU.mult),
             reads=[pI, q_], writes=[qI[d]])
        yield
        k.op("dve", lambda e: e.tensor_tensor(out=swT[d][:, :], in0=pS[:, :], in1=Dsb[d][:, :], op=ALU.mult),
             reads=[pS, Dsb[d]], writes=[swT[d]])
        if d == 0:
            k.op("act", lambda e: e.activation(out=Cbf[:, :, :], in_=Cst[0][:, :, :], func=AF.Copy),
                 reads=[Cst[0]], writes=[Cbf])
        yield
        for h in range(4):
            cprev = (lambda h=h: Cbf[:, h, :]) if d == 0 else (lambda h=h: Csnap[:, ci, h, :])
            ctile = Cbf if d == 0 else Csnap
            k.op("pe", lambda e, h=h: e.matmul(pG[:, h * 128:(h + 1) * 128], lhsT=swT[d][:, h * 128:(h + 1) * 128],
                                               rhs=va_[:, h, 0:128], start=True, stop=False),
                 reads=[swT[d], va_], writes=[pG], inc=False)
            k.op("pe", lambda e, h=h, cprev=cprev: e.matmul(pG[:, h * 128:(h + 1) * 128],
                                                            lhsT=qI[d][:, h * 128:(h + 1) * 128],
                                                            rhs=cprev()[:, 0:128], start=False, stop=True),
                 reads=[qI[d], ctile], writes=[pG], inc=False)
            k.op("pe", lambda e, h=h: e.matmul(pT[:, 16 + h:17 + h], lhsT=swT[d][:, h * 128:(h + 1) * 128],
                                               rhs=va_[:, h, 128:129], start=True, stop=False),
                 reads=[swT[d], va_], writes=[pT], inc=False)
            k.op("pe", lambda e, h=h, cprev=cprev: e.matmul(pT[:, 16 + h:17 + h],
                                                            lhsT=qI[d][:, h * 128:(h + 1) * 128],
                                                            rhs=cprev()[:, 128:129], start=False, stop=True),
                 reads=[qI[d], ctile], writes=[pT])
            yield
        k.op("dve", lambda e: e.tensor_scalar(out=cl[:, 20:24], in0=pT[:, 16:20], scalar1=-1.0, scalar2=None,
                                              op0=ALU.mult), reads=[pT], writes=[cl])
        yield
        k.op("dve", lambda e: e.tensor_tensor(out=cl[:, 16:20], in0=pT[:, 16:20], in1=cl[:, 20:24], op=ALU.max),
             reads=[pT, cl], writes=[cl])
        yield
        k.op("dve", lambda e: e.tensor_tensor(out=cl[:, 16:20], in0=cl[:, 16:20], in1=cl[:, 4:8], op=ALU.max),
             reads=[cl], writes=[cl])
        yield
        k.op("dve", lambda e: e.reciprocal(out=cl[:, 16:20], in_=cl[:, 16:20]), reads=[cl], writes=[cl])
        yield
        k.op("dve", lambda e: e.tensor_tensor(out=hd[d][:, :].rearrange("p (h e) -> p h e", h=4),
                                              in0=pG[:, :].rearrange("p (h e) -> p h e", h=4),
                                              in1=cl[:, 16:20].unsqueeze(2).to_broadcast([128, 4, 128]),
                                              op=ALU.mult), reads=[pG, cl], writes=[hd[d]])
        yield
        if d == 0:
            yield from state_update(0, kk_, va_)

    def main_chunk(ci):
        i2 = ci % 2
        load_chunk(ci, True)
        q_, kT_, kk_, va_, mo_, gt = qT[i2], kT[i2], kk[i2], va[i2], mo[i2], GT[i2]
        for h in range(4):
            k.op("pe", lambda e, h=h: e.matmul(pS[:, h * 128:(h + 1) * 128], lhsT=kT_[:, h, :], rhs=q_[:, h, :],
                                               start=True, stop=True), reads=[kT_, q_], writes=[pS], inc=(h == 3))
        run(dir_chain(0, ci, q_, kk_, va_, gt), dir_chain(1, ci, q_, kk_, va_, gt))
        k.op("pool", lambda e: e.tensor_tensor(out=hm[:, :], in0=hd[0][:, :], in1=hd[1][:, :], op=ALU.add),
             reads=[hd[0], hd[1]], writes=[hm])
        k.op("dve", lambda e: e.tensor_tensor(out=sqh[:, :], in0=hm[:, :], in1=hm[:, :], op=ALU.mult),
             reads=[hm], writes=[sqh])
        k.op("dve", lambda e: e.tensor_reduce(out=ss4[:, 0:4], in_=sqh[:, :].rearrange("p (h d) -> p h d", h=4),
                                              axis=AX.X, op=ALU.add), reads=[sqh], writes=[ss4])
        rstd_from_ss(k, ss4, rs4, 1.0 / 128, 4)
        k.op("dve", lambda e: e.tensor_tensor(out=hm[:, :].rearrange("p (h d) -> p h d", h=4),
                                              in0=hm[:, :].rearrange("p (h d) -> p h d", h=4),
                                              in1=rs4[:, 0:4].unsqueeze(2).to_broadcast([128, 4, 128]),
                                              op=ALU.mult), reads=[hm, rs4], writes=[hm])
        k.op("pool", lambda e: e.tensor_tensor(out=hm[:, :], in0=hm[:, :], in1=mnw[:, :], op=ALU.mult),
             reads=[hm, mnw], writes=[hm])
        k.op("pool", lambda e: e.tensor_tensor(out=sqh[:, :], in0=hm[:, :], in1=mo_[:, :], op=ALU.mult),
             reads=[hm, mo_], writes=[sqh])
        for h in range(4):
            k.op("pe", lambda e, h=h: e.transpose(pS[:, h * 128:(h + 1) * 128], sqh[:, h * 128:(h + 1) * 128],
                                                  ident[:, :]), reads=[sqh, ident], writes=[pS], inc=(h == 3))
        ho = hmT[i2]
        k.op("act", lambda e: e.activation(out=ho[:, :, :], in_=pS[:, :].rearrange("p (h t) -> p h t", h=4),
                                           func=AF.Copy), reads=[pS], writes=[ho])
        k.dma(c.HMTd[:, :, ci * 128:(ci + 1) * 128].rearrange("h p t -> p h t"), ho[:, :, :], reads=[ho])

    def main():
        late_alloc()
        for seq, chunks in seqs:
            init_dir(seq, 0)
            for ci in chunks:
                main_chunk(ci)
            if seq > 0:
                store_state(seq, 0)
        k.barrier()
        k.sb_phase = ctx["sb_phase0"]
        k.phase_mem()

    return pre_gen, main


def phase_b2(ctx):
    ctx["b2_main"]()


def phase_c1(ctx):
    c = NS(ctx)
    k = c.k
    PS = c.PS
    Wout = k.sb([128, 8, D], BF16, "Wout")
    stg = [k.sb([128, D], F32, f"stgo{i}") for i in range(2)]
    g1b = k.sb([128, D], F32, "g1b")
    mixT = [k.sb([128, 8, 512], BF16, f"mixT{i}") for i in range(2)]
    xb = [k.sb([128, D], F32, f"xc{i}") for i in range(3)]
    x1 = [k.sb([128, D], F32, f"x1{i}") for i in range(2)]
    tmp = k.sb([128, D], F32, "tmpc1")
    junk = k.sb([128, D], BF16, "junkc")
    xs = [k.sb([128, D], F32, f"xsc{i}") for i in range(3)]
    deferred = []

    def flush(keep=0):
        while len(deferred) > keep:
            deferred.pop(0)()
    ss = k.sb([128, 8], F32, "ssc")
    rstd = k.sb([128, 8], F32, "rstdc")
    h2T = [k.sb([128, 8, 512], BF16, f"h2T{i}") for i in range(2)]
    for kc in range(8):
        st = stg[kc % 2]
        k.dma(st[:, :], c.WOUT[kc * 128:(kc + 1) * 128, :], writes=[st])
        k.op("pool", lambda e, kc=kc, st=st: e.tensor_copy(out=Wout[:, kc, :], in_=st[:, :]), reads=[st],
             writes=[Wout])

    def st_body(s):
        cj = 0 if s < 8 else 1
        tok = slice(s * 512, (s + 1) * 512)
        mx = mixT[s % 2]
        h2 = h2T[s % 2]
        if s == 0 or s == 8:
            k.dma(g1b[:, :], c.MODS[cj, 2 * D:3 * D].partition_broadcast(128), writes=[g1b])
        k.dma(mx[:, 0:4, :], c.AOTd[:, :, tok].rearrange("c p t -> p c t"), writes=[mx])
        k.dma(mx[:, 4:8, :], c.HMTd[:, :, tok].rearrange("c p t -> p c t"), writes=[mx])

        def tile_body(i):
            ti = s * 4 + i
            xt = xb[ti % 3]
            xo = x1[ti % 2]
            k.dma(xt[:, :], c.X[ti * 128:(ti + 1) * 128, :], writes=[xt])
            for n in range(2):
                pb = PS[2 + (2 * ti + n) % 4]
                for kc in range(8):
                    k.op("pe", lambda e, pb=pb, kc=kc, n=n: e.matmul(
                        pb[:, :], lhsT=mx[:, kc, i * 128:(i + 1) * 128], rhs=Wout[:, kc, n * 512:(n + 1) * 512],
                        start=(kc == 0), stop=(kc == 7)), reads=[mx, Wout], writes=[pb], inc=(kc == 7))
                if n == 1:
                    flush(1)
                k.op("dve", lambda e, pb=pb, n=n: e.tensor_tensor(out=tmp[:, n * 512:(n + 1) * 512], in0=pb[:, :],
                                                                  in1=g1b[:, n * 512:(n + 1) * 512], op=ALU.mult),
                     reads=[pb, g1b], writes=[tmp])
            k.op("pool", lambda e: e.tensor_tensor(out=xo[:, :], in0=tmp[:, :], in1=xt[:, :], op=ALU.add),
                 reads=[tmp, xt], writes=[xo])
            k.dma(c.X1d[ti * 128:(ti + 1) * 128, :], xo[:, :], reads=[xo])
            norm_to_hT(k, c, xo, h2, i * 128,
                       lambda ch: c.G2[:, cj, ch:ch + 1], lambda ch: c.modc[:, cj, 3, ch:ch + 1],
                       (junk, ss, rstd, xs[ti % 3]), PS[0], PS[1], defer=deferred)

        for i in range(4):
            tile_body(i)
        flush()
        k.dma(c.H2Td[:, :, tok].rearrange("c p t -> p c t"), h2[:, :, :], reads=[h2])

    for s in range(9):
        st_body(s)
    k.barrier()
    k.phase_mem()


def phase_c2(ctx):
    c = NS(ctx)
    k = c.k
    PS = c.PS
    Wgb = [k.sb([128, 8, 256], BF16, f"Wgb{i}") for i in range(22)]
    Wdn = k.sb([128, NF, D], BF16, "Wdn")
    SW = 704
    stg = [k.sb([128, SW], F32, f"stgf{i}") for i in range(2)]
    g2b = k.sb([128, D], F32, "g2b")
    fnb = k.sb([128, D], F32, "fnb")
    h2 = k.sb([128, 8, 512], BF16, "h2c")
    actT = k.sb([128, NF, 512], BF16, "actT")
    sg = [k.sb([128, 512], F32, f"sg{i}") for i in range(2)]
    x1 = [k.sb([128, D], F32, f"x1c{i}") for i in range(2)]
    x2 = k.sb([128, D], F32, "x2c")
    yt = [k.sb([128, D], F32, f"yt{i}") for i in range(2)]
    junk = k.sb([128, D], BF16, "junkf")
    ss = k.sb([128, 8], F32, "ssf")
    rstd = k.sb([128, 8], F32, "rstdf")
    mhalf = k.sb([128, 1], F32, "mhalf")
    k.dma(fnb[:, :], c.FNW.partition_broadcast(128), writes=[fnb])
    k.dma(g2b[:, :], c.MODS[0, 5 * D:6 * D].partition_broadcast(128), writes=[g2b])
    k.dma(h2[:, :, :], c.H2Td[:, :, 0:512].rearrange("c p t -> p c t"), writes=[h2])
    it = 0
    for bi in range(11):
        for half in range(2):
            blkt = Wgb[2 * bi + half]
            c0 = half * DFF + bi * 256
            for kc0 in (0, 2, 4, 6):
                st = stg[it % 2]
                it += 1
                k.dma(st[:, 0:512].rearrange("p (a n) -> p a n", a=2),
                      c.WGU[kc0 * 128:(kc0 + 2) * 128, c0:c0 + 256].rearrange("(a p) n -> p a n", p=128), writes=[st])
                k.op("pool", lambda e, kc0=kc0, st=st, blkt=blkt: e.tensor_copy(
                    out=blkt[:, kc0:kc0 + 2, :], in_=st[:, 0:512].rearrange("p (a n) -> p a n", a=2)),
                    reads=[st], writes=[blkt])
    for f in range(NF):
        for j in range(2):
            st = stg[it % 2]
            it += 1
            k.dma(st[:, 0:512], c.WDN[f * 128:(f + 1) * 128, j * 512:(j + 1) * 512], writes=[st])
            k.op("pool", lambda e, f=f, j=j, st=st: e.tensor_copy(out=Wdn[:, f, j * 512:(j + 1) * 512],
                                                                  in_=st[:, 0:512]), reads=[st], writes=[Wdn])
    k.op("dve", lambda e: e.memset(mhalf[:, :], -0.5), writes=[mhalf])

    def st_body(s):
        cj = 0 if s < 8 else 1
        tok = slice(s * 512, (s + 1) * 512)
        if s == 8:
            k.dma(g2b[:, :], c.MODS[cj, 5 * D:6 * D].partition_broadcast(128), writes=[g2b])
        if s > 0:
            k.dma(h2[:, :, :], c.H2Td[:, :, tok].rearrange("c p t -> p c t"), writes=[h2])

        def up_body(f):
            pg, pu = PS[2 * (f % 2)], PS[2 * (f % 2) + 1]
            c0 = (f % 2) * 128
            for (pb, wt) in ((pg, Wgb[2 * (f // 2)]), (pu, Wgb[2 * (f // 2) + 1])):
                for kc in range(8):
                    k.op("pe", lambda e, pb=pb, wt=wt, kc=kc: e.matmul(
                        pb[:, :], lhsT=wt[:, kc, c0:c0 + 128], rhs=h2[:, kc, :], start=(kc == 0), stop=(kc == 7)),
                        reads=[wt, h2], writes=[pb], inc=(kc == 7))
            sgt = sg[f % 2]
            k.op("act", lambda e: e.activation(out=sgt[:, :], in_=pg[:, :], func=AF.Silu), reads=[pg], writes=[sgt])
            k.op("dve", lambda e: e.tensor_tensor(out=actT[:, f, :], in0=pu[:, :], in1=sgt[:, :], op=ALU.mult),
                 reads=[pu, sgt], writes=[actT])

        for f in range(NF):
            up_body(f)

        def tile_body(i):
            ti = s * 4 + i
            xt = x1[ti % 2]
            y = yt[ti % 2]
            k.dma(xt[:, :], c.X1d[ti * 128:(ti + 1) * 128, :], writes=[xt])
            for n in range(2):
                pb = PS[4 + (2 * ti + n) % 4]
                for f in range(NF):
                    k.op("pe", lambda e, pb=pb, f=f, n=n: e.matmul(
                        pb[:, :], lhsT=actT[:, f, i * 128:(i + 1) * 128], rhs=Wdn[:, f, n * 512:(n + 1) * 512],
                        start=(f == 0), stop=(f == NF - 1)), reads=[actT, Wdn], writes=[pb], inc=(f == NF - 1))
                k.op("dve", lambda e, pb=pb, n=n: e.tensor_tensor(out=x2[:, n * 512:(n + 1) * 512], in0=pb[:, :],
                                                                  in1=g2b[:, n * 512:(n + 1) * 512], op=ALU.mult),
                     reads=[pb, g2b], writes=[x2])
            k.op("pool", lambda e: e.tensor_tensor(out=x2[:, :], in0=x2[:, :], in1=xt[:, :], op=ALU.add),
                 reads=[x2, xt], writes=[x2])
            k.op("dve", lambda e: e.scalar_tensor_tensor(out=junk[:, :], in0=x2[:, :], scalar=1.0, in1=x2[:, :],
                                                         op0=ALU.mult, op1=ALU.mult, accum_out=ss[:, 0:1]),
                 reads=[x2], writes=[junk, ss])
            k.op("dve", lambda e: e.tensor_scalar(out=ss[:, 1:2], in0=ss[:, 0:1], scalar1=1.0 / D, scalar2=EPS,
                                                  op0=ALU.mult, op1=ALU.add), reads=[ss], writes=[ss])
            k.op("pool", lambda e: e.tensor_tensor(out=rstd[:, 0:1], in0=ss[:, 1:2], in1=mhalf[:, 0:1], op=ALU.pow),
                 reads=[ss, mhalf], writes=[rstd])
            k.op("dve", lambda e: e.scalar_tensor_tensor(out=y[:, :], in0=x2[:, :], scalar=rstd[:, 0:1],
                                                         in1=fnb[:, :], op0=ALU.mult, op1=ALU.mult),
                 reads=[x2, rstd, fnb], writes=[y])
            k.dma(c.Y[ti * 128:(ti + 1) * 128, :], y[:, :], reads=[y])

        for i in range(4):
            tile_body(i)

    for s in range(9):
        st_body(s)
    k.barrier()
    k.phase_mem()
```

```python
import numpy as np
import concourse.bass as bass
import concourse.mybir as mybir
from concourse.bass_utils import run_bass_kernel_spmd

F32 = mybir.dt.float32
BF16 = mybir.dt.bfloat16
AF = mybir.ActivationFunctionType
ALU = mybir.AluOpType
AX = mybir.AxisListType

D = 1024
NT = 36
TT = NT * 128
NIN = 2832
DFF = 2816
NF = 22
EPS = 1e-6
KSC = 128.0 ** -0.5
NKEY = 38
import os as _os
NDS = 64
PRE_IN_ATT = int(_os.environ.get('PRE_IN_ATT', '1'))
STRICT = bool(int(_os.environ.get('KSTRICT', '1')))


class Tl:
    def __init__(s, h):
        s.h = h
        s.w = []
        s.r = []

    def __getitem__(s, i):
        return s.h[i]


class K:
    ENG = ("pe", "act", "dve", "pool", "sp")

    def __init__(s, nc):
        s.nc = nc
        s.ops = {e: [] for e in s.ENG}
        s.cnt = {e: 0 for e in s.ENG}
        s.seen = {e: {} for e in s.ENG}
        s.sem = {}
        s.dsems = [nc.alloc_semaphore(f"d_{i}") for i in range(NDS)]
        s.dcnt = [0] * NDS
        s.dptr = 0
        for e in s.ENG:
            s.sem[e] = nc.alloc_semaphore(f"s_{e}")
        s.sb_ptr = 0
        s.sb_phase = 0
        s.nalloc = 0

    def sb(s, shape, dt, name=None):
        esz = 4 if dt == F32 else 2
        n = 1
        for d_ in shape[1:]:
            n *= d_
        nbytes = (n * esz + 63) // 64 * 64
        off = s.sb_ptr
        s.sb_ptr += nbytes
        assert s.sb_ptr <= s.sb_top, (name, s.sb_ptr, s.sb_top)
        s.nalloc += 1
        h = s.nc.alloc_sbuf_tensor_at(f"{name or 't'}_{s.nalloc}", list(shape), dt, offset=off)
        return Tl(h)

    def phase_mem(s):
        s.sb_ptr = s.sb_phase

    def op(s, e, fn, reads=(), writes=(), inc=True):
        deps = []
        for t in reads:
            for wt in t.w:
                if wt[0] != e or e != "pe":
                    deps.append(wt)
        for t in writes:
            for wt in t.w:
                if wt[0] != e or (STRICT and e != "pe"):
                    deps.append(wt)
            for rt in t.r:
                if rt[0] != e or (STRICT and e != "pe"):
                    deps.append(rt)
        need = {}
        for key, val in deps:
            if val > need.get(key, 0):
                need[key] = val
        waits = []
        for key, val in need.items():
            if s.seen[e].get(key, 0) >= val:
                continue
            s.seen[e][key] = val
            waits.append((key, val))
        tok = (e, s.cnt[e] + 1)
        if inc:
            s.cnt[e] += 1
        s.ops[e].append((waits, fn, inc, None))
        for t in writes:
            t.w = [tok]
            t.r = []
        for t in reads:
            if t not in writes:
                t.r.append(tok)
                if len(t.r) > 24:
                    t.r = s._compact(t.r)
        return tok

    @staticmethod
    def _compact(r):
        best = {}
        for key, val in r:
            if val > best.get(key, 0):
                best[key] = val
        return list(best.items())

    def dma(s, out, in_, reads=(), writes=(), q="sp", slow=False):
        deps = []
        for t in reads:
            deps.extend(t.w)
        for t in writes:
            for wt in t.w:
                if not isinstance(wt[0], int):
                    deps.append(wt)
            deps.extend(t.r)
        need = {}
        for key, val in deps:
            if val > need.get(key, 0):
                need[key] = val
        waits = []
        for key, val in need.items():
            if s.seen[q].get(key, 0) >= val:
                continue
            s.seen[q][key] = val
            waits.append((key, val))
        si = s.dptr
        s.dptr = (s.dptr + 1) % len(s.dsems)
        if s.dcnt[si] > s.seen[q].get(si, 0):
            s.seen[q][si] = s.dcnt[si]
            waits.append((si, s.dcnt[si]))
        s.dcnt[si] += 16
        tok = (si, s.dcnt[si])
        s.ops[q].append((waits, (out, in_, slow), False, si))
        for t in writes:
            if t.w and all(isinstance(wt[0], int) for wt in t.w):
                t.w = t.w + [tok]
            else:
                t.w = [tok]
            t.r = []
        for t in reads:
            t.r.append(tok)
        return tok

    def barrier(s):
        waits = []
        for e in s.ENG:
            if e != "sp" and s.cnt[e] > s.seen["sp"].get(e, 0):
                s.seen["sp"][e] = s.cnt[e]
                waits.append((e, s.cnt[e]))
        for si in range(len(s.dsems)):
            if s.dcnt[si] > s.seen["sp"].get(si, 0):
                s.seen["sp"][si] = s.dcnt[si]
                waits.append((si, s.dcnt[si]))
        s.cnt["sp"] += 1
        v = s.cnt["sp"]
        s.ops["sp"].append((waits, "inc", True, None))
        for e in s.ENG:
            if e != "sp":
                s.seen[e]["sp"] = v
                s.ops[e].append(([("sp", v)], None, False, None))
                for si in range(len(s.dsems)):
                    s.seen[e][si] = s.dcnt[si]
                for e2 in s.ENG:
                    s.seen[e][e2] = max(s.seen[e].get(e2, 0), s.cnt[e2])

    def semof(s, key):
        return s.dsems[key] if isinstance(key, int) else s.sem[key]

    def emit(s, e, eng):
        for waits, fn, inc, si in s.ops[e]:
            for key, val in waits:
                eng.wait_ge(s.semof(key), val)
            if fn is None:
                continue
            if fn == "inc":
                eng.sem_inc(s.sem[e], 1)
                continue
            if si is not None:
                out, in_, slow = fn
                if slow:
                    ins = eng.dma_start(out=out, in_=in_, allow_slow_non_contiguous=True)
                else:
                    ins = eng.dma_start(out=out, in_=in_)
                ins.then_inc(s.dsems[si], 16)
                continue
            ins = fn(eng)
            if inc:
                ins.then_inc(s.sem[e], 1)


def build_nc(debug=False, stop=99):
    import os
    stop = int(os.environ.get('KSTOP', stop))
    nc = bass.Bass("TRN2", target_bir_lowering=False)
    k = K(nc)
    k.sb_ptr = (nc.sbuf_base + 63) // 64 * 64
    k.sb_top = nc.sbuf_top

    def din(name, shape, dt=F32):
        return nc.dram_tensor(name, list(shape), dt, kind="ExternalInput").ap()

    def dout(name, shape, dt=F32):
        return nc.dram_tensor(name, list(shape), dt, kind="ExternalOutput").ap()

    def dscr(name, shape, dt=BF16):
        return nc.dram_tensor(name, list(shape), dt, kind="ExternalOutput" if debug else "Internal").ap()

    X = din("x", [TT, D])
    CK = din("cache_k", [256, 128])
    CV = din("cache_v", [256, 128])
    SC = din("state_C", [2, 4, 128, 128])
    SN = din("state_n", [2, 4, 128])
    SM = din("state_m", [2, 4])
    COND = din("cond", [2, D])
    WADA = din("w_ada", [D, 6 * D])
    BADA = din("b_ada", [6 * D])
    N1W = din("norm1_w", [D])
    WIN = din("w_in", [D, NIN])
    GB = din("gate_bias", [4, 4])
    QNW = din("q_norm_w", [64])
    KNW = din("k_norm_w", [64])
    MNW = din("mlstm_norm_w", [512])
    WOUT = din("w_out", [D, D])
    N2W = din("norm2_w", [D])
    WGU = din("w_gu", [D, 2 * DFF])
    WDN = din("w_down", [DFF, D])
    FNW = din("final_norm_w", [D])
    ROPE = din("rope", [128, 32, 2, 64])
    IDENT = din("ident", [128, 128])
    MASKS = din("masks", [128, 2, 512])

    Y = dout("y", [TT, D])
    NK = dout("nk", [512, 128])
    NV = dout("nv", [512, 128])
    NC_ = dout("nC", [2, 2, 4, 128, 128])
    NN = dout("nn", [2, 2, 4, 128])
    NM = dout("nm", [2, 2, 4])

    MODS = dscr("mods", [2, 6 * D], F32)
    QTd = dscr("QTd", [4, 128, TT])
    MQTd = dscr("MQTd", [4, 128, TT])
    MKTd = dscr("MKTd", [4, 128, TT])
    MKd = dscr("MKd", [TT, 512])
    MVd = dscr("MVd", [TT, 520])
    MOd = dscr("MOd", [TT, 512])
    GATd = dscr("GATd", [4, 4, TT], F32)
    AOTd = dscr("AOTd", [4, 128, TT])
    HMTd = dscr("HMTd", [4, 128, TT])
    H2Td = dscr("H2Td", [8, 128, TT])
    X1d = dscr("X1d", [TT, D], F32)

    PSALL = nc.alloc_psum_tensor("psall", [128, 4096], F32)
    PS = [Tl(PSALL[:, i * 512:(i + 1) * 512]) for i in range(8)]

    ident = k.sb([128, 128], F32, "ident")
    ones4 = k.sb([4, 128], F32, "ones4")
    eye4 = k.sb([4, 4], F32, "eye4")
    modc = k.sb([128, 2, 6, 8], F32, "modc")
    G1 = k.sb([128, 2, 8], F32, "G1")
    G2 = k.sb([128, 2, 8], F32, "G2")
    n1c = k.sb([128, 8], F32, "n1c")
    n2c = k.sb([128, 8], F32, "n2c")
    k.mhalf = k.sb([128, 8], F32, "mhalf")
    k.op("dve", lambda e: e.memset(k.mhalf[:, :], -0.5), writes=[k.mhalf])
    sb_phase0 = k.sb_ptr
    KT2 = k.sb([128, 2, TT + 256], BF16, "KT2")
    VA = k.sb([128, NKEY, 2, 192], BF16, "VA")
    sb_phase0_a = k.sb_ptr
    Win = k.sb([128, 8, NIN], BF16, "Win")
    Wg = k.sb([128, 8, 128], BF16, "Wg")
    stgw = [k.sb([128, NIN // 2], F32, f"stgw{i}") for i in range(2)]
    k.sb_phase = k.sb_ptr
    k.op("pool", lambda e: e.memset(Wg[:, :, :], 0.0), writes=[Wg])

    k.dma(ident[:, :], IDENT[:, :], writes=[ident])
    k.op("dve", lambda e: e.memset(ones4[:, :], 1.0), writes=[ones4])
    k.dma(eye4[:, :], IDENT[0:4, 0:4], writes=[eye4])

    condT = k.sb([128, 8, 2], F32, "condT")
    sil = k.sb([128, 8, 2], F32, "sil")
    tmpc = k.sb([128, 8, 2], F32, "tmpc")
    mods_sb = k.sb([2, 6 * D], F32, "mods_sb")
    bada = k.sb([2, 6 * D], F32, "bada")
    wa = [k.sb([128, 512], F32, f"wa{i}") for i in range(4)]
    for j in range(2):
        k.dma(condT[:, :, j], COND[j].rearrange("(c p) -> p c", p=128), writes=[condT], slow=True)
    for j in range(2):
        k.dma(bada[j:j + 1, :], BADA.rearrange("(o n) -> o n", o=1), writes=[bada])
    k.op("act", lambda e: e.activation(out=tmpc[:, :, :], in_=condT[:, :, :], func=AF.Exp, scale=-1.0),
         reads=[condT], writes=[tmpc])
    k.op("dve", lambda e: e.tensor_scalar_add(out=tmpc[:, :, :], in0=tmpc[:, :, :], scalar1=1.0),
         reads=[tmpc], writes=[tmpc])
    k.op("dve", lambda e: e.reciprocal(out=tmpc[:, :, :], in_=tmpc[:, :, :]), reads=[tmpc], writes=[tmpc])
    k.op("dve", lambda e: e.tensor_tensor(out=sil[:, :, :], in0=condT[:, :, :], in1=tmpc[:, :, :], op=ALU.mult),
         reads=[condT, tmpc], writes=[sil])
    win_steps = []
    HW = NIN // 2
    for kc in range(8):
        for hf in range(2):
            def step(kc=kc, hf=hf):
                st = stgw[hf]
                k.dma(st[:, :], WIN[kc * 128:(kc + 1) * 128, hf * HW:(hf + 1) * HW], writes=[st])
                k.op("pool", lambda e: e.tensor_copy(out=Win[:, kc, hf * HW:(hf + 1) * HW], in_=st[:, :]),
                     reads=[st], writes=[Win])
                if hf == 1:
                    k.op("pool", lambda e: e.tensor_copy(
                        out=Wg[:, kc, :].rearrange("p (j w) -> p j w", w=32)[:, :, 0:4],
                        in_=st[:, HW - 16:HW].rearrange("p (j w) -> p j w", w=4)), reads=[st], writes=[Wg])
            win_steps.append(step)
    it = 0
    for n in range(12):
        pb = PS[n % 2]
        for kc in range(8):
            w_t = wa[it % 4]
            if it % 6 == 0 and win_steps:
                win_steps.pop(0)()
            it += 1
            k.dma(w_t[:, :], WADA[kc * 128:(kc + 1) * 128, n * 512:(n + 1) * 512], writes=[w_t])
            k.op("pe", lambda e, w_t=w_t, kc=kc, pb=pb: e.matmul(pb[0:2, :], lhsT=sil[:, kc, :], rhs=w_t[:, :],
                                                                start=(kc == 0), stop=(kc == 7)),
                 reads=[w_t, sil], writes=[pb], inc=True)
        k.op("dve", lambda e, n=n, pb=pb: e.tensor_tensor(out=mods_sb[:, n * 512:(n + 1) * 512], in0=pb[0:2, :],
                                                          in1=bada[:, n * 512:(n + 1) * 512], op=ALU.add),
             reads=[pb, bada], writes=[mods_sb])
    while win_steps:
        win_steps.pop(0)()
    k.dma(MODS[:, :], mods_sb[:, :], reads=[mods_sb])
    k.barrier()
    for j in range(2):
        for s6 in range(6):
            k.dma(modc[:, j, s6, :], MODS[j, s6 * D:(s6 + 1) * D].rearrange("(c p) -> p c", p=128),
                  writes=[modc], slow=True)
    k.dma(n1c[:, :], N1W.rearrange("(c p) -> p c", p=128), writes=[n1c], slow=True)
    k.dma(n2c[:, :], N2W.rearrange("(c p) -> p c", p=128), writes=[n2c], slow=True)
    for j in range(2):
        k.op("dve", lambda e, j=j: e.scalar_tensor_tensor(out=G1[:, j, :], in0=modc[:, j, 1, :], scalar=1.0,
                                                          in1=n1c[:, :], op0=ALU.add, op1=ALU.mult),
             reads=[modc, n1c], writes=[G1])
        k.op("dve", lambda e, j=j: e.scalar_tensor_tensor(out=G2[:, j, :], in0=modc[:, j, 4, :], scalar=1.0,
                                                          in1=n2c[:, :], op0=ALU.add, op1=ALU.mult),
             reads=[modc, n2c], writes=[G2])
    k.barrier()
    k.phase_mem()
    ctx = dict(locals())
    if stop >= 1:
        phase_a(ctx)
    if stop >= 2:
        phase_b1(ctx)
    if stop >= 3:
        phase_b2(ctx)
    if stop >= 4:
        phase_c1(ctx)
    if stop >= 5:
        phase_c2(ctx)
    k.barrier()

    with nc.allow_low_precision(reason="bf16 matmul operands by design"), nc.Block() as block:
        names = {"pe": "tensor", "act": "scalar", "dve": "vector", "pool": "gpsimd", "sp": "sync"}
        for e in K.ENG:
            getattr(block, names[e])(lambda eng, e=e: k.emit(e, eng))
    return nc


class NS:
    def __init__(s, d):
        s.__dict__.update(d)


def rstd_from_ss(k, ss, out, n_inv, width):
    k.op("act", lambda e: e.activation(out=out[:, 0:width], in_=ss[:, 0:width], func=AF.Ln, scale=n_inv, bias=EPS),
         reads=[ss], writes=[out])
    k.op("act", lambda e: e.activation(out=out[:, 0:width], in_=out[:, 0:width], func=AF.Exp, scale=-0.5),
         reads=[out], writes=[out])


def norm_to_hT(k, c, xt, hT, col0, Gc, SHc, bufs, pA, pB, defer=None):
    junk, ss, rstd, xs = bufs
    k.op("dve", lambda e: e.scalar_tensor_tensor(out=junk[:, :], in0=xt[:, :], scalar=1.0, in1=xt[:, :],
                                                 op0=ALU.mult, op1=ALU.mult, accum_out=ss[:, 0:1]),
         reads=[xt], writes=[junk, ss])
    rstd_from_ss(k, ss, rstd, 1.0 / D, 1)
    k.op("pool", lambda e: e.tensor_scalar(out=xs[:, :], in0=xt[:, :], scalar1=rstd[:, 0:1], scalar2=1.0,
                                           op0=ALU.mult, op1=ALU.mult),
         reads=[xt, rstd], writes=[xs])

    def pe_part():
        for half, pb in ((0, pA), (1, pB)):
            for cc in range(4):
                ch = half * 4 + cc
                k.op("pe", lambda e, ch=ch, cc=cc, pb=pb: e.transpose(pb[:, cc * 128:(cc + 1) * 128],
                                                                      xs[:, ch * 128:(ch + 1) * 128], c.ident[:, :]),
                     reads=[xs, c.ident], writes=[pb], inc=(cc == 3))
            for cc in range(4):
                ch = half * 4 + cc
                k.op("act", lambda e, ch=ch, cc=cc, pb=pb: e.activation(
                    out=hT[:, ch, col0:col0 + 128], in_=pb[:, cc * 128:(cc + 1) * 128], func=AF.Identity,
                    scale=Gc(ch), bias=SHc(ch)), reads=[pb, c.G1, c.G2, c.modc], writes=[hT])

    if defer is None:
        pe_part()
    else:
        defer.append(pe_part)


def phase_a(ctx):
    c = NS(ctx)
    k = c.k
    PS = c.PS
    KT2, VA, Win, Wg = c.KT2, c.VA, c.Win, c.Wg
    k.op("pool", lambda e: e.memset(VA[:, :, :, :], 1.0), writes=[VA])
    xb = [k.sb([128, D], F32, f"xb{i}") for i in range(3)]
    junk = k.sb([128, D], BF16, "junk")
    xs = [k.sb([128, D], F32, f"xs{i}") for i in range(3)]
    ss = k.sb([128, 8], F32, "ss")
    rstd = k.sb([128, 8], F32, "rstd")
    hT = [k.sb([128, 8, 512], BF16, f"hT{i}") for i in range(2)]
    ropet = [k.sb([128, 4, 2, 64], F32, f"rope{i}") for i in range(2)]
    wq_bc = k.sb([128, 64], F32, "wq_bc")
    wk_bc = k.sb([128, 64], F32, "wk_bc")
    gbcol = k.sb([128, 1], F32, "gbcol")
    qf = k.sb([128, 512], F32, "qf")
    sq = k.sb([128, 512], F32, "sq")
    qn2 = [k.sb([128, 512], F32, f"qn{i}") for i in range(2)]
    t1 = k.sb([128, 512], F32, "t1")
    t2 = k.sb([128, 512], F32, "t2")
    qr2 = [k.sb([128, 512], F32, f"qr{i}") for i in range(2)]
    kvf = [k.sb([128, 256], F32, f"kvf{i}") for i in range(2)]
    kn = [k.sb([128, 128], F32, f"kn{i}") for i in range(2)]
    kt1 = k.sb([128, 128], F32, "kt1")
    kt2 = k.sb([128, 128], F32, "kt2")
    kr = k.sb([128, 128], F32, "kr")
    kdup2 = [k.sb([128, 2, 2, 64], F32, f"kdup{i}") for i in range(2)]
    ssq = k.sb([128, 8], F32, "ssq")
    rsq = k.sb([128, 8], F32, "rsq")
    ssk = k.sb([128, 8], F32, "ssk")
    rsk = k.sb([128, 8], F32, "rsk")
    mot = k.sb([128, 512], F32, "mot")
    gsb = k.sb([128, 512], F32, "gsb")
    gtmp = k.sb([128, 512], F32, "gtmp")
    cstg = k.sb([128, 2, 128], F32, "cstg")
    QTs = [k.sb([128, 4, 512], BF16, f"QTs{i}") for i in range(1)]
    MKs = [k.sb([128, 4, 512], BF16, f"MKs{i}") for i in range(1)]
    MVs = [k.sb([128, 4, 4, 130], BF16, f"MVs{i}") for i in range(1)]
    MOs = [k.sb([128, 4, 512], BF16, f"MOs{i}") for i in range(1)]
    MQTs = [k.sb([128, 4, 512], BF16, f"MQTs{i}") for i in range(1)]
    MKTs = [k.sb([128, 4, 512], BF16, f"MKTs{i}") for i in range(1)]

    k.dma(wq_bc[:, :], c.QNW.partition_broadcast(128), writes=[wq_bc])
    k.dma(wk_bc[:, :], c.KNW.partition_broadcast(128), writes=[wk_bc])
    k.op("dve", lambda e: e.tensor_scalar_mul(out=wq_bc[:, :], in0=wq_bc[:, :], scalar1=0.125),
         reads=[wq_bc], writes=[wq_bc])
    k.op("dve", lambda e: e.memset(gbcol[:, :], 0.0), writes=[gbcol])
    for j in range(4):
        k.dma(gbcol[32 * j:32 * j + 4, 0:1], c.GB[j].rearrange("(h o) -> h o", o=1), writes=[gbcol], slow=True)
    k.op("pool", lambda e: e.memset(MVs[0][:, :, :, :], 1.0), writes=[MVs[0]])
    kdup = kdup2[0]
    for blk in range(2):
        k.dma(cstg[:, 0, :], c.CK[blk * 128:(blk + 1) * 128, :], writes=[cstg])
        k.dma(cstg[:, 1, :], c.CV[blk * 128:(blk + 1) * 128, :], writes=[cstg])
        k.op("dve", lambda e: e.tensor_copy(out=kdup[:, :, :, :],
                                            in_=cstg[:, 0, :].rearrange("p (g o d) -> p g o d", g=2, o=1)
                                            .to_broadcast([128, 2, 2, 64])), reads=[cstg], writes=[kdup])
        pk = PS[2]
        for g in range(2):
            k.op("pe", lambda e, g=g: e.transpose(pk[:, g * 128:(g + 1) * 128],
                                                  kdup[:, g, :, :].rearrange("p a d -> p (a d)"), c.ident[:, :]),
                 reads=[kdup, c.ident], writes=[pk], inc=(g == 1))
        col = TT + blk * 128
        k.op("act", lambda e, col=col: e.activation(out=KT2[:, :, col:col + 128],
                                                    in_=pk[:, 0:256].rearrange("p (g t) -> p g t", g=2),
                                                    func=AF.Copy), reads=[pk], writes=[KT2])
        k.op("dve", lambda e, blk=blk: e.tensor_copy(out=VA[:, 36 + blk, :, 64:128],
                                                     in_=cstg[:, 1, :].rearrange("p (g d) -> p g d", g=2)),
             reads=[cstg], writes=[VA])

    class Deferred(list):
        cur = 0

        def append(self, fn):
            list.append(self, (self.cur, fn))

    deferred = Deferred()

    def flush(upto=10 ** 9):
        while deferred and deferred[0][0] <= upto:
            deferred.pop(0)[1]()

    def tile_a(s, i):
        cj = 0 if s < 8 else 1
        ti = s * 4 + i
        xt = xb[ti % 3]
        k.dma(xt[:, :], c.X[ti * 128:(ti + 1) * 128, :], writes=[xt])
        norm_to_hT(k, c, xt, hT[s % 2], i * 128,
                   lambda ch, cj=cj: c.G1[:, cj, ch:ch + 1], lambda ch, cj=cj: c.modc[:, cj, 0, ch:ch + 1],
                   (junk, ss, rstd, xs[ti % 3]), PS[0], PS[1], defer=deferred)

    def st_body(s):
        cj = 0 if s < 8 else 1
        smp = s < 8
        h = hT[s % 2]
        if smp:
            rp = ropet[s % 2]
            k.dma(rp[:, :, :, :], c.ROPE[:, s * 4:(s + 1) * 4, :, :], writes=[rp])
        QT_, MK_, MV_, MO_, MQT_, MKT_ = QTs[0], MKs[0], MVs[0], MOs[0], MQTs[0], MKTs[0]
        if s == 0:
            for i in range(4):
                tile_a(0, i)
                flush()

        def tile_b(i):
            ti = s * 4 + i
            tsl = slice(i * 128, (i + 1) * 128)
            pq, pkv, pmk, pmv, pmo = PS[2], PS[3], PS[4], PS[5], PS[6]
            for (pb, c0, c1) in ((pmk, 1280, 1792), (pmv, 1792, 2304), (pmo, 2304, 2816), (pq, 0, 512),
                                 (pkv, 512, 768)):
                for kc in range(8):
                    k.op("pe", lambda e, pb=pb, c0=c0, c1=c1, kc=kc, tsl=tsl: e.matmul(
                        pb[:, 0:c1 - c0], lhsT=h[:, kc, tsl], rhs=Win[:, kc, c0:c1], start=(kc == 0), stop=(kc == 7)),
                        reads=[h, Win], writes=[pb], inc=(kc == 7))
            flush(ti - 2)
            deferred.cur = ti
            qn, qr, kdup = qn2[ti % 2], qr2[ti % 2], kdup2[ti % 2]
            k.op("act", lambda e, i=i: e.activation(out=MK_[:, i, :], in_=pmk[:, :], func=AF.Copy, scale=KSC),
                 reads=[pmk], writes=[MK_])
            k.op("dve", lambda e, i=i: e.tensor_copy(out=MV_[:, i, :, 0:128],
                                                     in_=pmv[:, :].rearrange("p (h d) -> p h d", h=4)),
                 reads=[pmv], writes=[MV_])
            k.op("act", lambda e: e.activation(out=mot[:, :], in_=pmo[:, :], func=AF.Exp, scale=-1.0),
                 reads=[pmo], writes=[mot])
            k.op("act", lambda e: e.activation(out=mot[:, :], in_=mot[:, :], func=AF.Ln, bias=1.0),
                 reads=[mot], writes=[mot])
            k.op("act", lambda e, i=i: e.activation(out=MO_[:, i, :], in_=mot[:, :], func=AF.Exp, scale=-1.0),
                 reads=[mot], writes=[MO_])
            kv = kvf[ti % 2]
            kk = kn[ti % 2]
            k.op("act", lambda e: e.activation(out=qf[:, :], in_=pq[:, :], func=AF.Copy), reads=[pq], writes=[qf])
            k.op("dve", lambda e: e.tensor_tensor(out=sq[:, :], in0=qf[:, :], in1=qf[:, :], op=ALU.mult),
                 reads=[qf], writes=[sq])
            k.op("dve", lambda e: e.tensor_reduce(out=ssq[:, 0:8], in_=sq[:, :].rearrange("p (h d) -> p h d", d=64),
                                                  axis=AX.X, op=ALU.add), reads=[sq], writes=[ssq])
            rstd_from_ss(k, ssq, rsq, 1.0 / 64, 8)
            k.op("dve", lambda e: e.tensor_tensor(out=qn[:, :].rearrange("p (h d) -> p h d", d=64),
                                                  in0=qf[:, :].rearrange("p (h d) -> p h d", d=64),
                                                  in1=rsq[:, 0:8].unsqueeze(2).to_broadcast([128, 8, 64]),
                                                  op=ALU.mult), reads=[qf, rsq], writes=[qn])
            k.op("dve", lambda e: e.tensor_tensor(out=qn[:, :].rearrange("p (h d) -> p h d", d=64),
                                                  in0=qn[:, :].rearrange("p (h d) -> p h d", d=64),
                                                  in1=wq_bc[:, :].unsqueeze(1).to_broadcast([128, 8, 64]),
                                                  op=ALU.mult), reads=[qn, wq_bc], writes=[qn])
            if smp:
                k.op("pool", lambda e, i=i: e.tensor_tensor(
                    out=t1[:, :].rearrange("p (h d) -> p h d", d=64),
                    in0=qn[:, :].rearrange("p (h d) -> p h d", d=64),
                    in1=rp[:, i, 0, :].unsqueeze(1).to_broadcast([128, 8, 64]), op=ALU.mult),
                    reads=[qn, rp], writes=[t1])
                for jj in range(2):
                    k.op("pool", lambda e, i=i, jj=jj: e.tensor_tensor(
                        out=t2[:, :].rearrange("p (h a j w) -> p h a j w", h=8, a=2, j=2)[:, :, :, jj, :],
                        in0=qn[:, :].rearrange("p (h a j w) -> p h a j w", h=8, a=2, j=2)[:, :, :, 1 - jj, :],
                        in1=rp[:, i, 1, :].rearrange("p (a j w) -> p a j w", a=2, j=2)[:, :, jj, :].unsqueeze(1)
                        .to_broadcast([128, 8, 2, 16]), op=ALU.mult),
                        reads=[qn, rp], writes=[t2])
                k.op("pool", lambda e: e.tensor_tensor(out=qr[:, :], in0=t1[:, :], in1=t2[:, :], op=ALU.add),
                     reads=[t1, t2], writes=[qr])
                qsrc = qr
            else:
                qsrc = qn
            def q_tr(qsrc=qsrc, tsl=tsl):
                pt = PS[7]
                for j in range(4):
                    k.op("pe", lambda e, j=j: e.transpose(pt[:, j * 128:(j + 1) * 128],
                                                          qsrc[:, j * 128:(j + 1) * 128], c.ident[:, :]),
                         reads=[qsrc, c.ident], writes=[pt], inc=(j == 3))
                k.op("act", lambda e: e.activation(out=QT_[:, :, tsl],
                                                   in_=pt[:, :].rearrange("p (j t) -> p j t", j=4),
                                                   func=AF.Copy), reads=[pt], writes=[QT_])
            deferred.append(q_tr)
            k.op("act", lambda e, kv=kv: e.activation(out=kv[:, :], in_=pkv[:, 0:256], func=AF.Copy),
                 reads=[pkv], writes=[kv])
            k.op("dve", lambda e, kv=kv: e.tensor_tensor(out=kt1[:, :], in0=kv[:, 0:128], in1=kv[:, 0:128],
                                                         op=ALU.mult), reads=[kv], writes=[kt1])
            k.op("dve", lambda e: e.tensor_reduce(out=ssk[:, 0:2], in_=kt1[:, :].rearrange("p (h d) -> p h d", d=64),
                                                  axis=AX.X, op=ALU.add), reads=[kt1], writes=[ssk])
            rstd_from_ss(k, ssk, rsk, 1.0 / 64, 2)
            k.op("dve", lambda e, kv=kv, kk=kk: e.tensor_tensor(
                out=kk[:, :].rearrange("p (h d) -> p h d", d=64),
                in0=kv[:, 0:128].rearrange("p (h d) -> p h d", d=64),
                in1=rsk[:, 0:2].unsqueeze(2).to_broadcast([128, 2, 64]), op=ALU.mult),
                reads=[kv, rsk], writes=[kk])
            k.op("dve", lambda e, kk=kk: e.tensor_tensor(
                out=kk[:, :].rearrange("p (h d) -> p h d", d=64),
                in0=kk[:, :].rearrange("p (h d) -> p h d", d=64),
                in1=wk_bc[:, :].unsqueeze(1).to_broadcast([128, 2, 64]), op=ALU.mult),
                reads=[kk, wk_bc], writes=[kk])
            if smp:
                k.op("pool", lambda e, i=i, kk=kk: e.tensor_tensor(
                    out=kt1[:, :].rearrange("p (h d) -> p h d", d=64),
                    in0=kk[:, :].rearrange("p (h d) -> p h d", d=64),
                    in1=rp[:, i, 0, :].unsqueeze(1).to_broadcast([128, 2, 64]), op=ALU.mult),
                    reads=[kk, rp], writes=[kt1])
                for jj in range(2):
                    k.op("pool", lambda e, i=i, kk=kk, jj=jj: e.tensor_tensor(
                        out=kt2[:, :].rearrange("p (h a j w) -> p h a j w", h=2, a=2, j=2)[:, :, :, jj, :],
                        in0=kk[:, :].rearrange("p (h a j w) -> p h a j w", h=2, a=2, j=2)[:, :, :, 1 - jj, :],
                        in1=rp[:, i, 1, :].rearrange("p (a j w) -> p a j w", a=2, j=2)[:, :, jj, :].unsqueeze(1)
                        .to_broadcast([128, 2, 2, 16]), op=ALU.mult),
                        reads=[kk, rp], writes=[kt2])
                k.op("pool", lambda e: e.tensor_tensor(out=kr[:, :], in0=kt1[:, :], in1=kt2[:, :], op=ALU.add),
                     reads=[kt1, kt2], writes=[kr])
                ksrc = kr
            else:
                ksrc = kk
                pt0 = (ti - 32) * 128
                k.dma(c.NK[pt0:pt0 + 128, :], kk[:, :], reads=[kk])
                k.dma(c.NV[pt0:pt0 + 128, :], kv[:, 128:256], reads=[kv])
            k.op("dve", lambda e, ksrc=ksrc: e.tensor_copy(
                out=kdup[:, :, :, :], in_=ksrc[:, :].rearrange("p (g o d) -> p g o d", g=2, o=1)
                .to_broadcast([128, 2, 2, 64])), reads=[ksrc], writes=[kdup])
            def k_tr(ti=ti):
                pk = PS[7]
                for g in range(2):
                    k.op("pe", lambda e, g=g: e.transpose(pk[:, g * 128:(g + 1) * 128],
                                                          kdup[:, g, :, :].rearrange("p a d -> p (a d)"),
                                                          c.ident[:, :]),
                         reads=[kdup, c.ident], writes=[pk], inc=(g == 1))
                k.op("act", lambda e: e.activation(out=KT2[:, :, ti * 128:(ti + 1) * 128],
                                                   in_=pk[:, 0:256].rearrange("p (g t) -> p g t", g=2),
                                                   func=AF.Copy), reads=[pk], writes=[KT2])
            deferred.append(k_tr)
            k.op("dve", lambda e, ti=ti, kv=kv: e.tensor_copy(out=VA[:, ti, :, 64:128],
                                                              in_=kv[:, 128:256].rearrange("p (g d) -> p g d", g=2)),
                 reads=[kv], writes=[VA])
        for i in range(4):
            tile_b(i)
            if s + 1 < 9:
                tile_a(s + 1, i)
        for hh in range(8):
            pb = PS[2 + hh % 4]
            c0 = 768 + hh * 128
            for kc in range(8):
                k.op("pe", lambda e, pb=pb, c0=c0, kc=kc: e.matmul(
                    pb[:, :], lhsT=Win[:, kc, c0:c0 + 128], rhs=h[:, kc, :], start=(kc == 0), stop=(kc == 7)),
                    reads=[h, Win], writes=[pb], inc=(kc == 7))
            if hh == 3:
                flush()
            if hh < 4:
                k.op("dve", lambda e, pb=pb, hh=hh: e.tensor_copy(out=MQT_[:, hh, :], in_=pb[:, :]),
                     reads=[pb], writes=[MQT_])
            else:
                k.op("act", lambda e, pb=pb, hh=hh: e.activation(out=MKT_[:, hh - 4, :], in_=pb[:, :], func=AF.Copy,
                                                                 scale=KSC), reads=[pb], writes=[MKT_])
        pg = PS[6]
        for kc in range(8):
            k.op("pe", lambda e, kc=kc: e.matmul(pg[:, :], lhsT=Wg[:, kc, :], rhs=h[:, kc, :], start=(kc == 0),
                                                 stop=(kc == 7)), reads=[h, Wg], writes=[pg], inc=(kc == 7))
        k.op("act", lambda e: e.activation(out=gsb[:, :], in_=pg[:, :], func=AF.Identity, bias=gbcol[:, 0:1]),
             reads=[pg, gbcol], writes=[gsb])
        for r0 in (32, 96):
            k.op("act", lambda e, r0=r0: e.activation(out=gtmp[r0:r0 + 4, :], in_=gsb[r0:r0 + 4, :], func=AF.Exp,
                                                      scale=-1.0), reads=[gsb], writes=[gtmp])
            k.op("act", lambda e, r0=r0: e.activation(out=gtmp[r0:r0 + 4, :], in_=gtmp[r0:r0 + 4, :], func=AF.Ln,
                                                      bias=1.0), reads=[gtmp], writes=[gtmp])
            k.op("dve", lambda e, r0=r0: e.tensor_scalar_mul(out=gsb[r0:r0 + 4, :], in0=gtmp[r0:r0 + 4, :],
                                                             scalar1=-1.0), reads=[gtmp], writes=[gsb])
        flush()
        tok = slice(s * 512, (s + 1) * 512)
        k.dma(c.QTd[:, :, tok].rearrange("c p t -> p c t"), QT_[:, :, :], reads=[QT_])
        k.dma(c.MQTd[:, :, tok].rearrange("c p t -> p c t"), MQT_[:, :, :], reads=[MQT_])
        k.dma(c.MKTd[:, :, tok].rearrange("c p t -> p c t"), MKT_[:, :, :], reads=[MKT_])
        k.dma(c.MKd[tok, :].rearrange("(i p) f -> p i f", p=128), MK_[:, :, :], reads=[MK_])
        k.dma(c.MVd[tok, :].rearrange("(i p) f -> p i f", p=128), MV_[:, :, :, :].rearrange("p i h d -> p i (h d)"),
              reads=[MV_])
        k.dma(c.MOd[tok, :].rearrange("(i p) f -> p i f", p=128), MO_[:, :, :], reads=[MO_])
        for j in range(4):
            k.dma(c.GATd[j, :, tok], gsb[32 * j:32 * j + 4, :], reads=[gsb])
    for s in range(9):
        st_body(s)
    k.barrier()
    k.sb_phase = c.sb_phase0_a
    k.phase_mem()


def _consts():
    f = np.float32
    tok = np.arange(4096)
    row = (tok // 64).astype(f)
    col = (tok % 64).astype(f)
    inv = (f(10000.0) ** (-np.arange(0, 32, 2, dtype=f) / f(32))).astype(f)
    ang = np.stack([row[:, None] * inv[None, :], col[:, None] * inv[None, :]], axis=1).astype(f)
    cs, sn = np.cos(ang).astype(f), np.sin(ang).astype(f)
    Cf = np.stack([cs, cs], axis=2)
    Ss = np.stack([-sn, sn], axis=2)
    tab = np.stack([Cf.reshape(4096, 64), Ss.reshape(4096, 64)], axis=1)
    rope = np.ascontiguousarray(tab.reshape(32, 128, 2, 64).transpose(1, 0, 2, 3))
    ident = np.eye(128, dtype=f)
    sidx = np.arange(128)[:, None]
    lidx = np.arange(128)[None, :]
    mf = (lidx >= sidx).astype(f)
    mb = (lidx <= sidx).astype(f)
    masks = np.stack([np.tile(mf, (1, 4)), np.tile(mb, (1, 4))], axis=1)
    return rope, ident, np.ascontiguousarray(masks)


def make_in_maps(inp):
    rope, ident, masks = _consts()
    f = np.float32
    c = lambda a: np.ascontiguousarray(np.asarray(a), dtype=f)
    maps = []
    for b in range(8):
        x = np.concatenate([inp["x_sample"][b], inp["x_prompt"][2 * b], inp["x_prompt"][2 * b + 1]], axis=0)
        m = {
            "x": c(x),
            "cache_k": c(np.asarray(inp["cache_k"])[b, 0].reshape(256, 128)),
            "cache_v": c(np.asarray(inp["cache_v"])[b, 0].reshape(256, 128)),
            "state_C": c(np.asarray(inp["state_C"])[b, 0]),
            "state_n": c(np.asarray(inp["state_n"])[b, 0]),
            "state_m": c(np.asarray(inp["state_m"])[b, 0]),
            "cond": c(np.stack([np.asarray(inp["c"])[b], np.asarray(inp["c_ctx"])], axis=0)),
            "w_ada": c(np.asarray(inp["w_ada"])[0]), "b_ada": c(np.asarray(inp["b_ada"])[0]),
            "norm1_w": c(np.asarray(inp["norm1_w"])[0]), "w_in": c(np.asarray(inp["w_in"])[0]),
            "gate_bias": c(np.asarray(inp["gate_bias"])[0]), "q_norm_w": c(np.asarray(inp["q_norm_w"])[0]),
            "k_norm_w": c(np.asarray(inp["k_norm_w"])[0]), "mlstm_norm_w": c(np.asarray(inp["mlstm_norm_w"])[0]),
            "w_out": c(np.asarray(inp["w_out"])[0]), "norm2_w": c(np.asarray(inp["norm2_w"])[0]),
            "w_gu": c(np.asarray(inp["w_gu"])[0]), "w_down": c(np.asarray(inp["w_down"])[0]),
            "final_norm_w": c(inp["final_norm_w"]),
            "rope": rope, "ident": ident, "masks": masks,
        }
        maps.append(m)
    return maps


_NC = None


def kernel(**inp):
    global _NC
    if _NC is None:
        _NC = build_nc()
    maps = make_in_maps(inp)
    res = run_bass_kernel_spmd(_NC, maps, core_ids=list(range(8)))
    R = res.results
    f = np.float32
    y_s = np.stack([R[b]["y"][0:4096] for b in range(8)], axis=0).astype(f)
    y_p = np.stack([R[b]["y"][4096 + 256 * j:4096 + 256 * (j + 1)] for b in range(8) for j in range(2)], axis=0).astype(f)
    nk = np.stack([R[b]["nk"][256 * j:256 * (j + 1)].reshape(256, 2, 64) for b in range(8) for j in range(2)], axis=0)
    nv = np.stack([R[b]["nv"][256 * j:256 * (j + 1)].reshape(256, 2, 64) for b in range(8) for j in range(2)], axis=0)
    nC = np.stack([R[b]["nC"][j] for b in range(8) for j in range(2)], axis=0)
    nn = np.stack([R[b]["nn"][j] for b in range(8) for j in range(2)], axis=0)
    nm = np.stack([R[b]["nm"][j] for b in range(8) for j in range(2)], axis=0)
    return (y_p, y_s, nk[:, None].astype(f), nv[:, None].astype(f), nC[:, None].astype(f), nn[:, None].astype(f),
            nm[:, None].astype(f))


def phase_b1(ctx):
    c = NS(ctx)
    k = c.k
    PS = c.PS
    KT2, VA = c.KT2, c.VA
    pre_gen, b2_main = make_b2(ctx)
    ctx["b2_main"] = b2_main
    k.sb_phase = k.sb_ptr
    pre = pre_gen()
    SP = [Tl(c.PSALL[:, b * 1024:(b + 1) * 1024]) for b in range(2)]
    QTg = [k.sb([128, 4, 512], BF16, f"QTg{i}") for i in range(2)]
    PTP = [k.sb([128, 1024], BF16, f"PT{i}") for i in range(3)]
    AO = [k.sb([128, 4, 512], BF16, f"AO{i}") for i in range(2)]
    rden = [k.sb([128, 512], F32, f"rden{i}") for i in range(2)]
    obs = k.sb([128, 512], F32, "obs")
    groups = [(g * 512, 512, list(range(32)) + [36, 37]) for g in range(8)]
    groups += [(4096, 256, [32, 33]), (4352, 256, [34, 35])]
    cnt = [0]

    def group_body(gi, t0, nq, kbs):
        qt = QTg[gi % 2]
        ao = AO[gi % 2]
        k.dma(qt[:, :, 0:nq], c.QTd[:, :, t0:t0 + nq].rearrange("c p t -> p c t"), writes=[qt])
        iters = [(j, idx, kb) for j in range(4) for idx, kb in enumerate(kbs)]
        base = cnt[0]
        cnt[0] += len(iters)

        def emit_s(n):
            j, idx, kb = iters[n]
            g = j // 2
            kcol = kb * 128 if kb < 36 else TT + (kb - 36) * 128
            sp = SP[(base + n) % 2]
            k.op("pe", lambda e: e.matmul(sp[:, 0:nq], lhsT=KT2[0:64, g, kcol:kcol + 128], rhs=qt[0:64, j, 0:nq],
                                          start=True, stop=True), reads=[KT2, qt], writes=[sp], inc=False)
            k.op("pe", lambda e: e.matmul(sp[:, 512:512 + nq], lhsT=KT2[64:128, g, kcol:kcol + 128],
                                          rhs=qt[64:128, j, 0:nq], start=True, stop=True),
                 reads=[KT2, qt], writes=[sp])

        def emit_rest(n):
            j, idx, kb = iters[n]
            g = j // 2
            sp = SP[(base + n) % 2]
            pt = PTP[(base + n) % 3]
            oa, ob = (PS[4], PS[6])[j % 2], PS[5]
            k.op("act", lambda e: e.activation(out=pt[:, :].rearrange("p (a t) -> p a t", a=2)[:, :, 0:nq],
                                               in_=sp[:, :].rearrange("p (a t) -> p a t", a=2)[:, :, 0:nq],
                                               func=AF.Exp), reads=[sp], writes=[pt])
            first, last = idx == 0, idx == len(kbs) - 1
            k.op("pe", lambda e: e.matmul(oa[:, 0:nq], lhsT=VA[:, kb, g, 64:192], rhs=pt[:, 0:nq], start=first,
                                          stop=last), reads=[VA, pt], writes=[oa], inc=False)
            k.op("pe", lambda e: e.matmul(ob[:, 0:nq], lhsT=VA[:, kb, g, 0:128], rhs=pt[:, 512:512 + nq],
                                          start=first, stop=last), reads=[VA, pt], writes=[ob])
            if last:
                ra, rb = rden[0], rden[1]
                k.op("dve", lambda e: e.tensor_copy(out=obs[:, 0:nq], in_=ob[:, 0:nq]), reads=[ob], writes=[obs])
                k.op("dve", lambda e: e.reciprocal(out=rb[64:128, 0:nq], in_=obs[0:64, 0:nq]), reads=[obs],
                     writes=[rb])
                k.op("dve", lambda e: e.tensor_tensor(out=ao[64:128, j, 0:nq], in0=obs[64:128, 0:nq],
                                                      in1=rb[64:128, 0:nq], op=ALU.mult), reads=[obs, rb],
                     writes=[ao])
                k.op("dve", lambda e: e.reciprocal(out=ra[0:64, 0:nq], in_=oa[64:128, 0:nq]), reads=[oa], writes=[ra])
                k.op("dve", lambda e: e.tensor_tensor(out=ao[0:64, j, 0:nq], in0=oa[0:64, 0:nq], in1=ra[0:64, 0:nq],
                                                      op=ALU.mult), reads=[oa, ra], writes=[ao])

        emit_s(0)
        for n in range(len(iters)):
            if n + 1 < len(iters):
                emit_s(n + 1)
            emit_rest(n)
            if PRE_IN_ATT:
                next(pre, None)
        k.dma(c.AOTd[:, :, t0:t0 + nq].rearrange("c p t -> p c t"), ao[:, :, 0:nq], reads=[ao])

    for gi, (t0, nq, kbs) in enumerate(groups):
        group_body(gi, t0, nq, kbs)
    for _ in pre:
        pass
    k.barrier()
    k.phase_mem()


def make_b2(ctx):
    c = NS(ctx)
    k = c.k
    PS = c.PS
    pS = PS[0]
    pGd, pId, pTd = (PS[1], PS[4]), (PS[2], PS[5]), (PS[3], PS[6])
    pX = PS[7]
    Cst = [k.sb([128, 4, 130], F32, f"Cst{i}") for i in range(2)]
    Csnap = k.sb([128, NT, 4, 130], BF16, "Csnap")
    mst = [k.sb([4, 2], F32, f"mst{i}") for i in range(2)]
    mbprev = k.sb([4, NT + 4], F32, "mbprev")
    GT = [k.sb([4, 4, 128], F32, f"GT{i}") for i in range(2)]
    kk = [k.sb([128, 4, 128], BF16, f"kk{i}") for i in range(2)]
    va = [k.sb([128, 4, 130], BF16, f"va{i}") for i in range(2)]
    rows = [[k.sb([4, 128], F32, f"row{d}_{i}") for i in range(8)] for d in range(2)]
    dg = [k.sb([4, 4], F32, f"dg{d}") for d in range(2)]
    cols = [k.sb([128, 24], F32, f"cols{d}") for d in range(2)]
    kw = k.sb([128, 512], BF16, "kw")
    cols1p = [cols[1], k.sb([128, 24], F32, "cols1b")]
    ident, ones4, eye4 = c.ident, c.ones4, c.eye4
    maskneg = identb = None
    masks = mnw = Cbf = qT = kT = mo = blk = Dsb = swT = qI = hd = hm = sqh = ss4 = rs4 = hmT = None

    def late_alloc():
        nonlocal maskneg, identb
        nonlocal masks, mnw, Cbf, qT, kT, mo, blk, Dsb, swT, qI, hd, hm, sqh, ss4, rs4, hmT
        masks = k.sb([128, 2, 512], F32, "masks")
        mnw = k.sb([128, 512], F32, "mnw")
        Cbf = k.sb([128, 4, 130], BF16, "Cbf")
        qT = [k.sb([128, 4, 128], BF16, f"qT{i}") for i in range(2)]
        kT = [k.sb([128, 4, 128], BF16, f"kT{i}") for i in range(2)]
        mo = [k.sb([128, 512], BF16, f"mo{i}") for i in range(2)]
        blk = [[k.sb([4, 4, 128], F32, f"blk{d}_{i}") for i in range(2)] for d in range(2)]
        Dsb = [k.sb([128, 512], F32, f"Dsb{d}") for d in range(2)]
        swT = [k.sb([128, 512], BF16, f"swT{d}") for d in range(2)]
        qI = [k.sb([128, 512], BF16, f"qI{d}") for d in range(2)]
        hd = [k.sb([128, 512], F32, f"hd{d}") for d in range(2)]
        hm = k.sb([128, 512], F32, "hm")
        sqh = k.sb([128, 512], F32, "sqh")
        ss4 = k.sb([128, 8], F32, "ss4")
        rs4 = k.sb([128, 8], F32, "rs4")
        hmT = [k.sb([128, 4, 128], BF16, f"hmT{i}") for i in range(2)]
        maskneg = k.sb([128, 2, 512], BF16, "maskneg")
        identb = k.sb([128, 128], BF16, "identb")
        k.dma(masks[:, :, :], c.MASKS[:, :, :], writes=[masks])
        k.dma(mnw[:, :], c.MNW.partition_broadcast(128), writes=[mnw])
        k.op("dve", lambda e: e.tensor_scalar(out=maskneg[:, :, :], in0=masks[:, :, :], scalar1=-1.0, scalar2=30000.0,
                                              op0=ALU.add, op1=ALU.mult), reads=[masks], writes=[maskneg])
        k.op("dve", lambda e: e.tensor_copy(out=identb[:, :], in_=ident[:, :]), reads=[ident], writes=[identb])

    def load_chunk(ci, full):
        tok = slice(ci * 128, (ci + 1) * 128)
        i2 = ci % 2
        k.dma(GT[i2][:, :, :], c.GATd[:, :, tok].rearrange("t h n -> h t n"), writes=[GT[i2]])
        k.dma(kk[i2][:, :, :], c.MKd[tok, :].rearrange("p (h d) -> p h d", h=4), writes=[kk[i2]])
        k.dma(va[i2][:, :, :], c.MVd[tok, :].rearrange("p (h d) -> p h d", h=4), writes=[va[i2]])
        if full:
            k.dma(qT[i2][:, :, :], c.MQTd[:, :, tok].rearrange("h p t -> p h t"), writes=[qT[i2]])
            k.dma(kT[i2][:, :, :], c.MKTd[:, :, tok].rearrange("h p t -> p h t"), writes=[kT[i2]])
            k.dma(mo[i2][:, :], c.MOd[tok, :], writes=[mo[i2]])

    def gate_prep(d, gt, mprev, full, upd, cl=None, pT=None):
        mt, mc = mprev
        rb_, ra_, rg_, rng_, rin_, rgu_, rw_, rt_ = rows[d]
        pT = pT or pTd[d]
        cl = cl or cols[d]
        lf = lambda: gt[:, 1 + 2 * d, :]
        ig = lambda: gt[:, 2 * d, :]
        rv = (lambda ap: ap) if d == 0 else (lambda ap: ap[:, ::-1])
        last = 127 if d == 0 else 0
        k.op("dve", lambda e: e.tensor_tensor_scan(out=rv(rb_[:, :]), data0=rv(ones4[:, :]), data1=rv(lf()),
                                                   initial=0.0, op0=ALU.mult, op1=ALU.add),
             reads=[gt, ones4], writes=[rb_])
        yield
        k.op("dve", lambda e: e.tensor_tensor(out=ra_[:, :], in0=ig(), in1=rb_[:, :], op=ALU.subtract),
             reads=[gt, rb_], writes=[ra_])
        yield
        k.op("pe", lambda e: e.transpose(pT[:, 0:4], ra_[:, :], ident[0:4, 0:4]), reads=[ra_, ident], writes=[pT])
        k.op("dve", lambda e: e.tensor_tensor_scan(out=rv(rg_[:, :]), data0=rv(ra_[:, :]), data1=rv(ra_[:, :]),
                                                   initial=mt[:, mc:mc + 1], op0=ALU.max, op1=ALU.max),
             reads=[ra_, mt], writes=[rg_])
        yield
        k.op("dve", lambda e: e.tensor_scalar_mul(out=rng_[:, :], in0=rg_[:, :], scalar1=-1.0),
             reads=[rg_], writes=[rng_])
        if full:
            k.op("dve", lambda e: e.tensor_tensor(out=rt_[:, :], in0=rb_[:, :], in1=rg_[:, :], op=ALU.add),
                 reads=[rb_, rg_], writes=[rt_])
            yield
            bk0 = blk[d][0]
            k.op("dve", lambda e: e.tensor_tensor(
                out=bk0[:, :, :], in0=rng_[:, :].unsqueeze(1).to_broadcast([4, 4, 128]),
                in1=eye4[:, :].unsqueeze(2).to_broadcast([4, 4, 128]), op=ALU.mult),
                reads=[rng_, eye4], writes=[bk0])
            yield
            k.op("pe", lambda e: e.matmul(pGd[d][:, :], lhsT=ones4[:, :],
                                          rhs=bk0[:, :, :].rearrange("k h l -> k (h l)"),
                                          start=True, stop=False), reads=[ones4, bk0], writes=[pGd[d]], inc=False)
            k.op("pe", lambda e: e.matmul(pGd[d][:, :], lhsT=identb[:, :], rhs=maskneg[:, d, :],
                                          start=False, stop=True), reads=[identb, maskneg], writes=[pGd[d]])
        yield
        k.op("act", lambda e: e.activation(out=rin_[:, :], in_=rng_[:, :], func=AF.Exp, bias=mt[:, mc:mc + 1]),
             reads=[rng_, mt], writes=[rin_])
        if full:
            yield
            bk1 = blk[d][1]
            k.op("dve", lambda e: e.tensor_tensor(
                out=bk1[:, :, :], in0=rin_[:, :].unsqueeze(1).to_broadcast([4, 4, 128]),
                in1=eye4[:, :].unsqueeze(2).to_broadcast([4, 4, 128]), op=ALU.mult),
                reads=[rin_, eye4], writes=[bk1])
            yield
            k.op("pe", lambda e: e.matmul(pId[d][:, :], lhsT=ones4[:, :],
                                          rhs=bk1[:, :, :].rearrange("k h l -> k (h l)"),
                                          start=True, stop=True), reads=[ones4, bk1], writes=[pId[d]])
        if full:
            k.op("act", lambda e: e.activation(out=rgu_[:, :], in_=rt_[:, :], func=AF.Exp, scale=-1.0),
                 reads=[rt_], writes=[rgu_])
        if upd:
            k.op("act", lambda e: e.activation(out=rw_[:, :], in_=ra_[:, :], func=AF.Exp,
                                               bias=rng_[:, last:last + 1]), reads=[ra_, rng_], writes=[rw_])
        yield
        if full:
            k.op("pe", lambda e: e.transpose(pT[:, 4:8], rgu_[:, :], ident[0:4, 0:4]), reads=[rgu_, ident],
                 writes=[pT])
        if upd:
            k.op("pe", lambda e: e.transpose(pT[:, 8:12], rw_[:, :], ident[0:4, 0:4]), reads=[rw_, ident],
                 writes=[pT])
            k.op("dve", lambda e: e.tensor_scalar(out=dg[d][:, :], in0=eye4[:, :], scalar1=rin_[:, last:last + 1],
                                                  scalar2=None, op0=ALU.mult), reads=[eye4, rin_], writes=[dg[d]])
            yield
            k.op("pe", lambda e: e.matmul(pT[:, 12:16], lhsT=ones4[:, :], rhs=dg[d][:, :], start=True, stop=True),
                 reads=[ones4, dg[d]], writes=[pT])
        yield
        hi = 16 if upd else 8
        if not full and upd:
            k.op("dve", lambda e: e.tensor_copy(out=cl[:, 0:4], in_=pT[:, 0:4]), reads=[pT], writes=[cl])
            k.op("dve", lambda e: e.tensor_copy(out=cl[:, 8:16], in_=pT[:, 8:16]), reads=[pT], writes=[cl])
        else:
            k.op("dve", lambda e: e.tensor_copy(out=cl[:, 0:hi], in_=pT[:, 0:hi]), reads=[pT], writes=[cl])
        yield

    def new_m(d, mt_out, mc_out):
        rb_, rg_ = rows[d][0], rows[d][2]
        last = 127 if d == 0 else 0
        k.op("dve", lambda e: e.tensor_tensor(out=mt_out[:, mc_out:mc_out + 1], in0=rb_[:, last:last + 1],
                                              in1=rg_[:, last:last + 1], op=ALU.add),
             reads=[rb_, rg_], writes=[mt_out])

    def state_update(d, kk_, va_, banks=None, cl=None, off=0):
        banks = banks or (pX, pId[d])
        Cs = Cst[d]
        cl = cl or cols[d]
        k.op("dve", lambda e: e.tensor_tensor(out=kw[:, :].rearrange("p (h d) -> p h d", h=4), in0=kk_[:, :, :],
                                              in1=cl[:, 8:12].unsqueeze(2).to_broadcast([128, 4, 128]),
                                              op=ALU.mult), reads=[kk_, cl], writes=[kw])
        yield
        for h in range(4):
            pd = banks[h % 2]
            k.op("pe", lambda e, h=h, pd=pd: e.matmul(pd[:, off:off + 129], lhsT=kw[:, h * 128:(h + 1) * 128],
                                                      rhs=va_[:, h, 0:129], start=True, stop=True),
                 reads=[kw, va_], writes=[pd])
            yield
            k.op("dve", lambda e, h=h, pd=pd: e.scalar_tensor_tensor(
                out=Cs[:, h, 0:129], in0=Cs[:, h, 0:129], scalar=cl[:, 12 + h:13 + h], in1=pd[:, off:off + 129],
                op0=ALU.mult, op1=ALU.add), reads=[Cs, cl, pd], writes=[Cs])
            yield

    def run(*gens):
        gens = list(gens)
        while gens:
            for g in list(gens):
                try:
                    next(g)
                except StopIteration:
                    gens.remove(g)

    def init_dir(seq, d):
        Cs = Cst[d]
        if seq == 0:
            k.op("pool", lambda e: e.memset(Cs[:, :, :], 0.0), writes=[Cs])
            k.op("dve", lambda e: e.memset(mst[d][:, :], 0.0), writes=[mst[d]])
            k.dma(Cs[:, :, 0:128], c.SC[d].rearrange("h p e -> p h e"), writes=[Cs])
            k.dma(Cs[:, :, 128], c.SN[d].rearrange("h p -> p h"), writes=[Cs], slow=True)
            k.dma(mst[d][:, 0:1], c.SM[d].rearrange("(h o) -> h o", o=1), writes=[mst[d]], slow=True)
        else:
            k.op("pool", lambda e: e.memset(Cs[:, :, :], 0.0), writes=[Cs])
            k.op("dve", lambda e: e.memset(mst[d][:, :], 0.0), writes=[mst[d]])

    def store_state(seq, d):
        p = seq - 1
        Cs = Cst[d]
        k.dma(c.NC_[p, d].rearrange("h p e -> p h e"), Cs[:, :, 0:128], reads=[Cs])
        k.dma(c.NN[p, d].rearrange("h p -> p h"), Cs[:, :, 128], reads=[Cs], slow=True)
        k.dma(c.NM[p, d].rearrange("(h o) -> h o", o=1), mst[d][:, 0:1], reads=[mst[d]], slow=True)

    def pre_G(ci):
        i2 = ci % 2
        load_chunk(ci, False)
        yield
        k.op("dve", lambda e: e.tensor_copy(out=mbprev[:, ci:ci + 1], in_=mst[1][:, 0:1]), reads=[mst[1]],
             writes=[mbprev])
        yield
        yield from gate_prep(1, GT[i2], (mst[1], 0), False, True, cl=cols1p[i2], pT=pX)
        new_m(1, mst[1], 0)
        yield

    def pre_U(ci):
        i2 = ci % 2
        k.op("act", lambda e: e.activation(out=Csnap[:, ci, :, :], in_=Cst[1][:, :, :], func=AF.Copy),
             reads=[Cst[1]], writes=[Csnap])
        yield
        yield from state_update(1, kk[i2], va[i2], banks=(pX, pX), cl=cols1p[i2], off=128)

    def zip2(g1, g2):
        gens = [g for g in (g1, g2) if g is not None]
        while gens:
            for g in list(gens):
                try:
                    next(g)
                except StopIteration:
                    gens.remove(g)
            yield

    seqs = [(0, list(range(32))), (1, [32, 33]), (2, [34, 35])]

    def pre_gen():
        for seq, chunks in seqs:
            init_dir(seq, 1)
            yield
            order = list(reversed(chunks))
            yield from pre_G(order[0])
            for n, ci in enumerate(order):
                nxt = pre_G(order[n + 1]) if n + 1 < len(order) else None
                yield from zip2(nxt, pre_U(ci))
            if seq > 0:
                store_state(seq, 1)
                yield

    def dir_chain(d, ci, q_, kk_, va_, gt):
        mprev = (mst[0], 0) if d == 0 else (mbprev, ci)
        rng_, rin_ = rows[d][3], rows[d][4]
        pG, pI, pT, cl = pGd[d], pId[d], pTd[d], cols[d]
        yield from gate_prep(d, gt, mprev, True, d == 0)
        if d == 0:
            new_m(0, mst[0], 0)
        for h in range(4):
            k.op("act", lambda e, h=h: e.activation(out=Dsb[d][:, h * 128:(h + 1) * 128],
                                                    in_=pG[:, h * 128:(h + 1) * 128], func=AF.Exp,
                                                    bias=cl[:, h:h + 1]), reads=[pG, cl], writes=[Dsb[d]])
        k.op("dve", lambda e: e.tensor_tensor(out=qI[d][:, :], in0=pI[:, :],
                                              in1=q_[:, :, :].rearrange("p h t -> p (h t)"), op=ALU.mult),
             reads=[pI, q_], writes=[qI[d]])
        yield
        k.op("dve", lambda e: e.tensor_tensor(out=swT[d][:, :], in0=pS[:, :], in1=Dsb[d][:, :], op=ALU.mult),
             reads=[pS, Dsb[d]], writes=[swT[d]])
        if d == 0:
            k.op("act", lambda e: e.activation(out=Cbf[:, :, :], in_=Cst[0][:, :, :], func=AF.Copy),
                 reads=[Cst[0]], writes=[Cbf])
        yield
        for h in range(4):
            cprev = (lambda h=h: Cbf[:, h, :]) if d == 0 else (lambda h=h: Csnap[:, ci, h, :])
            ctile = Cbf if d == 0 else Csnap
            k.op("pe", lambda e, h=h: e.matmul(pG[:, h * 128:(h + 1) * 128], lhsT=swT[d][:, h * 128:(h + 1) * 128],
                                               rhs=va_[:, h, 0:128], start=True, stop=False),
                 reads=[swT[d], va_], writes=[pG], inc=False)
            k.op("pe", lambda e, h=h, cprev=cprev: e.matmul(pG[:, h * 128:(h + 1) * 128],
                                                            lhsT=qI[d][:, h * 128:(h + 1) * 128],
                                                            rhs=cprev()[:, 0:128], start=False, stop=True),
                 reads=[qI[d], ctile], writes=[pG], inc=False)
            k.op("pe", lambda e, h=h: e.matmul(pT[:, 16 + h:17 + h], lhsT=swT[d][:, h * 128:(h + 1) * 128],
                                               rhs=va_[:, h, 128:129], start=True, stop=False),
                 reads=[swT[d], va_], writes=[pT], inc=False)
            k.op("pe", lambda e, h=h, cprev=cprev: e.matmul(pT[:, 16 + h:17 + h],
                                                            lhsT=qI[d][:, h * 128:(h + 1) * 128],
                                                            rhs=cprev()[:, 128:129], start=False, stop=True),
                 reads=[qI[d], ctile], writes=[pT])
            yield
        k.op("dve", lambda e: e.tensor_scalar(out=cl[:, 20:24], in0=pT[:, 16:20], scalar1=-1.0, scalar2=None,
                                              op0=ALU.mult), reads=[pT], writes=[cl])
        yield
        k.op("dve", lambda e: e.tensor_tensor(out=cl[:, 16:20], in0=pT[:, 16:20], in1=cl[:, 20:24], op=ALU.max),
             reads=[pT, cl], writes=[cl])
        yield
        k.op("dve", lambda e: e.tensor_tensor(out=cl[:, 16:20], in0=cl[:, 16:20], in1=cl[:, 4:8], op=ALU.max),
             reads=[cl], writes=[cl])
        yield
        k.op("dve", lambda e: e.reciprocal(out=cl[:, 16:20], in_=cl[:, 16:20]), reads=[cl], writes=[cl])
        yield
        k.op("dve", lambda e: e.tensor_tensor(out=hd[d][:, :].rearrange("p (h e) -> p h e", h=4),
                                              in0=pG[:, :].rearrange("p (h e) -> p h e", h=4),
                                              in1=cl[:, 16:20].unsqueeze(2).to_broadcast([128, 4, 128]),
                                              op=ALU.mult), reads=[pG, cl], writes=[hd[d]])
        yield
        if d == 0:
            yield from state_update(0, kk_, va_)

    def main_chunk(ci):
        i2 = ci % 2
        load_chunk(ci, True)
        q_, kT_, kk_, va_, mo_, gt = qT[i2], kT[i2], kk[i2], va[i2], mo[i2], GT[i2]
        for h in range(4):
            k.op("pe", lambda e, h=h: e.matmul(pS[:, h * 128:(h + 1) * 128], lhsT=kT_[:, h, :], rhs=q_[:, h, :],
                                               start=True, stop=True), reads=[kT_, q_], writes=[pS], inc=(h == 3))
        run(dir_chain(0, ci, q_, kk_, va_, gt), dir_chain(1, ci, q_, kk_, va_, gt))
        k.op("pool", lambda e: e.tensor_tensor(out=hm[:, :], in0=hd[0][:, :], in1=hd[1][:, :], op=ALU.add),
             reads=[hd[0], hd[1]], writes=[hm])
        k.op("dve", lambda e: e.tensor_tensor(out=sqh[:, :], in0=hm[:, :], in1=hm[:, :], op=ALU.mult),
             reads=[hm], writes=[sqh])
        k.op("dve", lambda e: e.tensor_reduce(out=ss4[:, 0:4], in_=sqh[:, :].rearrange("p (h d) -> p h d", h=4),
                                              axis=AX.X, op=ALU.add), reads=[sqh], writes=[ss4])
        rstd_from_ss(k, ss4, rs4, 1.0 / 128, 4)
        k.op("dve", lambda e: e.tensor_tensor(out=hm[:, :].rearrange("p (h d) -> p h d", h=4),
                                              in0=hm[:, :].rearrange("p (h d) -> p h d", h=4),
                                              in1=rs4[:, 0:4].unsqueeze(2).to_broadcast([128, 4, 128]),
                                              op=ALU.mult), reads=[hm, rs4], writes=[hm])
        k.op("pool", lambda e: e.tensor_tensor(out=hm[:, :], in0=hm[:, :], in1=mnw[:, :], op=ALU.mult),
             reads=[hm, mnw], writes=[hm])
        k.op("pool", lambda e: e.tensor_tensor(out=sqh[:, :], in0=hm[:, :], in1=mo_[:, :], op=ALU.mult),
             reads=[hm, mo_], writes=[sqh])
        for h in range(4):
            k.op("pe", lambda e, h=h: e.transpose(pS[:, h * 128:(h + 1) * 128], sqh[:, h * 128:(h + 1) * 128],
                                                  ident[:, :]), reads=[sqh, ident], writes=[pS], inc=(h == 3))
        ho = hmT[i2]
        k.op("act", lambda e: e.activation(out=ho[:, :, :], in_=pS[:, :].rearrange("p (h t) -> p h t", h=4),
                                           func=AF.Copy), reads=[pS], writes=[ho])
        k.dma(c.HMTd[:, :, ci * 128:(ci + 1) * 128].rearrange("h p t -> p h t"), ho[:, :, :], reads=[ho])

    def main():
        late_alloc()
        for seq, chunks in seqs:
            init_dir(seq, 0)
            for ci in chunks:
                main_chunk(ci)
            if seq > 0:
                store_state(seq, 0)
        k.barrier()
        k.sb_phase = ctx["sb_phase0"]
        k.phase_mem()

    return pre_gen, main


def phase_b2(ctx):
    ctx["b2_main"]()


def phase_c1(ctx):
    c = NS(ctx)
    k = c.k
    PS = c.PS
    Wout = k.sb([128, 8, D], BF16, "Wout")
    stg = [k.sb([128, D], F32, f"stgo{i}") for i in range(2)]
    g1b = k.sb([128, D], F32, "g1b")
    mixT = [k.sb([128, 8, 512], BF16, f"mixT{i}") for i in range(2)]
    xb = [k.sb([128, D], F32, f"xc{i}") for i in range(3)]
    x1 = [k.sb([128, D], F32, f"x1{i}") for i in range(2)]
    tmp = k.sb([128, D], F32, "tmpc1")
    junk = k.sb([128, D], BF16, "junkc")
    xs = [k.sb([128, D], F32, f"xsc{i}") for i in range(3)]
    deferred = []

    def flush(keep=0):
        while len(deferred) > keep:
            deferred.pop(0)()
    ss = k.sb([128, 8], F32, "ssc")
    rstd = k.sb([128, 8], F32, "rstdc")
    h2T = [k.sb([128, 8, 512], BF16, f"h2T{i}") for i in range(2)]
    for kc in range(8):
        st = stg[kc % 2]
        k.dma(st[:, :], c.WOUT[kc * 128:(kc + 1) * 128, :], writes=[st])
        k.op("pool", lambda e, kc=kc, st=st: e.tensor_copy(out=Wout[:, kc, :], in_=st[:, :]), reads=[st],
             writes=[Wout])

    def st_body(s):
        cj = 0 if s < 8 else 1
        tok = slice(s * 512, (s + 1) * 512)
        mx = mixT[s % 2]
        h2 = h2T[s % 2]
        if s == 0 or s == 8:
            k.dma(g1b[:, :], c.MODS[cj, 2 * D:3 * D].partition_broadcast(128), writes=[g1b])
        k.dma(mx[:, 0:4, :], c.AOTd[:, :, tok].rearrange("c p t -> p c t"), writes=[mx])
        k.dma(mx[:, 4:8, :], c.HMTd[:, :, tok].rearrange("c p t -> p c t"), writes=[mx])

        def tile_body(i):
            ti = s * 4 + i
            xt = xb[ti % 3]
            xo = x1[ti % 2]
            k.dma(xt[:, :], c.X[ti * 128:(ti + 1) * 128, :], writes=[xt])
            for n in range(2):
                pb = PS[2 + (2 * ti + n) % 4]
                for kc in range(8):
                    k.op("pe", lambda e, pb=pb, kc=kc, n=n: e.matmul(
                        pb[:, :], lhsT=mx[:, kc, i * 128:(i + 1) * 128], rhs=Wout[:, kc, n * 512:(n + 1) * 512],
                        start=(kc == 0), stop=(kc == 7)), reads=[mx, Wout], writes=[pb], inc=(kc == 7))
                if n == 1:
                    flush(1)
                k.op("dve", lambda e, pb=pb, n=n: e.tensor_tensor(out=tmp[:, n * 512:(n + 1) * 512], in0=pb[:, :],
                                                                  in1=g1b[:, n * 512:(n + 1) * 512], op=ALU.mult),
                     reads=[pb, g1b], writes=[tmp])
            k.op("pool", lambda e: e.tensor_tensor(out=xo[:, :], in0=tmp[:, :], in1=xt[:, :], op=ALU.add),
                 reads=[tmp, xt], writes=[xo])
            k.dma(c.X1d[ti * 128:(ti + 1) * 128, :], xo[:, :], reads=[xo])
            norm_to_hT(k, c, xo, h2, i * 128,
                       lambda ch: c.G2[:, cj, ch:ch + 1], lambda ch: c.modc[:, cj, 3, ch:ch + 1],
                       (junk, ss, rstd, xs[ti % 3]), PS[0], PS[1], defer=deferred)

        for i in range(4):
            tile_body(i)
        flush()
        k.dma(c.H2Td[:, :, tok].rearrange("c p t -> p c t"), h2[:, :, :], reads=[h2])

    for s in range(9):
        st_body(s)
    k.barrier()
    k.phase_mem()


def phase_c2(ctx):
    c = NS(ctx)
    k = c.k
    PS = c.PS
    Wgb = [k.sb([128, 8, 256], BF16, f"Wgb{i}") for i in range(22)]
    Wdn = k.sb([128, NF, D], BF16, "Wdn")
    SW = 704
    stg = [k.sb([128, SW], F32, f"stgf{i}") for i in range(2)]
    g2b = k.sb([128, D], F32, "g2b")
    fnb = k.sb([128, D], F32, "fnb")
    h2 = k.sb([128, 8, 512], BF16, "h2c")
    actT = k.sb([128, NF, 512], BF16, "actT")
    sg = [k.sb([128, 512], F32, f"sg{i}") for i in range(2)]
    x1 = [k.sb([128, D], F32, f"x1c{i}") for i in range(2)]
    x2 = k.sb([128, D], F32, "x2c")
    yt = [k.sb([128, D], F32, f"yt{i}") for i in range(2)]
    junk = k.sb([128, D], BF16, "junkf")
    ss = k.sb([128, 8], F32, "ssf")
    rstd = k.sb([128, 8], F32, "rstdf")
    mhalf = k.sb([128, 1], F32, "mhalf")
    k.dma(fnb[:, :], c.FNW.partition_broadcast(128), writes=[fnb])
    k.dma(g2b[:, :], c.MODS[0, 5 * D:6 * D].partition_broadcast(128), writes=[g2b])
    k.dma(h2[:, :, :], c.H2Td[:, :, 0:512].rearrange("c p t -> p c t"), writes=[h2])
    it = 0
    for bi in range(11):
        for half in range(2):
            blkt = Wgb[2 * bi + half]
            c0 = half * DFF + bi * 256
            for kc0 in (0, 2, 4, 6):
                st = stg[it % 2]
                it += 1
                k.dma(st[:, 0:512].rearrange("p (a n) -> p a n", a=2),
                      c.WGU[kc0 * 128:(kc0 + 2) * 128, c0:c0 + 256].rearrange("(a p) n -> p a n", p=128), writes=[st])
                k.op("pool", lambda e, kc0=kc0, st=st, blkt=blkt: e.tensor_copy(
                    out=blkt[:, kc0:kc0 + 2, :], in_=st[:, 0:512].rearrange("p (a n) -> p a n", a=2)),
                    reads=[st], writes=[blkt])
    for f in range(NF):
        for j in range(2):
            st = stg[it % 2]
            it += 1
            k.dma(st[:, 0:512], c.WDN[f * 128:(f + 1) * 128, j * 512:(j + 1) * 512], writes=[st])
            k.op("pool", lambda e, f=f, j=j, st=st: e.tensor_copy(out=Wdn[:, f, j * 512:(j + 1) * 512],
                                                                  in_=st[:, 0:512]), reads=[st], writes=[Wdn])
    k.op("dve", lambda e: e.memset(mhalf[:, :], -0.5), writes=[mhalf])

    def st_body(s):
        cj = 0 if s < 8 else 1
        tok = slice(s * 512, (s + 1) * 512)
        if s == 8:
            k.dma(g2b[:, :], c.MODS[cj, 5 * D:6 * D].partition_broadcast(128), writes=[g2b])
        if s > 0:
            k.dma(h2[:, :, :], c.H2Td[:, :, tok].rearrange("c p t -> p c t"), writes=[h2])

        def up_body(f):
            pg, pu = PS[2 * (f % 2)], PS[2 * (f % 2) + 1]
            c0 = (f % 2) * 128
            for (pb, wt) in ((pg, Wgb[2 * (f // 2)]), (pu, Wgb[2 * (f // 2) + 1])):
                for kc in range(8):
                    k.op("pe", lambda e, pb=pb, wt=wt, kc=kc: e.matmul(
                        pb[:, :], lhsT=wt[:, kc, c0:c0 + 128], rhs=h2[:, kc, :], start=(kc == 0), stop=(kc == 7)),
                        reads=[wt, h2], writes=[pb], inc=(kc == 7))
            sgt = sg[f % 2]
            k.op("act", lambda e: e.activation(out=sgt[:, :], in_=pg[:, :], func=AF.Silu), reads=[pg], writes=[sgt])
            k.op("dve", lambda e: e.tensor_tensor(out=actT[:, f, :], in0=pu[:, :], in1=sgt[:, :], op=ALU.mult),
                 reads=[pu, sgt], writes=[actT])

        for f in range(NF):
            up_body(f)

        def tile_body(i):
            ti = s * 4 + i
            xt = x1[ti % 2]
            y = yt[ti % 2]
            k.dma(xt[:, :], c.X1d[ti * 128:(ti + 1) * 128, :], writes=[xt])
            for n in range(2):
                pb = PS[4 + (2 * ti + n) % 4]
                for f in range(NF):
                    k.op("pe", lambda e, pb=pb, f=f, n=n: e.matmul(
                        pb[:, :], lhsT=actT[:, f, i * 128:(i + 1) * 128], rhs=Wdn[:, f, n * 512:(n + 1) * 512],
                        start=(f == 0), stop=(f == NF - 1)), reads=[actT, Wdn], writes=[pb], inc=(f == NF - 1))
                k.op("dve", lambda e, pb=pb, n=n: e.tensor_tensor(out=x2[:, n * 512:(n + 1) * 512], in0=pb[:, :],
                                                                  in1=g2b[:, n * 512:(n + 1) * 512], op=ALU.mult),
                     reads=[pb, g2b], writes=[x2])
            k.op("pool", lambda e: e.tensor_tensor(out=x2[:, :], in0=x2[:, :], in1=xt[:, :], op=ALU.add),
                 reads=[x2, xt], writes=[x2])
            k.op("dve", lambda e: e.scalar_tensor_tensor(out=junk[:, :], in0=x2[:, :], scalar=1.0, in1=x2[:, :],
                                                         op0=ALU.mult, op1=ALU.mult, accum_out=ss[:, 0:1]),
                 reads=[x2], writes=[junk, ss])
            k.op("dve", lambda e: e.tensor_scalar(out=ss[:, 1:2], in0=ss[:, 0:1], scalar1=1.0 / D, scalar2=EPS,
                                                  op0=ALU.mult, op1=ALU.add), reads=[ss], writes=[ss])
            k.op("pool", lambda e: e.tensor_tensor(out=rstd[:, 0:1], in0=ss[:, 1:2], in1=mhalf[:, 0:1], op=ALU.pow),
                 reads=[ss, mhalf], writes=[rstd])
            k.op("dve", lambda e: e.scalar_tensor_tensor(out=y[:, :], in0=x2[:, :], scalar=rstd[:, 0:1],
                                                         in1=fnb[:, :], op0=ALU.mult, op1=ALU.mult),
                 reads=[x2, rstd, fnb], writes=[y])
            k.dma(c.Y[ti * 128:(ti + 1) * 128, :], y[:, :], reads=[y])

        for i in range(4):
            tile_body(i)

    for s in range(9):
        st_body(s)
    k.barrier()
    k.phase_mem()
```

```python
import numpy as np
import concourse.bass as bass
import concourse.mybir as mybir
from concourse.bass_utils import run_bass_kernel_spmd

F32 = mybir.dt.float32
BF16 = mybir.dt.bfloat16
AF = mybir.ActivationFunctionType
ALU = mybir.AluOpType
AX = mybir.AxisListType

D = 1024
NT = 36
TT = NT * 128
NIN = 2832
DFF = 2816
NF = 22
EPS = 1e-6
KSC = 128.0 ** -0.5
NKEY = 38
import os as _os
NDS = 64
PRE_IN_ATT = int(_os.environ.get('PRE_IN_ATT', '1'))
STRICT = bool(int(_os.environ.get('KSTRICT', '1')))


class Tl:
    def __init__(s, h):
        s.h = h
        s.w = []
        s.r = []

    def __getitem__(s, i):
        return s.h[i]


class K:
    ENG = ("pe", "act", "dve", "pool", "sp")

    def __init__(s, nc):
        s.nc = nc
        s.ops = {e: [] for e in s.ENG}
        s.cnt = {e: 0 for e in s.ENG}
        s.seen = {e: {} for e in s.ENG}
        s.sem = {}
        s.dsems = [nc.alloc_semaphore(f"d_{i}") for i in range(NDS)]
        s.dcnt = [0] * NDS
        s.dptr = 0
        for e in s.ENG:
            s.sem[e] = nc.alloc_semaphore(f"s_{e}")
        s.sb_ptr = 0
        s.sb_phase = 0
        s.nalloc = 0

    def sb(s, shape, dt, name=None):
        esz = 4 if dt == F32 else 2
        n = 1
        for d_ in shape[1:]:
            n *= d_
        nbytes = (n * esz + 63) // 64 * 64
        off = s.sb_ptr
        s.sb_ptr += nbytes
        assert s.sb_ptr <= s.sb_top, (name, s.sb_ptr, s.sb_top)
        s.nalloc += 1
        h = s.nc.alloc_sbuf_tensor_at(f"{name or 't'}_{s.nalloc}", list(shape), dt, offset=off)
        return Tl(h)

    def phase_mem(s):
        s.sb_ptr = s.sb_phase

    def op(s, e, fn, reads=(), writes=(), inc=True):
        deps = []
        for t in reads:
            for wt in t.w:
                if wt[0] != e or e != "pe":
                    deps.append(wt)
        for t in writes:
            for wt in t.w:
                if wt[0] != e or (STRICT and e != "pe"):
                    deps.append(wt)
            for rt in t.r:
                if rt[0] != e or (STRICT and e != "pe"):
                    deps.append(rt)
        need = {}
        for key, val in deps:
            if val > need.get(key, 0):
                need[key] = val
        waits = []
        for key, val in need.items():
            if s.seen[e].get(key, 0) >= val:
                continue
            s.seen[e][key] = val
            waits.append((key, val))
        tok = (e, s.cnt[e] + 1)
        if inc:
            s.cnt[e] += 1
        s.ops[e].append((waits, fn, inc, None))
        for t in writes:
            t.w = [tok]
            t.r = []
        for t in reads:
            if t not in writes:
                t.r.append(tok)
                if len(t.r) > 24:
                    t.r = s._compact(t.r)
        return tok

    @staticmethod
    def _compact(r):
        best = {}
        for key, val in r:
            if val > best.get(key, 0):
                best[key] = val
        return list(best.items())

    def dma(s, out, in_, reads=(), writes=(), q="sp", slow=False):
        deps = []
        for t in reads:
            deps.extend(t.w)
        for t in writes:
            for wt in t.w:
                if not isinstance(wt[0], int):
                    deps.append(wt)
            deps.extend(t.r)
        need = {}
        for key, val in deps:
            if val > need.get(key, 0):
                need[key] = val
        waits = []
        for key, val in need.items():
            if s.seen[q].get(key, 0) >= val:
                continue
            s.seen[q][key] = val
            waits.append((key, val))
        si = s.dptr
        s.dptr = (s.dptr + 1) % len(s.dsems)
        if s.dcnt[si] > s.seen[q].get(si, 0):
            s.seen[q][si] = s.dcnt[si]
            waits.append((si, s.dcnt[si]))
        s.dcnt[si] += 16
        tok = (si, s.dcnt[si])
        s.ops[q].append((waits, (out, in_, slow), False, si))
        for t in writes:
            if t.w and all(isinstance(wt[0], int) for wt in t.w):
                t.w = t.w + [tok]
            else:
                t.w = [tok]
            t.r = []
        for t in reads:
            t.r.append(tok)
        return tok

    def barrier(s):
        waits = []
        for e in s.ENG:
            if e != "sp" and s.cnt[e] > s.seen["sp"].get(e, 0):
                s.seen["sp"][e] = s.cnt[e]
                waits.append((e, s.cnt[e]))
        for si in range(len(s.dsems)):
            if s.dcnt[si] > s.seen["sp"].get(si, 0):
                s.seen["sp"][si] = s.dcnt[si]
                waits.append((si, s.dcnt[si]))
        s.cnt["sp"] += 1
        v = s.cnt["sp"]
        s.ops["sp"].append((waits, "inc", True, None))
        for e in s.ENG:
            if e != "sp":
                s.seen[e]["sp"] = v
                s.ops[e].append(([("sp", v)], None, False, None))
                for si in range(len(s.dsems)):
                    s.seen[e][si] = s.dcnt[si]
                for e2 in s.ENG:
                    s.seen[e][e2] = max(s.seen[e].get(e2, 0), s.cnt[e2])

    def semof(s, key):
        return s.dsems[key] if isinstance(key, int) else s.sem[key]

    def emit(s, e, eng):
        for waits, fn, inc, si in s.ops[e]:
            for key, val in waits:
                eng.wait_ge(s.semof(key), val)
            if fn is None:
                continue
            if fn == "inc":
                eng.sem_inc(s.sem[e], 1)
                continue
            if si is not None:
                out, in_, slow = fn
                if slow:
                    ins = eng.dma_start(out=out, in_=in_, allow_slow_non_contiguous=True)
                else:
                    ins = eng.dma_start(out=out, in_=in_)
                ins.then_inc(s.dsems[si], 16)
                continue
            ins = fn(eng)
            if inc:
                ins.then_inc(s.sem[e], 1)


def build_nc(debug=False, stop=99):
    import os
    stop = int(os.environ.get('KSTOP', stop))
    nc = bass.Bass("TRN2", target_bir_lowering=False)
    k = K(nc)
    k.sb_ptr = (nc.sbuf_base + 63) // 64 * 64
    k.sb_top = nc.sbuf_top

    def din(name, shape, dt=F32):
        return nc.dram_tensor(name, list(shape), dt, kind="ExternalInput").ap()

    def dout(name, shape, dt=F32):
        return nc.dram_tensor(name, list(shape), dt, kind="ExternalOutput").ap()

    def dscr(name, shape, dt=BF16):
        return nc.dram_tensor(name, list(shape), dt, kind="ExternalOutput" if debug else "Internal").ap()

    X = din("x", [TT, D])
    CK = din("cache_k", [256, 128])
    CV = din("cache_v", [256, 128])
    SC = din("state_C", [2, 4, 128, 128])
    SN = din("state_n", [2, 4, 128])
    SM = din("state_m", [2, 4])
    COND = din("cond", [2, D])
    WADA = din("w_ada", [D, 6 * D])
    BADA = din("b_ada", [6 * D])
    N1W = din("norm1_w", [D])
    WIN = din("w_in", [D, NIN])
    GB = din("gate_bias", [4, 4])
    QNW = din("q_norm_w", [64])
    KNW = din("k_norm_w", [64])
    MNW = din("mlstm_norm_w", [512])
    WOUT = din("w_out", [D, D])
    N2W = din("norm2_w", [D])
    WGU = din("w_gu", [D, 2 * DFF])
    WDN = din("w_down", [DFF, D])
    FNW = din("final_norm_w", [D])
    ROPE = din("rope", [128, 32, 2, 64])
    IDENT = din("ident", [128, 128])
    MASKS = din("masks", [128, 2, 512])

    Y = dout("y", [TT, D])
    NK = dout("nk", [512, 128])
    NV = dout("nv", [512, 128])
    NC_ = dout("nC", [2, 2, 4, 128, 128])
    NN = dout("nn", [2, 2, 4, 128])
    NM = dout("nm", [2, 2, 4])

    MODS = dscr("mods", [2, 6 * D], F32)
    QTd = dscr("QTd", [4, 128, TT])
    MQTd = dscr("MQTd", [4, 128, TT])
    MKTd = dscr("MKTd", [4, 128, TT])
    MKd = dscr("MKd", [TT, 512])
    MVd = dscr("MVd", [TT, 520])
    MOd = dscr("MOd", [TT, 512])
    GATd = dscr("GATd", [4, 4, TT], F32)
    AOTd = dscr("AOTd", [4, 128, TT])
    HMTd = dscr("HMTd", [4, 128, TT])
    H2Td = dscr("H2Td", [8, 128, TT])
    X1d = dscr("X1d", [TT, D], F32)

    PSALL = nc.alloc_psum_tensor("psall", [128, 4096], F32)
    PS = [Tl(PSALL[:, i * 512:(i + 1) * 512]) for i in range(8)]

    ident = k.sb([128, 128], F32, "ident")
    ones4 = k.sb([4, 128], F32, "ones4")
    eye4 = k.sb([4, 4], F32, "eye4")
    modc = k.sb([128, 2, 6, 8], F32, "modc")
    G1 = k.sb([128, 2, 8], F32, "G1")
    G2 = k.sb([128, 2, 8], F32, "G2")
    n1c = k.sb([128, 8], F32, "n1c")
    n2c = k.sb([128, 8], F32, "n2c")
    k.mhalf = k.sb([128, 8], F32, "mhalf")
    k.op("dve", lambda e: e.memset(k.mhalf[:, :], -0.5), writes=[k.mhalf])
    sb_phase0 = k.sb_ptr
    KT2 = k.sb([128, 2, TT + 256], BF16, "KT2")
    VA = k.sb([128, NKEY, 2, 192], BF16, "VA")
    sb_phase0_a = k.sb_ptr
    Win = k.sb([128, 8, NIN], BF16, "Win")
    Wg = k.sb([128, 8, 128], BF16, "Wg")
    stgw = [k.sb([128, NIN // 2], F32, f"stgw{i}") for i in range(2)]
    k.sb_phase = k.sb_ptr
    k.op("pool", lambda e: e.memset(Wg[:, :, :], 0.0), writes=[Wg])

    k.dma(ident[:, :], IDENT[:, :], writes=[ident])
    k.op("dve", lambda e: e.memset(ones4[:, :], 1.0), writes=[ones4])
    k.dma(eye4[:, :], IDENT[0:4, 0:4], writes=[eye4])

    condT = k.sb([128, 8, 2], F32, "condT")
    sil = k.sb([128, 8, 2], F32, "sil")
    tmpc = k.sb([128, 8, 2], F32, "tmpc")
    mods_sb = k.sb([2, 6 * D], F32, "mods_sb")
    bada = k.sb([2, 6 * D], F32, "bada")
    wa = [k.sb([128, 512], F32, f"wa{i}") for i in range(4)]
    for j in range(2):
        k.dma(condT[:, :, j], COND[j].rearrange("(c p) -> p c", p=128), writes=[condT], slow=True)
    for j in range(2):
        k.dma(bada[j:j + 1, :], BADA.rearrange("(o n) -> o n", o=1), writes=[bada])
    k.op("act", lambda e: e.activation(out=tmpc[:, :, :], in_=condT[:, :, :], func=AF.Exp, scale=-1.0),
         reads=[condT], writes=[tmpc])
    k.op("dve", lambda e: e.tensor_scalar_add(out=tmpc[:, :, :], in0=tmpc[:, :, :], scalar1=1.0),
         reads=[tmpc], writes=[tmpc])
    k.op("dve", lambda e: e.reciprocal(out=tmpc[:, :, :], in_=tmpc[:, :, :]), reads=[tmpc], writes=[tmpc])
    k.op("dve", lambda e: e.tensor_tensor(out=sil[:, :, :], in0=condT[:, :, :], in1=tmpc[:, :, :], op=ALU.mult),
         reads=[condT, tmpc], writes=[sil])
    win_steps = []
    HW = NIN // 2
    for kc in range(8):
        for hf in range(2):
            def step(kc=kc, hf=hf):
                st = stgw[hf]
                k.dma(st[:, :], WIN[kc * 128:(kc + 1) * 128, hf * HW:(hf + 1) * HW], writes=[st])
                k.op("pool", lambda e: e.tensor_copy(out=Win[:, kc, hf * HW:(hf + 1) * HW], in_=st[:, :]),
                     reads=[st], writes=[Win])
                if hf == 1:
                    k.op("pool", lambda e: e.tensor_copy(
                        out=Wg[:, kc, :].rearrange("p (j w) -> p j w", w=32)[:, :, 0:4],
                        in_=st[:, HW - 16:HW].rearrange("p (j w) -> p j w", w=4)), reads=[st], writes=[Wg])
            win_steps.append(step)
    it = 0
    for n in range(12):
        pb = PS[n % 2]
        for kc in range(8):
            w_t = wa[it % 4]
            if it % 6 == 0 and win_steps:
                win_steps.pop(0)()
            it += 1
            k.dma(w_t[:, :], WADA[kc * 128:(kc + 1) * 128, n * 512:(n + 1) * 512], writes=[w_t])
            k.op("pe", lambda e, w_t=w_t, kc=kc, pb=pb: e.matmul(pb[0:2, :], lhsT=sil[:, kc, :], rhs=w_t[:, :],
                                                                start=(kc == 0), stop=(kc == 7)),
                 reads=[w_t, sil], writes=[pb], inc=True)
        k.op("dve", lambda e, n=n, pb=pb: e.tensor_tensor(out=mods_sb[:, n * 512:(n + 1) * 512], in0=pb[0:2, :],
                                                          in1=bada[:, n * 512:(n + 1) * 512], op=ALU.add),
             reads=[pb, bada], writes=[mods_sb])
    while win_steps:
        win_steps.pop(0)()
    k.dma(MODS[:, :], mods_sb[:, :], reads=[mods_sb])
    k.barrier()
    for j in range(2):
        for s6 in range(6):
            k.dma(modc[:, j, s6, :], MODS[j, s6 * D:(s6 + 1) * D].rearrange("(c p) -> p c", p=128),
                  writes=[modc], slow=True)
    k.dma(n1c[:, :], N1W.rearrange("(c p) -> p c", p=128), writes=[n1c], slow=True)
    k.dma(n2c[:, :], N2W.rearrange("(c p) -> p c", p=128), writes=[n2c], slow=True)
    for j in range(2):
        k.op("dve", lambda e, j=j: e.scalar_tensor_tensor(out=G1[:, j, :], in0=modc[:, j, 1, :], scalar=1.0,
                                                          in1=n1c[:, :], op0=ALU.add, op1=ALU.mult),
             reads=[modc, n1c], writes=[G1])
        k.op("dve", lambda e, j=j: e.scalar_tensor_tensor(out=G2[:, j, :], in0=modc[:, j, 4, :], scalar=1.0,
                                                          in1=n2c[:, :], op0=ALU.add, op1=ALU.mult),
             reads=[modc, n2c], writes=[G2])
    k.barrier()
    k.phase_mem()
    ctx = dict(locals())
    if stop >= 1:
        phase_a(ctx)
    if stop >= 2:
        phase_b1(ctx)
    if stop >= 3:
        phase_b2(ctx)
    if stop >= 4:
        phase_c1(ctx)
    if stop >= 5:
        phase_c2(ctx)
    k.barrier()

    with nc.allow_low_precision(reason="bf16 matmul operands by design"), nc.Block() as block:
        names = {"pe": "tensor", "act": "scalar", "dve": "vector", "pool": "gpsimd", "sp": "sync"}
        for e in K.ENG:
            getattr(block, names[e])(lambda eng, e=e: k.emit(e, eng))
    return nc


class NS:
    def __init__(s, d):
        s.__dict__.update(d)


def rstd_from_ss(k, ss, out, n_inv, width):
    k.op("act", lambda e: e.activation(out=out[:, 0:width], in_=ss[:, 0:width], func=AF.Ln, scale=n_inv, bias=EPS),
         reads=[ss], writes=[out])
    k.op("act", lambda e: e.activation(out=out[:, 0:width], in_=out[:, 0:width], func=AF.Exp, scale=-0.5),
         reads=[out], writes=[out])


def norm_to_hT(k, c, xt, hT, col0, Gc, SHc, bufs, pA, pB, defer=None):
    junk, ss, rstd, xs = bufs
    k.op("dve", lambda e: e.scalar_tensor_tensor(out=junk[:, :], in0=xt[:, :], scalar=1.0, in1=xt[:, :],
                                                 op0=ALU.mult, op1=ALU.mult, accum_out=ss[:, 0:1]),
         reads=[xt], writes=[junk, ss])
    rstd_from_ss(k, ss, rstd, 1.0 / D, 1)
    k.op("pool", lambda e: e.tensor_scalar(out=xs[:, :], in0=xt[:, :], scalar1=rstd[:, 0:1], scalar2=1.0,
                                           op0=ALU.mult, op1=ALU.mult),
         reads=[xt, rstd], writes=[xs])

    def pe_part():
        for half, pb in ((0, pA), (1, pB)):
            for cc in range(4):
                ch = half * 4 + cc
                k.op("pe", lambda e, ch=ch, cc=cc, pb=pb: e.transpose(pb[:, cc * 128:(cc + 1) * 128],
                                                                      xs[:, ch * 128:(ch + 1) * 128], c.ident[:, :]),
                     reads=[xs, c.ident], writes=[pb], inc=(cc == 3))
            for cc in range(4):
                ch = half * 4 + cc
                k.op("act", lambda e, ch=ch, cc=cc, pb=pb: e.activation(
                    out=hT[:, ch, col0:col0 + 128], in_=pb[:, cc * 128:(cc + 1) * 128], func=AF.Identity,
                    scale=Gc(ch), bias=SHc(ch)), reads=[pb, c.G1, c.G2, c.modc], writes=[hT])

    if defer is None:
        pe_part()
    else:
        defer.append(pe_part)


def phase_a(ctx):
    c = NS(ctx)
    k = c.k
    PS = c.PS
    KT2, VA, Win, Wg = c.KT2, c.VA, c.Win, c.Wg
    k.op("pool", lambda e: e.memset(VA[:, :, :, :], 1.0), writes=[VA])
    xb = [k.sb([128, D], F32, f"xb{i}") for i in range(3)]
    junk = k.sb([128, D], BF16, "junk")
    xs = [k.sb([128, D], F32, f"xs{i}") for i in range(3)]
    ss = k.sb([128, 8], F32, "ss")
    rstd = k.sb([128, 8], F32, "rstd")
    hT = [k.sb([128, 8, 512], BF16, f"hT{i}") for i in range(2)]
    ropet = [k.sb([128, 4, 2, 64], F32, f"rope{i}") for i in range(2)]
    wq_bc = k.sb([128, 64], F32, "wq_bc")
    wk_bc = k.sb([128, 64], F32, "wk_bc")
    gbcol = k.sb([128, 1], F32, "gbcol")
    qf = k.sb([128, 512], F32, "qf")
    sq = k.sb([128, 512], F32, "sq")
    qn2 = [k.sb([128, 512], F32, f"qn{i}") for i in range(2)]
    t1 = k.sb([128, 512], F32, "t1")
    t2 = k.sb([128, 512], F32, "t2")
    qr2 = [k.sb([128, 512], F32, f"qr{i}") for i in range(2)]
    kvf = [k.sb([128, 256], F32, f"kvf{i}") for i in range(2)]
    kn = [k.sb([128, 128], F32, f"kn{i}") for i in range(2)]
    kt1 = k.sb([128, 128], F32, "kt1")
    kt2 = k.sb([128, 128], F32, "kt2")
    kr = k.sb([128, 128], F32, "kr")
    kdup2 = [k.sb([128, 2, 2, 64], F32, f"kdup{i}") for i in range(2)]
    ssq = k.sb([128, 8], F32, "ssq")
    rsq = k.sb([128, 8], F32, "rsq")
    ssk = k.sb([128, 8], F32, "ssk")
    rsk = k.sb([128, 8], F32, "rsk")
    mot = k.sb([128, 512], F32, "mot")
    gsb = k.sb([128, 512], F32, "gsb")
    gtmp = k.sb([128, 512], F32, "gtmp")
    cstg = k.sb([128, 2, 128], F32, "cstg")
    QTs = [k.sb([128, 4, 512], BF16, f"QTs{i}") for i in range(1)]
    MKs = [k.sb([128, 4, 512], BF16, f"MKs{i}") for i in range(1)]
    MVs = [k.sb([128, 4, 4, 130], BF16, f"MVs{i}") for i in range(1)]
    MOs = [k.sb([128, 4, 512], BF16, f"MOs{i}") for i in range(1)]
    MQTs = [k.sb([128, 4, 512], BF16, f"MQTs{i}") for i in range(1)]
    MKTs = [k.sb([128, 4, 512], BF16, f"MKTs{i}") for i in range(1)]

    k.dma(wq_bc[:, :], c.QNW.partition_broadcast(128), writes=[wq_bc])
    k.dma(wk_bc[:, :], c.KNW.partition_broadcast(128), writes=[wk_bc])
    k.op("dve", lambda e: e.tensor_scalar_mul(out=wq_bc[:, :], in0=wq_bc[:, :], scalar1=0.125),
         reads=[wq_bc], writes=[wq_bc])
    k.op("dve", lambda e: e.memset(gbcol[:, :], 0.0), writes=[gbcol])
    for j in range(4):
        k.dma(gbcol[32 * j:32 * j + 4, 0:1], c.GB[j].rearrange("(h o) -> h o", o=1), writes=[gbcol], slow=True)
    k.op("pool", lambda e: e.memset(MVs[0][:, :, :, :], 1.0), writes=[MVs[0]])
    kdup = kdup2[0]
    for blk in range(2):
        k.dma(cstg[:, 0, :], c.CK[blk * 128:(blk + 1) * 128, :], writes=[cstg])
        k.dma(cstg[:, 1, :], c.CV[blk * 128:(blk + 1) * 128, :], writes=[cstg])
        k.op("dve", lambda e: e.tensor_copy(out=kdup[:, :, :, :],
                                            in_=cstg[:, 0, :].rearrange("p (g o d) -> p g o d", g=2, o=1)
                                            .to_broadcast([128, 2, 2, 64])), reads=[cstg], writes=[kdup])
        pk = PS[2]
        for g in range(2):
            k.op("pe", lambda e, g=g: e.transpose(pk[:, g * 128:(g + 1) * 128],
                                                  kdup[:, g, :, :].rearrange("p a d -> p (a d)"), c.ident[:, :]),
                 reads=[kdup, c.ident], writes=[pk], inc=(g == 1))
        col = TT + blk * 128
        k.op("act", lambda e, col=col: e.activation(out=KT2[:, :, col:col + 128],
                                                    in_=pk[:, 0:256].rearrange("p (g t) -> p g t", g=2),
                                                    func=AF.Copy), reads=[pk], writes=[KT2])
        k.op("dve", lambda e, blk=blk: e.tensor_copy(out=VA[:, 36 + blk, :, 64:128],
                                                     in_=cstg[:, 1, :].rearrange("p (g d) -> p g d", g=2)),
             reads=[cstg], writes=[VA])

    class Deferred(list):
        cur = 0

        def append(self, fn):
            list.append(self, (self.cur, fn))

    deferred = Deferred()

    def flush(upto=10 ** 9):
        while deferred and deferred[0][0] <= upto:
            deferred.pop(0)[1]()

    def tile_a(s, i):
        cj = 0 if s < 8 else 1
        ti = s * 4 + i
        xt = xb[ti % 3]
        k.dma(xt[:, :], c.X[ti * 128:(ti + 1) * 128, :], writes=[xt])
        norm_to_hT(k, c, xt, hT[s % 2], i * 128,
                   lambda ch, cj=cj: c.G1[:, cj, ch:ch + 1], lambda ch, cj=cj: c.modc[:, cj, 0, ch:ch + 1],
                   (junk, ss, rstd, xs[ti % 3]), PS[0], PS[1], defer=deferred)

    def st_body(s):
        cj = 0 if s < 8 else 1
        smp = s < 8
        h = hT[s % 2]
        if smp:
            rp = ropet[s % 2]
            k.dma(rp[:, :, :, :], c.ROPE[:, s * 4:(s + 1) * 4, :, :], writes=[rp])
        QT_, MK_, MV_, MO_, MQT_, MKT_ = QTs[0], MKs[0], MVs[0], MOs[0], MQTs[0], MKTs[0]
        if s == 0:
            for i in range(4):
                tile_a(0, i)
                flush()

        def tile_b(i):
            ti = s * 4 + i
            tsl = slice(i * 128, (i + 1) * 128)
            pq, pkv, pmk, pmv, pmo = PS[2], PS[3], PS[4], PS[5], PS[6]
            for (pb, c0, c1) in ((pmk, 1280, 1792), (pmv, 1792, 2304), (pmo, 2304, 2816), (pq, 0, 512),
                                 (pkv, 512, 768)):
                for kc in range(8):
                    k.op("pe", lambda e, pb=pb, c0=c0, c1=c1, kc=kc, tsl=tsl: e.matmul(
                        pb[:, 0:c1 - c0], lhsT=h[:, kc, tsl], rhs=Win[:, kc, c0:c1], start=(kc == 0), stop=(kc == 7)),
                        reads=[h, Win], writes=[pb], inc=(kc == 7))
            flush(ti - 2)
            deferred.cur = ti
            qn, qr, kdup = qn2[ti % 2], qr2[ti % 2], kdup2[ti % 2]
            k.op("act", lambda e, i=i: e.activation(out=MK_[:, i, :], in_=pmk[:, :], func=AF.Copy, scale=KSC),
                 reads=[pmk], writes=[MK_])
            k.op("dve", lambda e, i=i: e.tensor_copy(out=MV_[:, i, :, 0:128],
                                                     in_=pmv[:, :].rearrange("p (h d) -> p h d", h=4)),
                 reads=[pmv], writes=[MV_])
            k.op("act", lambda e: e.activation(out=mot[:, :], in_=pmo[:, :], func=AF.Exp, scale=-1.0),
                 reads=[pmo], writes=[mot])
            k.op("act", lambda e: e.activation(out=mot[:, :], in_=mot[:, :], func=AF.Ln, bias=1.0),
                 reads=[mot], writes=[mot])
            k.op("act", lambda e, i=i: e.activation(out=MO_[:, i, :], in_=mot[:, :], func=AF.Exp, scale=-1.0),
                 reads=[mot], writes=[MO_])
            kv = kvf[ti % 2]
            kk = kn[ti % 2]
            k.op("act", lambda e: e.activation(out=qf[:, :], in_=pq[:, :], func=AF.Copy), reads=[pq], writes=[qf])
            k.op("dve", lambda e: e.tensor_tensor(out=sq[:, :], in0=qf[:, :], in1=qf[:, :], op=ALU.mult),
                 reads=[qf], writes=[sq])
            k.op("dve", lambda e: e.tensor_reduce(out=ssq[:, 0:8], in_=sq[:, :].rearrange("p (h d) -> p h d", d=64),
                                                  axis=AX.X, op=ALU.add), reads=[sq], writes=[ssq])
            rstd_from_ss(k, ssq, rsq, 1.0 / 64, 8)
            k.op("dve", lambda e: e.tensor_tensor(out=qn[:, :].rearrange("p (h d) -> p h d", d=64),
                                                  in0=qf[:, :].rearrange("p (h d) -> p h d", d=64),
                                                  in1=rsq[:, 0:8].unsqueeze(2).to_broadcast([128, 8, 64]),
                                                  op=ALU.mult), reads=[qf, rsq], writes=[qn])
            k.op("dve", lambda e: e.tensor_tensor(out=qn[:, :].rearrange("p (h d) -> p h d", d=64),
                                                  in0=qn[:, :].rearrange("p (h d) -> p h d", d=64),
                                                  in1=wq_bc[:, :].unsqueeze(1).to_broadcast([128, 8, 64]),
                                                  op=ALU.mult), reads=[qn, wq_bc], writes=[qn])
            if smp:
                k.op("pool", lambda e, i=i: e.tensor_tensor(
                    out=t1[:, :].rearrange("p (h d) -> p h d", d=64),
                    in0=qn[:, :].rearrange("p (h d) -> p h d", d=64),
                    in1=rp[:, i, 0, :].unsqueeze(1).to_broadcast([128, 8, 64]), op=ALU.mult),
                    reads=[qn, rp], writes=[t1])
                for jj in range(2):
                    k.op("pool", lambda e, i=i, jj=jj: e.tensor_tensor(
                        out=t2[:, :].rearrange("p (h a j w) -> p h a j w", h=8, a=2, j=2)[:, :, :, jj, :],
                        in0=qn[:, :].rearrange("p (h a j w) -> p h a j w", h=8, a=2, j=2)[:, :, :, 1 - jj, :],
                        in1=rp[:, i, 1, :].rearrange("p (a j w) -> p a j w", a=2, j=2)[:, :, jj, :].unsqueeze(1)
                        .to_broadcast([128, 8, 2, 16]), op=ALU.mult),
                        reads=[qn, rp], writes=[t2])
                k.op("pool", lambda e: e.tensor_tensor(out=qr[:, :], in0=t1[:, :], in1=t2[:, :], op=ALU.add),
                     reads=[t1, t2], writes=[qr])
                qsrc = qr
            else:
                qsrc = qn
            def q_tr(qsrc=qsrc, tsl=tsl):
                pt = PS[7]
                for j in range(4):
                    k.op("pe", lambda e, j=j: e.transpose(pt[:, j * 128:(j + 1) * 128],
                                                          qsrc[:, j * 128:(j + 1) * 128], c.ident[:, :]),
                         reads=[qsrc, c.ident], writes=[pt], inc=(j == 3))
                k.op("act", lambda e: e.activation(out=QT_[:, :, tsl],
                                                   in_=pt[:, :].rearrange("p (j t) -> p j t", j=4),
                                                   func=AF.Copy), reads=[pt], writes=[QT_])
            deferred.append(q_tr)
            k.op("act", lambda e, kv=kv: e.activation(out=kv[:, :], in_=pkv[:, 0:256], func=AF.Copy),
                 reads=[pkv], writes=[kv])
            k.op("dve", lambda e, kv=kv: e.tensor_tensor(out=kt1[:, :], in0=kv[:, 0:128], in1=kv[:, 0:128],
                                                         op=ALU.mult), reads=[kv], writes=[kt1])
            k.op("dve", lambda e: e.tensor_reduce(out=ssk[:, 0:2], in_=kt1[:, :].rearrange("p (h d) -> p h d", d=64),
                                                  axis=AX.X, op=ALU.add), reads=[kt1], writes=[ssk])
            rstd_from_ss(k, ssk, rsk, 1.0 / 64, 2)
            k.op("dve", lambda e, kv=kv, kk=kk: e.tensor_tensor(
                out=kk[:, :].rearrange("p (h d) -> p h d", d=64),
                in0=kv[:, 0:128].rearrange("p (h d) -> p h d", d=64),
                in1=rsk[:, 0:2].unsqueeze(2).to_broadcast([128, 2, 64]), op=ALU.mult),
                reads=[kv, rsk], writes=[kk])
            k.op("dve", lambda e, kk=kk: e.tensor_tensor(
                out=kk[:, :].rearrange("p (h d) -> p h d", d=64),
                in0=kk[:, :].rearrange("p (h d) -> p h d", d=64),
                in1=wk_bc[:, :].unsqueeze(1).to_broadcast([128, 2, 64]), op=ALU.mult),
                reads=[kk, wk_bc], writes=[kk])
            if smp:
                k.op("pool", lambda e, i=i, kk=kk: e.tensor_tensor(
                    out=kt1[:, :].rearrange("p (h d) -> p h d", d=64),
                    in0=kk[:, :].rearrange("p (h d) -> p h d", d=64),
                    in1=rp[:, i, 0, :].unsqueeze(1).to_broadcast([128, 2, 64]), op=ALU.mult),
                    reads=[kk, rp], writes=[kt1])
                for jj in range(2):
                    k.op("pool", lambda e, i=i, kk=kk, jj=jj: e.tensor_tensor(
                        out=kt2[:, :].rearrange("p (h a j w) -> p h a j w", h=2, a=2, j=2)[:, :, :, jj, :],
                        in0=kk[:, :].rearrange("p (h a j w) -> p h a j w", h=2, a=2, j=2)[:, :, :, 1 - jj, :],
                        in1=rp[:, i, 1, :].rearrange("p (a j w) -> p a j w", a=2, j=2)[:, :, jj, :].unsqueeze(1)
                        .to_broadcast([128, 2, 2, 16]), op=ALU.mult),
                        reads=[kk, rp], writes=[kt2])
                k.op("pool", lambda e: e.tensor_tensor(out=kr[:, :], in0=kt1[:, :], in1=kt2[:, :], op=ALU.add),
                     reads=[kt1, kt2], writes=[kr])
                ksrc = kr
            else:
                ksrc = kk
                pt0 = (ti - 32) * 128
                k.dma(c.NK[pt0:pt0 + 128, :], kk[:, :], reads=[kk])
                k.dma(c.NV[pt0:pt0 + 128, :], kv[:, 128:256], reads=[kv])
            k.op("dve", lambda e, ksrc=ksrc: e.tensor_copy(
                out=kdup[:, :, :, :], in_=ksrc[:, :].rearrange("p (g o d) -> p g o d", g=2, o=1)
                .to_broadcast([128, 2, 2, 64])), reads=[ksrc], writes=[kdup])
            def k_tr(ti=ti):
                pk = PS[7]
                for g in range(2):
                    k.op("pe", lambda e, g=g: e.transpose(pk[:, g * 128:(g + 1) * 128],
                                                          kdup[:, g, :, :].rearrange("p a d -> p (a d)"),
                                                          c.ident[:, :]),
                         reads=[kdup, c.ident], writes=[pk], inc=(g == 1))
                k.op("act", lambda e: e.activation(out=KT2[:, :, ti * 128:(ti + 1) * 128],
                                                   in_=pk[:, 0:256].rearrange("p (g t) -> p g t", g=2),
                                                   func=AF.Copy), reads=[pk], writes=[KT2])
            deferred.append(k_tr)
            k.op("dve", lambda e, ti=ti, kv=kv: e.tensor_copy(out=VA[:, ti, :, 64:128],
                                                              in_=kv[:, 128:256].rearrange("p (g d) -> p g d", g=2)),
                 reads=[kv], writes=[VA])
        for i in range(4):
            tile_b(i)
            if s + 1 < 9:
                tile_a(s + 1, i)
        for hh in range(8):
            pb = PS[2 + hh % 4]
            c0 = 768 + hh * 128
            for kc in range(8):
                k.op("pe", lambda e, pb=pb, c0=c0, kc=kc: e.matmul(
                    pb[:, :], lhsT=Win[:, kc, c0:c0 + 128], rhs=h[:, kc, :], start=(kc == 0), stop=(kc == 7)),
                    reads=[h, Win], writes=[pb], inc=(kc == 7))
            if hh == 3:
                flush()
            if hh < 4:
                k.op("dve", lambda e, pb=pb, hh=hh: e.tensor_copy(out=MQT_[:, hh, :], in_=pb[:, :]),
                     reads=[pb], writes=[MQT_])
            else:
                k.op("act", lambda e, pb=pb, hh=hh: e.activation(out=MKT_[:, hh - 4, :], in_=pb[:, :], func=AF.Copy,
                                                                 scale=KSC), reads=[pb], writes=[MKT_])
        pg = PS[6]
        for kc in range(8):
            k.op("pe", lambda e, kc=kc: e.matmul(pg[:, :], lhsT=Wg[:, kc, :], rhs=h[:, kc, :], start=(kc == 0),
                                                 stop=(kc == 7)), reads=[h, Wg], writes=[pg], inc=(kc == 7))
        k.op("act", lambda e: e.activation(out=gsb[:, :], in_=pg[:, :], func=AF.Identity, bias=gbcol[:, 0:1]),
             reads=[pg, gbcol], writes=[gsb])
        for r0 in (32, 96):
            k.op("act", lambda e, r0=r0: e.activation(out=gtmp[r0:r0 + 4, :], in_=gsb[r0:r0 + 4, :], func=AF.Exp,
                                                      scale=-1.0), reads=[gsb], writes=[gtmp])
            k.op("act", lambda e, r0=r0: e.activation(out=gtmp[r0:r0 + 4, :], in_=gtmp[r0:r0 + 4, :], func=AF.Ln,
                                                      bias=1.0), reads=[gtmp], writes=[gtmp])
            k.op("dve", lambda e, r0=r0: e.tensor_scalar_mul(out=gsb[r0:r0 + 4, :], in0=gtmp[r0:r0 + 4, :],
                                                             scalar1=-1.0), reads=[gtmp], writes=[gsb])
        flush()
        tok = slice(s * 512, (s + 1) * 512)
        k.dma(c.QTd[:, :, tok].rearrange("c p t -> p c t"), QT_[:, :, :], reads=[QT_])
        k.dma(c.MQTd[:, :, tok].rearrange("c p t -> p c t"), MQT_[:, :, :], reads=[MQT_])
        k.dma(c.MKTd[:, :, tok].rearrange("c p t -> p c t"), MKT_[:, :, :], reads=[MKT_])
        k.dma(c.MKd[tok, :].rearrange("(i p) f -> p i f", p=128), MK_[:, :, :], reads=[MK_])
        k.dma(c.MVd[tok, :].rearrange("(i p) f -> p i f", p=128), MV_[:, :, :, :].rearrange("p i h d -> p i (h d)"),
              reads=[MV_])
        k.dma(c.MOd[tok, :].rearrange("(i p) f -> p i f", p=128), MO_[:, :, :], reads=[MO_])
        for j in range(4):
            k.dma(c.GATd[j, :, tok], gsb[32 * j:32 * j + 4, :], reads=[gsb])
    for s in range(9):
        st_body(s)
    k.barrier()
    k.sb_phase = c.sb_phase0_a
    k.phase_mem()


def _consts():
    f = np.float32
    tok = np.arange(4096)
    row = (tok // 64).astype(f)
    col = (tok % 64).astype(f)
    inv = (f(10000.0) ** (-np.arange(0, 32, 2, dtype=f) / f(32))).astype(f)
    ang = np.stack([row[:, None] * inv[None, :], col[:, None] * inv[None, :]], axis=1).astype(f)
    cs, sn = np.cos(ang).astype(f), np.sin(ang).astype(f)
    Cf = np.stack([cs, cs], axis=2)
    Ss = np.stack([-sn, sn], axis=2)
    tab = np.stack([Cf.reshape(4096, 64), Ss.reshape(4096, 64)], axis=1)
    rope = np.ascontiguousarray(tab.reshape(32, 128, 2, 64).transpose(1, 0, 2, 3))
    ident = np.eye(128, dtype=f)
    sidx = np.arange(128)[:, None]
    lidx = np.arange(128)[None, :]
    mf = (lidx >= sidx).astype(f)
    mb = (lidx <= sidx).astype(f)
    masks = np.stack([np.tile(mf, (1, 4)), np.tile(mb, (1, 4))], axis=1)
    return rope, ident, np.ascontiguousarray(masks)


def make_in_maps(inp):
    rope, ident, masks = _consts()
    f = np.float32
    c = lambda a: np.ascontiguousarray(np.asarray(a), dtype=f)
    maps = []
    for b in range(8):
        x = np.concatenate([inp["x_sample"][b], inp["x_prompt"][2 * b], inp["x_prompt"][2 * b + 1]], axis=0)
        m = {
            "x": c(x),
            "cache_k": c(np.asarray(inp["cache_k"])[b, 0].reshape(256, 128)),
            "cache_v": c(np.asarray(inp["cache_v"])[b, 0].reshape(256, 128)),
            "state_C": c(np.asarray(inp["state_C"])[b, 0]),
            "state_n": c(np.asarray(inp["state_n"])[b, 0]),
            "state_m": c(np.asarray(inp["state_m"])[b, 0]),
            "cond": c(np.stack([np.asarray(inp["c"])[b], np.asarray(inp["c_ctx"])], axis=0)),
            "w_ada": c(np.asarray(inp["w_ada"])[0]), "b_ada": c(np.asarray(inp["b_ada"])[0]),
            "norm1_w": c(np.asarray(inp["norm1_w"])[0]), "w_in": c(np.asarray(inp["w_in"])[0]),
            "gate_bias": c(np.asarray(inp["gate_bias"])[0]), "q_norm_w": c(np.asarray(inp["q_norm_w"])[0]),
            "k_norm_w": c(np.asarray(inp["k_norm_w"])[0]), "mlstm_norm_w": c(np.asarray(inp["mlstm_norm_w"])[0]),
            "w_out": c(np.asarray(inp["w_out"])[0]), "norm2_w": c(np.asarray(inp["norm2_w"])[0]),
            "w_gu": c(np.asarray(inp["w_gu"])[0]), "w_down": c(np.asarray(inp["w_down"])[0]),
            "final_norm_w": c(inp["final_norm_w"]),
            "rope": rope, "ident": ident, "masks": masks,
        }
        maps.append(m)
    return maps


_NC = None


def kernel(**inp):
    global _NC
    if _NC is None:
        _NC = build_nc()
    maps = make_in_maps(inp)
    res = run_bass_kernel_spmd(_NC, maps, core_ids=list(range(8)))
    R = res.results
    f = np.float32
    y_s = np.stack([R[b]["y"][0:4096] for b in range(8)], axis=0).astype(f)
    y_p = np.stack([R[b]["y"][4096 + 256 * j:4096 + 256 * (j + 1)] for b in range(8) for j in range(2)], axis=0).astype(f)
    nk = np.stack([R[b]["nk"][256 * j:256 * (j + 1)].reshape(256, 2, 64) for b in range(8) for j in range(2)], axis=0)
    nv = np.stack([R[b]["nv"][256 * j:256 * (j + 1)].reshape(256, 2, 64) for b in range(8) for j in range(2)], axis=0)
    nC = np.stack([R[b]["nC"][j] for b in range(8) for j in range(2)], axis=0)
    nn = np.stack([R[b]["nn"][j] for b in range(8) for j in range(2)], axis=0)
    nm = np.stack([R[b]["nm"][j] for b in range(8) for j in range(2)], axis=0)
    return (y_p, y_s, nk[:, None].astype(f), nv[:, None].astype(f), nC[:, None].astype(f), nn[:, None].astype(f),
            nm[:, None].astype(f))


def phase_b1(ctx):
    c = NS(ctx)
    k = c.k
    PS = c.PS
    KT2, VA = c.KT2, c.VA
    pre_gen, b2_main = make_b2(ctx)
    ctx["b2_main"] = b2_main
    k.sb_phase = k.sb_ptr
    pre = pre_gen()
    SP = [Tl(c.PSALL[:, b * 1024:(b + 1) * 1024]) for b in range(2)]
    QTg = [k.sb([128, 4, 512], BF16, f"QTg{i}") for i in range(2)]
    PTP = [k.sb([128, 1024], BF16, f"PT{i}") for i in range(3)]
    AO = [k.sb([128, 4, 512], BF16, f"AO{i}") for i in range(2)]
    rden = [k.sb([128, 512], F32, f"rden{i}") for i in range(2)]
    obs = k.sb([128, 512], F32, "obs")
    groups = [(g * 512, 512, list(range(32)) + [36, 37]) for g in range(8)]
    groups += [(4096, 256, [32, 33]), (4352, 256, [34, 35])]
    cnt = [0]

    def group_body(gi, t0, nq, kbs):
        qt = QTg[gi % 2]
        ao = AO[gi % 2]
        k.dma(qt[:, :, 0:nq], c.QTd[:, :, t0:t0 + nq].rearrange("c p t -> p c t"), writes=[qt])
        iters = [(j, idx, kb) for j in range(4) for idx, kb in enumerate(kbs)]
        base = cnt[0]
        cnt[0] += len(iters)

        def emit_s(n):
            j, idx, kb = iters[n]
            g = j // 2
            kcol = kb * 128 if kb < 36 else TT + (kb - 36) * 128
            sp = SP[(base + n) % 2]
            k.op("pe", lambda e: e.matmul(sp[:, 0:nq], lhsT=KT2[0:64, g, kcol:kcol + 128], rhs=qt[0:64, j, 0:nq],
                                          start=True, stop=True), reads=[KT2, qt], writes=[sp], inc=False)
            k.op("pe", lambda e: e.matmul(sp[:, 512:512 + nq], lhsT=KT2[64:128, g, kcol:kcol + 128],
                                          rhs=qt[64:128, j, 0:nq], start=True, stop=True),
                 reads=[KT2, qt], writes=[sp])

        def emit_rest(n):
            j, idx, kb = iters[n]
            g = j // 2
            sp = SP[(base + n) % 2]
            pt = PTP[(base + n) % 3]
            oa, ob = (PS[4], PS[6])[j % 2], PS[5]
            k.op("act", lambda e: e.activation(out=pt[:, :].rearrange("p (a t) -> p a t", a=2)[:, :, 0:nq],
                                               in_=sp[:, :].rearrange("p (a t) -> p a t", a=2)[:, :, 0:nq],
                                               func=AF.Exp), reads=[sp], writes=[pt])
            first, last = idx == 0, idx == len(kbs) - 1
            k.op("pe", lambda e: e.matmul(oa[:, 0:nq], lhsT=VA[:, kb, g, 64:192], rhs=pt[:, 0:nq], start=first,
                                          stop=last), reads=[VA, pt], writes=[oa], inc=False)
            k.op("pe", lambda e: e.matmul(ob[:, 0:nq], lhsT=VA[:, kb, g, 0:128], rhs=pt[:, 512:512 + nq],
                                          start=first, stop=last), reads=[VA, pt], writes=[ob])
            if last:
                ra, rb = rden[0], rden[1]
                k.op("dve", lambda e: e.tensor_copy(out=obs[:, 0:nq], in_=ob[:, 0:nq]), reads=[ob], writes=[obs])
                k.op("dve", lambda e: e.reciprocal(out=rb[64:128, 0:nq], in_=obs[0:64, 0:nq]), reads=[obs],
                     writes=[rb])
                k.op("dve", lambda e: e.tensor_tensor(out=ao[64:128, j, 0:nq], in0=obs[64:128, 0:nq],
                                                      in1=rb[64:128, 0:nq], op=ALU.mult), reads=[obs, rb],
                     writes=[ao])
                k.op("dve", lambda e: e.reciprocal(out=ra[0:64, 0:nq], in_=oa[64:128, 0:nq]), reads=[oa], writes=[ra])
                k.op("dve", lambda e: e.tensor_tensor(out=ao[0:64, j, 0:nq], in0=oa[0:64, 0:nq], in1=ra[0:64, 0:nq],
                                                      op=ALU.mult), reads=[oa, ra], writes=[ao])

        emit_s(0)
        for n in range(len(iters)):
            if n + 1 < len(iters):
                emit_s(n + 1)
            emit_rest(n)
            if PRE_IN_ATT:
                next(pre, None)
        k.dma(c.AOTd[:, :, t0:t0 + nq].rearrange("c p t -> p c t"), ao[:, :, 0:nq], reads=[ao])

    for gi, (t0, nq, kbs) in enumerate(groups):
        group_body(gi, t0, nq, kbs)
    for _ in pre:
        pass
    k.barrier()
    k.phase_mem()


def make_b2(ctx):
    c = NS(ctx)
    k = c.k
    PS = c.PS
    pS = PS[0]
    pGd, pId, pTd = (PS[1], PS[4]), (PS[2], PS[5]), (PS[3], PS[6])
    pX = PS[7]
    Cst = [k.sb([128, 4, 130], F32, f"Cst{i}") for i in range(2)]
    Csnap = k.sb([128, NT, 4, 130], BF16, "Csnap")
    mst = [k.sb([4, 2], F32, f"mst{i}") for i in range(2)]
    mbprev = k.sb([4, NT + 4], F32, "mbprev")
    GT = [k.sb([4, 4, 128], F32, f"GT{i}") for i in range(2)]
    kk = [k.sb([128, 4, 128], BF16, f"kk{i}") for i in range(2)]
    va = [k.sb([128, 4, 130], BF16, f"va{i}") for i in range(2)]
    rows = [[k.sb([4, 128], F32, f"row{d}_{i}") for i in range(8)] for d in range(2)]
    dg = [k.sb([4, 4], F32, f"dg{d}") for d in range(2)]
    cols = [k.sb([128, 24], F32, f"cols{d}") for d in range(2)]
    kw = k.sb([128, 512], BF16, "kw")
    cols1p = [cols[1], k.sb([128, 24], F32, "cols1b")]
    ident, ones4, eye4 = c.ident, c.ones4, c.eye4
    maskneg = identb = None
    masks = mnw = Cbf = qT = kT = mo = blk = Dsb = swT = qI = hd = hm = sqh = ss4 = rs4 = hmT = None

    def late_alloc():
        nonlocal maskneg, identb
        nonlocal masks, mnw, Cbf, qT, kT, mo, blk, Dsb, swT, qI, hd, hm, sqh, ss4, rs4, hmT
        masks = k.sb([128, 2, 512], F32, "masks")
        mnw = k.sb([128, 512], F32, "mnw")
        Cbf = k.sb([128, 4, 130], BF16, "Cbf")
        qT = [k.sb([128, 4, 128], BF16, f"qT{i}") for i in range(2)]
        kT = [k.sb([128, 4, 128], BF16, f"kT{i}") for i in range(2)]
        mo = [k.sb([128, 512], BF16, f"mo{i}") for i in range(2)]
        blk = [[k.sb([4, 4, 128], F32, f"blk{d}_{i}") for i in range(2)] for d in range(2)]
        Dsb = [k.sb([128, 512], F32, f"Dsb{d}") for d in range(2)]
        swT = [k.sb([128, 512], BF16, f"swT{d}") for d in range(2)]
        qI = [k.sb([128, 512], BF16, f"qI{d}") for d in range(2)]
        hd = [[k.sb([128, 512], F32, f"hd{p}_{d}") for d in range(2)] for p in range(2)]
        hm = k.sb([128, 512], F32, "hm")
        sqh = k.sb([128, 512], F32, "sqh")
        ss4 = k.sb([128, 8], F32, "ss4")
        rs4 = k.sb([128, 8], F32, "rs4")
        hmT = [k.sb([128, 4, 128], BF16, f"hmT{i}") for i in range(2)]
        maskneg = k.sb([128, 2, 512], BF16, "maskneg")
        identb = k.sb([128, 128], BF16, "identb")
        k.dma(masks[:, :, :], c.MASKS[:, :, :], writes=[masks])
        k.dma(mnw[:, :], c.MNW.partition_broadcast(128), writes=[mnw])
        k.op("dve", lambda e: e.tensor_scalar(out=maskneg[:, :, :], in0=masks[:, :, :], scalar1=-1.0, scalar2=30000.0,
                                              op0=ALU.add, op1=ALU.mult), reads=[masks], writes=[maskneg])
        k.op("dve", lambda e: e.tensor_copy(out=identb[:, :], in_=ident[:, :]), reads=[ident], writes=[identb])

    def load_chunk(ci, full):
        tok = slice(ci * 128, (ci + 1) * 128)
        i2 = ci % 2
        k.dma(GT[i2][:, :, :], c.GATd[:, :, tok].rearrange("t h n -> h t n"), writes=[GT[i2]])
        k.dma(kk[i2][:, :, :], c.MKd[tok, :].rearrange("p (h d) -> p h d", h=4), writes=[kk[i2]])
        k.dma(va[i2][:, :, :], c.MVd[tok, :].rearrange("p (h d) -> p h d", h=4), writes=[va[i2]])
        if full:
            k.dma(qT[i2][:, :, :], c.MQTd[:, :, tok].rearrange("h p t -> p h t"), writes=[qT[i2]])
            k.dma(kT[i2][:, :, :], c.MKTd[:, :, tok].rearrange("h p t -> p h t"), writes=[kT[i2]])
            k.dma(mo[i2][:, :], c.MOd[tok, :], writes=[mo[i2]])

    def gate_prep(d, gt, mprev, full, upd, cl=None, pT=None):
        mt, mc = mprev
        rb_, ra_, rg_, rng_, rin_, rgu_, rw_, rt_ = rows[d]
        pT = pT or pTd[d]
        cl = cl or cols[d]
        lf = lambda: gt[:, 1 + 2 * d, :]
        ig = lambda: gt[:, 2 * d, :]
        rv = (lambda ap: ap) if d == 0 else (lambda ap: ap[:, ::-1])
        last = 127 if d == 0 else 0
        k.op("dve", lambda e: e.tensor_tensor_scan(out=rv(rb_[:, :]), data0=rv(ones4[:, :]), data1=rv(lf()),
                                                   initial=0.0, op0=ALU.mult, op1=ALU.add),
             reads=[gt, ones4], writes=[rb_])
        yield
        k.op("dve", lambda e: e.tensor_tensor(out=ra_[:, :], in0=ig(), in1=rb_[:, :], op=ALU.subtract),
             reads=[gt, rb_], writes=[ra_])
        yield
        k.op("pe", lambda e: e.transpose(pT[:, 0:4], ra_[:, :], ident[0:4, 0:4]), reads=[ra_, ident], writes=[pT])
        k.op("dve", lambda e: e.tensor_tensor_scan(out=rv(rg_[:, :]), data0=rv(ra_[:, :]), data1=rv(ra_[:, :]),
                                                   initial=mt[:, mc:mc + 1], op0=ALU.max, op1=ALU.max),
             reads=[ra_, mt], writes=[rg_])
        yield
        k.op("dve", lambda e: e.tensor_scalar_mul(out=rng_[:, :], in0=rg_[:, :], scalar1=-1.0),
             reads=[rg_], writes=[rng_])
        if full:
            k.op("dve", lambda e: e.tensor_tensor(out=rt_[:, :], in0=rb_[:, :], in1=rg_[:, :], op=ALU.add),
                 reads=[rb_, rg_], writes=[rt_])
            yield
            bk0 = blk[d][0]
            k.op("dve", lambda e: e.tensor_tensor(
                out=bk0[:, :, :], in0=rng_[:, :].unsqueeze(1).to_broadcast([4, 4, 128]),
                in1=eye4[:, :].unsqueeze(2).to_broadcast([4, 4, 128]), op=ALU.mult),
                reads=[rng_, eye4], writes=[bk0])
            yield
            k.op("pe", lambda e: e.matmul(pGd[d][:, :], lhsT=ones4[:, :],
                                          rhs=bk0[:, :, :].rearrange("k h l -> k (h l)"),
                                          start=True, stop=False), reads=[ones4, bk0], writes=[pGd[d]], inc=False)
            k.op("pe", lambda e: e.matmul(pGd[d][:, :], lhsT=identb[:, :], rhs=maskneg[:, d, :],
                                          start=False, stop=True), reads=[identb, maskneg], writes=[pGd[d]])
        yield
        k.op("act", lambda e: e.activation(out=rin_[:, :], in_=rng_[:, :], func=AF.Exp, bias=mt[:, mc:mc + 1]),
             reads=[rng_, mt], writes=[rin_])
        if full:
            yield
            bk1 = blk[d][1]
            k.op("dve", lambda e: e.tensor_tensor(
                out=bk1[:, :, :], in0=rin_[:, :].unsqueeze(1).to_broadcast([4, 4, 128]),
                in1=eye4[:, :].unsqueeze(2).to_broadcast([4, 4, 128]), op=ALU.mult),
                reads=[rin_, eye4], writes=[bk1])
            yield
            k.op("pe", lambda e: e.matmul(pId[d][:, :], lhsT=ones4[:, :],
                                          rhs=bk1[:, :, :].rearrange("k h l -> k (h l)"),
                                          start=True, stop=True), reads=[ones4, bk1], writes=[pId[d]])
        if full:
            k.op("act", lambda e: e.activation(out=rgu_[:, :], in_=rt_[:, :], func=AF.Exp, scale=-1.0),
                 reads=[rt_], writes=[rgu_])
        if upd:
            k.op("act", lambda e: e.activation(out=rw_[:, :], in_=ra_[:, :], func=AF.Exp,
                                               bias=rng_[:, last:last + 1]), reads=[ra_, rng_], writes=[rw_])
        yield
        if full:
            k.op("pe", lambda e: e.transpose(pT[:, 4:8], rgu_[:, :], ident[0:4, 0:4]), reads=[rgu_, ident],
                 writes=[pT])
        if upd:
            k.op("pe", lambda e: e.transpose(pT[:, 8:12], rw_[:, :], ident[0:4, 0:4]), reads=[rw_, ident],
                 writes=[pT])
            k.op("dve", lambda e: e.tensor_scalar(out=dg[d][:, :], in0=eye4[:, :], scalar1=rin_[:, last:last + 1],
                                                  scalar2=None, op0=ALU.mult), reads=[eye4, rin_], writes=[dg[d]])
            yield
            k.op("pe", lambda e: e.matmul(pT[:, 12:16], lhsT=ones4[:, :], rhs=dg[d][:, :], start=True, stop=True),
                 reads=[ones4, dg[d]], writes=[pT])
        yield
        hi = 16 if upd else 8
        if not full and upd:
            k.op("dve", lambda e: e.tensor_copy(out=cl[:, 0:4], in_=pT[:, 0:4]), reads=[pT], writes=[cl])
            k.op("dve", lambda e: e.tensor_copy(out=cl[:, 8:16], in_=pT[:, 8:16]), reads=[pT], writes=[cl])
        else:
            k.op("dve", lambda e: e.tensor_copy(out=cl[:, 0:hi], in_=pT[:, 0:hi]), reads=[pT], writes=[cl])
        yield

    def new_m(d, mt_out, mc_out):
        rb_, rg_ = rows[d][0], rows[d][2]
        last = 127 if d == 0 else 0
        k.op("dve", lambda e: e.tensor_tensor(out=mt_out[:, mc_out:mc_out + 1], in0=rb_[:, last:last + 1],
                                              in1=rg_[:, last:last + 1], op=ALU.add),
             reads=[rb_, rg_], writes=[mt_out])

    def state_update(d, kk_, va_, banks=None, cl=None):
        banks = banks or ((pId[d], 0), (pTd[d], 128))
        Cs = Cst[d]
        cl = cl or cols[d]
        k.op("dve", lambda e: e.tensor_tensor(out=kw[:, :].rearrange("p (h d) -> p h d", h=4), in0=kk_[:, :, :],
                                              in1=cl[:, 8:12].unsqueeze(2).to_broadcast([128, 4, 128]),
                                              op=ALU.mult), reads=[kk_, cl], writes=[kw])
        yield
        for h in range(4):
            pd, off = banks[h % 2]
            k.op("pe", lambda e, h=h, pd=pd, off=off: e.matmul(pd[:, off:off + 129], lhsT=kw[:, h * 128:(h + 1) * 128],
                                                      rhs=va_[:, h, 0:129], start=True, stop=True),
                 reads=[kw, va_], writes=[pd])
            yield
            k.op("dve", lambda e, h=h, pd=pd, off=off: e.scalar_tensor_tensor(
                out=Cs[:, h, 0:129], in0=Cs[:, h, 0:129], scalar=cl[:, 12 + h:13 + h], in1=pd[:, off:off + 129],
                op0=ALU.mult, op1=ALU.add), reads=[Cs, cl, pd], writes=[Cs])
            yield

    def run(*gens):
        gens = list(gens)
        while gens:
            for g in list(gens):
                try:
                    next(g)
                except StopIteration:
                    gens.remove(g)

    def init_dir(seq, d):
        Cs = Cst[d]
        if seq == 0:
            k.op("pool", lambda e: e.memset(Cs[:, :, :], 0.0), writes=[Cs])
            k.op("dve", lambda e: e.memset(mst[d][:, :], 0.0), writes=[mst[d]])
            k.dma(Cs[:, :, 0:128], c.SC[d].rearrange("h p e -> p h e"), writes=[Cs])
            k.dma(Cs[:, :, 128], c.SN[d].rearrange("h p -> p h"), writes=[Cs], slow=True)
            k.dma(mst[d][:, 0:1], c.SM[d].rearrange("(h o) -> h o", o=1), writes=[mst[d]], slow=True)
        else:
            k.op("pool", lambda e: e.memset(Cs[:, :, :], 0.0), writes=[Cs])
            k.op("dve", lambda e: e.memset(mst[d][:, :], 0.0), writes=[mst[d]])

    def store_state(seq, d):
        p = seq - 1
        Cs = Cst[d]
        k.dma(c.NC_[p, d].rearrange("h p e -> p h e"), Cs[:, :, 0:128], reads=[Cs])
        k.dma(c.NN[p, d].rearrange("h p -> p h"), Cs[:, :, 128], reads=[Cs], slow=True)
        k.dma(c.NM[p, d].rearrange("(h o) -> h o", o=1), mst[d][:, 0:1], reads=[mst[d]], slow=True)

    def pre_G(ci):
        i2 = ci % 2
        load_chunk(ci, False)
        yield
        k.op("dve", lambda e: e.tensor_copy(out=mbprev[:, ci:ci + 1], in_=mst[1][:, 0:1]), reads=[mst[1]],
             writes=[mbprev])
        yield
        yield from gate_prep(1, GT[i2], (mst[1], 0), False, True, cl=cols1p[i2], pT=pX)
        new_m(1, mst[1], 0)
        yield

    def pre_U(ci):
        i2 = ci % 2
        k.op("act", lambda e: e.activation(out=Csnap[:, ci, :, :], in_=Cst[1][:, :, :], func=AF.Copy),
             reads=[Cst[1]], writes=[Csnap])
        yield
        yield from state_update(1, kk[i2], va[i2], banks=((pX, 128), (pX, 128)), cl=cols1p[i2])

    def zip2(g1, g2):
        gens = [g for g in (g1, g2) if g is not None]
        while gens:
            for g in list(gens):
                try:
                    next(g)
                except StopIteration:
                    gens.remove(g)
            yield

    seqs = [(0, list(range(32))), (1, [32, 33]), (2, [34, 35])]

    def pre_gen():
        for seq, chunks in seqs:
            init_dir(seq, 1)
            yield
            order = list(reversed(chunks))
            yield from pre_G(order[0])
            for n, ci in enumerate(order):
                nxt = pre_G(order[n + 1]) if n + 1 < len(order) else None
                yield from zip2(nxt, pre_U(ci))
            if seq > 0:
                store_state(seq, 1)
                yield

    def dir_chain(d, ci, q_, kk_, va_, gt):
        mprev = (mst[0], 0) if d == 0 else (mbprev, ci)
        rng_, rin_ = rows[d][3], rows[d][4]
        pG, pI, pT, cl = pGd[d], pId[d], pTd[d], cols[d]
        yield from gate_prep(d, gt, mprev, True, d == 0)
        if d == 0:
            new_m(0, mst[0], 0)
        for h in range(4):
            k.op("act", lambda e, h=h: e.activation(out=Dsb[d][:, h * 128:(h + 1) * 128],
                                                    in_=pG[:, h * 128:(h + 1) * 128], func=AF.Exp,
                                                    bias=cl[:, h:h + 1]), reads=[pG, cl], writes=[Dsb[d]])
        k.op("dve", lambda e: e.tensor_tensor(out=qI[d][:, :], in0=pI[:, :],
                                              in1=q_[:, :, :].rearrange("p h t -> p (h t)"), op=ALU.mult),
             reads=[pI, q_], writes=[qI[d]])
        yield
        k.op("dve", lambda e: e.tensor_tensor(out=swT[d][:, :], in0=pS[:, :], in1=Dsb[d][:, :], op=ALU.mult),
             reads=[pS, Dsb[d]], writes=[swT[d]])
        if d == 0:
            k.op("act", lambda e: e.activation(out=Cbf[:, :, :], in_=Cst[0][:, :, :], func=AF.Copy),
                 reads=[Cst[0]], writes=[Cbf])
        yield
        for h in range(4):
            cprev = (lambda h=h: Cbf[:, h, :]) if d == 0 else (lambda h=h: Csnap[:, ci, h, :])
            ctile = Cbf if d == 0 else Csnap
            k.op("pe", lambda e, h=h: e.matmul(pG[:, h * 128:(h + 1) * 128], lhsT=swT[d][:, h * 128:(h + 1) * 128],
                                               rhs=va_[:, h, 0:128], start=True, stop=False),
                 reads=[swT[d], va_], writes=[pG], inc=False)
            k.op("pe", lambda e, h=h, cprev=cprev: e.matmul(pG[:, h * 128:(h + 1) * 128],
                                                            lhsT=qI[d][:, h * 128:(h + 1) * 128],
                                                            rhs=cprev()[:, 0:128], start=False, stop=True),
                 reads=[qI[d], ctile], writes=[pG], inc=False)
            k.op("pe", lambda e, h=h: e.matmul(pT[:, 16 + h:17 + h], lhsT=swT[d][:, h * 128:(h + 1) * 128],
                                               rhs=va_[:, h, 128:129], start=True, stop=False),
                 reads=[swT[d], va_], writes=[pT], inc=False)
            k.op("pe", lambda e, h=h, cprev=cprev: e.matmul(pT[:, 16 + h:17 + h],
                                                            lhsT=qI[d][:, h * 128:(h + 1) * 128],
                                                            rhs=cprev()[:, 128:129], start=False, stop=True),
                 reads=[qI[d], ctile], writes=[pT])
            yield
        k.op("dve", lambda e: e.tensor_scalar(out=cl[:, 20:24], in0=pT[:, 16:20], scalar1=-1.0, scalar2=None,
                                              op0=ALU.mult), reads=[pT], writes=[cl])
        yield
        k.op("dve", lambda e: e.tensor_tensor(out=cl[:, 16:20], in0=pT[:, 16:20], in1=cl[:, 20:24], op=ALU.max),
             reads=[pT, cl], writes=[cl])
        yield
        k.op("dve", lambda e: e.tensor_tensor(out=cl[:, 16:20], in0=cl[:, 16:20], in1=cl[:, 4:8], op=ALU.max),
             reads=[cl], writes=[cl])
        yield
        k.op("dve", lambda e: e.reciprocal(out=cl[:, 16:20], in_=cl[:, 16:20]), reads=[cl], writes=[cl])
        yield
        hdt = hd[ci % 2][d]
        k.op("dve", lambda e: e.tensor_tensor(out=hdt[:, :].rearrange("p (h e) -> p h e", h=4),
                                              in0=pG[:, :].rearrange("p (h e) -> p h e", h=4),
                                              in1=cl[:, 16:20].unsqueeze(2).to_broadcast([128, 4, 128]),
                                              op=ALU.mult), reads=[pG, cl], writes=[hdt])
        yield
        if d == 0:
            yield from state_update(0, kk_, va_)

    def tail_chain(ci):
        i2 = ci % 2
        mo_ = mo[i2]
        h0, h1 = hd[i2]
        k.op("pool", lambda e: e.tensor_tensor(out=hm[:, :], in0=h0[:, :], in1=h1[:, :], op=ALU.add),
             reads=[h0, h1], writes=[hm])
        yield
        k.op("dve", lambda e: e.tensor_tensor(out=sqh[:, :], in0=hm[:, :], in1=hm[:, :], op=ALU.mult),
             reads=[hm], writes=[sqh])
        yield
        k.op("dve", lambda e: e.tensor_reduce(out=ss4[:, 0:4], in_=sqh[:, :].rearrange("p (h d) -> p h d", h=4),
                                              axis=AX.X, op=ALU.add), reads=[sqh], writes=[ss4])
        yield
        k.op("act", lambda e: e.activation(out=rs4[:, 0:4], in_=ss4[:, 0:4], func=AF.Ln, scale=1.0 / 128, bias=EPS),
             reads=[ss4], writes=[rs4])
        yield
        k.op("act", lambda e: e.activation(out=rs4[:, 0:4], in_=rs4[:, 0:4], func=AF.Exp, scale=-0.5),
             reads=[rs4], writes=[rs4])
        yield
        k.op("dve", lambda e: e.tensor_tensor(out=hm[:, :].rearrange("p (h d) -> p h d", h=4),
                                              in0=hm[:, :].rearrange("p (h d) -> p h d", h=4),
                                              in1=rs4[:, 0:4].unsqueeze(2).to_broadcast([128, 4, 128]),
                                              op=ALU.mult), reads=[hm, rs4], writes=[hm])
        yield
        k.op("pool", lambda e: e.tensor_tensor(out=hm[:, :], in0=hm[:, :], in1=mnw[:, :], op=ALU.mult),
             reads=[hm, mnw], writes=[hm])
        yield
        k.op("pool", lambda e: e.tensor_tensor(out=sqh[:, :], in0=hm[:, :], in1=mo_[:, :], op=ALU.mult),
             reads=[hm, mo_], writes=[sqh])
        yield
        for h in range(4):
            k.op("pe", lambda e, h=h: e.transpose(pX[:, h * 128:(h + 1) * 128], sqh[:, h * 128:(h + 1) * 128],
                                                  ident[:, :]), reads=[sqh, ident], writes=[pX], inc=(h == 3))
        yield
        ho = hmT[i2]
        k.op("act", lambda e: e.activation(out=ho[:, :, :], in_=pX[:, :].rearrange("p (h t) -> p h t", h=4),
                                           func=AF.Copy), reads=[pX], writes=[ho])
        yield
        k.dma(c.HMTd[:, :, ci * 128:(ci + 1) * 128].rearrange("h p t -> p h t"), ho[:, :, :], reads=[ho])

    def main_chunk(ci, prev):
        i2 = ci % 2
        load_chunk(ci, True)
        q_, kT_, kk_, va_, gt = qT[i2], kT[i2], kk[i2], va[i2], GT[i2]
        for h in range(4):
            k.op("pe", lambda e, h=h: e.matmul(pS[:, h * 128:(h + 1) * 128], lhsT=kT_[:, h, :], rhs=q_[:, h, :],
                                               start=True, stop=True), reads=[kT_, q_], writes=[pS], inc=(h == 3))
        gens = [dir_chain(0, ci, q_, kk_, va_, gt), dir_chain(1, ci, q_, kk_, va_, gt)]
        if prev is not None:
            gens.append(tail_chain(prev))
        run(*gens)

    def main():
        late_alloc()
        for seq, chunks in seqs:
            init_dir(seq, 0)
            prev = None
            for ci in chunks:
                main_chunk(ci, prev)
                prev = ci
            run(tail_chain(prev))
            if seq > 0:
                store_state(seq, 0)
        k.barrier()
        k.sb_phase = ctx["sb_phase0"]
        k.phase_mem()

    return pre_gen, main


def phase_b2(ctx):
    ctx["b2_main"]()


def phase_c1(ctx):
    c = NS(ctx)
    k = c.k
    PS = c.PS
    Wout = k.sb([128, 8, D], BF16, "Wout")
    stg = [k.sb([128, D], F32, f"stgo{i}") for i in range(2)]
    g1b = k.sb([128, D], F32, "g1b")
    mixT = [k.sb([128, 8, 512], BF16, f"mixT{i}") for i in range(2)]
    xb = [k.sb([128, D], F32, f"xc{i}") for i in range(3)]
    x1 = [k.sb([128, D], F32, f"x1{i}") for i in range(2)]
    tmp = k.sb([128, D], F32, "tmpc1")
    junk = k.sb([128, D], BF16, "junkc")
    xs = [k.sb([128, D], F32, f"xsc{i}") for i in range(3)]
    deferred = []

    def flush(keep=0):
        while len(deferred) > keep:
            deferred.pop(0)()
    ss = k.sb([128, 8], F32, "ssc")
    rstd = k.sb([128, 8], F32, "rstdc")
    h2T = [k.sb([128, 8, 512], BF16, f"h2T{i}") for i in range(2)]
    for kc in range(8):
        st = stg[kc % 2]
        k.dma(st[:, :], c.WOUT[kc * 128:(kc + 1) * 128, :], writes=[st])
        k.op("pool", lambda e, kc=kc, st=st: e.tensor_copy(out=Wout[:, kc, :], in_=st[:, :]), reads=[st],
             writes=[Wout])

    def st_body(s):
        cj = 0 if s < 8 else 1
        tok = slice(s * 512, (s + 1) * 512)
        mx = mixT[s % 2]
        h2 = h2T[s % 2]
        if s == 0 or s == 8:
            k.dma(g1b[:, :], c.MODS[cj, 2 * D:3 * D].partition_broadcast(128), writes=[g1b])
        k.dma(mx[:, 0:4, :], c.AOTd[:, :, tok].rearrange("c p t -> p c t"), writes=[mx])
        k.dma(mx[:, 4:8, :], c.HMTd[:, :, tok].rearrange("c p t -> p c t"), writes=[mx])

        def tile_body(i):
            ti = s * 4 + i
            xt = xb[ti % 3]
            xo = x1[ti % 2]
            k.dma(xt[:, :], c.X[ti * 128:(ti + 1) * 128, :], writes=[xt])
            for n in range(2):
                pb = PS[2 + (2 * ti + n) % 4]
                for kc in range(8):
                    k.op("pe", lambda e, pb=pb, kc=kc, n=n: e.matmul(
                        pb[:, :], lhsT=mx[:, kc, i * 128:(i + 1) * 128], rhs=Wout[:, kc, n * 512:(n + 1) * 512],
                        start=(kc == 0), stop=(kc == 7)), reads=[mx, Wout], writes=[pb], inc=(kc == 7))
                if n == 1:
                    flush(1)
                k.op("dve", lambda e, pb=pb, n=n: e.tensor_tensor(out=tmp[:, n * 512:(n + 1) * 512], in0=pb[:, :],
                                                                  in1=g1b[:, n * 512:(n + 1) * 512], op=ALU.mult),
                     reads=[pb, g1b], writes=[tmp])
            k.op("pool", lambda e: e.tensor_tensor(out=xo[:, :], in0=tmp[:, :], in1=xt[:, :], op=ALU.add),
                 reads=[tmp, xt], writes=[xo])
            k.dma(c.X1d[ti * 128:(ti + 1) * 128, :], xo[:, :], reads=[xo])
            norm_to_hT(k, c, xo, h2, i * 128,
                       lambda ch: c.G2[:, cj, ch:ch + 1], lambda ch: c.modc[:, cj, 3, ch:ch + 1],
                       (junk, ss, rstd, xs[ti % 3]), PS[0], PS[1], defer=deferred)

        for i in range(4):
            tile_body(i)
        flush()
        k.dma(c.H2Td[:, :, tok].rearrange("c p t -> p c t"), h2[:, :, :], reads=[h2])

    for s in range(9):
        st_body(s)
    k.barrier()
    k.phase_mem()


def phase_c2(ctx):
    c = NS(ctx)
    k = c.k
    PS = c.PS
    Wgb = [k.sb([128, 8, 256], BF16, f"Wgb{i}") for i in range(22)]
    Wdn = k.sb([128, NF, D], BF16, "Wdn")
    SW = 704
    stg = [k.sb([128, SW], F32, f"stgf{i}") for i in range(2)]
    g2b = k.sb([128, D], F32, "g2b")
    fnb = k.sb([128, D], F32, "fnb")
    h2 = k.sb([128, 8, 512], BF16, "h2c")
    actT = k.sb([128, NF, 512], BF16, "actT")
    sg = [k.sb([128, 512], F32, f"sg{i}") for i in range(2)]
    x1 = [k.sb([128, D], F32, f"x1c{i}") for i in range(2)]
    x2 = k.sb([128, D], F32, "x2c")
    yt = [k.sb([128, D], F32, f"yt{i}") for i in range(2)]
    junk = k.sb([128, D], BF16, "junkf")
    ss = k.sb([128, 8], F32, "ssf")
    rstd = k.sb([128, 8], F32, "rstdf")
    mhalf = k.sb([128, 1], F32, "mhalf")
    k.dma(fnb[:, :], c.FNW.partition_broadcast(128), writes=[fnb])
    k.dma(g2b[:, :], c.MODS[0, 5 * D:6 * D].partition_broadcast(128), writes=[g2b])
    k.dma(h2[:, :, :], c.H2Td[:, :, 0:512].rearrange("c p t -> p c t"), writes=[h2])
    it = 0
    for bi in range(11):
        for half in range(2):
            blkt = Wgb[2 * bi + half]
            c0 = half * DFF + bi * 256
            for kc0 in (0, 2, 4, 6):
                st = stg[it % 2]
                it += 1
                k.dma(st[:, 0:512].rearrange("p (a n) -> p a n", a=2),
                      c.WGU[kc0 * 128:(kc0 + 2) * 128, c0:c0 + 256].rearrange("(a p) n -> p a n", p=128), writes=[st])
                k.op("pool", lambda e, kc0=kc0, st=st, blkt=blkt: e.tensor_copy(
                    out=blkt[:, kc0:kc0 + 2, :], in_=st[:, 0:512].rearrange("p (a n) -> p a n", a=2)),
                    reads=[st], writes=[blkt])
    for f in range(NF):
        for j in range(2):
            st = stg[it % 2]
            it += 1
            k.dma(st[:, 0:512], c.WDN[f * 128:(f + 1) * 128, j * 512:(j + 1) * 512], writes=[st])
            k.op("pool", lambda e, f=f, j=j, st=st: e.tensor_copy(out=Wdn[:, f, j * 512:(j + 1) * 512],
                                                                  in_=st[:, 0:512]), reads=[st], writes=[Wdn])
    k.op("dve", lambda e: e.memset(mhalf[:, :], -0.5), writes=[mhalf])

    def st_body(s):
        cj = 0 if s < 8 else 1
        tok = slice(s * 512, (s + 1) * 512)
        if s == 8:
            k.dma(g2b[:, :], c.MODS[cj, 5 * D:6 * D].partition_broadcast(128), writes=[g2b])
        if s > 0:
            k.dma(h2[:, :, :], c.H2Td[:, :, tok].rearrange("c p t -> p c t"), writes=[h2])

        def up_body(f):
            pg, pu = PS[2 * (f % 2)], PS[2 * (f % 2) + 1]
            c0 = (f % 2) * 128
            for (pb, wt) in ((pg, Wgb[2 * (f // 2)]), (pu, Wgb[2 * (f // 2) + 1])):
                for kc in range(8):
                    k.op("pe", lambda e, pb=pb, wt=wt, kc=kc: e.matmul(
                        pb[:, :], lhsT=wt[:, kc, c0:c0 + 128], rhs=h2[:, kc, :], start=(kc == 0), stop=(kc == 7)),
                        reads=[wt, h2], writes=[pb], inc=(kc == 7))
            sgt = sg[f % 2]
            k.op("act", lambda e: e.activation(out=sgt[:, :], in_=pg[:, :], func=AF.Silu), reads=[pg], writes=[sgt])
            k.op("dve", lambda e: e.tensor_tensor(out=actT[:, f, :], in0=pu[:, :], in1=sgt[:, :], op=ALU.mult),
                 reads=[pu, sgt], writes=[actT])

        for f in range(NF):
            up_body(f)

        def tile_body(i):
            ti = s * 4 + i
            xt = x1[ti % 2]
            y = yt[ti % 2]
            k.dma(xt[:, :], c.X1d[ti * 128:(ti + 1) * 128, :], writes=[xt])
            for n in range(2):
                pb = PS[4 + (2 * ti + n) % 4]
                for f in range(NF):
                    k.op("pe", lambda e, pb=pb, f=f, n=n: e.matmul(
                        pb[:, :], lhsT=actT[:, f, i * 128:(i + 1) * 128], rhs=Wdn[:, f, n * 512:(n + 1) * 512],
                        start=(f == 0), stop=(f == NF - 1)), reads=[actT, Wdn], writes=[pb], inc=(f == NF - 1))
                k.op("dve", lambda e, pb=pb, n=n: e.tensor_tensor(out=x2[:, n * 512:(n + 1) * 512], in0=pb[:, :],
                                                                  in1=g2b[:, n * 512:(n + 1) * 512], op=ALU.mult),
                     reads=[pb, g2b], writes=[x2])
            k.op("pool", lambda e: e.tensor_tensor(out=x2[:, :], in0=x2[:, :], in1=xt[:, :], op=ALU.add),
                 reads=[x2, xt], writes=[x2])
            k.op("dve", lambda e: e.scalar_tensor_tensor(out=junk[:, :], in0=x2[:, :], scalar=1.0, in1=x2[:, :],
                                                         op0=ALU.mult, op1=ALU.mult, accum_out=ss[:, 0:1]),
                 reads=[x2], writes=[junk, ss])
            k.op("dve", lambda e: e.tensor_scalar(out=ss[:, 1:2], in0=ss[:, 0:1], scalar1=1.0 / D, scalar2=EPS,
                                                  op0=ALU.mult, op1=ALU.add), reads=[ss], writes=[ss])
            k.op("pool", lambda e: e.tensor_tensor(out=rstd[:, 0:1], in0=ss[:, 1:2], in1=mhalf[:, 0:1], op=ALU.pow),
                 reads=[ss, mhalf], writes=[rstd])
            k.op("dve", lambda e: e.scalar_tensor_tensor(out=y[:, :], in0=x2[:, :], scalar=rstd[:, 0:1],
                                                         in1=fnb[:, :], op0=ALU.mult, op1=ALU.mult),
                 reads=[x2, rstd, fnb], writes=[y])
            k.dma(c.Y[ti * 128:(ti + 1) * 128, :], y[:, :], reads=[y])

        for i in range(4):
            tile_body(i)

    for s in range(9):
        st_body(s)
    k.barrier()
    k.phase_mem()
```

```python
import numpy as np
import concourse.bass as bass
import concourse.mybir as mybir
from concourse.bass_utils import run_bass_kernel_spmd

F32 = mybir.dt.float32
BF16 = mybir.dt.bfloat16
AF = mybir.ActivationFunctionType
ALU = mybir.AluOpType
AX = mybir.AxisListType

D = 1024
NT = 36
TT = NT * 128
NIN = 2832
DFF = 2816
NF = 22
EPS = 1e-6
KSC = 128.0 ** -0.5
NKEY = 38
import os as _os
NDS = 64
PRE_IN_ATT = int(_os.environ.get('PRE_IN_ATT', '1'))
STRICT = bool(int(_os.environ.get('KSTRICT', '1')))


class Tl:
    def __init__(s, h):
        s.h = h
        s.w = []
        s.r = []

    def __getitem__(s, i):
        return s.h[i]


class K:
    ENG = ("pe", "act", "dve", "pool", "sp")

    def __init__(s, nc):
        s.nc = nc
        s.ops = {e: [] for e in s.ENG}
        s.cnt = {e: 0 for e in s.ENG}
        s.seen = {e: {} for e in s.ENG}
        s.sem = {}
        s.dsems = [nc.alloc_semaphore(f"d_{i}") for i in range(NDS)]
        s.dcnt = [0] * NDS
        s.dptr = 0
        for e in s.ENG:
            s.sem[e] = nc.alloc_semaphore(f"s_{e}")
        s.sb_ptr = 0
        s.sb_phase = 0
        s.cap = None
        s.nalloc = 0

    def sb(s, shape, dt, name=None):
        esz = 4 if dt == F32 else 2
        n = 1
        for d_ in shape[1:]:
            n *= d_
        nbytes = (n * esz + 63) // 64 * 64
        off = s.sb_ptr
        s.sb_ptr += nbytes
        assert s.sb_ptr <= s.sb_top, (name, s.sb_ptr, s.sb_top)
        s.nalloc += 1
        h = s.nc.alloc_sbuf_tensor_at(f"{name or 't'}_{s.nalloc}", list(shape), dt, offset=off)
        return Tl(h)

    def phase_mem(s):
        s.sb_ptr = s.sb_phase

    def capture(s):
        s.cap = []
        return s.cap

    def capture_end(s):
        lst, s.cap = s.cap, None
        return lst

    def emit_interleaved(s, *lists):
        lists = [list(l) for l in lists if l]
        while lists:
            for l in list(lists):
                kind, args, kw = l.pop(0)
                (s.op if kind == "op" else s.dma)(*args, **kw)
                if not l:
                    lists.remove(l)

    def op(s, e, fn, reads=(), writes=(), inc=True):
        if s.cap is not None:
            s.cap.append(("op", (e, fn), dict(reads=list(reads), writes=list(writes), inc=inc)))
            return None
        deps = []
        for t in reads:
            for wt in t.w:
                if wt[0] != e or e != "pe":
                    deps.append(wt)
        for t in writes:
            for wt in t.w:
                if wt[0] != e or (STRICT and e != "pe"):
                    deps.append(wt)
            for rt in t.r:
                if rt[0] != e or (STRICT and e != "pe"):
                    deps.append(rt)
        need = {}
        for key, val in deps:
            if val > need.get(key, 0):
                need[key] = val
        waits = []
        for key, val in need.items():
            if s.seen[e].get(key, 0) >= val:
                continue
            s.seen[e][key] = val
            waits.append((key, val))
        tok = (e, s.cnt[e] + 1)
        if inc:
            s.cnt[e] += 1
        s.ops[e].append((waits, fn, inc, None))
        for t in writes:
            t.w = [tok]
            t.r = []
        for t in reads:
            if t not in writes:
                t.r.append(tok)
                if len(t.r) > 24:
                    t.r = s._compact(t.r)
        return tok

    @staticmethod
    def _compact(r):
        best = {}
        for key, val in r:
            if val > best.get(key, 0):
                best[key] = val
        return list(best.items())

    def dma(s, out, in_, reads=(), writes=(), q="sp", slow=False):
        if s.cap is not None:
            s.cap.append(("dma", (out, in_), dict(reads=list(reads), writes=list(writes), q=q, slow=slow)))
            return None
        deps = []
        for t in reads:
            deps.extend(t.w)
        for t in writes:
            for wt in t.w:
                if not isinstance(wt[0], int):
                    deps.append(wt)
            deps.extend(t.r)
        need = {}
        for key, val in deps:
            if val > need.get(key, 0):
                need[key] = val
        waits = []
        for key, val in need.items():
            if s.seen[q].get(key, 0) >= val:
                continue
            s.seen[q][key] = val
            waits.append((key, val))
        si = s.dptr
        s.dptr = (s.dptr + 1) % len(s.dsems)
        if s.dcnt[si] > s.seen[q].get(si, 0):
            s.seen[q][si] = s.dcnt[si]
            waits.append((si, s.dcnt[si]))
        s.dcnt[si] += 16
        tok = (si, s.dcnt[si])
        s.ops[q].append((waits, (out, in_, slow), False, si))
        for t in writes:
            if t.w and all(isinstance(wt[0], int) for wt in t.w):
                t.w = t.w + [tok]
            else:
                t.w = [tok]
            t.r = []
        for t in reads:
            t.r.append(tok)
        return tok

    def barrier(s):
        waits = []
        for e in s.ENG:
            if e != "sp" and s.cnt[e] > s.seen["sp"].get(e, 0):
                s.seen["sp"][e] = s.cnt[e]
                waits.append((e, s.cnt[e]))
        for si in range(len(s.dsems)):
            if s.dcnt[si] > s.seen["sp"].get(si, 0):
                s.seen["sp"][si] = s.dcnt[si]
                waits.append((si, s.dcnt[si]))
        s.cnt["sp"] += 1
        v = s.cnt["sp"]
        s.ops["sp"].append((waits, "inc", True, None))
        for e in s.ENG:
            if e != "sp":
                s.seen[e]["sp"] = v
                s.ops[e].append(([("sp", v)], None, False, None))
                for si in range(len(s.dsems)):
                    s.seen[e][si] = s.dcnt[si]
                for e2 in s.ENG:
                    s.seen[e][e2] = max(s.seen[e].get(e2, 0), s.cnt[e2])

    def semof(s, key):
        return s.dsems[key] if isinstance(key, int) else s.sem[key]

    def emit(s, e, eng):
        for waits, fn, inc, si in s.ops[e]:
            for key, val in waits:
                eng.wait_ge(s.semof(key), val)
            if fn is None:
                continue
            if fn == "inc":
                eng.sem_inc(s.sem[e], 1)
                continue
            if si is not None:
                out, in_, slow = fn
                if slow:
                    ins = eng.dma_start(out=out, in_=in_, allow_slow_non_contiguous=True)
                else:
                    ins = eng.dma_start(out=out, in_=in_)
                ins.then_inc(s.dsems[si], 16)
                continue
            ins = fn(eng)
            if inc:
                ins.then_inc(s.sem[e], 1)


def build_nc(debug=False, stop=99):
    import os
    stop = int(os.environ.get('KSTOP', stop))
    nc = bass.Bass("TRN2", target_bir_lowering=False)
    k = K(nc)
    k.sb_ptr = (nc.sbuf_base + 63) // 64 * 64
    k.sb_top = nc.sbuf_top

    def din(name, shape, dt=F32):
        return nc.dram_tensor(name, list(shape), dt, kind="ExternalInput").ap()

    def dout(name, shape, dt=F32):
        return nc.dram_tensor(name, list(shape), dt, kind="ExternalOutput").ap()

    def dscr(name, shape, dt=BF16):
        return nc.dram_tensor(name, list(shape), dt, kind="ExternalOutput" if debug else "Internal").ap()

    X = din("x", [TT, D])
    CK = din("cache_k", [256, 128])
    CV = din("cache_v", [256, 128])
    SC = din("state_C", [2, 4, 128, 128])
    SN = din("state_n", [2, 4, 128])
    SM = din("state_m", [2, 4])
    COND = din("cond", [2, D])
    WADA = din("w_ada", [D, 6 * D])
    BADA = din("b_ada", [6 * D])
    N1W = din("norm1_w", [D])
    WIN = din("w_in", [D, NIN])
    GB = din("gate_bias", [4, 4])
    QNW = din("q_norm_w", [64])
    KNW = din("k_norm_w", [64])
    MNW = din("mlstm_norm_w", [512])
    WOUT = din("w_out", [D, D])
    N2W = din("norm2_w", [D])
    WGU = din("w_gu", [D, 2 * DFF])
    WDN = din("w_down", [DFF, D])
    FNW = din("final_norm_w", [D])
    ROPE = din("rope", [128, 32, 2, 64])
    IDENT = din("ident", [128, 128])
    MASKS = din("masks", [128, 2, 512])

    Y = dout("y", [TT, D])
    NK = dout("nk", [512, 128])
    NV = dout("nv", [512, 128])
    NC_ = dout("nC", [2, 2, 4, 128, 128])
    NN = dout("nn", [2, 2, 4, 128])
    NM = dout("nm", [2, 2, 4])

    MODS = dscr("mods", [2, 6 * D], F32)
    QTd = dscr("QTd", [4, 128, TT])
    MQTd = dscr("MQTd", [4, 128, TT])
    MKTd = dscr("MKTd", [4, 128, TT])
    MKd = dscr("MKd", [TT, 512])
    MVd = dscr("MVd", [TT, 520])
    MOd = dscr("MOd", [TT, 512])
    GATd = dscr("GATd", [4, 4, TT], F32)
    AOTd = dscr("AOTd", [4, 128, TT])
    HMTd = dscr("HMTd", [4, 128, TT])
    H2Td = dscr("H2Td", [8, 128, TT])
    X1d = dscr("X1d", [TT, D], F32)

    PSALL = nc.alloc_psum_tensor("psall", [128, 4096], F32)
    PS = [Tl(PSALL[:, i * 512:(i + 1) * 512]) for i in range(8)]

    ident = k.sb([128, 128], F32, "ident")
    ones4 = k.sb([4, 128], F32, "ones4")
    eye4 = k.sb([4, 4], F32, "eye4")
    modc = k.sb([128, 2, 6, 8], F32, "modc")
    G1 = k.sb([128, 2, 8], F32, "G1")
    G2 = k.sb([128, 2, 8], F32, "G2")
    n1c = k.sb([128, 8], F32, "n1c")
    n2c = k.sb([128, 8], F32, "n2c")
    k.mhalf = k.sb([128, 8], F32, "mhalf")
    k.op("dve", lambda e: e.memset(k.mhalf[:, :], -0.5), writes=[k.mhalf])
    sb_phase0 = k.sb_ptr
    KT2 = k.sb([128, 2, TT + 256], BF16, "KT2")
    VA = k.sb([128, NKEY, 2, 192], BF16, "VA")
    sb_phase0_a = k.sb_ptr
    Win = k.sb([128, 8, NIN], BF16, "Win")
    Wg = k.sb([128, 8, 128], BF16, "Wg")
    stgw = [k.sb([128, NIN // 2], F32, f"stgw{i}") for i in range(2)]
    k.sb_phase = k.sb_ptr
    k.op("pool", lambda e: e.memset(Wg[:, :, :], 0.0), writes=[Wg])

    k.dma(ident[:, :], IDENT[:, :], writes=[ident])
    k.op("dve", lambda e: e.memset(ones4[:, :], 1.0), writes=[ones4])
    k.dma(eye4[:, :], IDENT[0:4, 0:4], writes=[eye4])

    condT = k.sb([128, 8, 2], F32, "condT")
    sil = k.sb([128, 8, 2], F32, "sil")
    tmpc = k.sb([128, 8, 2], F32, "tmpc")
    mods_sb = k.sb([2, 6 * D], F32, "mods_sb")
    bada = k.sb([2, 6 * D], F32, "bada")
    wa = [k.sb([128, 512], F32, f"wa{i}") for i in range(4)]
    for j in range(2):
        k.dma(condT[:, :, j], COND[j].rearrange("(c p) -> p c", p=128), writes=[condT], slow=True)
    for j in range(2):
        k.dma(bada[j:j + 1, :], BADA.rearrange("(o n) -> o n", o=1), writes=[bada])
    k.op("act", lambda e: e.activation(out=tmpc[:, :, :], in_=condT[:, :, :], func=AF.Exp, scale=-1.0),
         reads=[condT], writes=[tmpc])
    k.op("dve", lambda e: e.tensor_scalar_add(out=tmpc[:, :, :], in0=tmpc[:, :, :], scalar1=1.0),
         reads=[tmpc], writes=[tmpc])
    k.op("dve", lambda e: e.reciprocal(out=tmpc[:, :, :], in_=tmpc[:, :, :]), reads=[tmpc], writes=[tmpc])
    k.op("dve", lambda e: e.tensor_tensor(out=sil[:, :, :], in0=condT[:, :, :], in1=tmpc[:, :, :], op=ALU.mult),
         reads=[condT, tmpc], writes=[sil])
    win_steps = []
    HW = NIN // 2
    for kc in range(8):
        for hf in range(2):
            def step(kc=kc, hf=hf):
                st = stgw[hf]
                k.dma(st[:, :], WIN[kc * 128:(kc + 1) * 128, hf * HW:(hf + 1) * HW], writes=[st])
                k.op("pool", lambda e: e.tensor_copy(out=Win[:, kc, hf * HW:(hf + 1) * HW], in_=st[:, :]),
                     reads=[st], writes=[Win])
                if hf == 1:
                    k.op("pool", lambda e: e.tensor_copy(
                        out=Wg[:, kc, :].rearrange("p (j w) -> p j w", w=32)[:, :, 0:4],
                        in_=st[:, HW - 16:HW].rearrange("p (j w) -> p j w", w=4)), reads=[st], writes=[Wg])
            win_steps.append(step)
    it = 0
    for n in range(12):
        pb = PS[n % 2]
        for kc in range(8):
            w_t = wa[it % 4]
            if it % 6 == 0 and win_steps:
                win_steps.pop(0)()
            it += 1
            k.dma(w_t[:, :], WADA[kc * 128:(kc + 1) * 128, n * 512:(n + 1) * 512], writes=[w_t])
            k.op("pe", lambda e, w_t=w_t, kc=kc, pb=pb: e.matmul(pb[0:2, :], lhsT=sil[:, kc, :], rhs=w_t[:, :],
                                                                start=(kc == 0), stop=(kc == 7)),
                 reads=[w_t, sil], writes=[pb], inc=True)
        k.op("dve", lambda e, n=n, pb=pb: e.tensor_tensor(out=mods_sb[:, n * 512:(n + 1) * 512], in0=pb[0:2, :],
                                                          in1=bada[:, n * 512:(n + 1) * 512], op=ALU.add),
             reads=[pb, bada], writes=[mods_sb])
    while win_steps:
        win_steps.pop(0)()
    k.dma(MODS[:, :], mods_sb[:, :], reads=[mods_sb])
    k.barrier()
    for j in range(2):
        for s6 in range(6):
            k.dma(modc[:, j, s6, :], MODS[j, s6 * D:(s6 + 1) * D].rearrange("(c p) -> p c", p=128),
                  writes=[modc], slow=True)
    k.dma(n1c[:, :], N1W.rearrange("(c p) -> p c", p=128), writes=[n1c], slow=True)
    k.dma(n2c[:, :], N2W.rearrange("(c p) -> p c", p=128), writes=[n2c], slow=True)
    for j in range(2):
        k.op("dve", lambda e, j=j: e.scalar_tensor_tensor(out=G1[:, j, :], in0=modc[:, j, 1, :], scalar=1.0,
                                                          in1=n1c[:, :], op0=ALU.add, op1=ALU.mult),
             reads=[modc, n1c], writes=[G1])
        k.op("dve", lambda e, j=j: e.scalar_tensor_tensor(out=G2[:, j, :], in0=modc[:, j, 4, :], scalar=1.0,
                                                          in1=n2c[:, :], op0=ALU.add, op1=ALU.mult),
             reads=[modc, n2c], writes=[G2])
    k.barrier()
    k.phase_mem()
    ctx = dict(locals())
    if stop >= 1:
        phase_a(ctx)
    if stop >= 2:
        phase_b1(ctx)
    if stop >= 3:
        phase_b2(ctx)
    if stop >= 4:
        phase_c1(ctx)
    if stop >= 5:
        phase_c2(ctx)
    k.barrier()

    with nc.allow_low_precision(reason="bf16 matmul operands by design"), nc.Block() as block:
        names = {"pe": "tensor", "act": "scalar", "dve": "vector", "pool": "gpsimd", "sp": "sync"}
        for e in K.ENG:
            getattr(block, names[e])(lambda eng, e=e: k.emit(e, eng))
    return nc


class NS:
    def __init__(s, d):
        s.__dict__.update(d)


def rstd_from_ss(k, ss, out, n_inv, width):
    k.op("act", lambda e: e.activation(out=out[:, 0:width], in_=ss[:, 0:width], func=AF.Ln, scale=n_inv, bias=EPS),
         reads=[ss], writes=[out])
    k.op("act", lambda e: e.activation(out=out[:, 0:width], in_=out[:, 0:width], func=AF.Exp, scale=-0.5),
         reads=[out], writes=[out])


def norm_to_hT(k, c, xt, hT, col0, Gc, SHc, bufs, pA, pB, defer=None):
    junk, ss, rstd, xs = bufs
    k.op("dve", lambda e: e.scalar_tensor_tensor(out=junk[:, :], in0=xt[:, :], scalar=1.0, in1=xt[:, :],
                                                 op0=ALU.mult, op1=ALU.mult, accum_out=ss[:, 0:1]),
         reads=[xt], writes=[junk, ss])
    rstd_from_ss(k, ss, rstd, 1.0 / D, 1)
    k.op("pool", lambda e: e.tensor_scalar(out=xs[:, :], in0=xt[:, :], scalar1=rstd[:, 0:1], scalar2=1.0,
                                           op0=ALU.mult, op1=ALU.mult),
         reads=[xt, rstd], writes=[xs])

    def pe_part():
        for half, pb in ((0, pA), (1, pB)):
            for cc in range(4):
                ch = half * 4 + cc
                k.op("pe", lambda e, ch=ch, cc=cc, pb=pb: e.transpose(pb[:, cc * 128:(cc + 1) * 128],
                                                                      xs[:, ch * 128:(ch + 1) * 128], c.ident[:, :]),
                     reads=[xs, c.ident], writes=[pb], inc=(cc == 3))
            for cc in range(4):
                ch = half * 4 + cc
                k.op("act", lambda e, ch=ch, cc=cc, pb=pb: e.activation(
                    out=hT[:, ch, col0:col0 + 128], in_=pb[:, cc * 128:(cc + 1) * 128], func=AF.Identity,
                    scale=Gc(ch), bias=SHc(ch)), reads=[pb, c.G1, c.G2, c.modc], writes=[hT])

    if defer is None:
        pe_part()
    else:
        defer.append(pe_part)


def phase_a(ctx):
    c = NS(ctx)
    k = c.k
    PS = c.PS
    KT2, VA, Win, Wg = c.KT2, c.VA, c.Win, c.Wg
    k.op("pool", lambda e: e.memset(VA[:, :, :, :], 1.0), writes=[VA])
    xb = [k.sb([128, D], F32, f"xb{i}") for i in range(3)]
    junk = k.sb([128, D], BF16, "junk")
    xs = [k.sb([128, D], F32, f"xs{i}") for i in range(3)]
    ss = k.sb([128, 8], F32, "ss")
    rstd = k.sb([128, 8], F32, "rstd")
    hT = [k.sb([128, 8, 512], BF16, f"hT{i}") for i in range(2)]
    ropet = [k.sb([128, 4, 2, 64], F32, f"rope{i}") for i in range(2)]
    wq_bc = k.sb([128, 64], F32, "wq_bc")
    wk_bc = k.sb([128, 64], F32, "wk_bc")
    gbcol = k.sb([128, 1], F32, "gbcol")
    qf = k.sb([128, 512], F32, "qf")
    sq = k.sb([128, 512], F32, "sq")
    qn2 = [k.sb([128, 512], F32, f"qn{i}") for i in range(2)]
    t1 = k.sb([128, 512], F32, "t1")
    t2 = k.sb([128, 512], F32, "t2")
    qr2 = [k.sb([128, 512], F32, f"qr{i}") for i in range(2)]
    kvf = [k.sb([128, 256], F32, f"kvf{i}") for i in range(2)]
    kn = [k.sb([128, 128], F32, f"kn{i}") for i in range(2)]
    kt1 = k.sb([128, 128], F32, "kt1")
    kt2 = k.sb([128, 128], F32, "kt2")
    kr = k.sb([128, 128], F32, "kr")
    kdup2 = [k.sb([128, 2, 2, 64], F32, f"kdup{i}") for i in range(2)]
    ssq = k.sb([128, 8], F32, "ssq")
    rsq = k.sb([128, 8], F32, "rsq")
    ssk = k.sb([128, 8], F32, "ssk")
    rsk = k.sb([128, 8], F32, "rsk")
    mot = k.sb([128, 512], F32, "mot")
    gsb = k.sb([128, 512], F32, "gsb")
    gtmp = k.sb([128, 512], F32, "gtmp")
    cstg = k.sb([128, 2, 128], F32, "cstg")
    QTs = [k.sb([128, 4, 512], BF16, f"QTs{i}") for i in range(1)]
    MKs = [k.sb([128, 4, 512], BF16, f"MKs{i}") for i in range(1)]
    MVs = [k.sb([128, 4, 4, 130], BF16, f"MVs{i}") for i in range(1)]
    MOs = [k.sb([128, 4, 512], BF16, f"MOs{i}") for i in range(1)]
    MQTs = [k.sb([128, 4, 512], BF16, f"MQTs{i}") for i in range(1)]
    MKTs = [k.sb([128, 4, 512], BF16, f"MKTs{i}") for i in range(1)]

    k.dma(wq_bc[:, :], c.QNW.partition_broadcast(128), writes=[wq_bc])
    k.dma(wk_bc[:, :], c.KNW.partition_broadcast(128), writes=[wk_bc])
    k.op("dve", lambda e: e.tensor_scalar_mul(out=wq_bc[:, :], in0=wq_bc[:, :], scalar1=0.125),
         reads=[wq_bc], writes=[wq_bc])
    k.op("dve", lambda e: e.memset(gbcol[:, :], 0.0), writes=[gbcol])
    for j in range(4):
        k.dma(gbcol[32 * j:32 * j + 4, 0:1], c.GB[j].rearrange("(h o) -> h o", o=1), writes=[gbcol], slow=True)
    k.op("pool", lambda e: e.memset(MVs[0][:, :, :, :], 1.0), writes=[MVs[0]])
    kdup = kdup2[0]
    for blk in range(2):
        k.dma(cstg[:, 0, :], c.CK[blk * 128:(blk + 1) * 128, :], writes=[cstg])
        k.dma(cstg[:, 1, :], c.CV[blk * 128:(blk + 1) * 128, :], writes=[cstg])
        k.op("dve", lambda e: e.tensor_copy(out=kdup[:, :, :, :],
                                            in_=cstg[:, 0, :].rearrange("p (g o d) -> p g o d", g=2, o=1)
                                            .to_broadcast([128, 2, 2, 64])), reads=[cstg], writes=[kdup])
        pk = PS[2]
        for g in range(2):
            k.op("pe", lambda e, g=g: e.transpose(pk[:, g * 128:(g + 1) * 128],
                                                  kdup[:, g, :, :].rearrange("p a d -> p (a d)"), c.ident[:, :]),
                 reads=[kdup, c.ident], writes=[pk], inc=(g == 1))
        col = TT + blk * 128
        k.op("act", lambda e, col=col: e.activation(out=KT2[:, :, col:col + 128],
                                                    in_=pk[:, 0:256].rearrange("p (g t) -> p g t", g=2),
                                                    func=AF.Copy), reads=[pk], writes=[KT2])
        k.op("dve", lambda e, blk=blk: e.tensor_copy(out=VA[:, 36 + blk, :, 64:128],
                                                     in_=cstg[:, 1, :].rearrange("p (g d) -> p g d", g=2)),
             reads=[cstg], writes=[VA])

    class Deferred(list):
        cur = 0

        def append(self, fn):
            list.append(self, (self.cur, fn))

    deferred = Deferred()

    def flush(upto=10 ** 9):
        while deferred and deferred[0][0] <= upto:
            deferred.pop(0)[1]()

    def tile_a(s, i):
        cj = 0 if s < 8 else 1
        ti = s * 4 + i
        xt = xb[ti % 3]
        k.dma(xt[:, :], c.X[ti * 128:(ti + 1) * 128, :], writes=[xt])
        norm_to_hT(k, c, xt, hT[s % 2], i * 128,
                   lambda ch, cj=cj: c.G1[:, cj, ch:ch + 1], lambda ch, cj=cj: c.modc[:, cj, 0, ch:ch + 1],
                   (junk, ss, rstd, xs[ti % 3]), PS[0], PS[1], defer=deferred)

    def st_body(s):
        cj = 0 if s < 8 else 1
        smp = s < 8
        h = hT[s % 2]
        if smp:
            rp = ropet[s % 2]
            k.dma(rp[:, :, :, :], c.ROPE[:, s * 4:(s + 1) * 4, :, :], writes=[rp])
        QT_, MK_, MV_, MO_, MQT_, MKT_ = QTs[0], MKs[0], MVs[0], MOs[0], MQTs[0], MKTs[0]
        if s == 0:
            for i in range(4):
                tile_a(0, i)
                flush()

        def tile_b(i):
            ti = s * 4 + i
            tsl = slice(i * 128, (i + 1) * 128)
            pq, pkv, pmk, pmv, pmo = PS[2], PS[3], PS[4], PS[5], PS[6]
            for (pb, c0, c1) in ((pmk, 1280, 1792), (pmv, 1792, 2304), (pmo, 2304, 2816), (pq, 0, 512),
                                 (pkv, 512, 768)):
                for kc in range(8):
                    k.op("pe", lambda e, pb=pb, c0=c0, c1=c1, kc=kc, tsl=tsl: e.matmul(
                        pb[:, 0:c1 - c0], lhsT=h[:, kc, tsl], rhs=Win[:, kc, c0:c1], start=(kc == 0), stop=(kc == 7)),
                        reads=[h, Win], writes=[pb], inc=(kc == 7))
            flush(ti - 2)
            deferred.cur = ti
            qn, qr, kdup = qn2[ti % 2], qr2[ti % 2], kdup2[ti % 2]
            k.capture()
            k.op("act", lambda e, i=i: e.activation(out=MK_[:, i, :], in_=pmk[:, :], func=AF.Copy, scale=KSC),
                 reads=[pmk], writes=[MK_])
            k.op("dve", lambda e, i=i: e.tensor_copy(out=MV_[:, i, :, 0:128],
                                                     in_=pmv[:, :].rearrange("p (h d) -> p h d", h=4)),
                 reads=[pmv], writes=[MV_])
            k.op("act", lambda e: e.activation(out=mot[:, :], in_=pmo[:, :], func=AF.Exp, scale=-1.0),
                 reads=[pmo], writes=[mot])
            k.op("act", lambda e: e.activation(out=mot[:, :], in_=mot[:, :], func=AF.Ln, bias=1.0),
                 reads=[mot], writes=[mot])
            k.op("act", lambda e, i=i: e.activation(out=MO_[:, i, :], in_=mot[:, :], func=AF.Exp, scale=-1.0),
                 reads=[mot], writes=[MO_])
            Lm = k.capture_end()
            kv = kvf[ti % 2]
            kk = kn[ti % 2]
            k.capture()
            k.op("act", lambda e: e.activation(out=qf[:, :], in_=pq[:, :], func=AF.Copy), reads=[pq], writes=[qf])
            k.op("dve", lambda e: e.tensor_tensor(out=sq[:, :], in0=qf[:, :], in1=qf[:, :], op=ALU.mult),
                 reads=[qf], writes=[sq])
            k.op("dve", lambda e: e.tensor_reduce(out=ssq[:, 0:8], in_=sq[:, :].rearrange("p (h d) -> p h d", d=64),
                                                  axis=AX.X, op=ALU.add), reads=[sq], writes=[ssq])
            rstd_from_ss(k, ssq, rsq, 1.0 / 64, 8)
            k.op("dve", lambda e: e.tensor_tensor(out=qn[:, :].rearrange("p (h d) -> p h d", d=64),
                                                  in0=qf[:, :].rearrange("p (h d) -> p h d", d=64),
                                                  in1=rsq[:, 0:8].unsqueeze(2).to_broadcast([128, 8, 64]),
                                                  op=ALU.mult), reads=[qf, rsq], writes=[qn])
            k.op("dve", lambda e: e.tensor_tensor(out=qn[:, :].rearrange("p (h d) -> p h d", d=64),
                                                  in0=qn[:, :].rearrange("p (h d) -> p h d", d=64),
                                                  in1=wq_bc[:, :].unsqueeze(1).to_broadcast([128, 8, 64]),
                                                  op=ALU.mult), reads=[qn, wq_bc], writes=[qn])
            if smp:
                k.op("pool", lambda e, i=i: e.tensor_tensor(
                    out=t1[:, :].rearrange("p (h d) -> p h d", d=64),
                    in0=qn[:, :].rearrange("p (h d) -> p h d", d=64),
                    in1=rp[:, i, 0, :].unsqueeze(1).to_broadcast([128, 8, 64]), op=ALU.mult),
                    reads=[qn, rp], writes=[t1])
                for jj in range(2):
                    k.op("pool", lambda e, i=i, jj=jj: e.tensor_tensor(
                        out=t2[:, :].rearrange("p (h a j w) -> p h a j w", h=8, a=2, j=2)[:, :, :, jj, :],
                        in0=qn[:, :].rearrange("p (h a j w) -> p h a j w", h=8, a=2, j=2)[:, :, :, 1 - jj, :],
                        in1=rp[:, i, 1, :].rearrange("p (a j w) -> p a j w", a=2, j=2)[:, :, jj, :].unsqueeze(1)
                        .to_broadcast([128, 8, 2, 16]), op=ALU.mult),
                        reads=[qn, rp], writes=[t2])
                k.op("pool", lambda e: e.tensor_tensor(out=qr[:, :], in0=t1[:, :], in1=t2[:, :], op=ALU.add),
                     reads=[t1, t2], writes=[qr])
                qsrc = qr
            else:
                qsrc = qn
            def q_tr(qsrc=qsrc, tsl=tsl):
                pt = PS[7]
                for j in range(4):
                    k.op("pe", lambda e, j=j: e.transpose(pt[:, j * 128:(j + 1) * 128],
                                                          qsrc[:, j * 128:(j + 1) * 128], c.ident[:, :]),
                         reads=[qsrc, c.ident], writes=[pt], inc=(j == 3))
                k.op("act", lambda e: e.activation(out=QT_[:, :, tsl],
                                                   in_=pt[:, :].rearrange("p (j t) -> p j t", j=4),
                                                   func=AF.Copy), reads=[pt], writes=[QT_])
            deferred.append(q_tr)
            Lq = k.capture_end()
            k.capture()
            k.op("act", lambda e, kv=kv: e.activation(out=kv[:, :], in_=pkv[:, 0:256], func=AF.Copy),
                 reads=[pkv], writes=[kv])
            k.op("dve", lambda e, kv=kv: e.tensor_tensor(out=kt1[:, :], in0=kv[:, 0:128], in1=kv[:, 0:128],
                                                         op=ALU.mult), reads=[kv], writes=[kt1])
            k.op("dve", lambda e: e.tensor_reduce(out=ssk[:, 0:2], in_=kt1[:, :].rearrange("p (h d) -> p h d", d=64),
                                                  axis=AX.X, op=ALU.add), reads=[kt1], writes=[ssk])
            rstd_from_ss(k, ssk, rsk, 1.0 / 64, 2)
            k.op("dve", lambda e, kv=kv, kk=kk: e.tensor_tensor(
                out=kk[:, :].rearrange("p (h d) -> p h d", d=64),
                in0=kv[:, 0:128].rearrange("p (h d) -> p h d", d=64),
                in1=rsk[:, 0:2].unsqueeze(2).to_broadcast([128, 2, 64]), op=ALU.mult),
                reads=[kv, rsk], writes=[kk])
            k.op("dve", lambda e, kk=kk: e.tensor_tensor(
                out=kk[:, :].rearrange("p (h d) -> p h d", d=64),
                in0=kk[:, :].rearrange("p (h d) -> p h d", d=64),
                in1=wk_bc[:, :].unsqueeze(1).to_broadcast([128, 2, 64]), op=ALU.mult),
                reads=[kk, wk_bc], writes=[kk])
            if smp:
                k.op("pool", lambda e, i=i, kk=kk: e.tensor_tensor(
                    out=kt1[:, :].rearrange("p (h d) -> p h d", d=64),
                    in0=kk[:, :].rearrange("p (h d) -> p h d", d=64),
                    in1=rp[:, i, 0, :].unsqueeze(1).to_broadcast([128, 2, 64]), op=ALU.mult),
                    reads=[kk, rp], writes=[kt1])
                for jj in range(2):
                    k.op("pool", lambda e, i=i, kk=kk, jj=jj: e.tensor_tensor(
                        out=kt2[:, :].rearrange("p (h a j w) -> p h a j w", h=2, a=2, j=2)[:, :, :, jj, :],
                        in0=kk[:, :].rearrange("p (h a j w) -> p h a j w", h=2, a=2, j=2)[:, :, :, 1 - jj, :],
                        in1=rp[:, i, 1, :].rearrange("p (a j w) -> p a j w", a=2, j=2)[:, :, jj, :].unsqueeze(1)
                        .to_broadcast([128, 2, 2, 16]), op=ALU.mult),
                        reads=[kk, rp], writes=[kt2])
                k.op("pool", lambda e: e.tensor_tensor(out=kr[:, :], in0=kt1[:, :], in1=kt2[:, :], op=ALU.add),
                     reads=[kt1, kt2], writes=[kr])
                ksrc = kr
            else:
                ksrc = kk
                pt0 = (ti - 32) * 128
                k.dma(c.NK[pt0:pt0 + 128, :], kk[:, :], reads=[kk])
                k.dma(c.NV[pt0:pt0 + 128, :], kv[:, 128:256], reads=[kv])
            k.op("dve", lambda e, ksrc=ksrc: e.tensor_copy(
                out=kdup[:, :, :, :], in_=ksrc[:, :].rearrange("p (g o d) -> p g o d", g=2, o=1)
                .to_broadcast([128, 2, 2, 64])), reads=[ksrc], writes=[kdup])
            def k_tr(ti=ti):
                pk = PS[7]
                for g in range(2):
                    k.op("pe", lambda e, g=g: e.transpose(pk[:, g * 128:(g + 1) * 128],
                                                          kdup[:, g, :, :].rearrange("p a d -> p (a d)"),
                                                          c.ident[:, :]),
                         reads=[kdup, c.ident], writes=[pk], inc=(g == 1))
                k.op("act", lambda e: e.activation(out=KT2[:, :, ti * 128:(ti + 1) * 128],
                                                   in_=pk[:, 0:256].rearrange("p (g t) -> p g t", g=2),
                                                   func=AF.Copy), reads=[pk], writes=[KT2])
            deferred.append(k_tr)
            k.op("dve", lambda e, ti=ti, kv=kv: e.tensor_copy(out=VA[:, ti, :, 64:128],
                                                              in_=kv[:, 128:256].rearrange("p (g d) -> p g d", g=2)),
                 reads=[kv], writes=[VA])
            Lk = k.capture_end()
            k.emit_interleaved(Lm, Lq, Lk)

        for i in range(4):
            tile_b(i)
            if s + 1 < 9:
                tile_a(s + 1, i)
        for hh in range(8):
            pb = PS[2 + hh % 4]
            c0 = 768 + hh * 128
            for kc in range(8):
                k.op("pe", lambda e, pb=pb, c0=c0, kc=kc: e.matmul(
                    pb[:, :], lhsT=Win[:, kc, c0:c0 + 128], rhs=h[:, kc, :], start=(kc == 0), stop=(kc == 7)),
                    reads=[h, Win], writes=[pb], inc=(kc == 7))
            if hh == 3:
                flush()
            if hh < 4:
                k.op("dve", lambda e, pb=pb, hh=hh: e.tensor_copy(out=MQT_[:, hh, :], in_=pb[:, :]),
                     reads=[pb], writes=[MQT_])
            else:
                k.op("act", lambda e, pb=pb, hh=hh: e.activation(out=MKT_[:, hh - 4, :], in_=pb[:, :], func=AF.Copy,
                                                                 scale=KSC), reads=[pb], writes=[MKT_])
        pg = PS[6]
        for kc in range(8):
            k.op("pe", lambda e, kc=kc: e.matmul(pg[:, :], lhsT=Wg[:, kc, :], rhs=h[:, kc, :], start=(kc == 0),
                                                 stop=(kc == 7)), reads=[h, Wg], writes=[pg], inc=(kc == 7))
        k.op("act", lambda e: e.activation(out=gsb[:, :], in_=pg[:, :], func=AF.Identity, bias=gbcol[:, 0:1]),
             reads=[pg, gbcol], writes=[gsb])
        for r0 in (32, 96):
            k.op("act", lambda e, r0=r0: e.activation(out=gtmp[r0:r0 + 4, :], in_=gsb[r0:r0 + 4, :], func=AF.Exp,
                                                      scale=-1.0), reads=[gsb], writes=[gtmp])
            k.op("act", lambda e, r0=r0: e.activation(out=gtmp[r0:r0 + 4, :], in_=gtmp[r0:r0 + 4, :], func=AF.Ln,
                                                      bias=1.0), reads=[gtmp], writes=[gtmp])
            k.op("dve", lambda e, r0=r0: e.tensor_scalar_mul(out=gsb[r0:r0 + 4, :], in0=gtmp[r0:r0 + 4, :],
                                                             scalar1=-1.0), reads=[gtmp], writes=[gsb])
        flush()
        tok = slice(s * 512, (s + 1) * 512)
        k.dma(c.QTd[:, :, tok].rearrange("c p t -> p c t"), QT_[:, :, :], reads=[QT_])
        k.dma(c.MQTd[:, :, tok].rearrange("c p t -> p c t"), MQT_[:, :, :], reads=[MQT_])
        k.dma(c.MKTd[:, :, tok].rearrange("c p t -> p c t"), MKT_[:, :, :], reads=[MKT_])
        k.dma(c.MKd[tok, :].rearrange("(i p) f -> p i f", p=128), MK_[:, :, :], reads=[MK_])
        k.dma(c.MVd[tok, :].rearrange("(i p) f -> p i f", p=128), MV_[:, :, :, :].rearrange("p i h d -> p i (h d)"),
              reads=[MV_])
        k.dma(c.MOd[tok, :].rearrange("(i p) f -> p i f", p=128), MO_[:, :, :], reads=[MO_])
        for j in range(4):
            k.dma(c.GATd[j, :, tok], gsb[32 * j:32 * j + 4, :], reads=[gsb])
    for s in range(9):
        st_body(s)
    k.barrier()
    k.sb_phase = c.sb_phase0_a
    k.phase_mem()


def _consts():
    f = np.float32
    tok = np.arange(4096)
    row = (tok // 64).astype(f)
    col = (tok % 64).astype(f)
    inv = (f(10000.0) ** (-np.arange(0, 32, 2, dtype=f) / f(32))).astype(f)
    ang = np.stack([row[:, None] * inv[None, :], col[:, None] * inv[None, :]], axis=1).astype(f)
    cs, sn = np.cos(ang).astype(f), np.sin(ang).astype(f)
    Cf = np.stack([cs, cs], axis=2)
    Ss = np.stack([-sn, sn], axis=2)
    tab = np.stack([Cf.reshape(4096, 64), Ss.reshape(4096, 64)], axis=1)
    rope = np.ascontiguousarray(tab.reshape(32, 128, 2, 64).transpose(1, 0, 2, 3))
    ident = np.eye(128, dtype=f)
    sidx = np.arange(128)[:, None]
    lidx = np.arange(128)[None, :]
    mf = (lidx >= sidx).astype(f)
    mb = (lidx <= sidx).astype(f)
    masks = np.stack([np.tile(mf, (1, 4)), np.tile(mb, (1, 4))], axis=1)
    return rope, ident, np.ascontiguousarray(masks)


def make_in_maps(inp):
    rope, ident, masks = _consts()
    f = np.float32
    c = lambda a: np.ascontiguousarray(np.asarray(a), dtype=f)
    maps = []
    for b in range(8):
        x = np.concatenate([inp["x_sample"][b], inp["x_prompt"][2 * b], inp["x_prompt"][2 * b + 1]], axis=0)
        m = {
            "x": c(x),
            "cache_k": c(np.asarray(inp["cache_k"])[b, 0].reshape(256, 128)),
            "cache_v": c(np.asarray(inp["cache_v"])[b, 0].reshape(256, 128)),
            "state_C": c(np.asarray(inp["state_C"])[b, 0]),
            "state_n": c(np.asarray(inp["state_n"])[b, 0]),
            "state_m": c(np.asarray(inp["state_m"])[b, 0]),
            "cond": c(np.stack([np.asarray(inp["c"])[b], np.asarray(inp["c_ctx"])], axis=0)),
            "w_ada": c(np.asarray(inp["w_ada"])[0]), "b_ada": c(np.asarray(inp["b_ada"])[0]),
            "norm1_w": c(np.asarray(inp["norm1_w"])[0]), "w_in": c(np.asarray(inp["w_in"])[0]),
            "gate_bias": c(np.asarray(inp["gate_bias"])[0]), "q_norm_w": c(np.asarray(inp["q_norm_w"])[0]),
            "k_norm_w": c(np.asarray(inp["k_norm_w"])[0]), "mlstm_norm_w": c(np.asarray(inp["mlstm_norm_w"])[0]),
            "w_out": c(np.asarray(inp["w_out"])[0]), "norm2_w": c(np.asarray(inp["norm2_w"])[0]),
            "w_gu": c(np.asarray(inp["w_gu"])[0]), "w_down": c(np.asarray(inp["w_down"])[0]),
            "final_norm_w": c(inp["final_norm_w"]),
            "rope": rope, "ident": ident, "masks": masks,
        }
        maps.append(m)
    return maps


_NC = None


def kernel(**inp):
    global _NC
    if _NC is None:
        _NC = build_nc()
    maps = make_in_maps(inp)
    res = run_bass_kernel_spmd(_NC, maps, core_ids=list(range(8)))
    R = res.results
    f = np.float32
    y_s = np.stack([R[b]["y"][0:4096] for b in range(8)], axis=0).astype(f)
    y_p = np.stack([R[b]["y"][4096 + 256 * j:4096 + 256 * (j + 1)] for b in range(8) for j in range(2)], axis=0).astype(f)
    nk = np.stack([R[b]["nk"][256 * j:256 * (j + 1)].reshape(256, 2, 64) for b in range(8) for j in range(2)], axis=0)
    nv = np.stack([R[b]["nv"][256 * j:256 * (j + 1)].reshape(256, 2, 64) for b in range(8) for j in range(2)], axis=0)
    nC = np.stack([R[b]["nC"][j] for b in range(8) for j in range(2)], axis=0)
    nn = np.stack([R[b]["nn"][j] for b in range(8) for j in range(2)], axis=0)
    nm = np.stack([R[b]["nm"][j] for b in range(8) for j in range(2)], axis=0)
    return (y_p, y_s, nk[:, None].astype(f), nv[:, None].astype(f), nC[:, None].astype(f), nn[:, None].astype(f),
            nm[:, None].astype(f))


def phase_b1(ctx):
    c = NS(ctx)
    k = c.k
    PS = c.PS
    KT2, VA = c.KT2, c.VA
    pre_gen, b2_main = make_b2(ctx)
    ctx["b2_main"] = b2_main
    k.sb_phase = k.sb_ptr
    pre = pre_gen()
    SP = [Tl(c.PSALL[:, b * 1024:(b + 1) * 1024]) for b in range(2)]
    QTg = [k.sb([128, 4, 512], BF16, f"QTg{i}") for i in range(2)]
    PTP = [k.sb([128, 1024], BF16, f"PT{i}") for i in range(3)]
    AO = [k.sb([128, 4, 512], BF16, f"AO{i}") for i in range(2)]
    rden = [k.sb([128, 512], F32, f"rden{i}") for i in range(2)]
    obs = k.sb([128, 512], F32, "obs")
    groups = [(g * 512, 512, list(range(32)) + [36, 37]) for g in range(8)]
    groups += [(4096, 256, [32, 33]), (4352, 256, [34, 35])]
    cnt = [0]

    def group_body(gi, t0, nq, kbs):
        qt = QTg[gi % 2]
        ao = AO[gi % 2]
        k.dma(qt[:, :, 0:nq], c.QTd[:, :, t0:t0 + nq].rearrange("c p t -> p c t"), writes=[qt])
        iters = [(j, idx, kb) for j in range(4) for idx, kb in enumerate(kbs)]
        base = cnt[0]
        cnt[0] += len(iters)

        def emit_s(n):
            j, idx, kb = iters[n]
            g = j // 2
            kcol = kb * 128 if kb < 36 else TT + (kb - 36) * 128
            sp = SP[(base + n) % 2]
            k.op("pe", lambda e: e.matmul(sp[:, 0:nq], lhsT=KT2[0:64, g, kcol:kcol + 128], rhs=qt[0:64, j, 0:nq],
                                          start=True, stop=True), reads=[KT2, qt], writes=[sp], inc=False)
            k.op("pe", lambda e: e.matmul(sp[:, 512:512 + nq], lhsT=KT2[64:128, g, kcol:kcol + 128],
                                          rhs=qt[64:128, j, 0:nq], start=True, stop=True),
                 reads=[KT2, qt], writes=[sp])

        def emit_rest(n):
            j, idx, kb = iters[n]
            g = j // 2
            sp = SP[(base + n) % 2]
            pt = PTP[(base + n) % 3]
            oa, ob = (PS[4], PS[6])[j % 2], PS[5]
            k.op("act", lambda e: e.activation(out=pt[:, :].rearrange("p (a t) -> p a t", a=2)[:, :, 0:nq],
                                               in_=sp[:, :].rearrange("p (a t) -> p a t", a=2)[:, :, 0:nq],
                                               func=AF.Exp), reads=[sp], writes=[pt])
            first, last = idx == 0, idx == len(kbs) - 1
            k.op("pe", lambda e: e.matmul(oa[:, 0:nq], lhsT=VA[:, kb, g, 64:192], rhs=pt[:, 0:nq], start=first,
                                          stop=last), reads=[VA, pt], writes=[oa], inc=False)
            k.op("pe", lambda e: e.matmul(ob[:, 0:nq], lhsT=VA[:, kb, g, 0:128], rhs=pt[:, 512:512 + nq],
                                          start=first, stop=last), reads=[VA, pt], writes=[ob])
            if last:
                ra, rb = rden[0], rden[1]
                k.op("dve", lambda e: e.tensor_copy(out=obs[:, 0:nq], in_=ob[:, 0:nq]), reads=[ob], writes=[obs])
                k.op("dve", lambda e: e.reciprocal(out=rb[64:128, 0:nq], in_=obs[0:64, 0:nq]), reads=[obs],
                     writes=[rb])
                k.op("dve", lambda e: e.tensor_tensor(out=ao[64:128, j, 0:nq], in0=obs[64:128, 0:nq],
                                                      in1=rb[64:128, 0:nq], op=ALU.mult), reads=[obs, rb],
                     writes=[ao])
                k.op("dve", lambda e: e.reciprocal(out=ra[0:64, 0:nq], in_=oa[64:128, 0:nq]), reads=[oa], writes=[ra])
                k.op("dve", lambda e: e.tensor_tensor(out=ao[0:64, j, 0:nq], in0=oa[0:64, 0:nq], in1=ra[0:64, 0:nq],
                                                      op=ALU.mult), reads=[oa, ra], writes=[ao])

        emit_s(0)
        for n in range(len(iters)):
            if n + 1 < len(iters):
                emit_s(n + 1)
            emit_rest(n)
            if PRE_IN_ATT:
                next(pre, None)
        k.dma(c.AOTd[:, :, t0:t0 + nq].rearrange("c p t -> p c t"), ao[:, :, 0:nq], reads=[ao])

    for gi, (t0, nq, kbs) in enumerate(groups):
        group_body(gi, t0, nq, kbs)
    for _ in pre:
        pass
    k.barrier()
    k.phase_mem()


def make_b2(ctx):
    c = NS(ctx)
    k = c.k
    PS = c.PS
    pS = PS[0]
    pGd, pId, pTd = (PS[1], PS[4]), (PS[2], PS[5]), (PS[3], PS[6])
    pX = PS[7]
    Cst = [k.sb([128, 4, 130], F32, f"Cst{i}") for i in range(2)]
    Csnap = k.sb([128, NT, 4, 130], BF16, "Csnap")
    mst = [k.sb([4, 2], F32, f"mst{i}") for i in range(2)]
    mbprev = k.sb([4, NT + 4], F32, "mbprev")
    GT = [k.sb([4, 4, 128], F32, f"GT{i}") for i in range(2)]
    kk = [k.sb([128, 4, 128], BF16, f"kk{i}") for i in range(2)]
    va = [k.sb([128, 4, 130], BF16, f"va{i}") for i in range(2)]
    rows = [[k.sb([4, 128], F32, f"row{d}_{i}") for i in range(8)] for d in range(2)]
    dg = [k.sb([4, 4], F32, f"dg{d}") for d in range(2)]
    cols = [k.sb([128, 24], F32, f"cols{d}") for d in range(2)]
    kw = k.sb([128, 512], BF16, "kw")
    cols1p = [cols[1], k.sb([128, 24], F32, "cols1b")]
    ident, ones4, eye4 = c.ident, c.ones4, c.eye4
    maskneg = identb = None
    masks = mnw = Cbf = qT = kT = mo = blk = Dsb = swT = qI = hd = hm = sqh = ss4 = rs4 = hmT = None

    def late_alloc():
        nonlocal maskneg, identb
        nonlocal masks, mnw, Cbf, qT, kT, mo, blk, Dsb, swT, qI, hd, hm, sqh, ss4, rs4, hmT
        masks = k.sb([128, 2, 512], F32, "masks")
        mnw = k.sb([128, 512], F32, "mnw")
        Cbf = k.sb([128, 4, 130], BF16, "Cbf")
        qT = [k.sb([128, 4, 128], BF16, f"qT{i}") for i in range(2)]
        kT = [k.sb([128, 4, 128], BF16, f"kT{i}") for i in range(2)]
        mo = [k.sb([128, 512], BF16, f"mo{i}") for i in range(2)]
        blk = [[k.sb([4, 4, 128], F32, f"blk{d}_{i}") for i in range(2)] for d in range(2)]
        Dsb = [k.sb([128, 512], F32, f"Dsb{d}") for d in range(2)]
        swT = [k.sb([128, 512], BF16, f"swT{d}") for d in range(2)]
        qI = [k.sb([128, 512], BF16, f"qI{d}") for d in range(2)]
        hd = [[k.sb([128, 512], F32, f"hd{p}_{d}") for d in range(2)] for p in range(2)]
        hm = k.sb([128, 512], F32, "hm")
        sqh = k.sb([128, 512], F32, "sqh")
        ss4 = k.sb([128, 8], F32, "ss4")
        rs4 = k.sb([128, 8], F32, "rs4")
        hmT = [k.sb([128, 4, 128], BF16, f"hmT{i}") for i in range(2)]
        maskneg = k.sb([128, 2, 512], BF16, "maskneg")
        identb = k.sb([128, 128], BF16, "identb")
        k.dma(masks[:, :, :], c.MASKS[:, :, :], writes=[masks])
        k.dma(mnw[:, :], c.MNW.partition_broadcast(128), writes=[mnw])
        k.op("dve", lambda e: e.tensor_scalar(out=maskneg[:, :, :], in0=masks[:, :, :], scalar1=-1.0, scalar2=30000.0,
                                              op0=ALU.add, op1=ALU.mult), reads=[masks], writes=[maskneg])
        k.op("dve", lambda e: e.tensor_copy(out=identb[:, :], in_=ident[:, :]), reads=[ident], writes=[identb])

    def load_chunk(ci, full):
        tok = slice(ci * 128, (ci + 1) * 128)
        i2 = ci % 2
        k.dma(GT[i2][:, :, :], c.GATd[:, :, tok].rearrange("t h n -> h t n"), writes=[GT[i2]])
        k.dma(kk[i2][:, :, :], c.MKd[tok, :].rearrange("p (h d) -> p h d", h=4), writes=[kk[i2]])
        k.dma(va[i2][:, :, :], c.MVd[tok, :].rearrange("p (h d) -> p h d", h=4), writes=[va[i2]])
        if full:
            k.dma(qT[i2][:, :, :], c.MQTd[:, :, tok].rearrange("h p t -> p h t"), writes=[qT[i2]])
            k.dma(kT[i2][:, :, :], c.MKTd[:, :, tok].rearrange("h p t -> p h t"), writes=[kT[i2]])
            k.dma(mo[i2][:, :], c.MOd[tok, :], writes=[mo[i2]])

    def gate_prep(d, gt, mprev, full, upd, cl=None, pT=None):
        mt, mc = mprev
        rb_, ra_, rg_, rng_, rin_, rgu_, rw_, rt_ = rows[d]
        pT = pT or pTd[d]
        cl = cl or cols[d]
        lf = lambda: gt[:, 1 + 2 * d, :]
        ig = lambda: gt[:, 2 * d, :]
        rv = (lambda ap: ap) if d == 0 else (lambda ap: ap[:, ::-1])
        last = 127 if d == 0 else 0
        k.op("dve", lambda e: e.tensor_tensor_scan(out=rv(rb_[:, :]), data0=rv(ones4[:, :]), data1=rv(lf()),
                                                   initial=0.0, op0=ALU.mult, op1=ALU.add),
             reads=[gt, ones4], writes=[rb_])
        yield
        k.op("dve", lambda e: e.tensor_tensor(out=ra_[:, :], in0=ig(), in1=rb_[:, :], op=ALU.subtract),
             reads=[gt, rb_], writes=[ra_])
        yield
        k.op("pe", lambda e: e.transpose(pT[:, 0:4], ra_[:, :], ident[0:4, 0:4]), reads=[ra_, ident], writes=[pT])
        k.op("dve", lambda e: e.tensor_tensor_scan(out=rv(rg_[:, :]), data0=rv(ra_[:, :]), data1=rv(ra_[:, :]),
                                                   initial=mt[:, mc:mc + 1], op0=ALU.max, op1=ALU.max),
             reads=[ra_, mt], writes=[rg_])
        yield
        k.op("dve", lambda e: e.tensor_scalar_mul(out=rng_[:, :], in0=rg_[:, :], scalar1=-1.0),
             reads=[rg_], writes=[rng_])
        if full:
            k.op("dve", lambda e: e.tensor_tensor(out=rt_[:, :], in0=rb_[:, :], in1=rg_[:, :], op=ALU.add),
                 reads=[rb_, rg_], writes=[rt_])
            yield
            bk0 = blk[d][0]
            k.op("dve", lambda e: e.tensor_tensor(
                out=bk0[:, :, :], in0=rng_[:, :].unsqueeze(1).to_broadcast([4, 4, 128]),
                in1=eye4[:, :].unsqueeze(2).to_broadcast([4, 4, 128]), op=ALU.mult),
                reads=[rng_, eye4], writes=[bk0])
            yield
            k.op("pe", lambda e: e.matmul(pGd[d][:, :], lhsT=ones4[:, :],
                                          rhs=bk0[:, :, :].rearrange("k h l -> k (h l)"),
                                          start=True, stop=False), reads=[ones4, bk0], writes=[pGd[d]], inc=False)
            k.op("pe", lambda e: e.matmul(pGd[d][:, :], lhsT=identb[:, :], rhs=maskneg[:, d, :],
                                          start=False, stop=True), reads=[identb, maskneg], writes=[pGd[d]])
        yield
        k.op("act", lambda e: e.activation(out=rin_[:, :], in_=rng_[:, :], func=AF.Exp, bias=mt[:, mc:mc + 1]),
             reads=[rng_, mt], writes=[rin_])
        if full:
            yield
            bk1 = blk[d][1]
            k.op("dve", lambda e: e.tensor_tensor(
                out=bk1[:, :, :], in0=rin_[:, :].unsqueeze(1).to_broadcast([4, 4, 128]),
                in1=eye4[:, :].unsqueeze(2).to_broadcast([4, 4, 128]), op=ALU.mult),
                reads=[rin_, eye4], writes=[bk1])
            yield
            k.op("pe", lambda e: e.matmul(pId[d][:, :], lhsT=ones4[:, :],
                                          rhs=bk1[:, :, :].rearrange("k h l -> k (h l)"),
                                          start=True, stop=True), reads=[ones4, bk1], writes=[pId[d]])
        if full:
            k.op("act", lambda e: e.activation(out=rgu_[:, :], in_=rt_[:, :], func=AF.Exp, scale=-1.0),
                 reads=[rt_], writes=[rgu_])
        if upd:
            k.op("act", lambda e: e.activation(out=rw_[:, :], in_=ra_[:, :], func=AF.Exp,
                                               bias=rng_[:, last:last + 1]), reads=[ra_, rng_], writes=[rw_])
        yield
        if full:
            k.op("pe", lambda e: e.transpose(pT[:, 4:8], rgu_[:, :], ident[0:4, 0:4]), reads=[rgu_, ident],
                 writes=[pT])
        if upd:
            k.op("pe", lambda e: e.transpose(pT[:, 8:12], rw_[:, :], ident[0:4, 0:4]), reads=[rw_, ident],
                 writes=[pT])
            k.op("dve", lambda e: e.tensor_scalar(out=dg[d][:, :], in0=eye4[:, :], scalar1=rin_[:, last:last + 1],
                                                  scalar2=None, op0=ALU.mult), reads=[eye4, rin_], writes=[dg[d]])
            yield
            k.op("pe", lambda e: e.matmul(pT[:, 12:16], lhsT=ones4[:, :], rhs=dg[d][:, :], start=True, stop=True),
                 reads=[ones4, dg[d]], writes=[pT])
        yield
        hi = 16 if upd else 8
        if not full and upd:
            k.op("dve", lambda e: e.tensor_copy(out=cl[:, 0:4], in_=pT[:, 0:4]), reads=[pT], writes=[cl])
            k.op("dve", lambda e: e.tensor_copy(out=cl[:, 8:16], in_=pT[:, 8:16]), reads=[pT], writes=[cl])
        else:
            k.op("dve", lambda e: e.tensor_copy(out=cl[:, 0:hi], in_=pT[:, 0:hi]), reads=[pT], writes=[cl])
        yield

    def new_m(d, mt_out, mc_out):
        rb_, rg_ = rows[d][0], rows[d][2]
        last = 127 if d == 0 else 0
        k.op("dve", lambda e: e.tensor_tensor(out=mt_out[:, mc_out:mc_out + 1], in0=rb_[:, last:last + 1],
                                              in1=rg_[:, last:last + 1], op=ALU.add),
             reads=[rb_, rg_], writes=[mt_out])

    def state_update(d, kk_, va_, banks=None, cl=None):
        banks = banks or ((pId[d], 0), (pTd[d], 128))
        Cs = Cst[d]
        cl = cl or cols[d]
        k.op("dve", lambda e: e.tensor_tensor(out=kw[:, :].rearrange("p (h d) -> p h d", h=4), in0=kk_[:, :, :],
                                              in1=cl[:, 8:12].unsqueeze(2).to_broadcast([128, 4, 128]),
                                              op=ALU.mult), reads=[kk_, cl], writes=[kw])
        yield
        for h in range(4):
            pd, off = banks[h % 2]
            k.op("pe", lambda e, h=h, pd=pd, off=off: e.matmul(pd[:, off:off + 129], lhsT=kw[:, h * 128:(h + 1) * 128],
                                                      rhs=va_[:, h, 0:129], start=True, stop=True),
                 reads=[kw, va_], writes=[pd])
            yield
            k.op("dve", lambda e, h=h, pd=pd, off=off: e.scalar_tensor_tensor(
                out=Cs[:, h, 0:129], in0=Cs[:, h, 0:129], scalar=cl[:, 12 + h:13 + h], in1=pd[:, off:off + 129],
                op0=ALU.mult, op1=ALU.add), reads=[Cs, cl, pd], writes=[Cs])
            yield

    def run(*gens):
        gens = list(gens)
        while gens:
            for g in list(gens):
                try:
                    next(g)
                except StopIteration:
                    gens.remove(g)

    def init_dir(seq, d):
        Cs = Cst[d]
        if seq == 0:
            k.op("pool", lambda e: e.memset(Cs[:, :, :], 0.0), writes=[Cs])
            k.op("dve", lambda e: e.memset(mst[d][:, :], 0.0), writes=[mst[d]])
            k.dma(Cs[:, :, 0:128], c.SC[d].rearrange("h p e -> p h e"), writes=[Cs])
            k.dma(Cs[:, :, 128], c.SN[d].rearrange("h p -> p h"), writes=[Cs], slow=True)
            k.dma(mst[d][:, 0:1], c.SM[d].rearrange("(h o) -> h o", o=1), writes=[mst[d]], slow=True)
        else:
            k.op("pool", lambda e: e.memset(Cs[:, :, :], 0.0), writes=[Cs])
            k.op("dve", lambda e: e.memset(mst[d][:, :], 0.0), writes=[mst[d]])

    def store_state(seq, d):
        p = seq - 1
        Cs = Cst[d]
        k.dma(c.NC_[p, d].rearrange("h p e -> p h e"), Cs[:, :, 0:128], reads=[Cs])
        k.dma(c.NN[p, d].rearrange("h p -> p h"), Cs[:, :, 128], reads=[Cs], slow=True)
        k.dma(c.NM[p, d].rearrange("(h o) -> h o", o=1), mst[d][:, 0:1], reads=[mst[d]], slow=True)

    def pre_G(ci):
        i2 = ci % 2
        load_chunk(ci, False)
        yield
        k.op("dve", lambda e: e.tensor_copy(out=mbprev[:, ci:ci + 1], in_=mst[1][:, 0:1]), reads=[mst[1]],
             writes=[mbprev])
        yield
        yield from gate_prep(1, GT[i2], (mst[1], 0), False, True, cl=cols1p[i2], pT=pX)
        new_m(1, mst[1], 0)
        yield

    def pre_U(ci):
        i2 = ci % 2
        k.op("act", lambda e: e.activation(out=Csnap[:, ci, :, :], in_=Cst[1][:, :, :], func=AF.Copy),
             reads=[Cst[1]], writes=[Csnap])
        yield
        yield from state_update(1, kk[i2], va[i2], banks=((pX, 128), (pX, 128)), cl=cols1p[i2])

    def zip2(g1, g2):
        gens = [g for g in (g1, g2) if g is not None]
        while gens:
            for g in list(gens):
                try:
                    next(g)
                except StopIteration:
                    gens.remove(g)
            yield

    seqs = [(0, list(range(32))), (1, [32, 33]), (2, [34, 35])]

    def pre_gen():
        for seq, chunks in seqs:
            init_dir(seq, 1)
            yield
            order = list(reversed(chunks))
            yield from pre_G(order[0])
            for n, ci in enumerate(order):
                nxt = pre_G(order[n + 1]) if n + 1 < len(order) else None
                yield from zip2(nxt, pre_U(ci))
            if seq > 0:
                store_state(seq, 1)
                yield

    def dir_chain(d, ci, q_, kk_, va_, gt):
        mprev = (mst[0], 0) if d == 0 else (mbprev, ci)
        rng_, rin_ = rows[d][3], rows[d][4]
        pG, pI, pT, cl = pGd[d], pId[d], pTd[d], cols[d]
        yield from gate_prep(d, gt, mprev, True, d == 0)
        if d == 0:
            new_m(0, mst[0], 0)
        for h in range(4):
            k.op("act", lambda e, h=h: e.activation(out=Dsb[d][:, h * 128:(h + 1) * 128],
                                                    in_=pG[:, h * 128:(h + 1) * 128], func=AF.Exp,
                                                    bias=cl[:, h:h + 1]), reads=[pG, cl], writes=[Dsb[d]])
        k.op("dve", lambda e: e.tensor_tensor(out=qI[d][:, :], in0=pI[:, :],
                                              in1=q_[:, :, :].rearrange("p h t -> p (h t)"), op=ALU.mult),
             reads=[pI, q_], writes=[qI[d]])
        yield
        k.op("dve", lambda e: e.tensor_tensor(out=swT[d][:, :], in0=pS[:, :], in1=Dsb[d][:, :], op=ALU.mult),
             reads=[pS, Dsb[d]], writes=[swT[d]])
        if d == 0:
            k.op("act", lambda e: e.activation(out=Cbf[:, :, :], in_=Cst[0][:, :, :], func=AF.Copy),
                 reads=[Cst[0]], writes=[Cbf])
        yield
        for h in range(4):
            cprev = (lambda h=h: Cbf[:, h, :]) if d == 0 else (lambda h=h: Csnap[:, ci, h, :])
            ctile = Cbf if d == 0 else Csnap
            k.op("pe", lambda e, h=h: e.matmul(pG[:, h * 128:(h + 1) * 128], lhsT=swT[d][:, h * 128:(h + 1) * 128],
                                               rhs=va_[:, h, 0:128], start=True, stop=False),
                 reads=[swT[d], va_], writes=[pG], inc=False)
            k.op("pe", lambda e, h=h, cprev=cprev: e.matmul(pG[:, h * 128:(h + 1) * 128],
                                                            lhsT=qI[d][:, h * 128:(h + 1) * 128],
                                                            rhs=cprev()[:, 0:128], start=False, stop=True),
                 reads=[qI[d], ctile], writes=[pG], inc=False)
            k.op("pe", lambda e, h=h: e.matmul(pT[:, 16 + h:17 + h], lhsT=swT[d][:, h * 128:(h + 1) * 128],
                                               rhs=va_[:, h, 128:129], start=True, stop=False),
                 reads=[swT[d], va_], writes=[pT], inc=False)
            k.op("pe", lambda e, h=h, cprev=cprev: e.matmul(pT[:, 16 + h:17 + h],
                                                            lhsT=qI[d][:, h * 128:(h + 1) * 128],
                                                            rhs=cprev()[:, 128:129], start=False, stop=True),
                 reads=[qI[d], ctile], writes=[pT])
            yield
        k.op("dve", lambda e: e.tensor_scalar(out=cl[:, 20:24], in0=pT[:, 16:20], scalar1=-1.0, scalar2=None,
                                              op0=ALU.mult), reads=[pT], writes=[cl])
        yield
        k.op("dve", lambda e: e.tensor_tensor(out=cl[:, 16:20], in0=pT[:, 16:20], in1=cl[:, 20:24], op=ALU.max),
             reads=[pT, cl], writes=[cl])
        yield
        k.op("dve", lambda e: e.tensor_tensor(out=cl[:, 16:20], in0=cl[:, 16:20], in1=cl[:, 4:8], op=ALU.max),
             reads=[cl], writes=[cl])
        yield
        k.op("dve", lambda e: e.reciprocal(out=cl[:, 16:20], in_=cl[:, 16:20]), reads=[cl], writes=[cl])
        yield
        hdt = hd[ci % 2][d]
        k.op("dve", lambda e: e.tensor_tensor(out=hdt[:, :].rearrange("p (h e) -> p h e", h=4),
                                              in0=pG[:, :].rearrange("p (h e) -> p h e", h=4),
                                              in1=cl[:, 16:20].unsqueeze(2).to_broadcast([128, 4, 128]),
                                              op=ALU.mult), reads=[pG, cl], writes=[hdt])
        yield
        if d == 0:
            yield from state_update(0, kk_, va_)

    def tail_chain(ci):
        i2 = ci % 2
        mo_ = mo[i2]
        h0, h1 = hd[i2]
        k.op("pool", lambda e: e.tensor_tensor(out=hm[:, :], in0=h0[:, :], in1=h1[:, :], op=ALU.add),
             reads=[h0, h1], writes=[hm])
        yield
        k.op("dve", lambda e: e.tensor_tensor(out=sqh[:, :], in0=hm[:, :], in1=hm[:, :], op=ALU.mult),
             reads=[hm], writes=[sqh])
        yield
        k.op("dve", lambda e: e.tensor_reduce(out=ss4[:, 0:4], in_=sqh[:, :].rearrange("p (h d) -> p h d", h=4),
                                              axis=AX.X, op=ALU.add), reads=[sqh], writes=[ss4])
        yield
        k.op("act", lambda e: e.activation(out=rs4[:, 0:4], in_=ss4[:, 0:4], func=AF.Ln, scale=1.0 / 128, bias=EPS),
             reads=[ss4], writes=[rs4])
        yield
        k.op("act", lambda e: e.activation(out=rs4[:, 0:4], in_=rs4[:, 0:4], func=AF.Exp, scale=-0.5),
             reads=[rs4], writes=[rs4])
        yield
        k.op("dve", lambda e: e.tensor_tensor(out=hm[:, :].rearrange("p (h d) -> p h d", h=4),
                                              in0=hm[:, :].rearrange("p (h d) -> p h d", h=4),
                                              in1=rs4[:, 0:4].unsqueeze(2).to_broadcast([128, 4, 128]),
                                              op=ALU.mult), reads=[hm, rs4], writes=[hm])
        yield
        k.op("pool", lambda e: e.tensor_tensor(out=hm[:, :], in0=hm[:, :], in1=mnw[:, :], op=ALU.mult),
             reads=[hm, mnw], writes=[hm])
        yield
        k.op("pool", lambda e: e.tensor_tensor(out=sqh[:, :], in0=hm[:, :], in1=mo_[:, :], op=ALU.mult),
             reads=[hm, mo_], writes=[sqh])
        yield
        for h in range(4):
            k.op("pe", lambda e, h=h: e.transpose(pX[:, h * 128:(h + 1) * 128], sqh[:, h * 128:(h + 1) * 128],
                                                  ident[:, :]), reads=[sqh, ident], writes=[pX], inc=(h == 3))
        yield
        ho = hmT[i2]
        k.op("act", lambda e: e.activation(out=ho[:, :, :], in_=pX[:, :].rearrange("p (h t) -> p h t", h=4),
                                           func=AF.Copy), reads=[pX], writes=[ho])
        yield
        k.dma(c.HMTd[:, :, ci * 128:(ci + 1) * 128].rearrange("h p t -> p h t"), ho[:, :, :], reads=[ho])

    def main_chunk(ci, prev):
        i2 = ci % 2
        load_chunk(ci, True)
        q_, kT_, kk_, va_, gt = qT[i2], kT[i2], kk[i2], va[i2], GT[i2]
        for h in range(4):
            k.op("pe", lambda e, h=h: e.matmul(pS[:, h * 128:(h + 1) * 128], lhsT=kT_[:, h, :], rhs=q_[:, h, :],
                                               start=True, stop=True), reads=[kT_, q_], writes=[pS], inc=(h == 3))
        gens = [dir_chain(0, ci, q_, kk_, va_, gt), dir_chain(1, ci, q_, kk_, va_, gt)]
        if prev is not None:
            gens.append(tail_chain(prev))
        run(*gens)

    def main():
        late_alloc()
        for seq, chunks in seqs:
            init_dir(seq, 0)
            prev = None
            for ci in chunks:
                main_chunk(ci, prev)
                prev = ci
            run(tail_chain(prev))
            if seq > 0:
                store_state(seq, 0)
        k.barrier()
        k.sb_phase = ctx["sb_phase0"]
        k.phase_mem()

    return pre_gen, main


def phase_b2(ctx):
    ctx["b2_main"]()


def phase_c1(ctx):
    c = NS(ctx)
    k = c.k
    PS = c.PS
    Wout = k.sb([128, 8, D], BF16, "Wout")
    stg = [k.sb([128, D], F32, f"stgo{i}") for i in range(2)]
    g1b = k.sb([128, D], F32, "g1b")
    mixT = [k.sb([128, 8, 512], BF16, f"mixT{i}") for i in range(2)]
    xb = [k.sb([128, D], F32, f"xc{i}") for i in range(3)]
    x1 = [k.sb([128, D], F32, f"x1{i}") for i in range(2)]
    tmp = k.sb([128, D], F32, "tmpc1")
    junk = k.sb([128, D], BF16, "junkc")
    xs = [k.sb([128, D], F32, f"xsc{i}") for i in range(3)]
    deferred = []

    def flush(keep=0):
        while len(deferred) > keep:
            deferred.pop(0)()
    ss = k.sb([128, 8], F32, "ssc")
    rstd = k.sb([128, 8], F32, "rstdc")
    h2T = [k.sb([128, 8, 512], BF16, f"h2T{i}") for i in range(2)]
    for kc in range(8):
        st = stg[kc % 2]
        k.dma(st[:, :], c.WOUT[kc * 128:(kc + 1) * 128, :], writes=[st])
        k.op("pool", lambda e, kc=kc, st=st: e.tensor_copy(out=Wout[:, kc, :], in_=st[:, :]), reads=[st],
             writes=[Wout])

    def st_body(s):
        cj = 0 if s < 8 else 1
        tok = slice(s * 512, (s + 1) * 512)
        mx = mixT[s % 2]
        h2 = h2T[s % 2]
        if s == 0 or s == 8:
            k.dma(g1b[:, :], c.MODS[cj, 2 * D:3 * D].partition_broadcast(128), writes=[g1b])
        k.dma(mx[:, 0:4, :], c.AOTd[:, :, tok].rearrange("c p t -> p c t"), writes=[mx])
        k.dma(mx[:, 4:8, :], c.HMTd[:, :, tok].rearrange("c p t -> p c t"), writes=[mx])

        def tile_body(i):
            ti = s * 4 + i
            xt = xb[ti % 3]
            xo = x1[ti % 2]
            k.dma(xt[:, :], c.X[ti * 128:(ti + 1) * 128, :], writes=[xt])
            for n in range(2):
                pb = PS[2 + (2 * ti + n) % 4]
                for kc in range(8):
                    k.op("pe", lambda e, pb=pb, kc=kc, n=n: e.matmul(
                        pb[:, :], lhsT=mx[:, kc, i * 128:(i + 1) * 128], rhs=Wout[:, kc, n * 512:(n + 1) * 512],
                        start=(kc == 0), stop=(kc == 7)), reads=[mx, Wout], writes=[pb], inc=(kc == 7))
                if n == 1:
                    flush(1)
                k.op("dve", lambda e, pb=pb, n=n: e.tensor_tensor(out=tmp[:, n * 512:(n + 1) * 512], in0=pb[:, :],
                                                                  in1=g1b[:, n * 512:(n + 1) * 512], op=ALU.mult),
                     reads=[pb, g1b], writes=[tmp])
            k.op("pool", lambda e: e.tensor_tensor(out=xo[:, :], in0=tmp[:, :], in1=xt[:, :], op=ALU.add),
                 reads=[tmp, xt], writes=[xo])
            k.dma(c.X1d[ti * 128:(ti + 1) * 128, :], xo[:, :], reads=[xo])
            norm_to_hT(k, c, xo, h2, i * 128,
                       lambda ch: c.G2[:, cj, ch:ch + 1], lambda ch: c.modc[:, cj, 3, ch:ch + 1],
                       (junk, ss, rstd, xs[ti % 3]), PS[0], PS[1], defer=deferred)

        for i in range(4):
            tile_body(i)
        flush()
        k.dma(c.H2Td[:, :, tok].rearrange("c p t -> p c t"), h2[:, :, :], reads=[h2])

    for s in range(9):
        st_body(s)
    k.barrier()
    k.phase_mem()


def phase_c2(ctx):
    c = NS(ctx)
    k = c.k
    PS = c.PS
    Wgb = [k.sb([128, 8, 256], BF16, f"Wgb{i}") for i in range(22)]
    Wdn = k.sb([128, NF, D], BF16, "Wdn")
    SW = 704
    stg = [k.sb([128, SW], F32, f"stgf{i}") for i in range(2)]
    g2b = k.sb([128, D], F32, "g2b")
    fnb = k.sb([128, D], F32, "fnb")
    h2 = k.sb([128, 8, 512], BF16, "h2c")
    actT = k.sb([128, NF, 512], BF16, "actT")
    sg = [k.sb([128, 512], F32, f"sg{i}") for i in range(2)]
    x1 = [k.sb([128, D], F32, f"x1c{i}") for i in range(2)]
    x2 = k.sb([128, D], F32, "x2c")
    yt = [k.sb([128, D], F32, f"yt{i}") for i in range(2)]
    junk = k.sb([128, D], BF16, "junkf")
    ss = k.sb([128, 8], F32, "ssf")
    rstd = k.sb([128, 8], F32, "rstdf")
    mhalf = k.sb([128, 1], F32, "mhalf")
    k.dma(fnb[:, :], c.FNW.partition_broadcast(128), writes=[fnb])
    k.dma(g2b[:, :], c.MODS[0, 5 * D:6 * D].partition_broadcast(128), writes=[g2b])
    k.dma(h2[:, :, :], c.H2Td[:, :, 0:512].rearrange("c p t -> p c t"), writes=[h2])
    it = 0
    for bi in range(11):
        for half in range(2):
            blkt = Wgb[2 * bi + half]
            c0 = half * DFF + bi * 256
            for kc0 in (0, 2, 4, 6):
                st = stg[it % 2]
                it += 1
                k.dma(st[:, 0:512].rearrange("p (a n) -> p a n", a=2),
                      c.WGU[kc0 * 128:(kc0 + 2) * 128, c0:c0 + 256].rearrange("(a p) n -> p a n", p=128), writes=[st])
                k.op("pool", lambda e, kc0=kc0, st=st, blkt=blkt: e.tensor_copy(
                    out=blkt[:, kc0:kc0 + 2, :], in_=st[:, 0:512].rearrange("p (a n) -> p a n", a=2)),
                    reads=[st], writes=[blkt])
    for f in range(NF):
        for j in range(2):
            st = stg[it % 2]
            it += 1
            k.dma(st[:, 0:512], c.WDN[f * 128:(f + 1) * 128, j * 512:(j + 1) * 512], writes=[st])
            k.op("pool", lambda e, f=f, j=j, st=st: e.tensor_copy(out=Wdn[:, f, j * 512:(j + 1) * 512],
                                                                  in_=st[:, 0:512]), reads=[st], writes=[Wdn])
    k.op("dve", lambda e: e.memset(mhalf[:, :], -0.5), writes=[mhalf])

    def st_body(s):
        cj = 0 if s < 8 else 1
        tok = slice(s * 512, (s + 1) * 512)
        if s == 8:
            k.dma(g2b[:, :], c.MODS[cj, 5 * D:6 * D].partition_broadcast(128), writes=[g2b])
        if s > 0:
            k.dma(h2[:, :, :], c.H2Td[:, :, tok].rearrange("c p t -> p c t"), writes=[h2])

        def up_body(f):
            pg, pu = PS[2 * (f % 2)], PS[2 * (f % 2) + 1]
            c0 = (f % 2) * 128
            for (pb, wt) in ((pg, Wgb[2 * (f // 2)]), (pu, Wgb[2 * (f // 2) + 1])):
                for kc in range(8):
                    k.op("pe", lambda e, pb=pb, wt=wt, kc=kc: e.matmul(
                        pb[:, :], lhsT=wt[:, kc, c0:c0 + 128], rhs=h2[:, kc, :], start=(kc == 0), stop=(kc == 7)),
                        reads=[wt, h2], writes=[pb], inc=(kc == 7))
            sgt = sg[f % 2]
            k.op("act", lambda e: e.activation(out=sgt[:, :], in_=pg[:, :], func=AF.Silu), reads=[pg], writes=[sgt])
            k.op("dve", lambda e: e.tensor_tensor(out=actT[:, f, :], in0=pu[:, :], in1=sgt[:, :], op=ALU.mult),
                 reads=[pu, sgt], writes=[actT])

        for f in range(NF):
            up_body(f)

        def tile_body(i):
            ti = s * 4 + i
            xt = x1[ti % 2]
            y = yt[ti % 2]
            k.dma(xt[:, :], c.X1d[ti * 128:(ti + 1) * 128, :], writes=[xt])
            for n in range(2):
                pb = PS[4 + (2 * ti + n) % 4]
                for f in range(NF):
                    k.op("pe", lambda e, pb=pb, f=f, n=n: e.matmul(
                        pb[:, :], lhsT=actT[:, f, i * 128:(i + 1) * 128], rhs=Wdn[:, f, n * 512:(n + 1) * 512],
                        start=(f == 0), stop=(f == NF - 1)), reads=[actT, Wdn], writes=[pb], inc=(f == NF - 1))
                k.op("dve", lambda e, pb=pb, n=n: e.tensor_tensor(out=x2[:, n * 512:(n + 1) * 512], in0=pb[:, :],
                                                                  in1=g2b[:, n * 512:(n + 1) * 512], op=ALU.mult),
                     reads=[pb, g2b], writes=[x2])
            k.op("pool", lambda e: e.tensor_tensor(out=x2[:, :], in0=x2[:, :], in1=xt[:, :], op=ALU.add),
                 reads=[x2, xt], writes=[x2])
            k.op("dve", lambda e: e.scalar_tensor_tensor(out=junk[:, :], in0=x2[:, :], scalar=1.0, in1=x2[:, :],
                                                         op0=ALU.mult, op1=ALU.mult, accum_out=ss[:, 0:1]),
                 reads=[x2], writes=[junk, ss])
            k.op("dve", lambda e: e.tensor_scalar(out=ss[:, 1:2], in0=ss[:, 0:1], scalar1=1.0 / D, scalar2=EPS,
                                                  op0=ALU.mult, op1=ALU.add), reads=[ss], writes=[ss])
            k.op("pool", lambda e: e.tensor_tensor(out=rstd[:, 0:1], in0=ss[:, 1:2], in1=mhalf[:, 0:1], op=ALU.pow),
                 reads=[ss, mhalf], writes=[rstd])
            k.op("dve", lambda e: e.scalar_tensor_tensor(out=y[:, :], in0=x2[:, :], scalar=rstd[:, 0:1],
                                                         in1=fnb[:, :], op0=ALU.mult, op1=ALU.mult),
                 reads=[x2, rstd, fnb], writes=[y])
            k.dma(c.Y[ti * 128:(ti + 1) * 128, :], y[:, :], reads=[y])

        for i in range(4):
            tile_body(i)

    for s in range(9):
        st_body(s)
    k.barrier()
    k.phase_mem()
```

```python
import numpy as np
import concourse.bass as bass
import concourse.mybir as mybir
from concourse.bass_utils import run_bass_kernel_spmd

F32 = mybir.dt.float32
BF16 = mybir.dt.bfloat16
AF = mybir.ActivationFunctionType
ALU = mybir.AluOpType
AX = mybir.AxisListType

D = 1024
NT = 36
TT = NT * 128
NIN = 2832
DFF = 2816
NF = 22
EPS = 1e-6
KSC = 128.0 ** -0.5
NKEY = 38
import os as _os
NDS = 64
PRE_IN_ATT = int(_os.environ.get('PRE_IN_ATT', '1'))
STRICT = bool(int(_os.environ.get('KSTRICT', '1')))


class Tl:
    def __init__(s, h):
        s.h = h
        s.w = []
        s.r = []

    def __getitem__(s, i):
        return s.h[i]


class K:
    ENG = ("pe", "act", "dve", "pool", "sp")

    def __init__(s, nc):
        s.nc = nc
        s.ops = {e: [] for e in s.ENG}
        s.cnt = {e: 0 for e in s.ENG}
        s.seen = {e: {} for e in s.ENG}
        s.sem = {}
        s.dsems = [nc.alloc_semaphore(f"d_{i}") for i in range(NDS)]
        s.dcnt = [0] * NDS
        s.dptr = 0
        for e in s.ENG:
            s.sem[e] = nc.alloc_semaphore(f"s_{e}")
        s.sb_ptr = 0
        s.sb_phase = 0
        s.cap = None
        s.nalloc = 0

    def sb(s, shape, dt, name=None):
        esz = 4 if dt == F32 else 2
        n = 1
        for d_ in shape[1:]:
            n *= d_
        nbytes = (n * esz + 63) // 64 * 64
        off = s.sb_ptr
        s.sb_ptr += nbytes
        assert s.sb_ptr <= s.sb_top, (name, s.sb_ptr, s.sb_top)
        s.nalloc += 1
        h = s.nc.alloc_sbuf_tensor_at(f"{name or 't'}_{s.nalloc}", list(shape), dt, offset=off)
        return Tl(h)

    def phase_mem(s):
        s.sb_ptr = s.sb_phase

    def capture(s):
        s.cap = []
        return s.cap

    def capture_end(s):
        lst, s.cap = s.cap, None
        return lst

    def emit_interleaved(s, *lists):
        lists = [list(l) for l in lists if l]
        while lists:
            for l in list(lists):
                kind, args, kw = l.pop(0)
                (s.op if kind == "op" else s.dma)(*args, **kw)
                if not l:
                    lists.remove(l)

    def op(s, e, fn, reads=(), writes=(), inc=True):
        if s.cap is not None:
            s.cap.append(("op", (e, fn), dict(reads=list(reads), writes=list(writes), inc=inc)))
            return None
        deps = []
        for t in reads:
            for wt in t.w:
                if wt[0] != e or e != "pe":
                    deps.append(wt)
        for t in writes:
            for wt in t.w:
                if wt[0] != e or (STRICT and e != "pe"):
                    deps.append(wt)
            for rt in t.r:
                if rt[0] != e or (STRICT and e != "pe"):
                    deps.append(rt)
        need = {}
        for key, val in deps:
            if val > need.get(key, 0):
                need[key] = val
        waits = []
        for key, val in need.items():
            if s.seen[e].get(key, 0) >= val:
                continue
            s.seen[e][key] = val
            waits.append((key, val))
        tok = (e, s.cnt[e] + 1)
        if inc:
            s.cnt[e] += 1
        s.ops[e].append((waits, fn, inc, None))
        for t in writes:
            t.w = [tok]
            t.r = []
        for t in reads:
            if t not in writes:
                t.r.append(tok)
                if len(t.r) > 24:
                    t.r = s._compact(t.r)
        return tok

    @staticmethod
    def _compact(r):
        best = {}
        for key, val in r:
            if val > best.get(key, 0):
                best[key] = val
        return list(best.items())

    def dma(s, out, in_, reads=(), writes=(), q="sp", slow=False):
        if s.cap is not None:
            s.cap.append(("dma", (out, in_), dict(reads=list(reads), writes=list(writes), q=q, slow=slow)))
            return None
        deps = []
        for t in reads:
            deps.extend(t.w)
        for t in writes:
            for wt in t.w:
                if not isinstance(wt[0], int):
                    deps.append(wt)
            deps.extend(t.r)
        need = {}
        for key, val in deps:
            if val > need.get(key, 0):
                need[key] = val
        waits = []
        for key, val in need.items():
            if s.seen[q].get(key, 0) >= val:
                continue
            s.seen[q][key] = val
            waits.append((key, val))
        si = s.dptr
        s.dptr = (s.dptr + 1) % len(s.dsems)
        if s.dcnt[si] > s.seen[q].get(si, 0):
            s.seen[q][si] = s.dcnt[si]
            waits.append((si, s.dcnt[si]))
        s.dcnt[si] += 16
        tok = (si, s.dcnt[si])
        s.ops[q].append((waits, (out, in_, slow), False, si))
        for t in writes:
            if t.w and all(isinstance(wt[0], int) for wt in t.w):
                t.w = t.w + [tok]
            else:
                t.w = [tok]
            t.r = []
        for t in reads:
            t.r.append(tok)
        return tok

    def barrier(s):
        waits = []
        for e in s.ENG:
            if e != "sp" and s.cnt[e] > s.seen["sp"].get(e, 0):
                s.seen["sp"][e] = s.cnt[e]
                waits.append((e, s.cnt[e]))
        for si in range(len(s.dsems)):
            if s.dcnt[si] > s.seen["sp"].get(si, 0):
                s.seen["sp"][si] = s.dcnt[si]
                waits.append((si, s.dcnt[si]))
        s.cnt["sp"] += 1
        v = s.cnt["sp"]
        s.ops["sp"].append((waits, "inc", True, None))
        for e in s.ENG:
            if e != "sp":
                s.seen[e]["sp"] = v
                s.ops[e].append(([("sp", v)], None, False, None))
                for si in range(len(s.dsems)):
                    s.seen[e][si] = s.dcnt[si]
                for e2 in s.ENG:
                    s.seen[e][e2] = max(s.seen[e].get(e2, 0), s.cnt[e2])

    def semof(s, key):
        return s.dsems[key] if isinstance(key, int) else s.sem[key]

    def emit(s, e, eng):
        for waits, fn, inc, si in s.ops[e]:
            for key, val in waits:
                eng.wait_ge(s.semof(key), val)
            if fn is None:
                continue
            if fn == "inc":
                eng.sem_inc(s.sem[e], 1)
                continue
            if si is not None:
                out, in_, slow = fn
                if slow:
                    ins = eng.dma_start(out=out, in_=in_, allow_slow_non_contiguous=True)
                else:
                    ins = eng.dma_start(out=out, in_=in_)
                ins.then_inc(s.dsems[si], 16)
                continue
            ins = fn(eng)
            if inc:
                ins.then_inc(s.sem[e], 1)


def build_nc(debug=False, stop=99):
    import os
    stop = int(os.environ.get('KSTOP', stop))
    nc = bass.Bass("TRN2", target_bir_lowering=False)
    k = K(nc)
    k.sb_ptr = (nc.sbuf_base + 63) // 64 * 64
    k.sb_top = nc.sbuf_top

    def din(name, shape, dt=F32):
        return nc.dram_tensor(name, list(shape), dt, kind="ExternalInput").ap()

    def dout(name, shape, dt=F32):
        return nc.dram_tensor(name, list(shape), dt, kind="ExternalOutput").ap()

    def dscr(name, shape, dt=BF16):
        return nc.dram_tensor(name, list(shape), dt, kind="ExternalOutput" if debug else "Internal").ap()

    X = din("x", [TT, D])
    CK = din("cache_k", [256, 128])
    CV = din("cache_v", [256, 128])
    SC = din("state_C", [2, 4, 128, 128])
    SN = din("state_n", [2, 4, 128])
    SM = din("state_m", [2, 4])
    COND = din("cond", [2, D])
    WADA = din("w_ada", [D, 6 * D])
    BADA = din("b_ada", [6 * D])
    N1W = din("norm1_w", [D])
    WIN = din("w_in", [D, NIN])
    GB = din("gate_bias", [4, 4])
    QNW = din("q_norm_w", [64])
    KNW = din("k_norm_w", [64])
    MNW = din("mlstm_norm_w", [512])
    WOUT = din("w_out", [D, D])
    N2W = din("norm2_w", [D])
    WGU = din("w_gu", [D, 2 * DFF])
    WDN = din("w_down", [DFF, D])
    FNW = din("final_norm_w", [D])
    ROPE = din("rope", [128, 32, 2, 64])
    IDENT = din("ident", [128, 128])
    MASKS = din("masks", [128, 2, 512])

    Y = dout("y", [TT, D])
    NK = dout("nk", [512, 128])
    NV = dout("nv", [512, 128])
    NC_ = dout("nC", [2, 2, 4, 128, 128])
    NN = dout("nn", [2, 2, 4, 128])
    NM = dout("nm", [2, 2, 4])

    MODS = dscr("mods", [2, 6 * D], F32)
    QTd = dscr("QTd", [4, 128, TT])
    MQTd = dscr("MQTd", [4, 128, TT])
    MKTd = dscr("MKTd", [4, 128, TT])
    MKd = dscr("MKd", [TT, 512])
    MVd = dscr("MVd", [TT, 520])
    MOd = dscr("MOd", [TT, 512])
    GATd = dscr("GATd", [4, 4, TT], F32)
    AOTd = dscr("AOTd", [4, 128, TT])
    HMTd = dscr("HMTd", [4, 128, TT])
    H2Td = dscr("H2Td", [8, 128, TT])
    X1d = dscr("X1d", [TT, D], F32)

    PSALL = nc.alloc_psum_tensor("psall", [128, 4096], F32)
    PS = [Tl(PSALL[:, i * 512:(i + 1) * 512]) for i in range(8)]

    ident = k.sb([128, 128], F32, "ident")
    ones4 = k.sb([4, 128], F32, "ones4")
    eye4 = k.sb([4, 4], F32, "eye4")
    modc = k.sb([128, 2, 6, 8], F32, "modc")
    G1 = k.sb([128, 2, 8], F32, "G1")
    G2 = k.sb([128, 2, 8], F32, "G2")
    n1c = k.sb([128, 8], F32, "n1c")
    n2c = k.sb([128, 8], F32, "n2c")
    k.mhalf = k.sb([128, 8], F32, "mhalf")
    k.op("dve", lambda e: e.memset(k.mhalf[:, :], -0.5), writes=[k.mhalf])
    sb_phase0 = k.sb_ptr
    KT2 = k.sb([128, 2, TT + 256], BF16, "KT2")
    VA = k.sb([128, NKEY, 2, 192], BF16, "VA")
    sb_phase0_a = k.sb_ptr
    Win = k.sb([128, 8, NIN], BF16, "Win")
    Wg = k.sb([128, 8, 128], BF16, "Wg")
    stgw = [k.sb([128, NIN // 2], F32, f"stgw{i}") for i in range(2)]
    k.sb_phase = k.sb_ptr
    k.op("pool", lambda e: e.memset(Wg[:, :, :], 0.0), writes=[Wg])

    k.dma(ident[:, :], IDENT[:, :], writes=[ident])
    k.op("dve", lambda e: e.memset(ones4[:, :], 1.0), writes=[ones4])
    k.dma(eye4[:, :], IDENT[0:4, 0:4], writes=[eye4])

    condT = k.sb([128, 8, 2], F32, "condT")
    sil = k.sb([128, 8, 2], F32, "sil")
    tmpc = k.sb([128, 8, 2], F32, "tmpc")
    mods_sb = k.sb([2, 6 * D], F32, "mods_sb")
    bada = k.sb([2, 6 * D], F32, "bada")
    wa = [k.sb([128, 512], F32, f"wa{i}") for i in range(4)]
    for j in range(2):
        k.dma(condT[:, :, j], COND[j].rearrange("(c p) -> p c", p=128), writes=[condT], slow=True)
    for j in range(2):
        k.dma(bada[j:j + 1, :], BADA.rearrange("(o n) -> o n", o=1), writes=[bada])
    k.op("act", lambda e: e.activation(out=tmpc[:, :, :], in_=condT[:, :, :], func=AF.Exp, scale=-1.0),
         reads=[condT], writes=[tmpc])
    k.op("dve", lambda e: e.tensor_scalar_add(out=tmpc[:, :, :], in0=tmpc[:, :, :], scalar1=1.0),
         reads=[tmpc], writes=[tmpc])
    k.op("dve", lambda e: e.reciprocal(out=tmpc[:, :, :], in_=tmpc[:, :, :]), reads=[tmpc], writes=[tmpc])
    k.op("dve", lambda e: e.tensor_tensor(out=sil[:, :, :], in0=condT[:, :, :], in1=tmpc[:, :, :], op=ALU.mult),
         reads=[condT, tmpc], writes=[sil])
    win_steps = []
    HW = NIN // 2
    for kc in range(8):
        for hf in range(2):
            def step(kc=kc, hf=hf):
                st = stgw[hf]
                k.dma(st[:, :], WIN[kc * 128:(kc + 1) * 128, hf * HW:(hf + 1) * HW], writes=[st])
                k.op("pool", lambda e: e.tensor_copy(out=Win[:, kc, hf * HW:(hf + 1) * HW], in_=st[:, :]),
                     reads=[st], writes=[Win])
                if hf == 1:
                    k.op("pool", lambda e: e.tensor_copy(
                        out=Wg[:, kc, :].rearrange("p (j w) -> p j w", w=32)[:, :, 0:4],
                        in_=st[:, HW - 16:HW].rearrange("p (j w) -> p j w", w=4)), reads=[st], writes=[Wg])
            win_steps.append(step)
    it = 0
    for n in range(12):
        pb = PS[n % 2]
        for kc in range(8):
            w_t = wa[it % 4]
            if it % 6 == 0 and win_steps:
                win_steps.pop(0)()
            it += 1
            k.dma(w_t[:, :], WADA[kc * 128:(kc + 1) * 128, n * 512:(n + 1) * 512], writes=[w_t])
            k.op("pe", lambda e, w_t=w_t, kc=kc, pb=pb: e.matmul(pb[0:2, :], lhsT=sil[:, kc, :], rhs=w_t[:, :],
                                                                start=(kc == 0), stop=(kc == 7)),
                 reads=[w_t, sil], writes=[pb], inc=True)
        k.op("dve", lambda e, n=n, pb=pb: e.tensor_tensor(out=mods_sb[:, n * 512:(n + 1) * 512], in0=pb[0:2, :],
                                                          in1=bada[:, n * 512:(n + 1) * 512], op=ALU.add),
             reads=[pb, bada], writes=[mods_sb])
    while win_steps:
        win_steps.pop(0)()
    k.dma(MODS[:, :], mods_sb[:, :], reads=[mods_sb])
    k.barrier()
    for j in range(2):
        for s6 in range(6):
            k.dma(modc[:, j, s6, :], MODS[j, s6 * D:(s6 + 1) * D].rearrange("(c p) -> p c", p=128),
                  writes=[modc], slow=True)
    k.dma(n1c[:, :], N1W.rearrange("(c p) -> p c", p=128), writes=[n1c], slow=True)
    k.dma(n2c[:, :], N2W.rearrange("(c p) -> p c", p=128), writes=[n2c], slow=True)
    for j in range(2):
        k.op("dve", lambda e, j=j: e.scalar_tensor_tensor(out=G1[:, j, :], in0=modc[:, j, 1, :], scalar=1.0,
                                                          in1=n1c[:, :], op0=ALU.add, op1=ALU.mult),
             reads=[modc, n1c], writes=[G1])
        k.op("dve", lambda e, j=j: e.scalar_tensor_tensor(out=G2[:, j, :], in0=modc[:, j, 4, :], scalar=1.0,
                                                          in1=n2c[:, :], op0=ALU.add, op1=ALU.mult),
             reads=[modc, n2c], writes=[G2])
    k.barrier()
    k.phase_mem()
    ctx = dict(locals())
    if stop >= 1:
        phase_a(ctx)
    if stop >= 2:
        phase_b1(ctx)
    if stop >= 3:
        phase_b2(ctx)
    if stop >= 4:
        phase_c1(ctx)
    if stop >= 5:
        phase_c2(ctx)
    k.barrier()

    with nc.allow_low_precision(reason="bf16 matmul operands by design"), nc.Block() as block:
        names = {"pe": "tensor", "act": "scalar", "dve": "vector", "pool": "gpsimd", "sp": "sync"}
        for e in K.ENG:
            getattr(block, names[e])(lambda eng, e=e: k.emit(e, eng))
    return nc


class NS:
    def __init__(s, d):
        s.__dict__.update(d)


def rstd_from_ss(k, ss, out, n_inv, width):
    k.op("act", lambda e: e.activation(out=out[:, 0:width], in_=ss[:, 0:width], func=AF.Ln, scale=n_inv, bias=EPS),
         reads=[ss], writes=[out])
    k.op("act", lambda e: e.activation(out=out[:, 0:width], in_=out[:, 0:width], func=AF.Exp, scale=-0.5),
         reads=[out], writes=[out])


def norm_to_hT(k, c, xt, hT, col0, Gc, SHc, bufs, pA, pB, defer=None):
    junk, ss, rstd, xs = bufs
    k.op("dve", lambda e: e.scalar_tensor_tensor(out=junk[:, :], in0=xt[:, :], scalar=1.0, in1=xt[:, :],
                                                 op0=ALU.mult, op1=ALU.mult, accum_out=ss[:, 0:1]),
         reads=[xt], writes=[junk, ss])
    rstd_from_ss(k, ss, rstd, 1.0 / D, 1)
    k.op("pool", lambda e: e.tensor_scalar(out=xs[:, :], in0=xt[:, :], scalar1=rstd[:, 0:1], scalar2=1.0,
                                           op0=ALU.mult, op1=ALU.mult),
         reads=[xt, rstd], writes=[xs])

    def pe_part():
        for half, pb in ((0, pA), (1, pB)):
            for cc in range(4):
                ch = half * 4 + cc
                k.op("pe", lambda e, ch=ch, cc=cc, pb=pb: e.transpose(pb[:, cc * 128:(cc + 1) * 128],
                                                                      xs[:, ch * 128:(ch + 1) * 128], c.ident[:, :]),
                     reads=[xs, c.ident], writes=[pb], inc=(cc == 3))
            for cc in range(4):
                ch = half * 4 + cc
                k.op("act", lambda e, ch=ch, cc=cc, pb=pb: e.activation(
                    out=hT[:, ch, col0:col0 + 128], in_=pb[:, cc * 128:(cc + 1) * 128], func=AF.Identity,
                    scale=Gc(ch), bias=SHc(ch)), reads=[pb, c.G1, c.G2, c.modc], writes=[hT])

    if defer is None:
        pe_part()
    else:
        defer.append(pe_part)


def phase_a(ctx):
    c = NS(ctx)
    k = c.k
    PS = c.PS
    KT2, VA, Win, Wg = c.KT2, c.VA, c.Win, c.Wg
    k.op("pool", lambda e: e.memset(VA[:, :, :, :], 1.0), writes=[VA])
    xb = [k.sb([128, D], F32, f"xb{i}") for i in range(3)]
    junk = k.sb([128, D], BF16, "junk")
    xs = [k.sb([128, D], F32, f"xs{i}") for i in range(3)]
    ss = k.sb([128, 8], F32, "ss")
    rstd = k.sb([128, 8], F32, "rstd")
    hT = [k.sb([128, 8, 512], BF16, f"hT{i}") for i in range(2)]
    ropet = [k.sb([128, 4, 2, 64], F32, f"rope{i}") for i in range(2)]
    wq_bc = k.sb([128, 64], F32, "wq_bc")
    wk_bc = k.sb([128, 64], F32, "wk_bc")
    gbcol = k.sb([128, 1], F32, "gbcol")
    qf = k.sb([128, 512], F32, "qf")
    sq = k.sb([128, 512], F32, "sq")
    qn2 = [k.sb([128, 512], F32, f"qn{i}") for i in range(2)]
    t1 = k.sb([128, 512], F32, "t1")
    t2 = k.sb([128, 512], F32, "t2")
    qr2 = [k.sb([128, 512], F32, f"qr{i}") for i in range(2)]
    kvf = [k.sb([128, 256], F32, f"kvf{i}") for i in range(2)]
    kn = [k.sb([128, 128], F32, f"kn{i}") for i in range(2)]
    kt1 = k.sb([128, 128], F32, "kt1")
    kt2 = k.sb([128, 128], F32, "kt2")
    kr = k.sb([128, 128], F32, "kr")
    kdup2 = [k.sb([128, 2, 2, 64], F32, f"kdup{i}") for i in range(2)]
    ssq = k.sb([128, 8], F32, "ssq")
    rsq = k.sb([128, 8], F32, "rsq")
    ssk = k.sb([128, 8], F32, "ssk")
    rsk = k.sb([128, 8], F32, "rsk")
    mot = k.sb([128, 512], F32, "mot")
    gsb = k.sb([128, 512], F32, "gsb")
    gtmp = k.sb([128, 512], F32, "gtmp")
    cstg = k.sb([128, 2, 128], F32, "cstg")
    QTs = [k.sb([128, 4, 512], BF16, f"QTs{i}") for i in range(1)]
    MKs = [k.sb([128, 4, 512], BF16, f"MKs{i}") for i in range(1)]
    MVs = [k.sb([128, 4, 4, 130], BF16, f"MVs{i}") for i in range(1)]
    MOs = [k.sb([128, 4, 512], BF16, f"MOs{i}") for i in range(1)]
    MQTs = [k.sb([128, 4, 512], BF16, f"MQTs{i}") for i in range(1)]
    MKTs = [k.sb([128, 4, 512], BF16, f"MKTs{i}") for i in range(1)]

    k.dma(wq_bc[:, :], c.QNW.partition_broadcast(128), writes=[wq_bc])
    k.dma(wk_bc[:, :], c.KNW.partition_broadcast(128), writes=[wk_bc])
    k.op("dve", lambda e: e.tensor_scalar_mul(out=wq_bc[:, :], in0=wq_bc[:, :], scalar1=0.125),
         reads=[wq_bc], writes=[wq_bc])
    k.op("dve", lambda e: e.memset(gbcol[:, :], 0.0), writes=[gbcol])
    for j in range(4):
        k.dma(gbcol[32 * j:32 * j + 4, 0:1], c.GB[j].rearrange("(h o) -> h o", o=1), writes=[gbcol], slow=True)
    k.op("pool", lambda e: e.memset(MVs[0][:, :, :, :], 1.0), writes=[MVs[0]])
    kdup = kdup2[0]
    for blk in range(2):
        k.dma(cstg[:, 0, :], c.CK[blk * 128:(blk + 1) * 128, :], writes=[cstg])
        k.dma(cstg[:, 1, :], c.CV[blk * 128:(blk + 1) * 128, :], writes=[cstg])
        k.op("dve", lambda e: e.tensor_copy(out=kdup[:, :, :, :],
                                            in_=cstg[:, 0, :].rearrange("p (g o d) -> p g o d", g=2, o=1)
                                            .to_broadcast([128, 2, 2, 64])), reads=[cstg], writes=[kdup])
        pk = PS[2]
        for g in range(2):
            k.op("pe", lambda e, g=g: e.transpose(pk[:, g * 128:(g + 1) * 128],
                                                  kdup[:, g, :, :].rearrange("p a d -> p (a d)"), c.ident[:, :]),
                 reads=[kdup, c.ident], writes=[pk], inc=(g == 1))
        col = TT + blk * 128
        k.op("act", lambda e, col=col: e.activation(out=KT2[:, :, col:col + 128],
                                                    in_=pk[:, 0:256].rearrange("p (g t) -> p g t", g=2),
                                                    func=AF.Copy), reads=[pk], writes=[KT2])
        k.op("dve", lambda e, blk=blk: e.tensor_copy(out=VA[:, 36 + blk, :, 64:128],
                                                     in_=cstg[:, 1, :].rearrange("p (g d) -> p g d", g=2)),
             reads=[cstg], writes=[VA])

    class Deferred(list):
        cur = 0

        def append(self, fn):
            list.append(self, (self.cur, fn))

    deferred = Deferred()

    def flush(upto=10 ** 9):
        while deferred and deferred[0][0] <= upto:
            deferred.pop(0)[1]()

    def tile_a(s, i):
        cj = 0 if s < 8 else 1
        ti = s * 4 + i
        xt = xb[ti % 3]
        k.dma(xt[:, :], c.X[ti * 128:(ti + 1) * 128, :], writes=[xt])
        norm_to_hT(k, c, xt, hT[s % 2], i * 128,
                   lambda ch, cj=cj: c.G1[:, cj, ch:ch + 1], lambda ch, cj=cj: c.modc[:, cj, 0, ch:ch + 1],
                   (junk, ss, rstd, xs[ti % 3]), PS[0], PS[1], defer=deferred)

    def st_body(s):
        cj = 0 if s < 8 else 1
        smp = s < 8
        h = hT[s % 2]
        if smp:
            rp = ropet[s % 2]
            k.dma(rp[:, :, :, :], c.ROPE[:, s * 4:(s + 1) * 4, :, :], writes=[rp])
        QT_, MK_, MV_, MO_, MQT_, MKT_ = QTs[0], MKs[0], MVs[0], MOs[0], MQTs[0], MKTs[0]
        if s == 0:
            for i in range(4):
                tile_a(0, i)
                flush()

        def tile_b(i):
            ti = s * 4 + i
            tsl = slice(i * 128, (i + 1) * 128)
            pq, pkv, pmk, pmv, pmo = PS[2], PS[3], PS[4], PS[5], PS[6]
            for (pb, c0, c1) in ((pmk, 1280, 1792), (pmv, 1792, 2304), (pmo, 2304, 2816), (pq, 0, 512),
                                 (pkv, 512, 768)):
                for kc in range(8):
                    k.op("pe", lambda e, pb=pb, c0=c0, c1=c1, kc=kc, tsl=tsl: e.matmul(
                        pb[:, 0:c1 - c0], lhsT=h[:, kc, tsl], rhs=Win[:, kc, c0:c1], start=(kc == 0), stop=(kc == 7)),
                        reads=[h, Win], writes=[pb], inc=(kc == 7))
            flush(ti - 2)
            deferred.cur = ti
            qn, qr, kdup = qn2[ti % 2], qr2[ti % 2], kdup2[ti % 2]
            k.capture()
            k.op("act", lambda e, i=i: e.activation(out=MK_[:, i, :], in_=pmk[:, :], func=AF.Copy, scale=KSC),
                 reads=[pmk], writes=[MK_])
            k.op("dve", lambda e, i=i: e.tensor_copy(out=MV_[:, i, :, 0:128],
                                                     in_=pmv[:, :].rearrange("p (h d) -> p h d", h=4)),
                 reads=[pmv], writes=[MV_])
            k.op("act", lambda e: e.activation(out=mot[:, :], in_=pmo[:, :], func=AF.Exp, scale=-1.0),
                 reads=[pmo], writes=[mot])
            k.op("act", lambda e: e.activation(out=mot[:, :], in_=mot[:, :], func=AF.Ln, bias=1.0),
                 reads=[mot], writes=[mot])
            k.op("act", lambda e, i=i: e.activation(out=MO_[:, i, :], in_=mot[:, :], func=AF.Exp, scale=-1.0),
                 reads=[mot], writes=[MO_])
            Lm = k.capture_end()
            kv = kvf[ti % 2]
            kk = kn[ti % 2]
            k.capture()
            k.op("act", lambda e: e.activation(out=qf[:, :], in_=pq[:, :], func=AF.Copy), reads=[pq], writes=[qf])
            k.op("dve", lambda e: e.tensor_tensor(out=sq[:, :], in0=qf[:, :], in1=qf[:, :], op=ALU.mult),
                 reads=[qf], writes=[sq])
            k.op("dve", lambda e: e.tensor_reduce(out=ssq[:, 0:8], in_=sq[:, :].rearrange("p (h d) -> p h d", d=64),
                                                  axis=AX.X, op=ALU.add), reads=[sq], writes=[ssq])
            rstd_from_ss(k, ssq, rsq, 1.0 / 64, 8)
            k.op("dve", lambda e: e.tensor_tensor(out=qn[:, :].rearrange("p (h d) -> p h d", d=64),
                                                  in0=qf[:, :].rearrange("p (h d) -> p h d", d=64),
                                                  in1=rsq[:, 0:8].unsqueeze(2).to_broadcast([128, 8, 64]),
                                                  op=ALU.mult), reads=[qf, rsq], writes=[qn])
            k.op("dve", lambda e: e.tensor_tensor(out=qn[:, :].rearrange("p (h d) -> p h d", d=64),
                                                  in0=qn[:, :].rearrange("p (h d) -> p h d", d=64),
                                                  in1=wq_bc[:, :].unsqueeze(1).to_broadcast([128, 8, 64]),
                                                  op=ALU.mult), reads=[qn, wq_bc], writes=[qn])
            if smp:
                k.op("pool", lambda e, i=i: e.tensor_tensor(
                    out=t1[:, :].rearrange("p (h d) -> p h d", d=64),
                    in0=qn[:, :].rearrange("p (h d) -> p h d", d=64),
                    in1=rp[:, i, 0, :].unsqueeze(1).to_broadcast([128, 8, 64]), op=ALU.mult),
                    reads=[qn, rp], writes=[t1])
                for jj in range(2):
                    k.op("pool", lambda e, i=i, jj=jj: e.tensor_tensor(
                        out=t2[:, :].rearrange("p (h a j w) -> p h a j w", h=8, a=2, j=2)[:, :, :, jj, :],
                        in0=qn[:, :].rearrange("p (h a j w) -> p h a j w", h=8, a=2, j=2)[:, :, :, 1 - jj, :],
                        in1=rp[:, i, 1, :].rearrange("p (a j w) -> p a j w", a=2, j=2)[:, :, jj, :].unsqueeze(1)
                        .to_broadcast([128, 8, 2, 16]), op=ALU.mult),
                        reads=[qn, rp], writes=[t2])
                k.op("pool", lambda e: e.tensor_tensor(out=qr[:, :], in0=t1[:, :], in1=t2[:, :], op=ALU.add),
                     reads=[t1, t2], writes=[qr])
                qsrc = qr
            else:
                qsrc = qn
            def q_tr(qsrc=qsrc, tsl=tsl):
                pt = PS[7]
                for j in range(4):
                    k.op("pe", lambda e, j=j: e.transpose(pt[:, j * 128:(j + 1) * 128],
                                                          qsrc[:, j * 128:(j + 1) * 128], c.ident[:, :]),
                         reads=[qsrc, c.ident], writes=[pt], inc=(j == 3))
                k.op("act", lambda e: e.activation(out=QT_[:, :, tsl],
                                                   in_=pt[:, :].rearrange("p (j t) -> p j t", j=4),
                                                   func=AF.Copy), reads=[pt], writes=[QT_])
            deferred.append(q_tr)
            Lq = k.capture_end()
            k.capture()
            k.op("act", lambda e, kv=kv: e.activation(out=kv[:, :], in_=pkv[:, 0:256], func=AF.Copy),
                 reads=[pkv], writes=[kv])
            k.op("dve", lambda e, kv=kv: e.tensor_tensor(out=kt1[:, :], in0=kv[:, 0:128], in1=kv[:, 0:128],
                                                         op=ALU.mult), reads=[kv], writes=[kt1])
            k.op("dve", lambda e: e.tensor_reduce(out=ssk[:, 0:2], in_=kt1[:, :].rearrange("p (h d) -> p h d", d=64),
                                                  axis=AX.X, op=ALU.add), reads=[kt1], writes=[ssk])
            rstd_from_ss(k, ssk, rsk, 1.0 / 64, 2)
            k.op("dve", lambda e, kv=kv, kk=kk: e.tensor_tensor(
                out=kk[:, :].rearrange("p (h d) -> p h d", d=64),
                in0=kv[:, 0:128].rearrange("p (h d) -> p h d", d=64),
                in1=rsk[:, 0:2].unsqueeze(2).to_broadcast([128, 2, 64]), op=ALU.mult),
                reads=[kv, rsk], writes=[kk])
            k.op("dve", lambda e, kk=kk: e.tensor_tensor(
                out=kk[:, :].rearrange("p (h d) -> p h d", d=64),
                in0=kk[:, :].rearrange("p (h d) -> p h d", d=64),
                in1=wk_bc[:, :].unsqueeze(1).to_broadcast([128, 2, 64]), op=ALU.mult),
                reads=[kk, wk_bc], writes=[kk])
            if smp:
                k.op("pool", lambda e, i=i, kk=kk: e.tensor_tensor(
                    out=kt1[:, :].rearrange("p (h d) -> p h d", d=64),
                    in0=kk[:, :].rearrange("p (h d) -> p h d", d=64),
                    in1=rp[:, i, 0, :].unsqueeze(1).to_broadcast([128, 2, 64]), op=ALU.mult),
                    reads=[kk, rp], writes=[kt1])
                for jj in range(2):
                    k.op("pool", lambda e, i=i, kk=kk, jj=jj: e.tensor_tensor(
                        out=kt2[:, :].rearrange("p (h a j w) -> p h a j w", h=2, a=2, j=2)[:, :, :, jj, :],
                        in0=kk[:, :].rearrange("p (h a j w) -> p h a j w", h=2, a=2, j=2)[:, :, :, 1 - jj, :],
                        in1=rp[:, i, 1, :].rearrange("p (a j w) -> p a j w", a=2, j=2)[:, :, jj, :].unsqueeze(1)
                        .to_broadcast([128, 2, 2, 16]), op=ALU.mult),
                        reads=[kk, rp], writes=[kt2])
                k.op("pool", lambda e: e.tensor_tensor(out=kr[:, :], in0=kt1[:, :], in1=kt2[:, :], op=ALU.add),
                     reads=[kt1, kt2], writes=[kr])
                ksrc = kr
            else:
                ksrc = kk
                pt0 = (ti - 32) * 128
                k.dma(c.NK[pt0:pt0 + 128, :], kk[:, :], reads=[kk])
                k.dma(c.NV[pt0:pt0 + 128, :], kv[:, 128:256], reads=[kv])
            k.op("dve", lambda e, ksrc=ksrc: e.tensor_copy(
                out=kdup[:, :, :, :], in_=ksrc[:, :].rearrange("p (g o d) -> p g o d", g=2, o=1)
                .to_broadcast([128, 2, 2, 64])), reads=[ksrc], writes=[kdup])
            def k_tr(ti=ti):
                pk = PS[7]
                for g in range(2):
                    k.op("pe", lambda e, g=g: e.transpose(pk[:, g * 128:(g + 1) * 128],
                                                          kdup[:, g, :, :].rearrange("p a d -> p (a d)"),
                                                          c.ident[:, :]),
                         reads=[kdup, c.ident], writes=[pk], inc=(g == 1))
                k.op("act", lambda e: e.activation(out=KT2[:, :, ti * 128:(ti + 1) * 128],
                                                   in_=pk[:, 0:256].rearrange("p (g t) -> p g t", g=2),
                                                   func=AF.Copy), reads=[pk], writes=[KT2])
            deferred.append(k_tr)
            k.op("dve", lambda e, ti=ti, kv=kv: e.tensor_copy(out=VA[:, ti, :, 64:128],
                                                              in_=kv[:, 128:256].rearrange("p (g d) -> p g d", g=2)),
                 reads=[kv], writes=[VA])
            Lk = k.capture_end()
            k.emit_interleaved(Lm, Lq, Lk)

        for i in range(4):
            tile_b(i)
            if s + 1 < 9:
                tile_a(s + 1, i)
        for hh in range(8):
            pb = PS[2 + hh % 4]
            c0 = 768 + hh * 128
            for kc in range(8):
                k.op("pe", lambda e, pb=pb, c0=c0, kc=kc: e.matmul(
                    pb[:, :], lhsT=Win[:, kc, c0:c0 + 128], rhs=h[:, kc, :], start=(kc == 0), stop=(kc == 7)),
                    reads=[h, Win], writes=[pb], inc=(kc == 7))
            if hh == 3:
                flush()
            if hh < 4:
                k.op("dve", lambda e, pb=pb, hh=hh: e.tensor_copy(out=MQT_[:, hh, :], in_=pb[:, :]),
                     reads=[pb], writes=[MQT_])
            else:
                k.op("act", lambda e, pb=pb, hh=hh: e.activation(out=MKT_[:, hh - 4, :], in_=pb[:, :], func=AF.Copy,
                                                                 scale=KSC), reads=[pb], writes=[MKT_])
        pg = PS[6]
        for kc in range(8):
            k.op("pe", lambda e, kc=kc: e.matmul(pg[:, :], lhsT=Wg[:, kc, :], rhs=h[:, kc, :], start=(kc == 0),
                                                 stop=(kc == 7)), reads=[h, Wg], writes=[pg], inc=(kc == 7))
        k.op("act", lambda e: e.activation(out=gsb[:, :], in_=pg[:, :], func=AF.Identity, bias=gbcol[:, 0:1]),
             reads=[pg, gbcol], writes=[gsb])
        for r0 in (32, 96):
            k.op("act", lambda e, r0=r0: e.activation(out=gtmp[r0:r0 + 4, :], in_=gsb[r0:r0 + 4, :], func=AF.Exp,
                                                      scale=-1.0), reads=[gsb], writes=[gtmp])
            k.op("act", lambda e, r0=r0: e.activation(out=gtmp[r0:r0 + 4, :], in_=gtmp[r0:r0 + 4, :], func=AF.Ln,
                                                      bias=1.0), reads=[gtmp], writes=[gtmp])
            k.op("dve", lambda e, r0=r0: e.tensor_scalar_mul(out=gsb[r0:r0 + 4, :], in0=gtmp[r0:r0 + 4, :],
                                                             scalar1=-1.0), reads=[gtmp], writes=[gsb])
        flush()
        tok = slice(s * 512, (s + 1) * 512)
        k.dma(c.QTd[:, :, tok].rearrange("c p t -> p c t"), QT_[:, :, :], reads=[QT_])
        k.dma(c.MQTd[:, :, tok].rearrange("c p t -> p c t"), MQT_[:, :, :], reads=[MQT_])
        k.dma(c.MKTd[:, :, tok].rearrange("c p t -> p c t"), MKT_[:, :, :], reads=[MKT_])
        k.dma(c.MKd[tok, :].rearrange("(i p) f -> p i f", p=128), MK_[:, :, :], reads=[MK_])
        k.dma(c.MVd[tok, :].rearrange("(i p) f -> p i f", p=128), MV_[:, :, :, :].rearrange("p i h d -> p i (h d)"),
              reads=[MV_])
        k.dma(c.MOd[tok, :].rearrange("(i p) f -> p i f", p=128), MO_[:, :, :], reads=[MO_])
        for j in range(4):
            k.dma(c.GATd[j, :, tok], gsb[32 * j:32 * j + 4, :], reads=[gsb])
    for s in range(9):
        st_body(s)
    k.barrier()
    k.sb_phase = c.sb_phase0_a
    k.phase_mem()


def _consts():
    f = np.float32
    tok = np.arange(4096)
    row = (tok // 64).astype(f)
    col = (tok % 64).astype(f)
    inv = (f(10000.0) ** (-np.arange(0, 32, 2, dtype=f) / f(32))).astype(f)
    ang = np.stack([row[:, None] * inv[None, :], col[:, None] * inv[None, :]], axis=1).astype(f)
    cs, sn = np.cos(ang).astype(f), np.sin(ang).astype(f)
    Cf = np.stack([cs, cs], axis=2)
    Ss = np.stack([-sn, sn], axis=2)
    tab = np.stack([Cf.reshape(4096, 64), Ss.reshape(4096, 64)], axis=1)
    rope = np.ascontiguousarray(tab.reshape(32, 128, 2, 64).transpose(1, 0, 2, 3))
    ident = np.eye(128, dtype=f)
    sidx = np.arange(128)[:, None]
    lidx = np.arange(128)[None, :]
    mf = (lidx >= sidx).astype(f)
    mb = (lidx <= sidx).astype(f)
    masks = np.stack([np.tile(mf, (1, 4)), np.tile(mb, (1, 4))], axis=1)
    return rope, ident, np.ascontiguousarray(masks)


def make_in_maps(inp):
    rope, ident, masks = _consts()
    f = np.float32
    c = lambda a: np.ascontiguousarray(np.asarray(a), dtype=f)
    maps = []
    for b in range(8):
        x = np.concatenate([inp["x_sample"][b], inp["x_prompt"][2 * b], inp["x_prompt"][2 * b + 1]], axis=0)
        m = {
            "x": c(x),
            "cache_k": c(np.asarray(inp["cache_k"])[b, 0].reshape(256, 128)),
            "cache_v": c(np.asarray(inp["cache_v"])[b, 0].reshape(256, 128)),
            "state_C": c(np.asarray(inp["state_C"])[b, 0]),
            "state_n": c(np.asarray(inp["state_n"])[b, 0]),
            "state_m": c(np.asarray(inp["state_m"])[b, 0]),
            "cond": c(np.stack([np.asarray(inp["c"])[b], np.asarray(inp["c_ctx"])], axis=0)),
            "w_ada": c(np.asarray(inp["w_ada"])[0]), "b_ada": c(np.asarray(inp["b_ada"])[0]),
            "norm1_w": c(np.asarray(inp["norm1_w"])[0]), "w_in": c(np.asarray(inp["w_in"])[0]),
            "gate_bias": c(np.asarray(inp["gate_bias"])[0]), "q_norm_w": c(np.asarray(inp["q_norm_w"])[0]),
            "k_norm_w": c(np.asarray(inp["k_norm_w"])[0]), "mlstm_norm_w": c(np.asarray(inp["mlstm_norm_w"])[0]),
            "w_out": c(np.asarray(inp["w_out"])[0]), "norm2_w": c(np.asarray(inp["norm2_w"])[0]),
            "w_gu": c(np.asarray(inp["w_gu"])[0]), "w_down": c(np.asarray(inp["w_down"])[0]),
            "final_norm_w": c(inp["final_norm_w"]),
            "rope": rope, "ident": ident, "masks": masks,
        }
        maps.append(m)
    return maps


_NC = None


def kernel(**inp):
    global _NC
    if _NC is None:
        _NC = build_nc()
    maps = make_in_maps(inp)
    res = run_bass_kernel_spmd(_NC, maps, core_ids=list(range(8)))
    R = res.results
    f = np.float32
    y_s = np.stack([R[b]["y"][0:4096] for b in range(8)], axis=0).astype(f)
    y_p = np.stack([R[b]["y"][4096 + 256 * j:4096 + 256 * (j + 1)] for b in range(8) for j in range(2)], axis=0).astype(f)
    nk = np.stack([R[b]["nk"][256 * j:256 * (j + 1)].reshape(256, 2, 64) for b in range(8) for j in range(2)], axis=0)
    nv = np.stack([R[b]["nv"][256 * j:256 * (j + 1)].reshape(256, 2, 64) for b in range(8) for j in range(2)], axis=0)
    nC = np.stack([R[b]["nC"][j] for b in range(8) for j in range(2)], axis=0)
    nn = np.stack([R[b]["nn"][j] for b in range(8) for j in range(2)], axis=0)
    nm = np.stack([R[b]["nm"][j] for b in range(8) for j in range(2)], axis=0)
    return (y_p, y_s, nk[:, None].astype(f), nv[:, None].astype(f), nC[:, None].astype(f), nn[:, None].astype(f),
            nm[:, None].astype(f))


def phase_b1(ctx):
    c = NS(ctx)
    k = c.k
    PS = c.PS
    KT2, VA = c.KT2, c.VA
    pre_gen, b2_main = make_b2(ctx)
    ctx["b2_main"] = b2_main
    k.sb_phase = k.sb_ptr
    pre = pre_gen()
    SP = [Tl(c.PSALL[:, b * 1024:(b + 1) * 1024]) for b in range(2)]
    QTg = [k.sb([128, 4, 512], BF16, f"QTg{i}") for i in range(2)]
    PTP = [k.sb([128, 1024], BF16, f"PT{i}") for i in range(3)]
    AO = [k.sb([128, 4, 512], BF16, f"AO{i}") for i in range(2)]
    rden = [k.sb([128, 512], F32, f"rden{i}") for i in range(2)]
    obs = k.sb([128, 512], F32, "obs")
    groups = [(g * 512, 512, list(range(32)) + [36, 37]) for g in range(8)]
    groups += [(4096, 256, [32, 33]), (4352, 256, [34, 35])]
    cnt = [0]

    def group_body(gi, t0, nq, kbs):
        qt = QTg[gi % 2]
        ao = AO[gi % 2]
        k.dma(qt[:, :, 0:nq], c.QTd[:, :, t0:t0 + nq].rearrange("c p t -> p c t"), writes=[qt])
        iters = [(j, idx, kb) for j in range(4) for idx, kb in enumerate(kbs)]
        base = cnt[0]
        cnt[0] += len(iters)

        def emit_s(n):
            j, idx, kb = iters[n]
            g = j // 2
            kcol = kb * 128 if kb < 36 else TT + (kb - 36) * 128
            sp = SP[(base + n) % 2]
            k.op("pe", lambda e: e.matmul(sp[:, 0:nq], lhsT=KT2[0:64, g, kcol:kcol + 128], rhs=qt[0:64, j, 0:nq],
                                          start=True, stop=True), reads=[KT2, qt], writes=[sp], inc=False)
            k.op("pe", lambda e: e.matmul(sp[:, 512:512 + nq], lhsT=KT2[64:128, g, kcol:kcol + 128],
                                          rhs=qt[64:128, j, 0:nq], start=True, stop=True),
                 reads=[KT2, qt], writes=[sp])

        def emit_rest(n):
            j, idx, kb = iters[n]
            g = j // 2
            sp = SP[(base + n) % 2]
            pt = PTP[(base + n) % 3]
            oa, ob = (PS[4], PS[6])[j % 2], PS[5]
            k.op("act", lambda e: e.activation(out=pt[:, :].rearrange("p (a t) -> p a t", a=2)[:, :, 0:nq],
                                               in_=sp[:, :].rearrange("p (a t) -> p a t", a=2)[:, :, 0:nq],
                                               func=AF.Exp), reads=[sp], writes=[pt])
            first, last = idx == 0, idx == len(kbs) - 1
            k.op("pe", lambda e: e.matmul(oa[:, 0:nq], lhsT=VA[:, kb, g, 64:192], rhs=pt[:, 0:nq], start=first,
                                          stop=last), reads=[VA, pt], writes=[oa], inc=False)
            k.op("pe", lambda e: e.matmul(ob[:, 0:nq], lhsT=VA[:, kb, g, 0:128], rhs=pt[:, 512:512 + nq],
                                          start=first, stop=last), reads=[VA, pt], writes=[ob])
            if last:
                ra, rb = rden[0], rden[1]
                k.op("dve", lambda e: e.tensor_copy(out=obs[:, 0:nq], in_=ob[:, 0:nq]), reads=[ob], writes=[obs])
                k.op("dve", lambda e: e.reciprocal(out=rb[64:128, 0:nq], in_=obs[0:64, 0:nq]), reads=[obs],
                     writes=[rb])
                k.op("dve", lambda e: e.tensor_tensor(out=ao[64:128, j, 0:nq], in0=obs[64:128, 0:nq],
                                                      in1=rb[64:128, 0:nq], op=ALU.mult), reads=[obs, rb],
                     writes=[ao])
                k.op("dve", lambda e: e.reciprocal(out=ra[0:64, 0:nq], in_=oa[64:128, 0:nq]), reads=[oa], writes=[ra])
                k.op("dve", lambda e: e.tensor_tensor(out=ao[0:64, j, 0:nq], in0=oa[0:64, 0:nq], in1=ra[0:64, 0:nq],
                                                      op=ALU.mult), reads=[oa, ra], writes=[ao])

        emit_s(0)
        for n in range(len(iters)):
            if n + 1 < len(iters):
                emit_s(n + 1)
            emit_rest(n)
            if PRE_IN_ATT:
                next(pre, None)
        k.dma(c.AOTd[:, :, t0:t0 + nq].rearrange("c p t -> p c t"), ao[:, :, 0:nq], reads=[ao])

    for gi, (t0, nq, kbs) in enumerate(groups):
        group_body(gi, t0, nq, kbs)
    for _ in pre:
        pass
    k.barrier()
    k.phase_mem()


def make_b2(ctx):
    c = NS(ctx)
    k = c.k
    PS = c.PS
    pS = PS[0]
    pGd, pId, pTd = (PS[1], PS[4]), (PS[2], PS[5]), (PS[3], PS[6])
    pX = PS[7]
    Cst = [k.sb([128, 4, 130], F32, f"Cst{i}") for i in range(2)]
    Csnap = k.sb([128, NT, 4, 130], BF16, "Csnap")
    mst = [k.sb([4, 2], F32, f"mst{i}") for i in range(2)]
    mbprev = k.sb([4, NT + 4], F32, "mbprev")
    GT = [k.sb([4, 4, 128], F32, f"GT{i}") for i in range(2)]
    kk = [k.sb([128, 4, 128], BF16, f"kk{i}") for i in range(2)]
    va = [k.sb([128, 4, 130], BF16, f"va{i}") for i in range(2)]
    rows = [[k.sb([4, 128], F32, f"row{d}_{i}") for i in range(8)] for d in range(2)]
    dg = [k.sb([4, 4], F32, f"dg{d}") for d in range(2)]
    cols = [k.sb([128, 24], F32, f"cols{d}") for d in range(2)]
    kw = k.sb([128, 512], BF16, "kw")
    cols1p = [cols[1], k.sb([128, 24], F32, "cols1b")]
    ident, ones4, eye4 = c.ident, c.ones4, c.eye4
    maskneg = identb = None
    masks = mnw = Cbf = qT = kT = mo = blk = Dsb = swT = qI = hd = hm = sqh = ss4 = rs4 = hmT = None

    def late_alloc():
        nonlocal maskneg, identb
        nonlocal masks, mnw, Cbf, qT, kT, mo, blk, Dsb, swT, qI, hd, hm, sqh, ss4, rs4, hmT
        masks = k.sb([128, 2, 512], F32, "masks")
        mnw = k.sb([128, 512], F32, "mnw")
        Cbf = k.sb([128, 4, 130], BF16, "Cbf")
        qT = [k.sb([128, 4, 128], BF16, f"qT{i}") for i in range(2)]
        kT = [k.sb([128, 4, 128], BF16, f"kT{i}") for i in range(2)]
        mo = [k.sb([128, 512], BF16, f"mo{i}") for i in range(2)]
        blk = [[k.sb([4, 4, 128], F32, f"blk{d}_{i}") for i in range(2)] for d in range(2)]
        Dsb = [k.sb([128, 512], F32, f"Dsb{d}") for d in range(2)]
        swT = [k.sb([128, 512], BF16, f"swT{d}") for d in range(2)]
        qI = [k.sb([128, 512], BF16, f"qI{d}") for d in range(2)]
        hd = [[k.sb([128, 512], F32, f"hd{p}_{d}") for d in range(2)] for p in range(2)]
        hm = k.sb([128, 512], F32, "hm")
        sqh = k.sb([128, 512], F32, "sqh")
        ss4 = k.sb([128, 8], F32, "ss4")
        rs4 = k.sb([128, 8], F32, "rs4")
        hmT = [k.sb([128, 4, 128], BF16, f"hmT{i}") for i in range(2)]
        maskneg = k.sb([128, 2, 512], BF16, "maskneg")
        identb = k.sb([128, 128], BF16, "identb")
        k.dma(masks[:, :, :], c.MASKS[:, :, :], writes=[masks])
        k.dma(mnw[:, :], c.MNW.partition_broadcast(128), writes=[mnw])
        k.op("dve", lambda e: e.tensor_scalar(out=maskneg[:, :, :], in0=masks[:, :, :], scalar1=-1.0, scalar2=30000.0,
                                              op0=ALU.add, op1=ALU.mult), reads=[masks], writes=[maskneg])
        k.op("dve", lambda e: e.tensor_copy(out=identb[:, :], in_=ident[:, :]), reads=[ident], writes=[identb])

    def load_chunk(ci, full):
        tok = slice(ci * 128, (ci + 1) * 128)
        i2 = ci % 2
        k.dma(GT[i2][:, :, :], c.GATd[:, :, tok].rearrange("t h n -> h t n"), writes=[GT[i2]])
        k.dma(kk[i2][:, :, :], c.MKd[tok, :].rearrange("p (h d) -> p h d", h=4), writes=[kk[i2]])
        k.dma(va[i2][:, :, :], c.MVd[tok, :].rearrange("p (h d) -> p h d", h=4), writes=[va[i2]])
        if full:
            k.dma(qT[i2][:, :, :], c.MQTd[:, :, tok].rearrange("h p t -> p h t"), writes=[qT[i2]])
            k.dma(kT[i2][:, :, :], c.MKTd[:, :, tok].rearrange("h p t -> p h t"), writes=[kT[i2]])
            k.dma(mo[i2][:, :], c.MOd[tok, :], writes=[mo[i2]])

    def gate_prep(d, gt, mprev, full, upd, cl=None, pT=None):
        mt, mc = mprev
        rb_, ra_, rg_, rng_, rin_, rgu_, rw_, rt_ = rows[d]
        pT = pT or pTd[d]
        cl = cl or cols[d]
        lf = lambda: gt[:, 1 + 2 * d, :]
        ig = lambda: gt[:, 2 * d, :]
        rv = (lambda ap: ap) if d == 0 else (lambda ap: ap[:, ::-1])
        last = 127 if d == 0 else 0
        k.op("dve", lambda e: e.tensor_tensor_scan(out=rv(rb_[:, :]), data0=rv(ones4[:, :]), data1=rv(lf()),
                                                   initial=0.0, op0=ALU.mult, op1=ALU.add),
             reads=[gt, ones4], writes=[rb_])
        yield
        k.op("dve", lambda e: e.tensor_tensor(out=ra_[:, :], in0=ig(), in1=rb_[:, :], op=ALU.subtract),
             reads=[gt, rb_], writes=[ra_])
        yield
        k.op("pe", lambda e: e.transpose(pT[:, 0:4], ra_[:, :], ident[0:4, 0:4]), reads=[ra_, ident], writes=[pT])
        k.op("dve", lambda e: e.tensor_tensor_scan(out=rv(rg_[:, :]), data0=rv(ra_[:, :]), data1=rv(ra_[:, :]),
                                                   initial=mt[:, mc:mc + 1], op0=ALU.max, op1=ALU.max),
             reads=[ra_, mt], writes=[rg_])
        yield
        k.op("dve", lambda e: e.tensor_scalar_mul(out=rng_[:, :], in0=rg_[:, :], scalar1=-1.0),
             reads=[rg_], writes=[rng_])
        if full:
            k.op("dve", lambda e: e.tensor_tensor(out=rt_[:, :], in0=rb_[:, :], in1=rg_[:, :], op=ALU.add),
                 reads=[rb_, rg_], writes=[rt_])
            yield
            bk0 = blk[d][0]
            k.op("dve", lambda e: e.tensor_tensor(
                out=bk0[:, :, :], in0=rng_[:, :].unsqueeze(1).to_broadcast([4, 4, 128]),
                in1=eye4[:, :].unsqueeze(2).to_broadcast([4, 4, 128]), op=ALU.mult),
                reads=[rng_, eye4], writes=[bk0])
            yield
            k.op("pe", lambda e: e.matmul(pGd[d][:, :], lhsT=ones4[:, :],
                                          rhs=bk0[:, :, :].rearrange("k h l -> k (h l)"),
                                          start=True, stop=False), reads=[ones4, bk0], writes=[pGd[d]], inc=False)
            k.op("pe", lambda e: e.matmul(pGd[d][:, :], lhsT=identb[:, :], rhs=maskneg[:, d, :],
                                          start=False, stop=True), reads=[identb, maskneg], writes=[pGd[d]])
        yield
        k.op("act", lambda e: e.activation(out=rin_[:, :], in_=rng_[:, :], func=AF.Exp, bias=mt[:, mc:mc + 1]),
             reads=[rng_, mt], writes=[rin_])
        if full:
            yield
            bk1 = blk[d][1]
            k.op("dve", lambda e: e.tensor_tensor(
                out=bk1[:, :, :], in0=rin_[:, :].unsqueeze(1).to_broadcast([4, 4, 128]),
                in1=eye4[:, :].unsqueeze(2).to_broadcast([4, 4, 128]), op=ALU.mult),
                reads=[rin_, eye4], writes=[bk1])
            yield
            k.op("pe", lambda e: e.matmul(pId[d][:, :], lhsT=ones4[:, :],
                                          rhs=bk1[:, :, :].rearrange("k h l -> k (h l)"),
                                          start=True, stop=True), reads=[ones4, bk1], writes=[pId[d]])
        if full:
            k.op("act", lambda e: e.activation(out=rgu_[:, :], in_=rt_[:, :], func=AF.Exp, scale=-1.0),
                 reads=[rt_], writes=[rgu_])
        if upd:
            k.op("act", lambda e: e.activation(out=rw_[:, :], in_=ra_[:, :], func=AF.Exp,
                                               bias=rng_[:, last:last + 1]), reads=[ra_, rng_], writes=[rw_])
        yield
        if full:
            k.op("pe", lambda e: e.transpose(pT[:, 4:8], rgu_[:, :], ident[0:4, 0:4]), reads=[rgu_, ident],
                 writes=[pT])
        if upd:
            k.op("pe", lambda e: e.transpose(pT[:, 8:12], rw_[:, :], ident[0:4, 0:4]), reads=[rw_, ident],
                 writes=[pT])
            k.op("dve", lambda e: e.tensor_scalar(out=dg[d][:, :], in0=eye4[:, :], scalar1=rin_[:, last:last + 1],
                                                  scalar2=None, op0=ALU.mult), reads=[eye4, rin_], writes=[dg[d]])
            yield
            k.op("pe", lambda e: e.matmul(pT[:, 12:16], lhsT=ones4[:, :], rhs=dg[d][:, :], start=True, stop=True),
                 reads=[ones4, dg[d]], writes=[pT])
        yield
        hi = 16 if upd else 8
        if not full and upd:
            k.op("dve", lambda e: e.tensor_copy(out=cl[:, 0:4], in_=pT[:, 0:4]), reads=[pT], writes=[cl])
            k.op("dve", lambda e: e.tensor_copy(out=cl[:, 8:16], in_=pT[:, 8:16]), reads=[pT], writes=[cl])
        else:
            k.op("dve", lambda e: e.tensor_copy(out=cl[:, 0:hi], in_=pT[:, 0:hi]), reads=[pT], writes=[cl])
        yield

    def new_m(d, mt_out, mc_out):
        rb_, rg_ = rows[d][0], rows[d][2]
        last = 127 if d == 0 else 0
        k.op("dve", lambda e: e.tensor_tensor(out=mt_out[:, mc_out:mc_out + 1], in0=rb_[:, last:last + 1],
                                              in1=rg_[:, last:last + 1], op=ALU.add),
             reads=[rb_, rg_], writes=[mt_out])

    def state_update(d, kk_, va_, banks=None, cl=None):
        banks = banks or ((pId[d], 0), (pTd[d], 128))
        Cs = Cst[d]
        cl = cl or cols[d]
        k.op("dve", lambda e: e.tensor_tensor(out=kw[:, :].rearrange("p (h d) -> p h d", h=4), in0=kk_[:, :, :],
                                              in1=cl[:, 8:12].unsqueeze(2).to_broadcast([128, 4, 128]),
                                              op=ALU.mult), reads=[kk_, cl], writes=[kw])
        yield
        for h in range(4):
            pd, off = banks[h % 2]
            k.op("pe", lambda e, h=h, pd=pd, off=off: e.matmul(pd[:, off:off + 129], lhsT=kw[:, h * 128:(h + 1) * 128],
                                                      rhs=va_[:, h, 0:129], start=True, stop=True),
                 reads=[kw, va_], writes=[pd])
            yield
            k.op("dve", lambda e, h=h, pd=pd, off=off: e.scalar_tensor_tensor(
                out=Cs[:, h, 0:129], in0=Cs[:, h, 0:129], scalar=cl[:, 12 + h:13 + h], in1=pd[:, off:off + 129],
                op0=ALU.mult, op1=ALU.add), reads=[Cs, cl, pd], writes=[Cs])
            yield

    def run(*gens):
        gens = list(gens)
        while gens:
            for g in list(gens):
                try:
                    next(g)
                except StopIteration:
                    gens.remove(g)

    def init_dir(seq, d):
        Cs = Cst[d]
        if seq == 0:
            k.op("pool", lambda e: e.memset(Cs[:, :, :], 0.0), writes=[Cs])
            k.op("dve", lambda e: e.memset(mst[d][:, :], 0.0), writes=[mst[d]])
            k.dma(Cs[:, :, 0:128], c.SC[d].rearrange("h p e -> p h e"), writes=[Cs])
            k.dma(Cs[:, :, 128], c.SN[d].rearrange("h p -> p h"), writes=[Cs], slow=True)
            k.dma(mst[d][:, 0:1], c.SM[d].rearrange("(h o) -> h o", o=1), writes=[mst[d]], slow=True)
        else:
            k.op("pool", lambda e: e.memset(Cs[:, :, :], 0.0), writes=[Cs])
            k.op("dve", lambda e: e.memset(mst[d][:, :], 0.0), writes=[mst[d]])

    def store_state(seq, d):
        p = seq - 1
        Cs = Cst[d]
        k.dma(c.NC_[p, d].rearrange("h p e -> p h e"), Cs[:, :, 0:128], reads=[Cs])
        k.dma(c.NN[p, d].rearrange("h p -> p h"), Cs[:, :, 128], reads=[Cs], slow=True)
        k.dma(c.NM[p, d].rearrange("(h o) -> h o", o=1), mst[d][:, 0:1], reads=[mst[d]], slow=True)

    def pre_G(ci):
        i2 = ci % 2
        load_chunk(ci, False)
        yield
        k.op("dve", lambda e: e.tensor_copy(out=mbprev[:, ci:ci + 1], in_=mst[1][:, 0:1]), reads=[mst[1]],
             writes=[mbprev])
        yield
        yield from gate_prep(1, GT[i2], (mst[1], 0), False, True, cl=cols1p[i2], pT=pX)
        new_m(1, mst[1], 0)
        yield

    def pre_U(ci):
        i2 = ci % 2
        k.op("act", lambda e: e.activation(out=Csnap[:, ci, :, :], in_=Cst[1][:, :, :], func=AF.Copy),
             reads=[Cst[1]], writes=[Csnap])
        yield
        yield from state_update(1, kk[i2], va[i2], banks=((pX, 128), (pX, 128)), cl=cols1p[i2])

    def zip2(g1, g2):
        gens = [g for g in (g1, g2) if g is not None]
        while gens:
            for g in list(gens):
                try:
                    next(g)
                except StopIteration:
                    gens.remove(g)
            yield

    seqs = [(0, list(range(32))), (1, [32, 33]), (2, [34, 35])]

    def pre_gen():
        for seq, chunks in seqs:
            init_dir(seq, 1)
            yield
            order = list(reversed(chunks))
            yield from pre_G(order[0])
            for n, ci in enumerate(order):
                nxt = pre_G(order[n + 1]) if n + 1 < len(order) else None
                yield from zip2(nxt, pre_U(ci))
            if seq > 0:
                store_state(seq, 1)
                yield

    def dir_chain(d, ci, q_, kk_, va_, gt):
        mprev = (mst[0], 0) if d == 0 else (mbprev, ci)
        rng_, rin_ = rows[d][3], rows[d][4]
        pG, pI, pT, cl = pGd[d], pId[d], pTd[d], cols[d]
        yield from gate_prep(d, gt, mprev, True, d == 0)
        if d == 0:
            new_m(0, mst[0], 0)
        for h in range(4):
            k.op("act", lambda e, h=h: e.activation(out=Dsb[d][:, h * 128:(h + 1) * 128],
                                                    in_=pG[:, h * 128:(h + 1) * 128], func=AF.Exp,
                                                    bias=cl[:, h:h + 1]), reads=[pG, cl], writes=[Dsb[d]])
        k.op("dve", lambda e: e.tensor_tensor(out=qI[d][:, :], in0=pI[:, :],
                                              in1=q_[:, :, :].rearrange("p h t -> p (h t)"), op=ALU.mult),
             reads=[pI, q_], writes=[qI[d]])
        yield
        k.op("dve", lambda e: e.tensor_tensor(out=swT[d][:, :], in0=pS[:, :], in1=Dsb[d][:, :], op=ALU.mult),
             reads=[pS, Dsb[d]], writes=[swT[d]])
        if d == 0:
            k.op("act", lambda e: e.activation(out=Cbf[:, :, :], in_=Cst[0][:, :, :], func=AF.Copy),
                 reads=[Cst[0]], writes=[Cbf])
        yield
        for h in range(4):
            cprev = (lambda h=h: Cbf[:, h, :]) if d == 0 else (lambda h=h: Csnap[:, ci, h, :])
            ctile = Cbf if d == 0 else Csnap
            k.op("pe", lambda e, h=h: e.matmul(pG[:, h * 128:(h + 1) * 128], lhsT=swT[d][:, h * 128:(h + 1) * 128],
                                               rhs=va_[:, h, 0:128], start=True, stop=False),
                 reads=[swT[d], va_], writes=[pG], inc=False)
            k.op("pe", lambda e, h=h, cprev=cprev: e.matmul(pG[:, h * 128:(h + 1) * 128],
                                                            lhsT=qI[d][:, h * 128:(h + 1) * 128],
                                                            rhs=cprev()[:, 0:128], start=False, stop=True),
                 reads=[qI[d], ctile], writes=[pG], inc=False)
            k.op("pe", lambda e, h=h: e.matmul(pT[:, 16 + h:17 + h], lhsT=swT[d][:, h * 128:(h + 1) * 128],
                                               rhs=va_[:, h, 128:129], start=True, stop=False),
                 reads=[swT[d], va_], writes=[pT], inc=False)
            k.op("pe", lambda e, h=h, cprev=cprev: e.matmul(pT[:, 16 + h:17 + h],
                                                            lhsT=qI[d][:, h * 128:(h + 1) * 128],
                                                            rhs=cprev()[:, 128:129], start=False, stop=True),
                 reads=[qI[d], ctile], writes=[pT])
            yield
        k.op("dve", lambda e: e.tensor_scalar(out=cl[:, 20:24], in0=pT[:, 16:20], scalar1=-1.0, scalar2=None,
                                              op0=ALU.mult), reads=[pT], writes=[cl])
        yield
        k.op("dve", lambda e: e.tensor_tensor(out=cl[:, 16:20], in0=pT[:, 16:20], in1=cl[:, 20:24], op=ALU.max),
             reads=[pT, cl], writes=[cl])
        yield
        k.op("dve", lambda e: e.tensor_tensor(out=cl[:, 16:20], in0=cl[:, 16:20], in1=cl[:, 4:8], op=ALU.max),
             reads=[cl], writes=[cl])
        yield
        k.op("dve", lambda e: e.reciprocal(out=cl[:, 16:20], in_=cl[:, 16:20]), reads=[cl], writes=[cl])
        yield
        hdt = hd[ci % 2][d]
        k.op("dve", lambda e: e.tensor_tensor(out=hdt[:, :].rearrange("p (h e) -> p h e", h=4),
                                              in0=pG[:, :].rearrange("p (h e) -> p h e", h=4),
                                              in1=cl[:, 16:20].unsqueeze(2).to_broadcast([128, 4, 128]),
                                              op=ALU.mult), reads=[pG, cl], writes=[hdt])
        yield
        if d == 0:
            yield from state_update(0, kk_, va_)

    def tail_chain(ci):
        i2 = ci % 2
        mo_ = mo[i2]
        h0, h1 = hd[i2]
        k.op("pool", lambda e: e.tensor_tensor(out=hm[:, :], in0=h0[:, :], in1=h1[:, :], op=ALU.add),
             reads=[h0, h1], writes=[hm])
        yield
        k.op("dve", lambda e: e.tensor_tensor(out=sqh[:, :], in0=hm[:, :], in1=hm[:, :], op=ALU.mult),
             reads=[hm], writes=[sqh])
        yield
        k.op("dve", lambda e: e.tensor_reduce(out=ss4[:, 0:4], in_=sqh[:, :].rearrange("p (h d) -> p h d", h=4),
                                              axis=AX.X, op=ALU.add), reads=[sqh], writes=[ss4])
        yield
        k.op("act", lambda e: e.activation(out=rs4[:, 0:4], in_=ss4[:, 0:4], func=AF.Ln, scale=1.0 / 128, bias=EPS),
             reads=[ss4], writes=[rs4])
        yield
        k.op("act", lambda e: e.activation(out=rs4[:, 0:4], in_=rs4[:, 0:4], func=AF.Exp, scale=-0.5),
             reads=[rs4], writes=[rs4])
        yield
        k.op("dve", lambda e: e.tensor_tensor(out=hm[:, :].rearrange("p (h d) -> p h d", h=4),
                                              in0=hm[:, :].rearrange("p (h d) -> p h d", h=4),
                                              in1=rs4[:, 0:4].unsqueeze(2).to_broadcast([128, 4, 128]),
                                              op=ALU.mult), reads=[hm, rs4], writes=[hm])
        yield
        k.op("pool", lambda e: e.tensor_tensor(out=hm[:, :], in0=hm[:, :], in1=mnw[:, :], op=ALU.mult),
             reads=[hm, mnw], writes=[hm])
        yield
        k.op("pool", lambda e: e.tensor_tensor(out=sqh[:, :], in0=hm[:, :], in1=mo_[:, :], op=ALU.mult),
             reads=[hm, mo_], writes=[sqh])
        yield
        for h in range(4):
            k.op("pe", lambda e, h=h: e.transpose(pX[:, h * 128:(h + 1) * 128], sqh[:, h * 128:(h + 1) * 128],
                                                  ident[:, :]), reads=[sqh, ident], writes=[pX], inc=(h == 3))
        yield
        ho = hmT[i2]
        k.op("act", lambda e: e.activation(out=ho[:, :, :], in_=pX[:, :].rearrange("p (h t) -> p h t", h=4),
                                           func=AF.Copy), reads=[pX], writes=[ho])
        yield
        k.dma(c.HMTd[:, :, ci * 128:(ci + 1) * 128].rearrange("h p t -> p h t"), ho[:, :, :], reads=[ho])

    def main_chunk(ci, prev):
        i2 = ci % 2
        load_chunk(ci, True)
        q_, kT_, kk_, va_, gt = qT[i2], kT[i2], kk[i2], va[i2], GT[i2]
        for h in range(4):
            k.op("pe", lambda e, h=h: e.matmul(pS[:, h * 128:(h + 1) * 128], lhsT=kT_[:, h, :], rhs=q_[:, h, :],
                                               start=True, stop=True), reads=[kT_, q_], writes=[pS], inc=(h == 3))
        gens = [dir_chain(0, ci, q_, kk_, va_, gt), dir_chain(1, ci, q_, kk_, va_, gt)]
        if prev is not None:
            gens.append(tail_chain(prev))
        run(*gens)

    def main():
        late_alloc()
        for seq, chunks in seqs:
            init_dir(seq, 0)
            prev = None
            for ci in chunks:
                main_chunk(ci, prev)
                prev = ci
            run(tail_chain(prev))
            if seq > 0:
                store_state(seq, 0)
        k.barrier()
        k.sb_phase = ctx["sb_phase0"]
        k.phase_mem()

    return pre_gen, main


def phase_b2(ctx):
    ctx["b2_main"]()


def phase_c1(ctx):
    c = NS(ctx)
    k = c.k
    PS = c.PS
    Wout = k.sb([128, 8, D], BF16, "Wout")
    stg = [k.sb([128, D], F32, f"stgo{i}") for i in range(2)]
    g1b = k.sb([128, D], F32, "g1b")
    mixT = [k.sb([128, 8, 512], BF16, f"mixT{i}") for i in range(2)]
    xb = [k.sb([128, D], F32, f"xc{i}") for i in range(3)]
    x1 = [k.sb([128, D], F32, f"x1{i}") for i in range(2)]
    tmp2 = [k.sb([128, D], F32, f"tmpc1_{i}") for i in range(2)]
    junk = k.sb([128, D], BF16, "junkc")
    xs = [k.sb([128, D], F32, f"xsc{i}") for i in range(3)]
    deferred = []

    def flush(keep=0):
        while len(deferred) > keep:
            deferred.pop(0)()
    ss2 = [k.sb([128, 8], F32, f"ssc{i}") for i in range(2)]
    rstd2 = [k.sb([128, 8], F32, f"rstdc{i}") for i in range(2)]
    h2T = [k.sb([128, 8, 512], BF16, f"h2T{i}") for i in range(2)]
    for kc in range(8):
        st = stg[kc % 2]
        k.dma(st[:, :], c.WOUT[kc * 128:(kc + 1) * 128, :], writes=[st])
        k.op("pool", lambda e, kc=kc, st=st: e.tensor_copy(out=Wout[:, kc, :], in_=st[:, :]), reads=[st],
             writes=[Wout])

    def st_body(s):
        cj = 0 if s < 8 else 1
        tok = slice(s * 512, (s + 1) * 512)
        mx = mixT[s % 2]
        h2 = h2T[s % 2]
        if s == 0 or s == 8:
            k.dma(g1b[:, :], c.MODS[cj, 2 * D:3 * D].partition_broadcast(128), writes=[g1b])
        k.dma(mx[:, 0:4, :], c.AOTd[:, :, tok].rearrange("c p t -> p c t"), writes=[mx])
        k.dma(mx[:, 4:8, :], c.HMTd[:, :, tok].rearrange("c p t -> p c t"), writes=[mx])

        def tile_mm(i):
            ti = s * 4 + i
            xt = xb[ti % 3]
            k.dma(xt[:, :], c.X[ti * 128:(ti + 1) * 128, :], writes=[xt])
            for n in range(2):
                pb = PS[2 + (2 * ti + n) % 4]
                for kc in range(8):
                    k.op("pe", lambda e, pb=pb, kc=kc, n=n: e.matmul(
                        pb[:, :], lhsT=mx[:, kc, i * 128:(i + 1) * 128], rhs=Wout[:, kc, n * 512:(n + 1) * 512],
                        start=(kc == 0), stop=(kc == 7)), reads=[mx, Wout], writes=[pb], inc=(kc == 7))
            flush(1)

        def tile_chain(i):
            ti = s * 4 + i
            xt = xb[ti % 3]
            xo = x1[ti % 2]
            tmp = tmp2[ti % 2]
            for n in range(2):
                pb = PS[2 + (2 * ti + n) % 4]
                k.op("dve", lambda e, pb=pb, n=n: e.tensor_tensor(out=tmp[:, n * 512:(n + 1) * 512], in0=pb[:, :],
                                                                  in1=g1b[:, n * 512:(n + 1) * 512], op=ALU.mult),
                     reads=[pb, g1b], writes=[tmp])
            k.op("pool", lambda e: e.tensor_tensor(out=xo[:, :], in0=tmp[:, :], in1=xt[:, :], op=ALU.add),
                 reads=[tmp, xt], writes=[xo])
            k.dma(c.X1d[ti * 128:(ti + 1) * 128, :], xo[:, :], reads=[xo])
            norm_to_hT(k, c, xo, h2, i * 128,
                       lambda ch: c.G2[:, cj, ch:ch + 1], lambda ch: c.modc[:, cj, 3, ch:ch + 1],
                       (junk, ss2[ti % 2], rstd2[ti % 2], xs[ti % 3]), PS[0], PS[1], defer=deferred)

        for i0 in (0, 2):
            tile_mm(i0)
            tile_mm(i0 + 1)
            k.capture()
            tile_chain(i0)
            La = k.capture_end()
            k.capture()
            tile_chain(i0 + 1)
            Lb = k.capture_end()
            k.emit_interleaved(La, Lb)
        flush()
        k.dma(c.H2Td[:, :, tok].rearrange("c p t -> p c t"), h2[:, :, :], reads=[h2])

    for s in range(9):
        st_body(s)
    k.barrier()
    k.phase_mem()


def phase_c2(ctx):
    c = NS(ctx)
    k = c.k
    PS = c.PS
    Wgb = [k.sb([128, 8, 256], BF16, f"Wgb{i}") for i in range(22)]
    Wdn = k.sb([128, NF, D], BF16, "Wdn")
    SW = 704
    stg = [k.sb([128, SW], F32, f"stgf{i}") for i in range(2)]
    g2b = k.sb([128, D], F32, "g2b")
    fnb = k.sb([128, D], F32, "fnb")
    h2 = k.sb([128, 8, 512], BF16, "h2c")
    actT = k.sb([128, NF, 512], BF16, "actT")
    sg = [k.sb([128, 512], F32, f"sg{i}") for i in range(2)]
    x1 = [k.sb([128, D], F32, f"x1c{i}") for i in range(2)]
    x2 = k.sb([128, D], F32, "x2c")
    yt = [k.sb([128, D], F32, f"yt{i}") for i in range(2)]
    junk = k.sb([128, D], BF16, "junkf")
    ss = k.sb([128, 8], F32, "ssf")
    rstd = k.sb([128, 8], F32, "rstdf")
    mhalf = k.sb([128, 1], F32, "mhalf")
    k.dma(fnb[:, :], c.FNW.partition_broadcast(128), writes=[fnb])
    k.dma(g2b[:, :], c.MODS[0, 5 * D:6 * D].partition_broadcast(128), writes=[g2b])
    k.dma(h2[:, :, :], c.H2Td[:, :, 0:512].rearrange("c p t -> p c t"), writes=[h2])
    it = 0
    for bi in range(11):
        for half in range(2):
            blkt = Wgb[2 * bi + half]
            c0 = half * DFF + bi * 256
            for kc0 in (0, 2, 4, 6):
                st = stg[it % 2]
                it += 1
                k.dma(st[:, 0:512].rearrange("p (a n) -> p a n", a=2),
                      c.WGU[kc0 * 128:(kc0 + 2) * 128, c0:c0 + 256].rearrange("(a p) n -> p a n", p=128), writes=[st])
                k.op("pool", lambda e, kc0=kc0, st=st, blkt=blkt: e.tensor_copy(
                    out=blkt[:, kc0:kc0 + 2, :], in_=st[:, 0:512].rearrange("p (a n) -> p a n", a=2)),
                    reads=[st], writes=[blkt])
    for f in range(NF):
        for j in range(2):
            st = stg[it % 2]
            it += 1
            k.dma(st[:, 0:512], c.WDN[f * 128:(f + 1) * 128, j * 512:(j + 1) * 512], writes=[st])
            k.op("pool", lambda e, f=f, j=j, st=st: e.tensor_copy(out=Wdn[:, f, j * 512:(j + 1) * 512],
                                                                  in_=st[:, 0:512]), reads=[st], writes=[Wdn])
    k.op("dve", lambda e: e.memset(mhalf[:, :], -0.5), writes=[mhalf])

    def st_body(s):
        cj = 0 if s < 8 else 1
        tok = slice(s * 512, (s + 1) * 512)
        if s == 8:
            k.dma(g2b[:, :], c.MODS[cj, 5 * D:6 * D].partition_broadcast(128), writes=[g2b])
        if s > 0:
            k.dma(h2[:, :, :], c.H2Td[:, :, tok].rearrange("c p t -> p c t"), writes=[h2])

        def up_body(f):
            pg, pu = PS[2 * (f % 2)], PS[2 * (f % 2) + 1]
            c0 = (f % 2) * 128
            for (pb, wt) in ((pg, Wgb[2 * (f // 2)]), (pu, Wgb[2 * (f // 2) + 1])):
                for kc in range(8):
                    k.op("pe", lambda e, pb=pb, wt=wt, kc=kc: e.matmul(
                        pb[:, :], lhsT=wt[:, kc, c0:c0 + 128], rhs=h2[:, kc, :], start=(kc == 0), stop=(kc == 7)),
                        reads=[wt, h2], writes=[pb], inc=(kc == 7))
            sgt = sg[f % 2]
            k.op("act", lambda e: e.activation(out=sgt[:, :], in_=pg[:, :], func=AF.Silu), reads=[pg], writes=[sgt])
            k.op("dve", lambda e: e.tensor_tensor(out=actT[:, f, :], in0=pu[:, :], in1=sgt[:, :], op=ALU.mult),
                 reads=[pu, sgt], writes=[actT])

        for f in range(NF):
            up_body(f)

        def tile_body(i):
            ti = s * 4 + i
            xt = x1[ti % 2]
            y = yt[ti % 2]
            k.dma(xt[:, :], c.X1d[ti * 128:(ti + 1) * 128, :], writes=[xt])
            for n in range(2):
                pb = PS[4 + (2 * ti + n) % 4]
                for f in range(NF):
                    k.op("pe", lambda e, pb=pb, f=f, n=n: e.matmul(
                        pb[:, :], lhsT=actT[:, f, i * 128:(i + 1) * 128], rhs=Wdn[:, f, n * 512:(n + 1) * 512],
                        start=(f == 0), stop=(f == NF - 1)), reads=[actT, Wdn], writes=[pb], inc=(f == NF - 1))
                k.op("dve", lambda e, pb=pb, n=n: e.tensor_tensor(out=x2[:, n * 512:(n + 1) * 512], in0=pb[:, :],
                                                                  in1=g2b[:, n * 512:(n + 1) * 512], op=ALU.mult),
                     reads=[pb, g2b], writes=[x2])
            k.op("pool", lambda e: e.tensor_tensor(out=x2[:, :], in0=x2[:, :], in1=xt[:, :], op=ALU.add),
                 reads=[x2, xt], writes=[x2])
            k.op("dve", lambda e: e.scalar_tensor_tensor(out=junk[:, :], in0=x2[:, :], scalar=1.0, in1=x2[:, :],
                                                         op0=ALU.mult, op1=ALU.mult, accum_out=ss[:, 0:1]),
                 reads=[x2], writes=[junk, ss])
            k.op("dve", lambda e: e.tensor_scalar(out=ss[:, 1:2], in0=ss[:, 0:1], scalar1=1.0 / D, scalar2=EPS,
                                                  op0=ALU.mult, op1=ALU.add), reads=[ss], writes=[ss])
            k.op("pool", lambda e: e.tensor_tensor(out=rstd[:, 0:1], in0=ss[:, 1:2], in1=mhalf[:, 0:1], op=ALU.pow),
                 reads=[ss, mhalf], writes=[rstd])
            k.op("dve", lambda e: e.scalar_tensor_tensor(out=y[:, :], in0=x2[:, :], scalar=rstd[:, 0:1],
                                                         in1=fnb[:, :], op0=ALU.mult, op1=ALU.mult),
                 reads=[x2, rstd, fnb], writes=[y])
            k.dma(c.Y[ti * 128:(ti + 1) * 128, :], y[:, :], reads=[y])

        for i in range(4):
            tile_body(i)

    for s in range(9):
        st_body(s)
    k.barrier()
    k.phase_mem()
```

```python
import numpy as np
import concourse.bass as bass
import concourse.mybir as mybir
from concourse.bass_utils import run_bass_kernel_spmd

F32 = mybir.dt.float32
BF16 = mybir.dt.bfloat16
AF = mybir.ActivationFunctionType
ALU = mybir.AluOpType
AX = mybir.AxisListType

D = 1024
NT = 36
TT = NT * 128
NIN = 2832
DFF = 2816
NF = 22
EPS = 1e-6
KSC = 128.0 ** -0.5
NKEY = 38
import os as _os
NDS = 64
PRE_IN_ATT = int(_os.environ.get('PRE_IN_ATT', '1'))
STRICT = bool(int(_os.environ.get('KSTRICT', '1')))


class Tl:
    def __init__(s, h):
        s.h = h
        s.w = []
        s.r = []

    def __getitem__(s, i):
        return s.h[i]


class K:
    ENG = ("pe", "act", "dve", "pool", "sp")

    def __init__(s, nc):
        s.nc = nc
        s.ops = {e: [] for e in s.ENG}
        s.cnt = {e: 0 for e in s.ENG}
        s.seen = {e: {} for e in s.ENG}
        s.sem = {}
        s.dsems = [nc.alloc_semaphore(f"d_{i}") for i in range(NDS)]
        s.dcnt = [0] * NDS
        s.dptr = 0
        for e in s.ENG:
            s.sem[e] = nc.alloc_semaphore(f"s_{e}")
        s.sb_ptr = 0
        s.sb_phase = 0
        s.cap = None
        s.nalloc = 0

    def sb(s, shape, dt, name=None):
        esz = 4 if dt == F32 else 2
        n = 1
        for d_ in shape[1:]:
            n *= d_
        nbytes = (n * esz + 63) // 64 * 64
        off = s.sb_ptr
        s.sb_ptr += nbytes
        assert s.sb_ptr <= s.sb_top, (name, s.sb_ptr, s.sb_top)
        s.nalloc += 1
        h = s.nc.alloc_sbuf_tensor_at(f"{name or 't'}_{s.nalloc}", list(shape), dt, offset=off)
        return Tl(h)

    def phase_mem(s):
        s.sb_ptr = s.sb_phase

    def capture(s):
        s.cap = []
        return s.cap

    def capture_end(s):
        lst, s.cap = s.cap, None
        return lst

    def emit_interleaved(s, *lists):
        lists = [list(l) for l in lists if l]
        while lists:
            for l in list(lists):
                kind, args, kw = l.pop(0)
                (s.op if kind == "op" else s.dma)(*args, **kw)
                if not l:
                    lists.remove(l)

    def op(s, e, fn, reads=(), writes=(), inc=True):
        if s.cap is not None:
            s.cap.append(("op", (e, fn), dict(reads=list(reads), writes=list(writes), inc=inc)))
            return None
        deps = []
        for t in reads:
            for wt in t.w:
                if wt[0] != e or e != "pe":
                    deps.append(wt)
        for t in writes:
            for wt in t.w:
                if wt[0] != e or (STRICT and e != "pe"):
                    deps.append(wt)
            for rt in t.r:
                if rt[0] != e or (STRICT and e != "pe"):
                    deps.append(rt)
        need = {}
        for key, val in deps:
            if val > need.get(key, 0):
                need[key] = val
        waits = []
        for key, val in need.items():
            if s.seen[e].get(key, 0) >= val:
                continue
            s.seen[e][key] = val
            waits.append((key, val))
        tok = (e, s.cnt[e] + 1)
        if inc:
            s.cnt[e] += 1
        s.ops[e].append((waits, fn, inc, None))
        for t in writes:
            t.w = [tok]
            t.r = []
        for t in reads:
            if t not in writes:
                t.r.append(tok)
                if len(t.r) > 24:
                    t.r = s._compact(t.r)
        return tok

    @staticmethod
    def _compact(r):
        best = {}
        for key, val in r:
            if val > best.get(key, 0):
                best[key] = val
        return list(best.items())

    def dma(s, out, in_, reads=(), writes=(), q="sp", slow=False):
        if s.cap is not None:
            s.cap.append(("dma", (out, in_), dict(reads=list(reads), writes=list(writes), q=q, slow=slow)))
            return None
        deps = []
        for t in reads:
            deps.extend(t.w)
        for t in writes:
            for wt in t.w:
                if not isinstance(wt[0], int):
                    deps.append(wt)
            deps.extend(t.r)
        need = {}
        for key, val in deps:
            if val > need.get(key, 0):
                need[key] = val
        waits = []
        for key, val in need.items():
            if s.seen[q].get(key, 0) >= val:
                continue
            s.seen[q][key] = val
            waits.append((key, val))
        si = s.dptr
        s.dptr = (s.dptr + 1) % len(s.dsems)
        if s.dcnt[si] > s.seen[q].get(si, 0):
            s.seen[q][si] = s.dcnt[si]
            waits.append((si, s.dcnt[si]))
        s.dcnt[si] += 16
        tok = (si, s.dcnt[si])
        s.ops[q].append((waits, (out, in_, slow), False, si))
        for t in writes:
            if t.w and all(isinstance(wt[0], int) for wt in t.w):
                t.w = t.w + [tok]
            else:
                t.w = [tok]
            t.r = []
        for t in reads:
            t.r.append(tok)
        return tok

    def barrier(s):
        waits = []
        for e in s.ENG:
            if e != "sp" and s.cnt[e] > s.seen["sp"].get(e, 0):
                s.seen["sp"][e] = s.cnt[e]
                waits.append((e, s.cnt[e]))
        for si in range(len(s.dsems)):
            if s.dcnt[si] > s.seen["sp"].get(si, 0):
                s.seen["sp"][si] = s.dcnt[si]
                waits.append((si, s.dcnt[si]))
        s.cnt["sp"] += 1
        v = s.cnt["sp"]
        s.ops["sp"].append((waits, "inc", True, None))
        for e in s.ENG:
            if e != "sp":
                s.seen[e]["sp"] = v
                s.ops[e].append(([("sp", v)], None, False, None))
                for si in range(len(s.dsems)):
                    s.seen[e][si] = s.dcnt[si]
                for e2 in s.ENG:
                    s.seen[e][e2] = max(s.seen[e].get(e2, 0), s.cnt[e2])

    def semof(s, key):
        return s.dsems[key] if isinstance(key, int) else s.sem[key]

    def emit(s, e, eng):
        for waits, fn, inc, si in s.ops[e]:
            for key, val in waits:
                eng.wait_ge(s.semof(key), val)
            if fn is None:
                continue
            if fn == "inc":
                eng.sem_inc(s.sem[e], 1)
                continue
            if si is not None:
                out, in_, slow = fn
                if slow:
                    ins = eng.dma_start(out=out, in_=in_, allow_slow_non_contiguous=True)
                else:
                    ins = eng.dma_start(out=out, in_=in_)
                ins.then_inc(s.dsems[si], 16)
                continue
            ins = fn(eng)
            if inc:
                ins.then_inc(s.sem[e], 1)


def build_nc(debug=False, stop=99):
    import os
    stop = int(os.environ.get('KSTOP', stop))
    nc = bass.Bass("TRN2", target_bir_lowering=False)
    k = K(nc)
    k.sb_ptr = (nc.sbuf_base + 63) // 64 * 64
    k.sb_top = nc.sbuf_top

    def din(name, shape, dt=F32):
        return nc.dram_tensor(name, list(shape), dt, kind="ExternalInput").ap()

    def dout(name, shape, dt=F32):
        return nc.dram_tensor(name, list(shape), dt, kind="ExternalOutput").ap()

    def dscr(name, shape, dt=BF16):
        return nc.dram_tensor(name, list(shape), dt, kind="ExternalOutput" if debug else "Internal").ap()

    X = din("x", [TT, D])
    CK = din("cache_k", [256, 128])
    CV = din("cache_v", [256, 128])
    SC = din("state_C", [2, 4, 128, 128])
    SN = din("state_n", [2, 4, 128])
    SM = din("state_m", [2, 4])
    COND = din("cond", [2, D])
    WADA = din("w_ada", [D, 6 * D])
    BADA = din("b_ada", [6 * D])
    N1W = din("norm1_w", [D])
    WIN = din("w_in", [D, NIN])
    GB = din("gate_bias", [4, 4])
    QNW = din("q_norm_w", [64])
    KNW = din("k_norm_w", [64])
    MNW = din("mlstm_norm_w", [512])
    WOUT = din("w_out", [D, D])
    N2W = din("norm2_w", [D])
    WGU = din("w_gu", [D, 2 * DFF])
    WDN = din("w_down", [DFF, D])
    FNW = din("final_norm_w", [D])
    ROPE = din("rope", [128, 32, 2, 64])
    IDENT = din("ident", [128, 128])
    MASKS = din("masks", [128, 2, 512])

    Y = dout("y", [TT, D])
    NK = dout("nk", [512, 128])
    NV = dout("nv", [512, 128])
    NC_ = dout("nC", [2, 2, 4, 128, 128])
    NN = dout("nn", [2, 2, 4, 128])
    NM = dout("nm", [2, 2, 4])

    MODS = dscr("mods", [2, 6 * D], F32)
    QTd = dscr("QTd", [4, 128, TT])
    MQTd = dscr("MQTd", [4, 128, TT])
    MKTd = dscr("MKTd", [4, 128, TT])
    MKd = dscr("MKd", [TT, 512])
    MVd = dscr("MVd", [TT, 520])
    MOd = dscr("MOd", [TT, 512])
    GATd = dscr("GATd", [4, 4, TT], F32)
    AOTd = dscr("AOTd", [4, 128, TT])
    HMTd = dscr("HMTd", [4, 128, TT])
    H2Td = dscr("H2Td", [8, 128, TT])
    X1d = dscr("X1d", [TT, D], F32)

    PSALL = nc.alloc_psum_tensor("psall", [128, 4096], F32)
    PS = [Tl(PSALL[:, i * 512:(i + 1) * 512]) for i in range(8)]

    ident = k.sb([128, 128], F32, "ident")
    ones4 = k.sb([4, 128], F32, "ones4")
    eye4 = k.sb([4, 4], F32, "eye4")
    modc = k.sb([128, 2, 6, 8], F32, "modc")
    G1 = k.sb([128, 2, 8], F32, "G1")
    G2 = k.sb([128, 2, 8], F32, "G2")
    n1c = k.sb([128, 8], F32, "n1c")
    n2c = k.sb([128, 8], F32, "n2c")
    k.mhalf = k.sb([128, 8], F32, "mhalf")
    k.op("dve", lambda e: e.memset(k.mhalf[:, :], -0.5), writes=[k.mhalf])
    sb_phase0 = k.sb_ptr
    KT2 = k.sb([128, 2, TT + 256], BF16, "KT2")
    VA = k.sb([128, NKEY, 2, 192], BF16, "VA")
    sb_phase0_a = k.sb_ptr
    Win = k.sb([128, 8, NIN], BF16, "Win")
    Wg = k.sb([128, 8, 128], BF16, "Wg")
    stgw = [k.sb([128, NIN // 2], F32, f"stgw{i}") for i in range(2)]
    k.sb_phase = k.sb_ptr
    k.op("pool", lambda e: e.memset(Wg[:, :, :], 0.0), writes=[Wg])

    k.dma(ident[:, :], IDENT[:, :], writes=[ident])
    k.op("dve", lambda e: e.memset(ones4[:, :], 1.0), writes=[ones4])
    k.dma(eye4[:, :], IDENT[0:4, 0:4], writes=[eye4])

    condT = k.sb([128, 8, 2], F32, "condT")
    sil = k.sb([128, 8, 2], F32, "sil")
    tmpc = k.sb([128, 8, 2], F32, "tmpc")
    mods_sb = k.sb([2, 6 * D], F32, "mods_sb")
    bada = k.sb([2, 6 * D], F32, "bada")
    wa = [k.sb([128, 512], F32, f"wa{i}") for i in range(4)]
    for j in range(2):
        k.dma(condT[:, :, j], COND[j].rearrange("(c p) -> p c", p=128), writes=[condT], slow=True)
    for j in range(2):
        k.dma(bada[j:j + 1, :], BADA.rearrange("(o n) -> o n", o=1), writes=[bada])
    k.op("act", lambda e: e.activation(out=tmpc[:, :, :], in_=condT[:, :, :], func=AF.Exp, scale=-1.0),
         reads=[condT], writes=[tmpc])
    k.op("dve", lambda e: e.tensor_scalar_add(out=tmpc[:, :, :], in0=tmpc[:, :, :], scalar1=1.0),
         reads=[tmpc], writes=[tmpc])
    k.op("dve", lambda e: e.reciprocal(out=tmpc[:, :, :], in_=tmpc[:, :, :]), reads=[tmpc], writes=[tmpc])
    k.op("dve", lambda e: e.tensor_tensor(out=sil[:, :, :], in0=condT[:, :, :], in1=tmpc[:, :, :], op=ALU.mult),
         reads=[condT, tmpc], writes=[sil])
    win_steps = []
    HW = NIN // 2
    for kc in range(8):
        for hf in range(2):
            def step(kc=kc, hf=hf):
                st = stgw[hf]
                k.dma(st[:, :], WIN[kc * 128:(kc + 1) * 128, hf * HW:(hf + 1) * HW], writes=[st])
                k.op("pool", lambda e: e.tensor_copy(out=Win[:, kc, hf * HW:(hf + 1) * HW], in_=st[:, :]),
                     reads=[st], writes=[Win])
                if hf == 1:
                    k.op("pool", lambda e: e.tensor_copy(
                        out=Wg[:, kc, :].rearrange("p (j w) -> p j w", w=32)[:, :, 0:4],
                        in_=st[:, HW - 16:HW].rearrange("p (j w) -> p j w", w=4)), reads=[st], writes=[Wg])
            win_steps.append(step)
    it = 0
    for n in range(12):
        pb = PS[n % 2]
        for kc in range(8):
            w_t = wa[it % 4]
            if it % 6 == 0 and win_steps:
                win_steps.pop(0)()
            it += 1
            k.dma(w_t[:, :], WADA[kc * 128:(kc + 1) * 128, n * 512:(n + 1) * 512], writes=[w_t])
            k.op("pe", lambda e, w_t=w_t, kc=kc, pb=pb: e.matmul(pb[0:2, :], lhsT=sil[:, kc, :], rhs=w_t[:, :],
                                                                start=(kc == 0), stop=(kc == 7)),
                 reads=[w_t, sil], writes=[pb], inc=True)
        k.op("dve", lambda e, n=n, pb=pb: e.tensor_tensor(out=mods_sb[:, n * 512:(n + 1) * 512], in0=pb[0:2, :],
                                                          in1=bada[:, n * 512:(n + 1) * 512], op=ALU.add),
             reads=[pb, bada], writes=[mods_sb])
    while win_steps:
        win_steps.pop(0)()
    k.dma(MODS[:, :], mods_sb[:, :], reads=[mods_sb])
    k.barrier()
    for j in range(2):
        for s6 in range(6):
            k.dma(modc[:, j, s6, :], MODS[j, s6 * D:(s6 + 1) * D].rearrange("(c p) -> p c", p=128),
                  writes=[modc], slow=True)
    k.dma(n1c[:, :], N1W.rearrange("(c p) -> p c", p=128), writes=[n1c], slow=True)
    k.dma(n2c[:, :], N2W.rearrange("(c p) -> p c", p=128), writes=[n2c], slow=True)
    for j in range(2):
        k.op("dve", lambda e, j=j: e.scalar_tensor_tensor(out=G1[:, j, :], in0=modc[:, j, 1, :], scalar=1.0,
                                                          in1=n1c[:, :], op0=ALU.add, op1=ALU.mult),
             reads=[modc, n1c], writes=[G1])
        k.op("dve", lambda e, j=j: e.scalar_tensor_tensor(out=G2[:, j, :], in0=modc[:, j, 4, :], scalar=1.0,
                                                          in1=n2c[:, :], op0=ALU.add, op1=ALU.mult),
             reads=[modc, n2c], writes=[G2])
    k.barrier()
    k.phase_mem()
    ctx = dict(locals())
    if stop >= 1:
        phase_a(ctx)
    if stop >= 2:
        phase_b1(ctx)
    if stop >= 3:
        phase_b2(ctx)
    if stop >= 4:
        phase_c1(ctx)
    if stop >= 5:
        phase_c2(ctx)
    k.barrier()

    with nc.allow_low_precision(reason="bf16 matmul operands by design"), nc.Block() as block:
        names = {"pe": "tensor", "act": "scalar", "dve": "vector", "pool": "gpsimd", "sp": "sync"}
        for e in K.ENG:
            getattr(block, names[e])(lambda eng, e=e: k.emit(e, eng))
    return nc


class NS:
    def __init__(s, d):
        s.__dict__.update(d)


def rstd_from_ss(k, ss, out, n_inv, width):
    k.op("act", lambda e: e.activation(out=out[:, 0:width], in_=ss[:, 0:width], func=AF.Ln, scale=n_inv, bias=EPS),
         reads=[ss], writes=[out])
    k.op("act", lambda e: e.activation(out=out[:, 0:width], in_=out[:, 0:width], func=AF.Exp, scale=-0.5),
         reads=[out], writes=[out])


def norm_to_hT(k, c, xt, hT, col0, Gc, SHc, bufs, pA, pB, defer=None):
    junk, ss, rstd, xs = bufs
    k.op("dve", lambda e: e.scalar_tensor_tensor(out=junk[:, :], in0=xt[:, :], scalar=1.0, in1=xt[:, :],
                                                 op0=ALU.mult, op1=ALU.mult, accum_out=ss[:, 0:1]),
         reads=[xt], writes=[junk, ss])
    rstd_from_ss(k, ss, rstd, 1.0 / D, 1)
    k.op("pool", lambda e: e.tensor_scalar(out=xs[:, :], in0=xt[:, :], scalar1=rstd[:, 0:1], scalar2=1.0,
                                           op0=ALU.mult, op1=ALU.mult),
         reads=[xt, rstd], writes=[xs])

    def pe_part():
        for half, pb in ((0, pA), (1, pB)):
            for cc in range(4):
                ch = half * 4 + cc
                k.op("pe", lambda e, ch=ch, cc=cc, pb=pb: e.transpose(pb[:, cc * 128:(cc + 1) * 128],
                                                                      xs[:, ch * 128:(ch + 1) * 128], c.ident[:, :]),
                     reads=[xs, c.ident], writes=[pb], inc=(cc == 3))
            for cc in range(4):
                ch = half * 4 + cc
                k.op("act", lambda e, ch=ch, cc=cc, pb=pb: e.activation(
                    out=hT[:, ch, col0:col0 + 128], in_=pb[:, cc * 128:(cc + 1) * 128], func=AF.Identity,
                    scale=Gc(ch), bias=SHc(ch)), reads=[pb, c.G1, c.G2, c.modc], writes=[hT])

    if defer is None:
        pe_part()
    else:
        defer.append(pe_part)


def phase_a(ctx):
    c = NS(ctx)
    k = c.k
    PS = c.PS
    KT2, VA, Win, Wg = c.KT2, c.VA, c.Win, c.Wg
    k.op("pool", lambda e: e.memset(VA[:, :, :, :], 1.0), writes=[VA])
    xb = [k.sb([128, D], F32, f"xb{i}") for i in range(3)]
    junk = k.sb([128, D], BF16, "junk")
    xs = [k.sb([128, D], F32, f"xs{i}") for i in range(3)]
    ss = k.sb([128, 8], F32, "ss")
    rstd = k.sb([128, 8], F32, "rstd")
    hT = [k.sb([128, 8, 512], BF16, f"hT{i}") for i in range(2)]
    ropet = [k.sb([128, 4, 2, 64], F32, f"rope{i}") for i in range(2)]
    wq_bc = k.sb([128, 64], F32, "wq_bc")
    wk_bc = k.sb([128, 64], F32, "wk_bc")
    gbcol = k.sb([128, 1], F32, "gbcol")
    qf = k.sb([128, 512], F32, "qf")
    sq = k.sb([128, 512], F32, "sq")
    qn2 = [k.sb([128, 512], F32, f"qn{i}") for i in range(2)]
    t1 = k.sb([128, 512], F32, "t1")
    t2 = k.sb([128, 512], F32, "t2")
    qr2 = [k.sb([128, 512], F32, f"qr{i}") for i in range(2)]
    kvf = [k.sb([128, 256], F32, f"kvf{i}") for i in range(2)]
    kn = [k.sb([128, 128], F32, f"kn{i}") for i in range(2)]
    kt1 = k.sb([128, 128], F32, "kt1")
    kt2 = k.sb([128, 128], F32, "kt2")
    kr = k.sb([128, 128], F32, "kr")
    kdup2 = [k.sb([128, 2, 2, 64], F32, f"kdup{i}") for i in range(2)]
    ssq = k.sb([128, 8], F32, "ssq")
    rsq = k.sb([128, 8], F32, "rsq")
    ssk = k.sb([128, 8], F32, "ssk")
    rsk = k.sb([128, 8], F32, "rsk")
    mot = k.sb([128, 512], F32, "mot")
    gsb = k.sb([128, 512], F32, "gsb")
    gtmp = k.sb([128, 512], F32, "gtmp")
    cstg = k.sb([128, 2, 128], F32, "cstg")
    QTs = [k.sb([128, 4, 512], BF16, f"QTs{i}") for i in range(1)]
    MKs = [k.sb([128, 4, 512], BF16, f"MKs{i}") for i in range(1)]
    MVs = [k.sb([128, 4, 4, 130], BF16, f"MVs{i}") for i in range(1)]
    MOs = [k.sb([128, 4, 512], BF16, f"MOs{i}") for i in range(1)]
    MQTs = [k.sb([128, 4, 512], BF16, f"MQTs{i}") for i in range(1)]
    MKTs = [k.sb([128, 4, 512], BF16, f"MKTs{i}") for i in range(1)]

    k.dma(wq_bc[:, :], c.QNW.partition_broadcast(128), writes=[wq_bc])
    k.dma(wk_bc[:, :], c.KNW.partition_broadcast(128), writes=[wk_bc])
    k.op("dve", lambda e: e.tensor_scalar_mul(out=wq_bc[:, :], in0=wq_bc[:, :], scalar1=0.125),
         reads=[wq_bc], writes=[wq_bc])
    k.op("dve", lambda e: e.memset(gbcol[:, :], 0.0), writes=[gbcol])
    for j in range(4):
        k.dma(gbcol[32 * j:32 * j + 4, 0:1], c.GB[j].rearrange("(h o) -> h o", o=1), writes=[gbcol], slow=True)
    k.op("pool", lambda e: e.memset(MVs[0][:, :, :, :], 1.0), writes=[MVs[0]])
    kdup = kdup2[0]
    for blk in range(2):
        k.dma(cstg[:, 0, :], c.CK[blk * 128:(blk + 1) * 128, :], writes=[cstg])
        k.dma(cstg[:, 1, :], c.CV[blk * 128:(blk + 1) * 128, :], writes=[cstg])
        k.op("dve", lambda e: e.tensor_copy(out=kdup[:, :, :, :],
                                            in_=cstg[:, 0, :].rearrange("p (g o d) -> p g o d", g=2, o=1)
                                            .to_broadcast([128, 2, 2, 64])), reads=[cstg], writes=[kdup])
        pk = PS[2]
        for g in range(2):
            k.op("pe", lambda e, g=g: e.transpose(pk[:, g * 128:(g + 1) * 128],
                                                  kdup[:, g, :, :].rearrange("p a d -> p (a d)"), c.ident[:, :]),
                 reads=[kdup, c.ident], writes=[pk], inc=(g == 1))
        col = TT + blk * 128
        k.op("act", lambda e, col=col: e.activation(out=KT2[:, :, col:col + 128],
                                                    in_=pk[:, 0:256].rearrange("p (g t) -> p g t", g=2),
                                                    func=AF.Copy), reads=[pk], writes=[KT2])
        k.op("dve", lambda e, blk=blk: e.tensor_copy(out=VA[:, 36 + blk, :, 64:128],
                                                     in_=cstg[:, 1, :].rearrange("p (g d) -> p g d", g=2)),
             reads=[cstg], writes=[VA])

    class Deferred(list):
        cur = 0

        def append(self, fn):
            list.append(self, (self.cur, fn))

    deferred = Deferred()

    def flush(upto=10 ** 9):
        while deferred and deferred[0][0] <= upto:
            deferred.pop(0)[1]()

    def tile_a(s, i):
        cj = 0 if s < 8 else 1
        ti = s * 4 + i
        xt = xb[ti % 3]
        k.dma(xt[:, :], c.X[ti * 128:(ti + 1) * 128, :], writes=[xt])
        norm_to_hT(k, c, xt, hT[s % 2], i * 128,
                   lambda ch, cj=cj: c.G1[:, cj, ch:ch + 1], lambda ch, cj=cj: c.modc[:, cj, 0, ch:ch + 1],
                   (junk, ss, rstd, xs[ti % 3]), PS[0], PS[1], defer=deferred)

    def st_body(s):
        cj = 0 if s < 8 else 1
        smp = s < 8
        h = hT[s % 2]
        if smp:
            rp = ropet[s % 2]
            k.dma(rp[:, :, :, :], c.ROPE[:, s * 4:(s + 1) * 4, :, :], writes=[rp])
        QT_, MK_, MV_, MO_, MQT_, MKT_ = QTs[0], MKs[0], MVs[0], MOs[0], MQTs[0], MKTs[0]
        if s == 0:
            for i in range(4):
                tile_a(0, i)
                flush()

        def tile_b(i):
            ti = s * 4 + i
            tsl = slice(i * 128, (i + 1) * 128)
            pq, pkv, pmk, pmv, pmo = PS[2], PS[3], PS[4], PS[5], PS[6]
            for (pb, c0, c1) in ((pmk, 1280, 1792), (pmv, 1792, 2304), (pmo, 2304, 2816), (pq, 0, 512),
                                 (pkv, 512, 768)):
                for kc in range(8):
                    k.op("pe", lambda e, pb=pb, c0=c0, c1=c1, kc=kc, tsl=tsl: e.matmul(
                        pb[:, 0:c1 - c0], lhsT=h[:, kc, tsl], rhs=Win[:, kc, c0:c1], start=(kc == 0), stop=(kc == 7)),
                        reads=[h, Win], writes=[pb], inc=(kc == 7))
            flush(ti - 2)
            deferred.cur = ti
            qn, qr, kdup = qn2[ti % 2], qr2[ti % 2], kdup2[ti % 2]
            k.capture()
            k.op("act", lambda e, i=i: e.activation(out=MK_[:, i, :], in_=pmk[:, :], func=AF.Copy, scale=KSC),
                 reads=[pmk], writes=[MK_])
            k.op("dve", lambda e, i=i: e.tensor_copy(out=MV_[:, i, :, 0:128],
                                                     in_=pmv[:, :].rearrange("p (h d) -> p h d", h=4)),
                 reads=[pmv], writes=[MV_])
            k.op("act", lambda e: e.activation(out=mot[:, :], in_=pmo[:, :], func=AF.Exp, scale=-1.0),
                 reads=[pmo], writes=[mot])
            k.op("act", lambda e: e.activation(out=mot[:, :], in_=mot[:, :], func=AF.Ln, bias=1.0),
                 reads=[mot], writes=[mot])
            k.op("act", lambda e, i=i: e.activation(out=MO_[:, i, :], in_=mot[:, :], func=AF.Exp, scale=-1.0),
                 reads=[mot], writes=[MO_])
            Lm = k.capture_end()
            kv = kvf[ti % 2]
            kk = kn[ti % 2]
            k.capture()
            k.op("act", lambda e: e.activation(out=qf[:, :], in_=pq[:, :], func=AF.Copy), reads=[pq], writes=[qf])
            k.op("dve", lambda e: e.tensor_tensor(out=sq[:, :], in0=qf[:, :], in1=qf[:, :], op=ALU.mult),
                 reads=[qf], writes=[sq])
            k.op("dve", lambda e: e.tensor_reduce(out=ssq[:, 0:8], in_=sq[:, :].rearrange("p (h d) -> p h d", d=64),
                                                  axis=AX.X, op=ALU.add), reads=[sq], writes=[ssq])
            rstd_from_ss(k, ssq, rsq, 1.0 / 64, 8)
            k.op("dve", lambda e: e.tensor_tensor(out=qn[:, :].rearrange("p (h d) -> p h d", d=64),
                                                  in0=qf[:, :].rearrange("p (h d) -> p h d", d=64),
                                                  in1=rsq[:, 0:8].unsqueeze(2).to_broadcast([128, 8, 64]),
                                                  op=ALU.mult), reads=[qf, rsq], writes=[qn])
            k.op("dve", lambda e: e.tensor_tensor(out=qn[:, :].rearrange("p (h d) -> p h d", d=64),
                                                  in0=qn[:, :].rearrange("p (h d) -> p h d", d=64),
                                                  in1=wq_bc[:, :].unsqueeze(1).to_broadcast([128, 8, 64]),
                                                  op=ALU.mult), reads=[qn, wq_bc], writes=[qn])
            if smp:
                k.op("pool", lambda e, i=i: e.tensor_tensor(
                    out=t1[:, :].rearrange("p (h d) -> p h d", d=64),
                    in0=qn[:, :].rearrange("p (h d) -> p h d", d=64),
                    in1=rp[:, i, 0, :].unsqueeze(1).to_broadcast([128, 8, 64]), op=ALU.mult),
                    reads=[qn, rp], writes=[t1])
                for jj in range(2):
                    k.op("pool", lambda e, i=i, jj=jj: e.tensor_tensor(
                        out=t2[:, :].rearrange("p (h a j w) -> p h a j w", h=8, a=2, j=2)[:, :, :, jj, :],
                        in0=qn[:, :].rearrange("p (h a j w) -> p h a j w", h=8, a=2, j=2)[:, :, :, 1 - jj, :],
                        in1=rp[:, i, 1, :].rearrange("p (a j w) -> p a j w", a=2, j=2)[:, :, jj, :].unsqueeze(1)
                        .to_broadcast([128, 8, 2, 16]), op=ALU.mult),
                        reads=[qn, rp], writes=[t2])
                k.op("pool", lambda e: e.tensor_tensor(out=qr[:, :], in0=t1[:, :], in1=t2[:, :], op=ALU.add),
                     reads=[t1, t2], writes=[qr])
                qsrc = qr
            else:
                qsrc = qn
            def q_tr(qsrc=qsrc, tsl=tsl):
                pt = PS[7]
                for j in range(4):
                    k.op("pe", lambda e, j=j: e.transpose(pt[:, j * 128:(j + 1) * 128],
                                                          qsrc[:, j * 128:(j + 1) * 128], c.ident[:, :]),
                         reads=[qsrc, c.ident], writes=[pt], inc=(j == 3))
                k.op("act", lambda e: e.activation(out=QT_[:, :, tsl],
                                                   in_=pt[:, :].rearrange("p (j t) -> p j t", j=4),
                                                   func=AF.Copy), reads=[pt], writes=[QT_])
            deferred.append(q_tr)
            Lq = k.capture_end()
            k.capture()
            k.op("act", lambda e, kv=kv: e.activation(out=kv[:, :], in_=pkv[:, 0:256], func=AF.Copy),
                 reads=[pkv], writes=[kv])
            k.op("dve", lambda e, kv=kv: e.tensor_tensor(out=kt1[:, :], in0=kv[:, 0:128], in1=kv[:, 0:128],
                                                         op=ALU.mult), reads=[kv], writes=[kt1])
            k.op("dve", lambda e: e.tensor_reduce(out=ssk[:, 0:2], in_=kt1[:, :].rearrange("p (h d) -> p h d", d=64),
                                                  axis=AX.X, op=ALU.add), reads=[kt1], writes=[ssk])
            rstd_from_ss(k, ssk, rsk, 1.0 / 64, 2)
            k.op("dve", lambda e, kv=kv, kk=kk: e.tensor_tensor(
                out=kk[:, :].rearrange("p (h d) -> p h d", d=64),
                in0=kv[:, 0:128].rearrange("p (h d) -> p h d", d=64),
                in1=rsk[:, 0:2].unsqueeze(2).to_broadcast([128, 2, 64]), op=ALU.mult),
                reads=[kv, rsk], writes=[kk])
            k.op("dve", lambda e, kk=kk: e.tensor_tensor(
                out=kk[:, :].rearrange("p (h d) -> p h d", d=64),
                in0=kk[:, :].rearrange("p (h d) -> p h d", d=64),
                in1=wk_bc[:, :].unsqueeze(1).to_broadcast([128, 2, 64]), op=ALU.mult),
                reads=[kk, wk_bc], writes=[kk])
            if smp:
                k.op("pool", lambda e, i=i, kk=kk: e.tensor_tensor(
                    out=kt1[:, :].rearrange("p (h d) -> p h d", d=64),
                    in0=kk[:, :].rearrange("p (h d) -> p h d", d=64),
                    in1=rp[:, i, 0, :].unsqueeze(1).to_broadcast([128, 2, 64]), op=ALU.mult),
                    reads=[kk, rp], writes=[kt1])
                for jj in range(2):
                    k.op("pool", lambda e, i=i, kk=kk, jj=jj: e.tensor_tensor(
                        out=kt2[:, :].rearrange("p (h a j w) -> p h a j w", h=2, a=2, j=2)[:, :, :, jj, :],
                        in0=kk[:, :].rearrange("p (h a j w) -> p h a j w", h=2, a=2, j=2)[:, :, :, 1 - jj, :],
                        in1=rp[:, i, 1, :].rearrange("p (a j w) -> p a j w", a=2, j=2)[:, :, jj, :].unsqueeze(1)
                        .to_broadcast([128, 2, 2, 16]), op=ALU.mult),
                        reads=[kk, rp], writes=[kt2])
                k.op("pool", lambda e: e.tensor_tensor(out=kr[:, :], in0=kt1[:, :], in1=kt2[:, :], op=ALU.add),
                     reads=[kt1, kt2], writes=[kr])
                ksrc = kr
            else:
                ksrc = kk
                pt0 = (ti - 32) * 128
                k.dma(c.NK[pt0:pt0 + 128, :], kk[:, :], reads=[kk])
                k.dma(c.NV[pt0:pt0 + 128, :], kv[:, 128:256], reads=[kv])
            k.op("dve", lambda e, ksrc=ksrc: e.tensor_copy(
                out=kdup[:, :, :, :], in_=ksrc[:, :].rearrange("p (g o d) -> p g o d", g=2, o=1)
                .to_broadcast([128, 2, 2, 64])), reads=[ksrc], writes=[kdup])
            def k_tr(ti=ti):
                pk = PS[7]
                for g in range(2):
                    k.op("pe", lambda e, g=g: e.transpose(pk[:, g * 128:(g + 1) * 128],
                                                          kdup[:, g, :, :].rearrange("p a d -> p (a d)"),
                                                          c.ident[:, :]),
                         reads=[kdup, c.ident], writes=[pk], inc=(g == 1))
                k.op("act", lambda e: e.activation(out=KT2[:, :, ti * 128:(ti + 1) * 128],
                                                   in_=pk[:, 0:256].rearrange("p (g t) -> p g t", g=2),
                                                   func=AF.Copy), reads=[pk], writes=[KT2])
            deferred.append(k_tr)
            k.op("dve", lambda e, ti=ti, kv=kv: e.tensor_copy(out=VA[:, ti, :, 64:128],
                                                              in_=kv[:, 128:256].rearrange("p (g d) -> p g d", g=2)),
                 reads=[kv], writes=[VA])
            Lk = k.capture_end()
            k.emit_interleaved(Lm, Lq, Lk)

        for i in range(4):
            tile_b(i)
            if s + 1 < 9:
                tile_a(s + 1, i)
        for hh in range(8):
            pb = PS[2 + hh % 4]
            c0 = 768 + hh * 128
            for kc in range(8):
                k.op("pe", lambda e, pb=pb, c0=c0, kc=kc: e.matmul(
                    pb[:, :], lhsT=Win[:, kc, c0:c0 + 128], rhs=h[:, kc, :], start=(kc == 0), stop=(kc == 7)),
                    reads=[h, Win], writes=[pb], inc=(kc == 7))
            if hh == 3:
                flush()
            if hh < 4:
                k.op("dve", lambda e, pb=pb, hh=hh: e.tensor_copy(out=MQT_[:, hh, :], in_=pb[:, :]),
                     reads=[pb], writes=[MQT_])
            else:
                k.op("act", lambda e, pb=pb, hh=hh: e.activation(out=MKT_[:, hh - 4, :], in_=pb[:, :], func=AF.Copy,
                                                                 scale=KSC), reads=[pb], writes=[MKT_])
        pg = PS[6]
        for kc in range(8):
            k.op("pe", lambda e, kc=kc: e.matmul(pg[:, :], lhsT=Wg[:, kc, :], rhs=h[:, kc, :], start=(kc == 0),
                                                 stop=(kc == 7)), reads=[h, Wg], writes=[pg], inc=(kc == 7))
        k.op("act", lambda e: e.activation(out=gsb[:, :], in_=pg[:, :], func=AF.Identity, bias=gbcol[:, 0:1]),
             reads=[pg, gbcol], writes=[gsb])
        for r0 in (32, 96):
            k.op("act", lambda e, r0=r0: e.activation(out=gtmp[r0:r0 + 4, :], in_=gsb[r0:r0 + 4, :], func=AF.Exp,
                                                      scale=-1.0), reads=[gsb], writes=[gtmp])
            k.op("act", lambda e, r0=r0: e.activation(out=gtmp[r0:r0 + 4, :], in_=gtmp[r0:r0 + 4, :], func=AF.Ln,
                                                      bias=1.0), reads=[gtmp], writes=[gtmp])
            k.op("dve", lambda e, r0=r0: e.tensor_scalar_mul(out=gsb[r0:r0 + 4, :], in0=gtmp[r0:r0 + 4, :],
                                                             scalar1=-1.0), reads=[gtmp], writes=[gsb])
        flush()
        tok = slice(s * 512, (s + 1) * 512)
        k.dma(c.QTd[:, :, tok].rearrange("c p t -> p c t"), QT_[:, :, :], reads=[QT_])
        k.dma(c.MQTd[:, :, tok].rearrange("c p t -> p c t"), MQT_[:, :, :], reads=[MQT_])
        k.dma(c.MKTd[:, :, tok].rearrange("c p t -> p c t"), MKT_[:, :, :], reads=[MKT_])
        k.dma(c.MKd[tok, :].rearrange("(i p) f -> p i f", p=128), MK_[:, :, :], reads=[MK_])
        k.dma(c.MVd[tok, :].rearrange("(i p) f -> p i f", p=128), MV_[:, :, :, :].rearrange("p i h d -> p i (h d)"),
              reads=[MV_])
        k.dma(c.MOd[tok, :].rearrange("(i p) f -> p i f", p=128), MO_[:, :, :], reads=[MO_])
        for j in range(4):
            k.dma(c.GATd[j, :, tok], gsb[32 * j:32 * j + 4, :], reads=[gsb])
    for s in range(9):
        st_body(s)
    k.barrier()
    k.sb_phase = c.sb_phase0_a
    k.phase_mem()


def _consts():
    f = np.float32
    tok = np.arange(4096)
    row = (tok // 64).astype(f)
    col = (tok % 64).astype(f)
    inv = (f(10000.0) ** (-np.arange(0, 32, 2, dtype=f) / f(32))).astype(f)
    ang = np.stack([row[:, None] * inv[None, :], col[:, None] * inv[None, :]], axis=1).astype(f)
    cs, sn = np.cos(ang).astype(f), np.sin(ang).astype(f)
    Cf = np.stack([cs, cs], axis=2)
    Ss = np.stack([-sn, sn], axis=2)
    tab = np.stack([Cf.reshape(4096, 64), Ss.reshape(4096, 64)], axis=1)
    rope = np.ascontiguousarray(tab.reshape(32, 128, 2, 64).transpose(1, 0, 2, 3))
    ident = np.eye(128, dtype=f)
    sidx = np.arange(128)[:, None]
    lidx = np.arange(128)[None, :]
    mf = (lidx >= sidx).astype(f)
    mb = (lidx <= sidx).astype(f)
    masks = np.stack([np.tile(mf, (1, 4)), np.tile(mb, (1, 4))], axis=1)
    return rope, ident, np.ascontiguousarray(masks)


def make_in_maps(inp):
    rope, ident, masks = _consts()
    f = np.float32
    c = lambda a: np.ascontiguousarray(np.asarray(a), dtype=f)
    maps = []
    for b in range(8):
        x = np.concatenate([inp["x_sample"][b], inp["x_prompt"][2 * b], inp["x_prompt"][2 * b + 1]], axis=0)
        m = {
            "x": c(x),
            "cache_k": c(np.asarray(inp["cache_k"])[b, 0].reshape(256, 128)),
            "cache_v": c(np.asarray(inp["cache_v"])[b, 0].reshape(256, 128)),
            "state_C": c(np.asarray(inp["state_C"])[b, 0]),
            "state_n": c(np.asarray(inp["state_n"])[b, 0]),
            "state_m": c(np.asarray(inp["state_m"])[b, 0]),
            "cond": c(np.stack([np.asarray(inp["c"])[b], np.asarray(inp["c_ctx"])], axis=0)),
            "w_ada": c(np.asarray(inp["w_ada"])[0]), "b_ada": c(np.asarray(inp["b_ada"])[0]),
            "norm1_w": c(np.asarray(inp["norm1_w"])[0]), "w_in": c(np.asarray(inp["w_in"])[0]),
            "gate_bias": c(np.asarray(inp["gate_bias"])[0]), "q_norm_w": c(np.asarray(inp["q_norm_w"])[0]),
            "k_norm_w": c(np.asarray(inp["k_norm_w"])[0]), "mlstm_norm_w": c(np.asarray(inp["mlstm_norm_w"])[0]),
            "w_out": c(np.asarray(inp["w_out"])[0]), "norm2_w": c(np.asarray(inp["norm2_w"])[0]),
            "w_gu": c(np.asarray(inp["w_gu"])[0]), "w_down": c(np.asarray(inp["w_down"])[0]),
            "final_norm_w": c(inp["final_norm_w"]),
            "rope": rope, "ident": ident, "masks": masks,
        }
        maps.append(m)
    return maps


_NC = None


def kernel(**inp):
    global _NC
    if _NC is None:
        _NC = build_nc()
    maps = make_in_maps(inp)
    res = run_bass_kernel_spmd(_NC, maps, core_ids=list(range(8)))
    R = res.results
    f = np.float32
    y_s = np.stack([R[b]["y"][0:4096] for b in range(8)], axis=0).astype(f)
    y_p = np.stack([R[b]["y"][4096 + 256 * j:4096 + 256 * (j + 1)] for b in range(8) for j in range(2)], axis=0).astype(f)
    nk = np.stack([R[b]["nk"][256 * j:256 * (j + 1)].reshape(256, 2, 64) for b in range(8) for j in range(2)], axis=0)
    nv = np.stack([R[b]["nv"][256 * j:256 * (j + 1)].reshape(256, 2, 64) for b in range(8) for j in range(2)], axis=0)
    nC = np.stack([R[b]["nC"][j] for b in range(8) for j in range(2)], axis=0)
    nn = np.stack([R[b]["nn"][j] for b in range(8) for j in range(2)], axis=0)
    nm = np.stack([R[b]["nm"][j] for b in range(8) for j in range(2)], axis=0)
    return (y_p, y_s, nk[:, None].astype(f), nv[:, None].astype(f), nC[:, None].astype(f), nn[:, None].astype(f),
            nm[:, None].astype(f))


def phase_b1(ctx):
    c = NS(ctx)
    k = c.k
    PS = c.PS
    KT2, VA = c.KT2, c.VA
    pre_gen, b2_main = make_b2(ctx)
    ctx["b2_main"] = b2_main
    k.sb_phase = k.sb_ptr
    pre = pre_gen()
    SP = [Tl(c.PSALL[:, b * 1024:(b + 1) * 1024]) for b in range(2)]
    QTg = [k.sb([128, 4, 512], BF16, f"QTg{i}") for i in range(2)]
    PTP = [k.sb([128, 1024], BF16, f"PT{i}") for i in range(3)]
    AO = [k.sb([128, 4, 512], BF16, f"AO{i}") for i in range(2)]
    rden = [k.sb([128, 512], F32, f"rden{i}") for i in range(2)]
    obs = k.sb([128, 512], F32, "obs")
    groups = [(g * 512, 512, list(range(32)) + [36, 37]) for g in range(8)]
    groups += [(4096, 256, [32, 33]), (4352, 256, [34, 35])]
    cnt = [0]

    def group_body(gi, t0, nq, kbs):
        qt = QTg[gi % 2]
        ao = AO[gi % 2]
        if gi == 0:
            k.dma(qt[:, :, 0:nq], c.QTd[:, :, t0:t0 + nq].rearrange("c p t -> p c t"), writes=[qt])
        if gi + 1 < len(groups):
            t0n, nqn, _ = groups[gi + 1]
            qtn = QTg[(gi + 1) % 2]
            k.dma(qtn[:, :, 0:nqn], c.QTd[:, :, t0n:t0n + nqn].rearrange("c p t -> p c t"), writes=[qtn])
        iters = [(j, idx, kb) for j in range(4) for idx, kb in enumerate(kbs)]
        base = cnt[0]
        cnt[0] += len(iters)

        def emit_s(n):
            j, idx, kb = iters[n]
            g = j // 2
            kcol = kb * 128 if kb < 36 else TT + (kb - 36) * 128
            sp = SP[(base + n) % 2]
            k.op("pe", lambda e: e.matmul(sp[:, 0:nq], lhsT=KT2[0:64, g, kcol:kcol + 128], rhs=qt[0:64, j, 0:nq],
                                          start=True, stop=True), reads=[KT2, qt], writes=[sp], inc=False)
            k.op("pe", lambda e: e.matmul(sp[:, 512:512 + nq], lhsT=KT2[64:128, g, kcol:kcol + 128],
                                          rhs=qt[64:128, j, 0:nq], start=True, stop=True),
                 reads=[KT2, qt], writes=[sp])

        def emit_rest(n):
            j, idx, kb = iters[n]
            g = j // 2
            sp = SP[(base + n) % 2]
            pt = PTP[(base + n) % 3]
            oa, ob = (PS[4], PS[6])[j % 2], PS[5]
            k.op("act", lambda e: e.activation(out=pt[:, :].rearrange("p (a t) -> p a t", a=2)[:, :, 0:nq],
                                               in_=sp[:, :].rearrange("p (a t) -> p a t", a=2)[:, :, 0:nq],
                                               func=AF.Exp), reads=[sp], writes=[pt])
            first, last = idx == 0, idx == len(kbs) - 1
            k.op("pe", lambda e: e.matmul(oa[:, 0:nq], lhsT=VA[:, kb, g, 64:192], rhs=pt[:, 0:nq], start=first,
                                          stop=last), reads=[VA, pt], writes=[oa], inc=False)
            k.op("pe", lambda e: e.matmul(ob[:, 0:nq], lhsT=VA[:, kb, g, 0:128], rhs=pt[:, 512:512 + nq],
                                          start=first, stop=last), reads=[VA, pt], writes=[ob])
            if last:
                ra, rb = rden[0], rden[1]
                k.op("dve", lambda e: e.tensor_copy(out=obs[:, 0:nq], in_=ob[:, 0:nq]), reads=[ob], writes=[obs])
                k.op("dve", lambda e: e.reciprocal(out=rb[64:128, 0:nq], in_=obs[0:64, 0:nq]), reads=[obs],
                     writes=[rb])
                k.op("dve", lambda e: e.tensor_tensor(out=ao[64:128, j, 0:nq], in0=obs[64:128, 0:nq],
                                                      in1=rb[64:128, 0:nq], op=ALU.mult), reads=[obs, rb],
                     writes=[ao])
                k.op("dve", lambda e: e.reciprocal(out=ra[0:64, 0:nq], in_=oa[64:128, 0:nq]), reads=[oa], writes=[ra])
                k.op("dve", lambda e: e.tensor_tensor(out=ao[0:64, j, 0:nq], in0=oa[0:64, 0:nq], in1=ra[0:64, 0:nq],
                                                      op=ALU.mult), reads=[oa, ra], writes=[ao])

        emit_s(0)
        for n in range(len(iters)):
            if n + 1 < len(iters):
                emit_s(n + 1)
            emit_rest(n)
            if PRE_IN_ATT:
                next(pre, None)
        k.dma(c.AOTd[:, :, t0:t0 + nq].rearrange("c p t -> p c t"), ao[:, :, 0:nq], reads=[ao])

    for gi, (t0, nq, kbs) in enumerate(groups):
        group_body(gi, t0, nq, kbs)
    for _ in pre:
        pass
    k.barrier()
    k.phase_mem()


def make_b2(ctx):
    c = NS(ctx)
    k = c.k
    PS = c.PS
    pS = PS[0]
    pGd, pId, pTd = (PS[1], PS[4]), (PS[2], PS[5]), (PS[3], PS[6])
    pX = PS[7]
    Cst = [k.sb([128, 4, 130], F32, f"Cst{i}") for i in range(2)]
    Csnap = k.sb([128, NT, 4, 130], BF16, "Csnap")
    mst = [k.sb([4, 2], F32, f"mst{i}") for i in range(2)]
    mbprev = k.sb([4, NT + 4], F32, "mbprev")
    GT = [k.sb([4, 4, 128], F32, f"GT{i}") for i in range(2)]
    kk = [k.sb([128, 4, 128], BF16, f"kk{i}") for i in range(2)]
    va = [k.sb([128, 4, 130], BF16, f"va{i}") for i in range(2)]
    rows = [[k.sb([4, 128], F32, f"row{d}_{i}") for i in range(8)] for d in range(2)]
    dg = [k.sb([4, 4], F32, f"dg{d}") for d in range(2)]
    cols = [k.sb([128, 24], F32, f"cols{d}") for d in range(2)]
    kw = k.sb([128, 512], BF16, "kw")
    cols1p = [cols[1], k.sb([128, 24], F32, "cols1b")]
    ident, ones4, eye4 = c.ident, c.ones4, c.eye4
    maskneg = identb = None
    masks = mnw = Cbf = qT = kT = mo = blk = Dsb = swT = qI = hd = hm = sqh = ss4 = rs4 = hmT = None

    def late_alloc():
        nonlocal maskneg, identb
        nonlocal masks, mnw, Cbf, qT, kT, mo, blk, Dsb, swT, qI, hd, hm, sqh, ss4, rs4, hmT
        masks = k.sb([128, 2, 512], F32, "masks")
        mnw = k.sb([128, 512], F32, "mnw")
        Cbf = k.sb([128, 4, 130], BF16, "Cbf")
        qT = [k.sb([128, 4, 128], BF16, f"qT{i}") for i in range(2)]
        kT = [k.sb([128, 4, 128], BF16, f"kT{i}") for i in range(2)]
        mo = [k.sb([128, 512], BF16, f"mo{i}") for i in range(2)]
        blk = [[k.sb([4, 4, 128], F32, f"blk{d}_{i}") for i in range(2)] for d in range(2)]
        Dsb = [k.sb([128, 512], F32, f"Dsb{d}") for d in range(2)]
        swT = [k.sb([128, 512], BF16, f"swT{d}") for d in range(2)]
        qI = [k.sb([128, 512], BF16, f"qI{d}") for d in range(2)]
        hd = [[k.sb([128, 512], F32, f"hd{p}_{d}") for d in range(2)] for p in range(2)]
        hm = k.sb([128, 512], F32, "hm")
        sqh = k.sb([128, 512], F32, "sqh")
        ss4 = k.sb([128, 8], F32, "ss4")
        rs4 = k.sb([128, 8], F32, "rs4")
        hmT = [k.sb([128, 4, 128], BF16, f"hmT{i}") for i in range(2)]
        maskneg = k.sb([128, 2, 512], BF16, "maskneg")
        identb = k.sb([128, 128], BF16, "identb")
        k.dma(masks[:, :, :], c.MASKS[:, :, :], writes=[masks])
        k.dma(mnw[:, :], c.MNW.partition_broadcast(128), writes=[mnw])
        k.op("dve", lambda e: e.tensor_scalar(out=maskneg[:, :, :], in0=masks[:, :, :], scalar1=-1.0, scalar2=30000.0,
                                              op0=ALU.add, op1=ALU.mult), reads=[masks], writes=[maskneg])
        k.op("dve", lambda e: e.tensor_copy(out=identb[:, :], in_=ident[:, :]), reads=[ident], writes=[identb])

    def load_chunk(ci, full):
        tok = slice(ci * 128, (ci + 1) * 128)
        i2 = ci % 2
        k.dma(GT[i2][:, :, :], c.GATd[:, :, tok].rearrange("t h n -> h t n"), writes=[GT[i2]])
        k.dma(kk[i2][:, :, :], c.MKd[tok, :].rearrange("p (h d) -> p h d", h=4), writes=[kk[i2]])
        k.dma(va[i2][:, :, :], c.MVd[tok, :].rearrange("p (h d) -> p h d", h=4), writes=[va[i2]])
        if full:
            k.dma(qT[i2][:, :, :], c.MQTd[:, :, tok].rearrange("h p t -> p h t"), writes=[qT[i2]])
            k.dma(kT[i2][:, :, :], c.MKTd[:, :, tok].rearrange("h p t -> p h t"), writes=[kT[i2]])

    def load_mo(ci):
        tok = slice(ci * 128, (ci + 1) * 128)
        k.dma(mo[ci % 2][:, :], c.MOd[tok, :], writes=[mo[ci % 2]])

    def gate_prep(d, gt, mprev, full, upd, cl=None, pT=None):
        mt, mc = mprev
        rb_, ra_, rg_, rng_, rin_, rgu_, rw_, rt_ = rows[d]
        pT = pT or pTd[d]
        cl = cl or cols[d]
        lf = lambda: gt[:, 1 + 2 * d, :]
        ig = lambda: gt[:, 2 * d, :]
        rv = (lambda ap: ap) if d == 0 else (lambda ap: ap[:, ::-1])
        last = 127 if d == 0 else 0
        k.op("dve", lambda e: e.tensor_tensor_scan(out=rv(rb_[:, :]), data0=rv(ones4[:, :]), data1=rv(lf()),
                                                   initial=0.0, op0=ALU.mult, op1=ALU.add),
             reads=[gt, ones4], writes=[rb_])
        yield
        k.op("dve", lambda e: e.tensor_tensor(out=ra_[:, :], in0=ig(), in1=rb_[:, :], op=ALU.subtract),
             reads=[gt, rb_], writes=[ra_])
        yield
        k.op("pe", lambda e: e.transpose(pT[:, 0:4], ra_[:, :], ident[0:4, 0:4]), reads=[ra_, ident], writes=[pT])
        k.op("dve", lambda e: e.tensor_tensor_scan(out=rv(rg_[:, :]), data0=rv(ra_[:, :]), data1=rv(ra_[:, :]),
                                                   initial=mt[:, mc:mc + 1], op0=ALU.max, op1=ALU.max),
             reads=[ra_, mt], writes=[rg_])
        yield
        k.op("dve", lambda e: e.tensor_scalar_mul(out=rng_[:, :], in0=rg_[:, :], scalar1=-1.0),
             reads=[rg_], writes=[rng_])
        if full:
            k.op("dve", lambda e: e.tensor_tensor(out=rt_[:, :], in0=rb_[:, :], in1=rg_[:, :], op=ALU.add),
                 reads=[rb_, rg_], writes=[rt_])
            yield
            bk0 = blk[d][0]
            k.op("dve", lambda e: e.tensor_tensor(
                out=bk0[:, :, :], in0=rng_[:, :].unsqueeze(1).to_broadcast([4, 4, 128]),
                in1=eye4[:, :].unsqueeze(2).to_broadcast([4, 4, 128]), op=ALU.mult),
                reads=[rng_, eye4], writes=[bk0])
            yield
            k.op("pe", lambda e: e.matmul(pGd[d][:, :], lhsT=ones4[:, :],
                                          rhs=bk0[:, :, :].rearrange("k h l -> k (h l)"),
                                          start=True, stop=False), reads=[ones4, bk0], writes=[pGd[d]], inc=False)
            k.op("pe", lambda e: e.matmul(pGd[d][:, :], lhsT=identb[:, :], rhs=maskneg[:, d, :],
                                          start=False, stop=True), reads=[identb, maskneg], writes=[pGd[d]])
        yield
        k.op("act", lambda e: e.activation(out=rin_[:, :], in_=rng_[:, :], func=AF.Exp, bias=mt[:, mc:mc + 1]),
             reads=[rng_, mt], writes=[rin_])
        if full:
            yield
            bk1 = blk[d][1]
            k.op("dve", lambda e: e.tensor_tensor(
                out=bk1[:, :, :], in0=rin_[:, :].unsqueeze(1).to_broadcast([4, 4, 128]),
                in1=eye4[:, :].unsqueeze(2).to_broadcast([4, 4, 128]), op=ALU.mult),
                reads=[rin_, eye4], writes=[bk1])
            yield
            k.op("pe", lambda e: e.matmul(pId[d][:, :], lhsT=ones4[:, :],
                                          rhs=bk1[:, :, :].rearrange("k h l -> k (h l)"),
                                          start=True, stop=True), reads=[ones4, bk1], writes=[pId[d]])
        if full:
            k.op("act", lambda e: e.activation(out=rgu_[:, :], in_=rt_[:, :], func=AF.Exp, scale=-1.0),
                 reads=[rt_], writes=[rgu_])
        if upd:
            k.op("act", lambda e: e.activation(out=rw_[:, :], in_=ra_[:, :], func=AF.Exp,
                                               bias=rng_[:, last:last + 1]), reads=[ra_, rng_], writes=[rw_])
        yield
        if full:
            k.op("pe", lambda e: e.transpose(pT[:, 4:8], rgu_[:, :], ident[0:4, 0:4]), reads=[rgu_, ident],
                 writes=[pT])
        if upd:
            k.op("pe", lambda e: e.transpose(pT[:, 8:12], rw_[:, :], ident[0:4, 0:4]), reads=[rw_, ident],
                 writes=[pT])
            k.op("dve", lambda e: e.tensor_scalar(out=dg[d][:, :], in0=eye4[:, :], scalar1=rin_[:, last:last + 1],
                                                  scalar2=None, op0=ALU.mult), reads=[eye4, rin_], writes=[dg[d]])
            yield
            k.op("pe", lambda e: e.matmul(pT[:, 12:16], lhsT=ones4[:, :], rhs=dg[d][:, :], start=True, stop=True),
                 reads=[ones4, dg[d]], writes=[pT])
        yield
        hi = 16 if upd else 8
        if not full and upd:
            k.op("dve", lambda e: e.tensor_copy(out=cl[:, 0:4], in_=pT[:, 0:4]), reads=[pT], writes=[cl])
            k.op("dve", lambda e: e.tensor_copy(out=cl[:, 8:16], in_=pT[:, 8:16]), reads=[pT], writes=[cl])
        else:
            k.op("dve", lambda e: e.tensor_copy(out=cl[:, 0:hi], in_=pT[:, 0:hi]), reads=[pT], writes=[cl])
        yield

    def new_m(d, mt_out, mc_out):
        rb_, rg_ = rows[d][0], rows[d][2]
        last = 127 if d == 0 else 0
        k.op("dve", lambda e: e.tensor_tensor(out=mt_out[:, mc_out:mc_out + 1], in0=rb_[:, last:last + 1],
                                              in1=rg_[:, last:last + 1], op=ALU.add),
             reads=[rb_, rg_], writes=[mt_out])

    def state_update(d, kk_, va_, banks=None, cl=None):
        banks = banks or ((pId[d], 0), (pTd[d], 128))
        Cs = Cst[d]
        cl = cl or cols[d]
        k.op("dve", lambda e: e.tensor_tensor(out=kw[:, :].rearrange("p (h d) -> p h d", h=4), in0=kk_[:, :, :],
                                              in1=cl[:, 8:12].unsqueeze(2).to_broadcast([128, 4, 128]),
                                              op=ALU.mult), reads=[kk_, cl], writes=[kw])
        yield
        for h in range(4):
            pd, off = banks[h % 2]
            k.op("pe", lambda e, h=h, pd=pd, off=off: e.matmul(pd[:, off:off + 129], lhsT=kw[:, h * 128:(h + 1) * 128],
                                                      rhs=va_[:, h, 0:129], start=True, stop=True),
                 reads=[kw, va_], writes=[pd])
            yield
            k.op("dve", lambda e, h=h, pd=pd, off=off: e.scalar_tensor_tensor(
                out=Cs[:, h, 0:129], in0=Cs[:, h, 0:129], scalar=cl[:, 12 + h:13 + h], in1=pd[:, off:off + 129],
                op0=ALU.mult, op1=ALU.add), reads=[Cs, cl, pd], writes=[Cs])
            yield

    def run(*gens):
        gens = list(gens)
        while gens:
            for g in list(gens):
                try:
                    next(g)
                except StopIteration:
                    gens.remove(g)

    def init_dir(seq, d):
        Cs = Cst[d]
        if seq == 0:
            k.op("pool", lambda e: e.memset(Cs[:, :, :], 0.0), writes=[Cs])
            k.op("dve", lambda e: e.memset(mst[d][:, :], 0.0), writes=[mst[d]])
            k.dma(Cs[:, :, 0:128], c.SC[d].rearrange("h p e -> p h e"), writes=[Cs])
            k.dma(Cs[:, :, 128], c.SN[d].rearrange("h p -> p h"), writes=[Cs], slow=True)
            k.dma(mst[d][:, 0:1], c.SM[d].rearrange("(h o) -> h o", o=1), writes=[mst[d]], slow=True)
        else:
            k.op("pool", lambda e: e.memset(Cs[:, :, :], 0.0), writes=[Cs])
            k.op("dve", lambda e: e.memset(mst[d][:, :], 0.0), writes=[mst[d]])

    def store_state(seq, d):
        p = seq - 1
        Cs = Cst[d]
        k.dma(c.NC_[p, d].rearrange("h p e -> p h e"), Cs[:, :, 0:128], reads=[Cs])
        k.dma(c.NN[p, d].rearrange("h p -> p h"), Cs[:, :, 128], reads=[Cs], slow=True)
        k.dma(c.NM[p, d].rearrange("(h o) -> h o", o=1), mst[d][:, 0:1], reads=[mst[d]], slow=True)

    def pre_G(ci):
        i2 = ci % 2
        load_chunk(ci, False)
        yield
        k.op("dve", lambda e: e.tensor_copy(out=mbprev[:, ci:ci + 1], in_=mst[1][:, 0:1]), reads=[mst[1]],
             writes=[mbprev])
        yield
        yield from gate_prep(1, GT[i2], (mst[1], 0), False, True, cl=cols1p[i2], pT=pX)
        new_m(1, mst[1], 0)
        yield

    def pre_U(ci):
        i2 = ci % 2
        k.op("act", lambda e: e.activation(out=Csnap[:, ci, :, :], in_=Cst[1][:, :, :], func=AF.Copy),
             reads=[Cst[1]], writes=[Csnap])
        yield
        yield from state_update(1, kk[i2], va[i2], banks=((pX, 128), (pX, 128)), cl=cols1p[i2])

    def zip2(g1, g2):
        gens = [g for g in (g1, g2) if g is not None]
        while gens:
            for g in list(gens):
                try:
                    next(g)
                except StopIteration:
                    gens.remove(g)
            yield

    seqs = [(0, list(range(32))), (1, [32, 33]), (2, [34, 35])]

    def pre_gen():
        for seq, chunks in seqs:
            init_dir(seq, 1)
            yield
            order = list(reversed(chunks))
            yield from pre_G(order[0])
            for n, ci in enumerate(order):
                nxt = pre_G(order[n + 1]) if n + 1 < len(order) else None
                yield from zip2(nxt, pre_U(ci))
            if seq > 0:
                store_state(seq, 1)
                yield

    def dir_chain(d, ci, q_, kk_, va_, gt):
        mprev = (mst[0], 0) if d == 0 else (mbprev, ci)
        rng_, rin_ = rows[d][3], rows[d][4]
        pG, pI, pT, cl = pGd[d], pId[d], pTd[d], cols[d]
        yield from gate_prep(d, gt, mprev, True, d == 0)
        if d == 0:
            new_m(0, mst[0], 0)
        for h in range(4):
            k.op("act", lambda e, h=h: e.activation(out=Dsb[d][:, h * 128:(h + 1) * 128],
                                                    in_=pG[:, h * 128:(h + 1) * 128], func=AF.Exp,
                                                    bias=cl[:, h:h + 1]), reads=[pG, cl], writes=[Dsb[d]])
        k.op("dve", lambda e: e.tensor_tensor(out=qI[d][:, :], in0=pI[:, :],
                                              in1=q_[:, :, :].rearrange("p h t -> p (h t)"), op=ALU.mult),
             reads=[pI, q_], writes=[qI[d]])
        yield
        k.op("dve", lambda e: e.tensor_tensor(out=swT[d][:, :], in0=pS[:, :], in1=Dsb[d][:, :], op=ALU.mult),
             reads=[pS, Dsb[d]], writes=[swT[d]])
        if d == 0:
            k.op("act", lambda e: e.activation(out=Cbf[:, :, :], in_=Cst[0][:, :, :], func=AF.Copy),
                 reads=[Cst[0]], writes=[Cbf])
        yield
        for h in range(4):
            cprev = (lambda h=h: Cbf[:, h, :]) if d == 0 else (lambda h=h: Csnap[:, ci, h, :])
            ctile = Cbf if d == 0 else Csnap
            k.op("pe", lambda e, h=h: e.matmul(pG[:, h * 128:(h + 1) * 128], lhsT=swT[d][:, h * 128:(h + 1) * 128],
                                               rhs=va_[:, h, 0:128], start=True, stop=False),
                 reads=[swT[d], va_], writes=[pG], inc=False)
            k.op("pe", lambda e, h=h, cprev=cprev: e.matmul(pG[:, h * 128:(h + 1) * 128],
                                                            lhsT=qI[d][:, h * 128:(h + 1) * 128],
                                                            rhs=cprev()[:, 0:128], start=False, stop=True),
                 reads=[qI[d], ctile], writes=[pG], inc=False)
            k.op("pe", lambda e, h=h: e.matmul(pT[:, 16 + h:17 + h], lhsT=swT[d][:, h * 128:(h + 1) * 128],
                                               rhs=va_[:, h, 128:129], start=True, stop=False),
                 reads=[swT[d], va_], writes=[pT], inc=False)
            k.op("pe", lambda e, h=h, cprev=cprev: e.matmul(pT[:, 16 + h:17 + h],
                                                            lhsT=qI[d][:, h * 128:(h + 1) * 128],
                                                            rhs=cprev()[:, 128:129], start=False, stop=True),
                 reads=[qI[d], ctile], writes=[pT])
            yield
        k.op("dve", lambda e: e.tensor_scalar(out=cl[:, 20:24], in0=pT[:, 16:20], scalar1=-1.0, scalar2=None,
                                              op0=ALU.mult), reads=[pT], writes=[cl])
        yield
        k.op("dve", lambda e: e.tensor_tensor(out=cl[:, 16:20], in0=pT[:, 16:20], in1=cl[:, 20:24], op=ALU.max),
             reads=[pT, cl], writes=[cl])
        yield
        k.op("dve", lambda e: e.tensor_tensor(out=cl[:, 16:20], in0=cl[:, 16:20], in1=cl[:, 4:8], op=ALU.max),
             reads=[cl], writes=[cl])
        yield
        k.op("dve", lambda e: e.reciprocal(out=cl[:, 16:20], in_=cl[:, 16:20]), reads=[cl], writes=[cl])
        yield
        hdt = hd[ci % 2][d]
        k.op("dve", lambda e: e.tensor_tensor(out=hdt[:, :].rearrange("p (h e) -> p h e", h=4),
                                              in0=pG[:, :].rearrange("p (h e) -> p h e", h=4),
                                              in1=cl[:, 16:20].unsqueeze(2).to_broadcast([128, 4, 128]),
                                              op=ALU.mult), reads=[pG, cl], writes=[hdt])
        yield
        if d == 0:
            yield from state_update(0, kk_, va_)

    def tail_chain(ci):
        i2 = ci % 2
        mo_ = mo[i2]
        h0, h1 = hd[i2]
        k.op("pool", lambda e: e.tensor_tensor(out=hm[:, :], in0=h0[:, :], in1=h1[:, :], op=ALU.add),
             reads=[h0, h1], writes=[hm])
        yield
        k.op("dve", lambda e: e.tensor_tensor(out=sqh[:, :], in0=hm[:, :], in1=hm[:, :], op=ALU.mult),
             reads=[hm], writes=[sqh])
        yield
        k.op("dve", lambda e: e.tensor_reduce(out=ss4[:, 0:4], in_=sqh[:, :].rearrange("p (h d) -> p h d", h=4),
                                              axis=AX.X, op=ALU.add), reads=[sqh], writes=[ss4])
        yield
        k.op("act", lambda e: e.activation(out=rs4[:, 0:4], in_=ss4[:, 0:4], func=AF.Ln, scale=1.0 / 128, bias=EPS),
             reads=[ss4], writes=[rs4])
        yield
        k.op("act", lambda e: e.activation(out=rs4[:, 0:4], in_=rs4[:, 0:4], func=AF.Exp, scale=-0.5),
             reads=[rs4], writes=[rs4])
        yield
        k.op("dve", lambda e: e.tensor_tensor(out=hm[:, :].rearrange("p (h d) -> p h d", h=4),
                                              in0=hm[:, :].rearrange("p (h d) -> p h d", h=4),
                                              in1=rs4[:, 0:4].unsqueeze(2).to_broadcast([128, 4, 128]),
                                              op=ALU.mult), reads=[hm, rs4], writes=[hm])
        yield
        k.op("pool", lambda e: e.tensor_tensor(out=hm[:, :], in0=hm[:, :], in1=mnw[:, :], op=ALU.mult),
             reads=[hm, mnw], writes=[hm])
        yield
        k.op("pool", lambda e: e.tensor_tensor(out=sqh[:, :], in0=hm[:, :], in1=mo_[:, :], op=ALU.mult),
             reads=[hm, mo_], writes=[sqh])
        yield
        for h in range(4):
            k.op("pe", lambda e, h=h: e.transpose(pX[:, h * 128:(h + 1) * 128], sqh[:, h * 128:(h + 1) * 128],
                                                  ident[:, :]), reads=[sqh, ident], writes=[pX], inc=(h == 3))
        yield
        ho = hmT[i2]
        k.op("act", lambda e: e.activation(out=ho[:, :, :], in_=pX[:, :].rearrange("p (h t) -> p h t", h=4),
                                           func=AF.Copy), reads=[pX], writes=[ho])
        yield
        k.dma(c.HMTd[:, :, ci * 128:(ci + 1) * 128].rearrange("h p t -> p h t"), ho[:, :, :], reads=[ho])

    def main_chunk(ci, prev, nxt):
        i2 = ci % 2
        if nxt is not None:
            load_chunk(nxt, True)
        q_, kT_, kk_, va_, gt = qT[i2], kT[i2], kk[i2], va[i2], GT[i2]
        for h in range(4):
            k.op("pe", lambda e, h=h: e.matmul(pS[:, h * 128:(h + 1) * 128], lhsT=kT_[:, h, :], rhs=q_[:, h, :],
                                               start=True, stop=True), reads=[kT_, q_], writes=[pS], inc=(h == 3))
        gens = [dir_chain(0, ci, q_, kk_, va_, gt), dir_chain(1, ci, q_, kk_, va_, gt)]
        if prev is not None:
            gens.append(tail_chain(prev))
        run(*gens)
        if nxt is not None:
            load_mo(nxt)

    def main():
        late_alloc()
        allc = [ci for _, chunks in seqs for ci in chunks]
        load_chunk(allc[0], True)
        load_mo(allc[0])
        pos = 0
        for seq, chunks in seqs:
            init_dir(seq, 0)
            prev = None
            for ci in chunks:
                nxt = allc[pos + 1] if pos + 1 < len(allc) else None
                main_chunk(ci, prev, nxt)
                prev = ci
                pos += 1
            run(tail_chain(prev))
            if seq > 0:
                store_state(seq, 0)
        k.barrier()
        k.sb_phase = ctx["sb_phase0"]
        k.phase_mem()

    return pre_gen, main


def phase_b2(ctx):
    ctx["b2_main"]()


def phase_c1(ctx):
    c = NS(ctx)
    k = c.k
    PS = c.PS
    Wout = k.sb([128, 8, D], BF16, "Wout")
    stg = [k.sb([128, D], F32, f"stgo{i}") for i in range(2)]
    g1b = k.sb([128, D], F32, "g1b")
    mixT = [k.sb([128, 8, 512], BF16, f"mixT{i}") for i in range(2)]
    xb = [k.sb([128, D], F32, f"xc{i}") for i in range(3)]
    x1 = [k.sb([128, D], F32, f"x1{i}") for i in range(2)]
    tmp = k.sb([128, D], F32, "tmpc1")
    junk = k.sb([128, D], BF16, "junkc")
    xs = [k.sb([128, D], F32, f"xsc{i}") for i in range(3)]
    deferred = []

    def flush(keep=0):
        while len(deferred) > keep:
            deferred.pop(0)()
    ss = k.sb([128, 8], F32, "ssc")
    rstd = k.sb([128, 8], F32, "rstdc")
    h2T = [k.sb([128, 8, 512], BF16, f"h2T{i}") for i in range(2)]
    for kc in range(8):
        st = stg[kc % 2]
        k.dma(st[:, :], c.WOUT[kc * 128:(kc + 1) * 128, :], writes=[st])
        k.op("pool", lambda e, kc=kc, st=st: e.tensor_copy(out=Wout[:, kc, :], in_=st[:, :]), reads=[st],
             writes=[Wout])

    def st_body(s):
        cj = 0 if s < 8 else 1
        tok = slice(s * 512, (s + 1) * 512)
        mx = mixT[s % 2]
        h2 = h2T[s % 2]
        if s == 0 or s == 8:
            k.dma(g1b[:, :], c.MODS[cj, 2 * D:3 * D].partition_broadcast(128), writes=[g1b])
        k.dma(mx[:, 0:4, :], c.AOTd[:, :, tok].rearrange("c p t -> p c t"), writes=[mx])
        k.dma(mx[:, 4:8, :], c.HMTd[:, :, tok].rearrange("c p t -> p c t"), writes=[mx])

        def tile_body(i):
            ti = s * 4 + i
            xt = xb[ti % 3]
            xo = x1[ti % 2]
            k.dma(xt[:, :], c.X[ti * 128:(ti + 1) * 128, :], writes=[xt])
            for n in range(2):
                pb = PS[2 + (2 * ti + n) % 4]
                for kc in range(8):
                    k.op("pe", lambda e, pb=pb, kc=kc, n=n: e.matmul(
                        pb[:, :], lhsT=mx[:, kc, i * 128:(i + 1) * 128], rhs=Wout[:, kc, n * 512:(n + 1) * 512],
                        start=(kc == 0), stop=(kc == 7)), reads=[mx, Wout], writes=[pb], inc=(kc == 7))
                if n == 1:
                    flush(1)
                k.op("dve", lambda e, pb=pb, n=n: e.tensor_tensor(out=tmp[:, n * 512:(n + 1) * 512], in0=pb[:, :],
                                                                  in1=g1b[:, n * 512:(n + 1) * 512], op=ALU.mult),
                     reads=[pb, g1b], writes=[tmp])
            k.op("pool", lambda e: e.tensor_tensor(out=xo[:, :], in0=tmp[:, :], in1=xt[:, :], op=ALU.add),
                 reads=[tmp, xt], writes=[xo])
            k.dma(c.X1d[ti * 128:(ti + 1) * 128, :], xo[:, :], reads=[xo])
            norm_to_hT(k, c, xo, h2, i * 128,
                       lambda ch: c.G2[:, cj, ch:ch + 1], lambda ch: c.modc[:, cj, 3, ch:ch + 1],
                       (junk, ss, rstd, xs[ti % 3]), PS[0], PS[1], defer=deferred)

        for i in range(4):
            tile_body(i)
        flush()
        k.dma(c.H2Td[:, :, tok].rearrange("c p t -> p c t"), h2[:, :, :], reads=[h2])

    for s in range(9):
        st_body(s)
    k.barrier()
    k.phase_mem()


def phase_c2(ctx):
    c = NS(ctx)
    k = c.k
    PS = c.PS
    Wgb = [k.sb([128, 8, 256], BF16, f"Wgb{i}") for i in range(22)]
    Wdn = k.sb([128, NF, D], BF16, "Wdn")
    SW = 704
    stg = [k.sb([128, SW], F32, f"stgf{i}") for i in range(2)]
    g2b = k.sb([128, D], F32, "g2b")
    fnb = k.sb([128, D], F32, "fnb")
    h2 = k.sb([128, 8, 512], BF16, "h2c")
    actT = k.sb([128, NF, 512], BF16, "actT")
    sg = [k.sb([128, 512], F32, f"sg{i}") for i in range(2)]
    x1 = [k.sb([128, D], F32, f"x1c{i}") for i in range(2)]
    x2 = k.sb([128, D], F32, "x2c")
    yt = [k.sb([128, D], F32, f"yt{i}") for i in range(2)]
    junk = k.sb([128, D], BF16, "junkf")
    ss = k.sb([128, 8], F32, "ssf")
    rstd = k.sb([128, 8], F32, "rstdf")
    mhalf = k.sb([128, 1], F32, "mhalf")
    k.dma(fnb[:, :], c.FNW.partition_broadcast(128), writes=[fnb])
    k.dma(g2b[:, :], c.MODS[0, 5 * D:6 * D].partition_broadcast(128), writes=[g2b])
    k.dma(h2[:, :, :], c.H2Td[:, :, 0:512].rearrange("c p t -> p c t"), writes=[h2])
    it = 0
    for bi in range(11):
        for half in range(2):
            blkt = Wgb[2 * bi + half]
            c0 = half * DFF + bi * 256
            for kc0 in (0, 2, 4, 6):
                st = stg[it % 2]
                it += 1
                k.dma(st[:, 0:512].rearrange("p (a n) -> p a n", a=2),
                      c.WGU[kc0 * 128:(kc0 + 2) * 128, c0:c0 + 256].rearrange("(a p) n -> p a n", p=128), writes=[st])
                k.op("pool", lambda e, kc0=kc0, st=st, blkt=blkt: e.tensor_copy(
                    out=blkt[:, kc0:kc0 + 2, :], in_=st[:, 0:512].rearrange("p (a n) -> p a n", a=2)),
                    reads=[st], writes=[blkt])
    for f in range(NF):
        for j in range(2):
            st = stg[it % 2]
            it += 1
            k.dma(st[:, 0:512], c.WDN[f * 128:(f + 1) * 128, j * 512:(j + 1) * 512], writes=[st])
            k.op("pool", lambda e, f=f, j=j, st=st: e.tensor_copy(out=Wdn[:, f, j * 512:(j + 1) * 512],
                                                                  in_=st[:, 0:512]), reads=[st], writes=[Wdn])
    k.op("dve", lambda e: e.memset(mhalf[:, :], -0.5), writes=[mhalf])

    def st_body(s):
        cj = 0 if s < 8 else 1
        tok = slice(s * 512, (s + 1) * 512)
        if s == 8:
            k.dma(g2b[:, :], c.MODS[cj, 5 * D:6 * D].partition_broadcast(128), writes=[g2b])

        def up_body(f):
            pg, pu = PS[2 * (f % 2)], PS[2 * (f % 2) + 1]
            c0 = (f % 2) * 128
            for (pb, wt) in ((pg, Wgb[2 * (f // 2)]), (pu, Wgb[2 * (f // 2) + 1])):
                for kc in range(8):
                    k.op("pe", lambda e, pb=pb, wt=wt, kc=kc: e.matmul(
                        pb[:, :], lhsT=wt[:, kc, c0:c0 + 128], rhs=h2[:, kc, :], start=(kc == 0), stop=(kc == 7)),
                        reads=[wt, h2], writes=[pb], inc=(kc == 7))
            sgt = sg[f % 2]
            k.op("act", lambda e: e.activation(out=sgt[:, :], in_=pg[:, :], func=AF.Silu), reads=[pg], writes=[sgt])
            k.op("dve", lambda e: e.tensor_tensor(out=actT[:, f, :], in0=pu[:, :], in1=sgt[:, :], op=ALU.mult),
                 reads=[pu, sgt], writes=[actT])

        for f in range(NF):
            up_body(f)
        if s + 1 < 9:
            tokn = slice((s + 1) * 512, (s + 2) * 512)
            k.dma(h2[:, :, :], c.H2Td[:, :, tokn].rearrange("c p t -> p c t"), writes=[h2])

        def tile_body(i):
            ti = s * 4 + i
            xt = x1[ti % 2]
            y = yt[ti % 2]
            k.dma(xt[:, :], c.X1d[ti * 128:(ti + 1) * 128, :], writes=[xt])
            for n in range(2):
                pb = PS[4 + (2 * ti + n) % 4]
                for f in range(NF):
                    k.op("pe", lambda e, pb=pb, f=f, n=n: e.matmul(
                        pb[:, :], lhsT=actT[:, f, i * 128:(i + 1) * 128], rhs=Wdn[:, f, n * 512:(n + 1) * 512],
                        start=(f == 0), stop=(f == NF - 1)), reads=[actT, Wdn], writes=[pb], inc=(f == NF - 1))
                k.op("dve", lambda e, pb=pb, n=n: e.tensor_tensor(out=x2[:, n * 512:(n + 1) * 512], in0=pb[:, :],
                                                                  in1=g2b[:, n * 512:(n + 1) * 512], op=ALU.mult),
                     reads=[pb, g2b], writes=[x2])
            k.op("pool", lambda e: e.tensor_tensor(out=x2[:, :], in0=x2[:, :], in1=xt[:, :], op=ALU.add),
                 reads=[x2, xt], writes=[x2])
            k.op("dve", lambda e: e.scalar_tensor_tensor(out=junk[:, :], in0=x2[:, :], scalar=1.0, in1=x2[:, :],
                                                         op0=ALU.mult, op1=ALU.mult, accum_out=ss[:, 0:1]),
                 reads=[x2], writes=[junk, ss])
            k.op("dve", lambda e: e.tensor_scalar(out=ss[:, 1:2], in0=ss[:, 0:1], scalar1=1.0 / D, scalar2=EPS,
                                                  op0=ALU.mult, op1=ALU.add), reads=[ss], writes=[ss])
            k.op("pool", lambda e: e.tensor_tensor(out=rstd[:, 0:1], in0=ss[:, 1:2], in1=mhalf[:, 0:1], op=ALU.pow),
                 reads=[ss, mhalf], writes=[rstd])
            k.op("dve", lambda e: e.scalar_tensor_tensor(out=y[:, :], in0=x2[:, :], scalar=rstd[:, 0:1],
                                                         in1=fnb[:, :], op0=ALU.mult, op1=ALU.mult),
                 reads=[x2, rstd, fnb], writes=[y])
            k.dma(c.Y[ti * 128:(ti + 1) * 128, :], y[:, :], reads=[y])

        for i in range(4):
            tile_body(i)

    for s in range(9):
        st_body(s)
    k.barrier()
    k.phase_mem()
```

```python
import numpy as np
import concourse.bass as bass
import concourse.mybir as mybir
from concourse.bass_utils import run_bass_kernel_spmd

F32 = mybir.dt.float32
BF16 = mybir.dt.bfloat16
AF = mybir.ActivationFunctionType
ALU = mybir.AluOpType
AX = mybir.AxisListType

D = 1024
NT = 36
TT = NT * 128
NIN = 2832
DFF = 2816
NF = 22
EPS = 1e-6
KSC = 128.0 ** -0.5
NKEY = 38
import os as _os
NDS = 64
PRE_IN_ATT = int(_os.environ.get('PRE_IN_ATT', '1'))
STRICT = bool(int(_os.environ.get('KSTRICT', '1')))


class Tl:
    def __init__(s, h):
        s.h = h
        s.w = []
        s.r = []

    def __getitem__(s, i):
        return s.h[i]


class K:
    ENG = ("pe", "act", "dve", "pool", "sp")

    def __init__(s, nc):
        s.nc = nc
        s.ops = {e: [] for e in s.ENG}
        s.cnt = {e: 0 for e in s.ENG}
        s.seen = {e: {} for e in s.ENG}
        s.sem = {}
        s.dsems = [nc.alloc_semaphore(f"d_{i}") for i in range(NDS)]
        s.dcnt = [0] * NDS
        s.dptr = 0
        for e in s.ENG:
            s.sem[e] = nc.alloc_semaphore(f"s_{e}")
        s.sb_ptr = 0
        s.sb_phase = 0
        s.cap = None
        s.nalloc = 0

    def sb(s, shape, dt, name=None):
        esz = 4 if dt == F32 else 2
        n = 1
        for d_ in shape[1:]:
            n *= d_
        nbytes = (n * esz + 63) // 64 * 64
        off = s.sb_ptr
        s.sb_ptr += nbytes
        assert s.sb_ptr <= s.sb_top, (name, s.sb_ptr, s.sb_top)
        s.nalloc += 1
        h = s.nc.alloc_sbuf_tensor_at(f"{name or 't'}_{s.nalloc}", list(shape), dt, offset=off)
        return Tl(h)

    def phase_mem(s):
        s.sb_ptr = s.sb_phase

    def capture(s):
        s.cap = []
        return s.cap

    def capture_end(s):
        lst, s.cap = s.cap, None
        return lst

    def emit_interleaved(s, *lists):
        lists = [list(l) for l in lists if l]
        while lists:
            for l in list(lists):
                kind, args, kw = l.pop(0)
                (s.op if kind == "op" else s.dma)(*args, **kw)
                if not l:
                    lists.remove(l)

    def op(s, e, fn, reads=(), writes=(), inc=True):
        if s.cap is not None:
            s.cap.append(("op", (e, fn), dict(reads=list(reads), writes=list(writes), inc=inc)))
            return None
        deps = []
        for t in reads:
            for wt in t.w:
                if wt[0] != e or e != "pe":
                    deps.append(wt)
        for t in writes:
            for wt in t.w:
                if wt[0] != e or (STRICT and e != "pe"):
                    deps.append(wt)
            for rt in t.r:
                if rt[0] != e or (STRICT and e != "pe"):
                    deps.append(rt)
        need = {}
        for key, val in deps:
            if val > need.get(key, 0):
                need[key] = val
        waits = []
        for key, val in need.items():
            if s.seen[e].get(key, 0) >= val:
                continue
            s.seen[e][key] = val
            waits.append((key, val))
        tok = (e, s.cnt[e] + 1)
        if inc:
            s.cnt[e] += 1
        s.ops[e].append((waits, fn, inc, None))
        for t in writes:
            t.w = [tok]
            t.r = []
        for t in reads:
            if t not in writes:
                t.r.append(tok)
                if len(t.r) > 24:
                    t.r = s._compact(t.r)
        return tok

    @staticmethod
    def _compact(r):
        best = {}
        for key, val in r:
            if val > best.get(key, 0):
                best[key] = val
        return list(best.items())

    def dma(s, out, in_, reads=(), writes=(), q="sp", slow=False):
        if s.cap is not None:
            s.cap.append(("dma", (out, in_), dict(reads=list(reads), writes=list(writes), q=q, slow=slow)))
            return None
        deps = []
        for t in reads:
            deps.extend(t.w)
        for t in writes:
            for wt in t.w:
                if not isinstance(wt[0], int):
                    deps.append(wt)
            deps.extend(t.r)
        need = {}
        for key, val in deps:
            if val > need.get(key, 0):
                need[key] = val
        waits = []
        for key, val in need.items():
            if s.seen[q].get(key, 0) >= val:
                continue
            s.seen[q][key] = val
            waits.append((key, val))
        si = s.dptr
        s.dptr = (s.dptr + 1) % len(s.dsems)
        if s.dcnt[si] > s.seen[q].get(si, 0):
            s.seen[q][si] = s.dcnt[si]
            waits.append((si, s.dcnt[si]))
        s.dcnt[si] += 16
        tok = (si, s.dcnt[si])
        s.ops[q].append((waits, (out, in_, slow), False, si))
        for t in writes:
            if t.w and all(isinstance(wt[0], int) for wt in t.w):
                t.w = t.w + [tok]
            else:
                t.w = [tok]
            t.r = []
        for t in reads:
            t.r.append(tok)
        return tok

    def barrier(s):
        waits = []
        for e in s.ENG:
            if e != "sp" and s.cnt[e] > s.seen["sp"].get(e, 0):
                s.seen["sp"][e] = s.cnt[e]
                waits.append((e, s.cnt[e]))
        for si in range(len(s.dsems)):
            if s.dcnt[si] > s.seen["sp"].get(si, 0):
                s.seen["sp"][si] = s.dcnt[si]
                waits.append((si, s.dcnt[si]))
        s.cnt["sp"] += 1
        v = s.cnt["sp"]
        s.ops["sp"].append((waits, "inc", True, None))
        for e in s.ENG:
            if e != "sp":
                s.seen[e]["sp"] = v
                s.ops[e].append(([("sp", v)], None, False, None))
                for si in range(len(s.dsems)):
                    s.seen[e][si] = s.dcnt[si]
                for e2 in s.ENG:
                    s.seen[e][e2] = max(s.seen[e].get(e2, 0), s.cnt[e2])

    def semof(s, key):
        return s.dsems[key] if isinstance(key, int) else s.sem[key]

    def emit(s, e, eng):
        for waits, fn, inc, si in s.ops[e]:
            for key, val in waits:
                eng.wait_ge(s.semof(key), val)
            if fn is None:
                continue
            if fn == "inc":
                eng.sem_inc(s.sem[e], 1)
                continue
            if si is not None:
                out, in_, slow = fn
                if slow:
                    ins = eng.dma_start(out=out, in_=in_, allow_slow_non_contiguous=True)
                else:
                    ins = eng.dma_start(out=out, in_=in_)
                ins.then_inc(s.dsems[si], 16)
                continue
            ins = fn(eng)
            if inc:
                ins.then_inc(s.sem[e], 1)


def build_nc(debug=False, stop=99):
    import os
    stop = int(os.environ.get('KSTOP', stop))
    nc = bass.Bass("TRN2", target_bir_lowering=False)
    k = K(nc)
    k.sb_ptr = (nc.sbuf_base + 63) // 64 * 64
    k.sb_top = nc.sbuf_top

    def din(name, shape, dt=F32):
        return nc.dram_tensor(name, list(shape), dt, kind="ExternalInput").ap()

    def dout(name, shape, dt=F32):
        return nc.dram_tensor(name, list(shape), dt, kind="ExternalOutput").ap()

    def dscr(name, shape, dt=BF16):
        return nc.dram_tensor(name, list(shape), dt, kind="ExternalOutput" if debug else "Internal").ap()

    X = din("x", [TT, D])
    CK = din("cache_k", [256, 128])
    CV = din("cache_v", [256, 128])
    SC = din("state_C", [2, 4, 128, 128])
    SN = din("state_n", [2, 4, 128])
    SM = din("state_m", [2, 4])
    COND = din("cond", [2, D])
    WADA = din("w_ada", [D, 6 * D])
    BADA = din("b_ada", [6 * D])
    N1W = din("norm1_w", [D])
    WIN = din("w_in", [D, NIN])
    GB = din("gate_bias", [4, 4])
    QNW = din("q_norm_w", [64])
    KNW = din("k_norm_w", [64])
    MNW = din("mlstm_norm_w", [512])
    WOUT = din("w_out", [D, D])
    N2W = din("norm2_w", [D])
    WGU = din("w_gu", [D, 2 * DFF])
    WDN = din("w_down", [DFF, D])
    FNW = din("final_norm_w", [D])
    ROPE = din("rope", [128, 32, 2, 64])
    IDENT = din("ident", [128, 128])
    MASKS = din("masks", [128, 2, 512])

    Y = dout("y", [TT, D])
    NK = dout("nk", [512, 128])
    NV = dout("nv", [512, 128])
    NC_ = dout("nC", [2, 2, 4, 128, 128])
    NN = dout("nn", [2, 2, 4, 128])
    NM = dout("nm", [2, 2, 4])

    MODS = dscr("mods", [2, 6 * D], F32)
    QTd = dscr("QTd", [4, 128, TT])
    MQTd = dscr("MQTd", [4, 128, TT])
    MKTd = dscr("MKTd", [4, 128, TT])
    MKd = dscr("MKd", [TT, 512])
    MVd = dscr("MVd", [TT, 520])
    MOd = dscr("MOd", [TT, 512])
    GATd = dscr("GATd", [4, 4, TT], F32)
    AOTd = dscr("AOTd", [4, 128, TT])
    HMTd = dscr("HMTd", [4, 128, TT])
    H2Td = dscr("H2Td", [8, 128, TT])
    X1d = dscr("X1d", [TT, D], F32)

    PSALL = nc.alloc_psum_tensor("psall", [128, 4096], F32)
    PS = [Tl(PSALL[:, i * 512:(i + 1) * 512]) for i in range(8)]

    ident = k.sb([128, 128], F32, "ident")
    ones4 = k.sb([4, 128], F32, "ones4")
    eye4 = k.sb([4, 4], F32, "eye4")
    modc = k.sb([128, 2, 6, 8], F32, "modc")
    G1 = k.sb([128, 2, 8], F32, "G1")
    G2 = k.sb([128, 2, 8], F32, "G2")
    n1c = k.sb([128, 8], F32, "n1c")
    n2c = k.sb([128, 8], F32, "n2c")
    k.mhalf = k.sb([128, 8], F32, "mhalf")
    k.op("dve", lambda e: e.memset(k.mhalf[:, :], -0.5), writes=[k.mhalf])
    sb_phase0 = k.sb_ptr
    KT2 = k.sb([128, 2, TT + 256], BF16, "KT2")
    VA = k.sb([128, NKEY, 2, 192], BF16, "VA")
    sb_phase0_a = k.sb_ptr
    Win = k.sb([128, 8, NIN], BF16, "Win")
    Wg = k.sb([128, 8, 128], BF16, "Wg")
    stgw = [k.sb([128, NIN // 2], F32, f"stgw{i}") for i in range(2)]
    k.sb_phase = k.sb_ptr
    k.op("pool", lambda e: e.memset(Wg[:, :, :], 0.0), writes=[Wg])

    k.dma(ident[:, :], IDENT[:, :], writes=[ident])
    k.op("dve", lambda e: e.memset(ones4[:, :], 1.0), writes=[ones4])
    k.dma(eye4[:, :], IDENT[0:4, 0:4], writes=[eye4])

    condT = k.sb([128, 8, 2], F32, "condT")
    sil = k.sb([128, 8, 2], F32, "sil")
    tmpc = k.sb([128, 8, 2], F32, "tmpc")
    mods_sb = k.sb([2, 6 * D], F32, "mods_sb")
    bada = k.sb([2, 6 * D], F32, "bada")
    wa = [k.sb([128, 512], F32, f"wa{i}") for i in range(4)]
    for j in range(2):
        k.dma(condT[:, :, j], COND[j].rearrange("(c p) -> p c", p=128), writes=[condT], slow=True)
    for j in range(2):
        k.dma(bada[j:j + 1, :], BADA.rearrange("(o n) -> o n", o=1), writes=[bada])
    k.op("act", lambda e: e.activation(out=tmpc[:, :, :], in_=condT[:, :, :], func=AF.Exp, scale=-1.0),
         reads=[condT], writes=[tmpc])
    k.op("dve", lambda e: e.tensor_scalar_add(out=tmpc[:, :, :], in0=tmpc[:, :, :], scalar1=1.0),
         reads=[tmpc], writes=[tmpc])
    k.op("dve", lambda e: e.reciprocal(out=tmpc[:, :, :], in_=tmpc[:, :, :]), reads=[tmpc], writes=[tmpc])
    k.op("dve", lambda e: e.tensor_tensor(out=sil[:, :, :], in0=condT[:, :, :], in1=tmpc[:, :, :], op=ALU.mult),
         reads=[condT, tmpc], writes=[sil])
    win_steps = []
    HW = NIN // 2
    for kc in range(8):
        for hf in range(2):
            def step(kc=kc, hf=hf):
                st = stgw[hf]
                k.dma(st[:, :], WIN[kc * 128:(kc + 1) * 128, hf * HW:(hf + 1) * HW], writes=[st])
                k.op("pool", lambda e: e.tensor_copy(out=Win[:, kc, hf * HW:(hf + 1) * HW], in_=st[:, :]),
                     reads=[st], writes=[Win])
                if hf == 1:
                    k.op("pool", lambda e: e.tensor_copy(
                        out=Wg[:, kc, :].rearrange("p (j w) -> p j w", w=32)[:, :, 0:4],
                        in_=st[:, HW - 16:HW].rearrange("p (j w) -> p j w", w=4)), reads=[st], writes=[Wg])
            win_steps.append(step)
    it = 0
    for n in range(12):
        pb = PS[n % 2]
        for kc in range(8):
            w_t = wa[it % 4]
            if it % 6 == 0 and win_steps:
                win_steps.pop(0)()
            it += 1
            k.dma(w_t[:, :], WADA[kc * 128:(kc + 1) * 128, n * 512:(n + 1) * 512], writes=[w_t])
            k.op("pe", lambda e, w_t=w_t, kc=kc, pb=pb: e.matmul(pb[0:2, :], lhsT=sil[:, kc, :], rhs=w_t[:, :],
                                                                start=(kc == 0), stop=(kc == 7)),
                 reads=[w_t, sil], writes=[pb], inc=True)
        k.op("dve", lambda e, n=n, pb=pb: e.tensor_tensor(out=mods_sb[:, n * 512:(n + 1) * 512], in0=pb[0:2, :],
                                                          in1=bada[:, n * 512:(n + 1) * 512], op=ALU.add),
             reads=[pb, bada], writes=[mods_sb])
    while win_steps:
        win_steps.pop(0)()
    k.dma(MODS[:, :], mods_sb[:, :], reads=[mods_sb])
    k.barrier()
    for j in range(2):
        for s6 in range(6):
            k.dma(modc[:, j, s6, :], MODS[j, s6 * D:(s6 + 1) * D].rearrange("(c p) -> p c", p=128),
                  writes=[modc], slow=True)
    k.dma(n1c[:, :], N1W.rearrange("(c p) -> p c", p=128), writes=[n1c], slow=True)
    k.dma(n2c[:, :], N2W.rearrange("(c p) -> p c", p=128), writes=[n2c], slow=True)
    for j in range(2):
        k.op("dve", lambda e, j=j: e.scalar_tensor_tensor(out=G1[:, j, :], in0=modc[:, j, 1, :], scalar=1.0,
                                                          in1=n1c[:, :], op0=ALU.add, op1=ALU.mult),
             reads=[modc, n1c], writes=[G1])
        k.op("dve", lambda e, j=j: e.scalar_tensor_tensor(out=G2[:, j, :], in0=modc[:, j, 4, :], scalar=1.0,
                                                          in1=n2c[:, :], op0=ALU.add, op1=ALU.mult),
             reads=[modc, n2c], writes=[G2])
    k.barrier()
    k.phase_mem()
    ctx = dict(locals())
    if stop >= 1:
        phase_a(ctx)
    if stop >= 2:
        phase_b1(ctx)
    if stop >= 3:
        phase_b2(ctx)
    if stop >= 4:
        phase_c1(ctx)
    if stop >= 5:
        phase_c2(ctx)
    k.barrier()

    with nc.allow_low_precision(reason="bf16 matmul operands by design"), nc.Block() as block:
        names = {"pe": "tensor", "act": "scalar", "dve": "vector", "pool": "gpsimd", "sp": "sync"}
        for e in K.ENG:
            getattr(block, names[e])(lambda eng, e=e: k.emit(e, eng))
    return nc


class NS:
    def __init__(s, d):
        s.__dict__.update(d)


def rstd_from_ss(k, ss, out, n_inv, width):
    k.op("act", lambda e: e.activation(out=out[:, 0:width], in_=ss[:, 0:width], func=AF.Ln, scale=n_inv, bias=EPS),
         reads=[ss], writes=[out])
    k.op("act", lambda e: e.activation(out=out[:, 0:width], in_=out[:, 0:width], func=AF.Exp, scale=-0.5),
         reads=[out], writes=[out])


def norm_to_hT(k, c, xt, hT, col0, Gc, SHc, bufs, pA, pB, defer=None):
    junk, ss, rstd, xs = bufs
    k.op("dve", lambda e: e.scalar_tensor_tensor(out=junk[:, :], in0=xt[:, :], scalar=1.0, in1=xt[:, :],
                                                 op0=ALU.mult, op1=ALU.mult, accum_out=ss[:, 0:1]),
         reads=[xt], writes=[junk, ss])
    rstd_from_ss(k, ss, rstd, 1.0 / D, 1)
    k.op("pool", lambda e: e.tensor_scalar(out=xs[:, :], in0=xt[:, :], scalar1=rstd[:, 0:1], scalar2=1.0,
                                           op0=ALU.mult, op1=ALU.mult),
         reads=[xt, rstd], writes=[xs])

    def pe_part():
        for half, pb in ((0, pA), (1, pB)):
            for cc in range(4):
                ch = half * 4 + cc
                k.op("pe", lambda e, ch=ch, cc=cc, pb=pb: e.transpose(pb[:, cc * 128:(cc + 1) * 128],
                                                                      xs[:, ch * 128:(ch + 1) * 128], c.ident[:, :]),
                     reads=[xs, c.ident], writes=[pb], inc=(cc == 3))
            for cc in range(4):
                ch = half * 4 + cc
                k.op("act", lambda e, ch=ch, cc=cc, pb=pb: e.activation(
                    out=hT[:, ch, col0:col0 + 128], in_=pb[:, cc * 128:(cc + 1) * 128], func=AF.Identity,
                    scale=Gc(ch), bias=SHc(ch)), reads=[pb, c.G1, c.G2, c.modc], writes=[hT])

    if defer is None:
        pe_part()
    else:
        defer.append(pe_part)


def phase_a(ctx):
    c = NS(ctx)
    k = c.k
    PS = c.PS
    KT2, VA, Win, Wg = c.KT2, c.VA, c.Win, c.Wg
    k.op("pool", lambda e: e.memset(VA[:, :, :, :], 1.0), writes=[VA])
    xb = [k.sb([128, D], F32, f"xb{i}") for i in range(3)]
    junk = k.sb([128, D], BF16, "junk")
    xs = [k.sb([128, D], F32, f"xs{i}") for i in range(3)]
    ss = k.sb([128, 8], F32, "ss")
    rstd = k.sb([128, 8], F32, "rstd")
    hT = [k.sb([128, 8, 512], BF16, f"hT{i}") for i in range(2)]
    ropet = [k.sb([128, 4, 2, 64], F32, f"rope{i}") for i in range(2)]
    wq_bc = k.sb([128, 64], F32, "wq_bc")
    wk_bc = k.sb([128, 64], F32, "wk_bc")
    gbcol = k.sb([128, 1], F32, "gbcol")
    qf = k.sb([128, 512], F32, "qf")
    sq = k.sb([128, 512], F32, "sq")
    qn2 = [k.sb([128, 512], F32, f"qn{i}") for i in range(2)]
    t1 = k.sb([128, 512], F32, "t1")
    t2 = k.sb([128, 512], F32, "t2")
    qr2 = [k.sb([128, 512], F32, f"qr{i}") for i in range(2)]
    kvf = [k.sb([128, 256], F32, f"kvf{i}") for i in range(2)]
    kn = [k.sb([128, 128], F32, f"kn{i}") for i in range(2)]
    kt1 = k.sb([128, 128], F32, "kt1")
    kt2 = k.sb([128, 128], F32, "kt2")
    kr = k.sb([128, 128], F32, "kr")
    kdup2 = [k.sb([128, 2, 2, 64], F32, f"kdup{i}") for i in range(2)]
    ssq = k.sb([128, 8], F32, "ssq")
    rsq = k.sb([128, 8], F32, "rsq")
    ssk = k.sb([128, 8], F32, "ssk")
    rsk = k.sb([128, 8], F32, "rsk")
    mot = k.sb([128, 512], F32, "mot")
    gsb = k.sb([128, 512], F32, "gsb")
    gtmp = k.sb([128, 512], F32, "gtmp")
    cstg = k.sb([128, 2, 128], F32, "cstg")
    QTs = [k.sb([128, 4, 512], BF16, f"QTs{i}") for i in range(1)]
    MKs = [k.sb([128, 4, 512], BF16, f"MKs{i}") for i in range(1)]
    MVs = [k.sb([128, 4, 4, 130], BF16, f"MVs{i}") for i in range(1)]
    MOs = [k.sb([128, 4, 512], BF16, f"MOs{i}") for i in range(1)]
    MQTs = [k.sb([128, 4, 512], BF16, f"MQTs{i}") for i in range(1)]
    MKTs = [k.sb([128, 4, 512], BF16, f"MKTs{i}") for i in range(1)]

    k.dma(wq_bc[:, :], c.QNW.partition_broadcast(128), writes=[wq_bc])
    k.dma(wk_bc[:, :], c.KNW.partition_broadcast(128), writes=[wk_bc])
    k.op("dve", lambda e: e.tensor_scalar_mul(out=wq_bc[:, :], in0=wq_bc[:, :], scalar1=0.125),
         reads=[wq_bc], writes=[wq_bc])
    k.op("dve", lambda e: e.memset(gbcol[:, :], 0.0), writes=[gbcol])
    for j in range(4):
        k.dma(gbcol[32 * j:32 * j + 4, 0:1], c.GB[j].rearrange("(h o) -> h o", o=1), writes=[gbcol], slow=True)
    k.op("pool", lambda e: e.memset(MVs[0][:, :, :, :], 1.0), writes=[MVs[0]])
    kdup = kdup2[0]
    for blk in range(2):
        k.dma(cstg[:, 0, :], c.CK[blk * 128:(blk + 1) * 128, :], writes=[cstg])
        k.dma(cstg[:, 1, :], c.CV[blk * 128:(blk + 1) * 128, :], writes=[cstg])
        k.op("dve", lambda e: e.tensor_copy(out=kdup[:, :, :, :],
                                            in_=cstg[:, 0, :].rearrange("p (g o d) -> p g o d", g=2, o=1)
                                            .to_broadcast([128, 2, 2, 64])), reads=[cstg], writes=[kdup])
        pk = PS[2]
        for g in range(2):
            k.op("pe", lambda e, g=g: e.transpose(pk[:, g * 128:(g + 1) * 128],
                                                  kdup[:, g, :, :].rearrange("p a d -> p (a d)"), c.ident[:, :]),
                 reads=[kdup, c.ident], writes=[pk], inc=(g == 1))
        col = TT + blk * 128
        k.op("act", lambda e, col=col: e.activation(out=KT2[:, :, col:col + 128],
                                                    in_=pk[:, 0:256].rearrange("p (g t) -> p g t", g=2),
                                                    func=AF.Copy), reads=[pk], writes=[KT2])
        k.op("dve", lambda e, blk=blk: e.tensor_copy(out=VA[:, 36 + blk, :, 64:128],
                                                     in_=cstg[:, 1, :].rearrange("p (g d) -> p g d", g=2)),
             reads=[cstg], writes=[VA])

    class Deferred(list):
        cur = 0

        def append(self, fn):
            list.append(self, (self.cur, fn))

    deferred = Deferred()

    def flush(upto=10 ** 9):
        while deferred and deferred[0][0] <= upto:
            deferred.pop(0)[1]()

    def tile_a(s, i):
        cj = 0 if s < 8 else 1
        ti = s * 4 + i
        xt = xb[ti % 3]
        k.dma(xt[:, :], c.X[ti * 128:(ti + 1) * 128, :], writes=[xt])
        norm_to_hT(k, c, xt, hT[s % 2], i * 128,
                   lambda ch, cj=cj: c.G1[:, cj, ch:ch + 1], lambda ch, cj=cj: c.modc[:, cj, 0, ch:ch + 1],
                   (junk, ss, rstd, xs[ti % 3]), PS[0], PS[1], defer=deferred)

    def st_body(s):
        cj = 0 if s < 8 else 1
        smp = s < 8
        h = hT[s % 2]
        if smp:
            rp = ropet[s % 2]
            k.dma(rp[:, :, :, :], c.ROPE[:, s * 4:(s + 1) * 4, :, :], writes=[rp])
        QT_, MK_, MV_, MO_, MQT_, MKT_ = QTs[0], MKs[0], MVs[0], MOs[0], MQTs[0], MKTs[0]
        if s == 0:
            for i in range(4):
                tile_a(0, i)
                flush()

        def tile_b(i):
            ti = s * 4 + i
            tsl = slice(i * 128, (i + 1) * 128)
            pq, pkv, pmk, pmv, pmo = PS[2], PS[3], PS[4], PS[5], PS[6]
            for (pb, c0, c1) in ((pmk, 1280, 1792), (pmv, 1792, 2304), (pmo, 2304, 2816), (pq, 0, 512),
                                 (pkv, 512, 768)):
                for kc in range(8):
                    k.op("pe", lambda e, pb=pb, c0=c0, c1=c1, kc=kc, tsl=tsl: e.matmul(
                        pb[:, 0:c1 - c0], lhsT=h[:, kc, tsl], rhs=Win[:, kc, c0:c1], start=(kc == 0), stop=(kc == 7)),
                        reads=[h, Win], writes=[pb], inc=(kc == 7))
            flush(ti - 2)
            deferred.cur = ti
            qn, qr, kdup = qn2[ti % 2], qr2[ti % 2], kdup2[ti % 2]
            k.capture()
            k.op("act", lambda e, i=i: e.activation(out=MK_[:, i, :], in_=pmk[:, :], func=AF.Copy, scale=KSC),
                 reads=[pmk], writes=[MK_])
            k.op("dve", lambda e, i=i: e.tensor_copy(out=MV_[:, i, :, 0:128],
                                                     in_=pmv[:, :].rearrange("p (h d) -> p h d", h=4)),
                 reads=[pmv], writes=[MV_])
            k.op("act", lambda e: e.activation(out=mot[:, :], in_=pmo[:, :], func=AF.Exp, scale=-1.0),
                 reads=[pmo], writes=[mot])
            k.op("act", lambda e: e.activation(out=mot[:, :], in_=mot[:, :], func=AF.Ln, bias=1.0),
                 reads=[mot], writes=[mot])
            k.op("act", lambda e, i=i: e.activation(out=MO_[:, i, :], in_=mot[:, :], func=AF.Exp, scale=-1.0),
                 reads=[mot], writes=[MO_])
            Lm = k.capture_end()
            kv = kvf[ti % 2]
            kk = kn[ti % 2]
            k.capture()
            k.op("act", lambda e: e.activation(out=qf[:, :], in_=pq[:, :], func=AF.Copy), reads=[pq], writes=[qf])
            k.op("dve", lambda e: e.tensor_tensor(out=sq[:, :], in0=qf[:, :], in1=qf[:, :], op=ALU.mult),
                 reads=[qf], writes=[sq])
            k.op("dve", lambda e: e.tensor_reduce(out=ssq[:, 0:8], in_=sq[:, :].rearrange("p (h d) -> p h d", d=64),
                                                  axis=AX.X, op=ALU.add), reads=[sq], writes=[ssq])
            rstd_from_ss(k, ssq, rsq, 1.0 / 64, 8)
            k.op("dve", lambda e: e.tensor_tensor(out=qn[:, :].rearrange("p (h d) -> p h d", d=64),
                                                  in0=qf[:, :].rearrange("p (h d) -> p h d", d=64),
                                                  in1=rsq[:, 0:8].unsqueeze(2).to_broadcast([128, 8, 64]),
                                                  op=ALU.mult), reads=[qf, rsq], writes=[qn])
            k.op("dve", lambda e: e.tensor_tensor(out=qn[:, :].rearrange("p (h d) -> p h d", d=64),
                                                  in0=qn[:, :].rearrange("p (h d) -> p h d", d=64),
                                                  in1=wq_bc[:, :].unsqueeze(1).to_broadcast([128, 8, 64]),
                                                  op=ALU.mult), reads=[qn, wq_bc], writes=[qn])
            if smp:
                k.op("pool", lambda e, i=i: e.tensor_tensor(
                    out=t1[:, :].rearrange("p (h d) -> p h d", d=64),
                    in0=qn[:, :].rearrange("p (h d) -> p h d", d=64),
                    in1=rp[:, i, 0, :].unsqueeze(1).to_broadcast([128, 8, 64]), op=ALU.mult),
                    reads=[qn, rp], writes=[t1])
                for jj in range(2):
                    k.op("pool", lambda e, i=i, jj=jj: e.tensor_tensor(
                        out=t2[:, :].rearrange("p (h a j w) -> p h a j w", h=8, a=2, j=2)[:, :, :, jj, :],
                        in0=qn[:, :].rearrange("p (h a j w) -> p h a j w", h=8, a=2, j=2)[:, :, :, 1 - jj, :],
                        in1=rp[:, i, 1, :].rearrange("p (a j w) -> p a j w", a=2, j=2)[:, :, jj, :].unsqueeze(1)
                        .to_broadcast([128, 8, 2, 16]), op=ALU.mult),
                        reads=[qn, rp], writes=[t2])
                k.op("pool", lambda e: e.tensor_tensor(out=qr[:, :], in0=t1[:, :], in1=t2[:, :], op=ALU.add),
                     reads=[t1, t2], writes=[qr])
                qsrc = qr
            else:
                qsrc = qn
            def q_tr(qsrc=qsrc, tsl=tsl):
                pt = PS[7]
                for j in range(4):
                    k.op("pe", lambda e, j=j: e.transpose(pt[:, j * 128:(j + 1) * 128],
                                                          qsrc[:, j * 128:(j + 1) * 128], c.ident[:, :]),
                         reads=[qsrc, c.ident], writes=[pt], inc=(j == 3))
                k.op("act", lambda e: e.activation(out=QT_[:, :, tsl],
                                                   in_=pt[:, :].rearrange("p (j t) -> p j t", j=4),
                                                   func=AF.Copy), reads=[pt], writes=[QT_])
            deferred.append(q_tr)
            Lq = k.capture_end()
            k.capture()
            k.op("act", lambda e, kv=kv: e.activation(out=kv[:, :], in_=pkv[:, 0:256], func=AF.Copy),
                 reads=[pkv], writes=[kv])
            k.op("dve", lambda e, kv=kv: e.tensor_tensor(out=kt1[:, :], in0=kv[:, 0:128], in1=kv[:, 0:128],
                                                         op=ALU.mult), reads=[kv], writes=[kt1])
            k.op("dve", lambda e: e.tensor_reduce(out=ssk[:, 0:2], in_=kt1[:, :].rearrange("p (h d) -> p h d", d=64),
                                                  axis=AX.X, op=ALU.add), reads=[kt1], writes=[ssk])
            rstd_from_ss(k, ssk, rsk, 1.0 / 64, 2)
            k.op("dve", lambda e, kv=kv, kk=kk: e.tensor_tensor(
                out=kk[:, :].rearrange("p (h d) -> p h d", d=64),
                in0=kv[:, 0:128].rearrange("p (h d) -> p h d", d=64),
                in1=rsk[:, 0:2].unsqueeze(2).to_broadcast([128, 2, 64]), op=ALU.mult),
                reads=[kv, rsk], writes=[kk])
            k.op("dve", lambda e, kk=kk: e.tensor_tensor(
                out=kk[:, :].rearrange("p (h d) -> p h d", d=64),
                in0=kk[:, :].rearrange("p (h d) -> p h d", d=64),
                in1=wk_bc[:, :].unsqueeze(1).to_broadcast([128, 2, 64]), op=ALU.mult),
                reads=[kk, wk_bc], writes=[kk])
            if smp:
                k.op("pool", lambda e, i=i, kk=kk: e.tensor_tensor(
                    out=kt1[:, :].rearrange("p (h d) -> p h d", d=64),
                    in0=kk[:, :].rearrange("p (h d) -> p h d", d=64),
                    in1=rp[:, i, 0, :].unsqueeze(1).to_broadcast([128, 2, 64]), op=ALU.mult),
                    reads=[kk, rp], writes=[kt1])
                for jj in range(2):
                    k.op("pool", lambda e, i=i, kk=kk, jj=jj: e.tensor_tensor(
                        out=kt2[:, :].rearrange("p (h a j w) -> p h a j w", h=2, a=2, j=2)[:, :, :, jj, :],
                        in0=kk[:, :].rearrange("p (h a j w) -> p h a j w", h=2, a=2, j=2)[:, :, :, 1 - jj, :],
                        in1=rp[:, i, 1, :].rearrange("p (a j w) -> p a j w", a=2, j=2)[:, :, jj, :].unsqueeze(1)
                        .to_broadcast([128, 2, 2, 16]), op=ALU.mult),
                        reads=[kk, rp], writes=[kt2])
                k.op("pool", lambda e: e.tensor_tensor(out=kr[:, :], in0=kt1[:, :], in1=kt2[:, :], op=ALU.add),
                     reads=[kt1, kt2], writes=[kr])
                ksrc = kr
            else:
                ksrc = kk
                pt0 = (ti - 32) * 128
                k.dma(c.NK[pt0:pt0 + 128, :], kk[:, :], reads=[kk])
                k.dma(c.NV[pt0:pt0 + 128, :], kv[:, 128:256], reads=[kv])
            k.op("dve", lambda e, ksrc=ksrc: e.tensor_copy(
                out=kdup[:, :, :, :], in_=ksrc[:, :].rearrange("p (g o d) -> p g o d", g=2, o=1)
                .to_broadcast([128, 2, 2, 64])), reads=[ksrc], writes=[kdup])
            def k_tr(ti=ti):
                pk = PS[7]
                for g in range(2):
                    k.op("pe", lambda e, g=g: e.transpose(pk[:, g * 128:(g + 1) * 128],
                                                          kdup[:, g, :, :].rearrange("p a d -> p (a d)"),
                                                          c.ident[:, :]),
                         reads=[kdup, c.ident], writes=[pk], inc=(g == 1))
                k.op("act", lambda e: e.activation(out=KT2[:, :, ti * 128:(ti + 1) * 128],
                                                   in_=pk[:, 0:256].rearrange("p (g t) -> p g t", g=2),
                                                   func=AF.Copy), reads=[pk], writes=[KT2])
            deferred.append(k_tr)
            k.op("dve", lambda e, ti=ti, kv=kv: e.tensor_copy(out=VA[:, ti, :, 64:128],
                                                              in_=kv[:, 128:256].rearrange("p (g d) -> p g d", g=2)),
                 reads=[kv], writes=[VA])
            Lk = k.capture_end()
            k.emit_interleaved(Lm, Lq, Lk)

        for i in range(4):
            tile_b(i)
            if s + 1 < 9:
                tile_a(s + 1, i)
        for hh in range(8):
            pb = PS[2 + hh % 4]
            c0 = 768 + hh * 128
            for kc in range(8):
                k.op("pe", lambda e, pb=pb, c0=c0, kc=kc: e.matmul(
                    pb[:, :], lhsT=Win[:, kc, c0:c0 + 128], rhs=h[:, kc, :], start=(kc == 0), stop=(kc == 7)),
                    reads=[h, Win], writes=[pb], inc=(kc == 7))
            if hh == 3:
                flush()
            if hh < 4:
                k.op("dve", lambda e, pb=pb, hh=hh: e.tensor_copy(out=MQT_[:, hh, :], in_=pb[:, :]),
                     reads=[pb], writes=[MQT_])
            else:
                k.op("act", lambda e, pb=pb, hh=hh: e.activation(out=MKT_[:, hh - 4, :], in_=pb[:, :], func=AF.Copy,
                                                                 scale=KSC), reads=[pb], writes=[MKT_])
        pg = PS[6]
        for kc in range(8):
            k.op("pe", lambda e, kc=kc: e.matmul(pg[:, :], lhsT=Wg[:, kc, :], rhs=h[:, kc, :], start=(kc == 0),
                                                 stop=(kc == 7)), reads=[h, Wg], writes=[pg], inc=(kc == 7))
        k.op("act", lambda e: e.activation(out=gsb[:, :], in_=pg[:, :], func=AF.Identity, bias=gbcol[:, 0:1]),
             reads=[pg, gbcol], writes=[gsb])
        for r0 in (32, 96):
            k.op("act", lambda e, r0=r0: e.activation(out=gtmp[r0:r0 + 4, :], in_=gsb[r0:r0 + 4, :], func=AF.Exp,
                                                      scale=-1.0), reads=[gsb], writes=[gtmp])
            k.op("act", lambda e, r0=r0: e.activation(out=gtmp[r0:r0 + 4, :], in_=gtmp[r0:r0 + 4, :], func=AF.Ln,
                                                      bias=1.0), reads=[gtmp], writes=[gtmp])
            k.op("dve", lambda e, r0=r0: e.tensor_scalar_mul(out=gsb[r0:r0 + 4, :], in0=gtmp[r0:r0 + 4, :],
                                                             scalar1=-1.0), reads=[gtmp], writes=[gsb])
        flush()
        tok = slice(s * 512, (s + 1) * 512)
        k.dma(c.QTd[:, :, tok].rearrange("c p t -> p c t"), QT_[:, :, :], reads=[QT_])
        k.dma(c.MQTd[:, :, tok].rearrange("c p t -> p c t"), MQT_[:, :, :], reads=[MQT_])
        k.dma(c.MKTd[:, :, tok].rearrange("c p t -> p c t"), MKT_[:, :, :], reads=[MKT_])
        k.dma(c.MKd[tok, :].rearrange("(i p) f -> p i f", p=128), MK_[:, :, :], reads=[MK_])
        k.dma(c.MVd[tok, :].rearrange("(i p) f -> p i f", p=128), MV_[:, :, :, :].rearrange("p i h d -> p i (h d)"),
              reads=[MV_])
        k.dma(c.MOd[tok, :].rearrange("(i p) f -> p i f", p=128), MO_[:, :, :], reads=[MO_])
        for j in range(4):
            k.dma(c.GATd[j, :, tok], gsb[32 * j:32 * j + 4, :], reads=[gsb])
    for s in range(9):
        st_body(s)
    k.barrier()
    k.sb_phase = c.sb_phase0_a
    k.phase_mem()


def _consts():
    f = np.float32
    tok = np.arange(4096)
    row = (tok // 64).astype(f)
    col = (tok % 64).astype(f)
    inv = (f(10000.0) ** (-np.arange(0, 32, 2, dtype=f) / f(32))).astype(f)
    ang = np.stack([row[:, None] * inv[None, :], col[:, None] * inv[None, :]], axis=1).astype(f)
    cs, sn = np.cos(ang).astype(f), np.sin(ang).astype(f)
    Cf = np.stack([cs, cs], axis=2)
    Ss = np.stack([-sn, sn], axis=2)
    tab = np.stack([Cf.reshape(4096, 64), Ss.reshape(4096, 64)], axis=1)
    rope = np.ascontiguousarray(tab.reshape(32, 128, 2, 64).transpose(1, 0, 2, 3))
    ident = np.eye(128, dtype=f)
    sidx = np.arange(128)[:, None]
    lidx = np.arange(128)[None, :]
    mf = (lidx >= sidx).astype(f)
    mb = (lidx <= sidx).astype(f)
    masks = np.stack([np.tile(mf, (1, 4)), np.tile(mb, (1, 4))], axis=1)
    return rope, ident, np.ascontiguousarray(masks)


def make_in_maps(inp):
    rope, ident, masks = _consts()
    f = np.float32
    c = lambda a: np.ascontiguousarray(np.asarray(a), dtype=f)
    maps = []
    for b in range(8):
        x = np.concatenate([inp["x_sample"][b], inp["x_prompt"][2 * b], inp["x_prompt"][2 * b + 1]], axis=0)
        m = {
            "x": c(x),
            "cache_k": c(np.asarray(inp["cache_k"])[b, 0].reshape(256, 128)),
            "cache_v": c(np.asarray(inp["cache_v"])[b, 0].reshape(256, 128)),
            "state_C": c(np.asarray(inp["state_C"])[b, 0]),
            "state_n": c(np.asarray(inp["state_n"])[b, 0]),
            "state_m": c(np.asarray(inp["state_m"])[b, 0]),
            "cond": c(np.stack([np.asarray(inp["c"])[b], np.asarray(inp["c_ctx"])], axis=0)),
            "w_ada": c(np.asarray(inp["w_ada"])[0]), "b_ada": c(np.asarray(inp["b_ada"])[0]),
            "norm1_w": c(np.asarray(inp["norm1_w"])[0]), "w_in": c(np.asarray(inp["w_in"])[0]),
            "gate_bias": c(np.asarray(inp["gate_bias"])[0]), "q_norm_w": c(np.asarray(inp["q_norm_w"])[0]),
            "k_norm_w": c(np.asarray(inp["k_norm_w"])[0]), "mlstm_norm_w": c(np.asarray(inp["mlstm_norm_w"])[0]),
            "w_out": c(np.asarray(inp["w_out"])[0]), "norm2_w": c(np.asarray(inp["norm2_w"])[0]),
            "w_gu": c(np.asarray(inp["w_gu"])[0]), "w_down": c(np.asarray(inp["w_down"])[0]),
            "final_norm_w": c(inp["final_norm_w"]),
            "rope": rope, "ident": ident, "masks": masks,
        }
        maps.append(m)
    return maps


_NC = None


def kernel(**inp):
    global _NC
    if _NC is None:
        _NC = build_nc()
    maps = make_in_maps(inp)
    res = run_bass_kernel_spmd(_NC, maps, core_ids=list(range(8)))
    R = res.results
    f = np.float32
    y_s = np.stack([R[b]["y"][0:4096] for b in range(8)], axis=0).astype(f)
    y_p = np.stack([R[b]["y"][4096 + 256 * j:4096 + 256 * (j + 1)] for b in range(8) for j in range(2)], axis=0).astype(f)
    nk = np.stack([R[b]["nk"][256 * j:256 * (j + 1)].reshape(256, 2, 64) for b in range(8) for j in range(2)], axis=0)
    nv = np.stack([R[b]["nv"][256 * j:256 * (j + 1)].reshape(256, 2, 64) for b in range(8) for j in range(2)], axis=0)
    nC = np.stack([R[b]["nC"][j] for b in range(8) for j in range(2)], axis=0)
    nn = np.stack([R[b]["nn"][j] for b in range(8) for j in range(2)], axis=0)
    nm = np.stack([R[b]["nm"][j] for b in range(8) for j in range(2)], axis=0)
    return (y_p, y_s, nk[:, None].astype(f), nv[:, None].astype(f), nC[:, None].astype(f), nn[:, None].astype(f),
            nm[:, None].astype(f))


def phase_b1(ctx):
    c = NS(ctx)
    k = c.k
    PS = c.PS
    KT2, VA = c.KT2, c.VA
    pre_gen, b2_main = make_b2(ctx)
    ctx["b2_main"] = b2_main
    k.sb_phase = k.sb_ptr
    pre = pre_gen()
    SP = [Tl(c.PSALL[:, b * 1024:(b + 1) * 1024]) for b in range(2)]
    QTg = [k.sb([128, 4, 512], BF16, f"QTg{i}") for i in range(2)]
    PTP = [k.sb([128, 1024], BF16, f"PT{i}") for i in range(3)]
    AO = [k.sb([128, 4, 512], BF16, f"AO{i}") for i in range(2)]
    rden = [k.sb([128, 512], F32, f"rden{i}") for i in range(2)]
    obs = k.sb([128, 512], F32, "obs")
    groups = [(g * 512, 512, list(range(32)) + [36, 37]) for g in range(8)]
    groups += [(4096, 256, [32, 33]), (4352, 256, [34, 35])]
    cnt = [0]

    def group_body(gi, t0, nq, kbs):
        qt = QTg[gi % 2]
        ao = AO[gi % 2]
        if gi == 0:
            k.dma(qt[:, :, 0:nq], c.QTd[:, :, t0:t0 + nq].rearrange("c p t -> p c t"), writes=[qt])
        if gi + 1 < len(groups):
            t0n, nqn, _ = groups[gi + 1]
            qtn = QTg[(gi + 1) % 2]
            k.dma(qtn[:, :, 0:nqn], c.QTd[:, :, t0n:t0n + nqn].rearrange("c p t -> p c t"), writes=[qtn])
        iters = [(j, idx, kb) for j in range(4) for idx, kb in enumerate(kbs)]
        base = cnt[0]
        cnt[0] += len(iters)

        def emit_s(n):
            j, idx, kb = iters[n]
            g = j // 2
            kcol = kb * 128 if kb < 36 else TT + (kb - 36) * 128
            sp = SP[(base + n) % 2]
            k.op("pe", lambda e: e.matmul(sp[:, 0:nq], lhsT=KT2[0:64, g, kcol:kcol + 128], rhs=qt[0:64, j, 0:nq],
                                          start=True, stop=True), reads=[KT2, qt], writes=[sp], inc=False)
            k.op("pe", lambda e: e.matmul(sp[:, 512:512 + nq], lhsT=KT2[64:128, g, kcol:kcol + 128],
                                          rhs=qt[64:128, j, 0:nq], start=True, stop=True),
                 reads=[KT2, qt], writes=[sp])

        def emit_rest(n):
            j, idx, kb = iters[n]
            g = j // 2
            sp = SP[(base + n) % 2]
            pt = PTP[(base + n) % 3]
            oa, ob = (PS[4], PS[6])[j % 2], PS[5]
            k.op("act", lambda e: e.activation(out=pt[:, :].rearrange("p (a t) -> p a t", a=2)[:, :, 0:nq],
                                               in_=sp[:, :].rearrange("p (a t) -> p a t", a=2)[:, :, 0:nq],
                                               func=AF.Exp), reads=[sp], writes=[pt])
            first, last = idx == 0, idx == len(kbs) - 1
            k.op("pe", lambda e: e.matmul(oa[:, 0:nq], lhsT=VA[:, kb, g, 64:192], rhs=pt[:, 0:nq], start=first,
                                          stop=last), reads=[VA, pt], writes=[oa], inc=False)
            k.op("pe", lambda e: e.matmul(ob[:, 0:nq], lhsT=VA[:, kb, g, 0:128], rhs=pt[:, 512:512 + nq],
                                          start=first, stop=last), reads=[VA, pt], writes=[ob])
            if last:
                ra, rb = rden[0], rden[1]
                k.op("dve", lambda e: e.tensor_copy(out=obs[:, 0:nq], in_=ob[:, 0:nq]), reads=[ob], writes=[obs])
                k.op("dve", lambda e: e.reciprocal(out=rb[64:128, 0:nq], in_=obs[0:64, 0:nq]), reads=[obs],
                     writes=[rb])
                k.op("dve", lambda e: e.tensor_tensor(out=ao[64:128, j, 0:nq], in0=obs[64:128, 0:nq],
                                                      in1=rb[64:128, 0:nq], op=ALU.mult), reads=[obs, rb],
                     writes=[ao])
                k.op("dve", lambda e: e.reciprocal(out=ra[0:64, 0:nq], in_=oa[64:128, 0:nq]), reads=[oa], writes=[ra])
                k.op("dve", lambda e: e.tensor_tensor(out=ao[0:64, j, 0:nq], in0=oa[0:64, 0:nq], in1=ra[0:64, 0:nq],
                                                      op=ALU.mult), reads=[oa, ra], writes=[ao])

        emit_s(0)
        for n in range(len(iters)):
            if n + 1 < len(iters):
                emit_s(n + 1)
            emit_rest(n)
            if PRE_IN_ATT:
                next(pre, None)
        k.dma(c.AOTd[:, :, t0:t0 + nq].rearrange("c p t -> p c t"), ao[:, :, 0:nq], reads=[ao])

    for gi, (t0, nq, kbs) in enumerate(groups):
        group_body(gi, t0, nq, kbs)
    for _ in pre:
        pass
    k.barrier()
    k.phase_mem()


def make_b2(ctx):
    c = NS(ctx)
    k = c.k
    PS = c.PS
    pS = PS[0]
    pGd, pId, pTd = (PS[1], PS[4]), (PS[2], PS[5]), (PS[3], PS[6])
    pX = PS[7]
    Cst = [k.sb([128, 4, 130], F32, f"Cst{i}") for i in range(2)]
    Csnap = k.sb([128, NT, 4, 130], BF16, "Csnap")
    mst = [k.sb([4, 2], F32, f"mst{i}") for i in range(2)]
    mbprev = k.sb([4, NT + 4], F32, "mbprev")
    GT = [k.sb([4, 4, 128], F32, f"GT{i}") for i in range(2)]
    kk = [k.sb([128, 4, 128], BF16, f"kk{i}") for i in range(2)]
    va = [k.sb([128, 4, 130], BF16, f"va{i}") for i in range(2)]
    rows = [[k.sb([4, 128], F32, f"row{d}_{i}") for i in range(8)] for d in range(2)]
    dg = [k.sb([4, 4], F32, f"dg{d}") for d in range(2)]
    cols = [k.sb([128, 24], F32, f"cols{d}") for d in range(2)]
    kw = k.sb([128, 512], BF16, "kw")
    cols1p = [cols[1], k.sb([128, 24], F32, "cols1b")]
    ident, ones4, eye4 = c.ident, c.ones4, c.eye4
    maskneg = identb = None
    masks = mnw = Cbf = qT = kT = mo = blk = Dsb = swT = qI = hd = hm = sqh = ss4 = rs4 = hmT = None

    def late_alloc():
        nonlocal maskneg, identb
        nonlocal masks, mnw, Cbf, qT, kT, mo, blk, Dsb, swT, qI, hd, hm, sqh, ss4, rs4, hmT
        masks = k.sb([128, 2, 512], F32, "masks")
        mnw = k.sb([128, 512], F32, "mnw")
        Cbf = k.sb([128, 4, 130], BF16, "Cbf")
        qT = [k.sb([128, 4, 128], BF16, f"qT{i}") for i in range(2)]
        kT = [k.sb([128, 4, 128], BF16, f"kT{i}") for i in range(2)]
        mo = [k.sb([128, 512], BF16, f"mo{i}") for i in range(2)]
        blk = [[k.sb([4, 4, 128], F32, f"blk{d}_{i}") for i in range(2)] for d in range(2)]
        Dsb = [k.sb([128, 512], F32, f"Dsb{d}") for d in range(2)]
        swT = [k.sb([128, 512], BF16, f"swT{d}") for d in range(2)]
        qI = [k.sb([128, 512], BF16, f"qI{d}") for d in range(2)]
        hd = [[k.sb([128, 512], F32, f"hd{p}_{d}") for d in range(2)] for p in range(2)]
        hm = k.sb([128, 512], F32, "hm")
        sqh = k.sb([128, 512], F32, "sqh")
        ss4 = k.sb([128, 8], F32, "ss4")
        rs4 = k.sb([128, 8], F32, "rs4")
        hmT = [k.sb([128, 4, 128], BF16, f"hmT{i}") for i in range(2)]
        maskneg = k.sb([128, 2, 512], BF16, "maskneg")
        identb = k.sb([128, 128], BF16, "identb")
        k.dma(masks[:, :, :], c.MASKS[:, :, :], writes=[masks])
        k.dma(mnw[:, :], c.MNW.partition_broadcast(128), writes=[mnw])
        k.op("dve", lambda e: e.tensor_scalar(out=maskneg[:, :, :], in0=masks[:, :, :], scalar1=-1.0, scalar2=30000.0,
                                              op0=ALU.add, op1=ALU.mult), reads=[masks], writes=[maskneg])
        k.op("dve", lambda e: e.tensor_copy(out=identb[:, :], in_=ident[:, :]), reads=[ident], writes=[identb])

    def load_chunk(ci, full):
        tok = slice(ci * 128, (ci + 1) * 128)
        i2 = ci % 2
        k.dma(GT[i2][:, :, :], c.GATd[:, :, tok].rearrange("t h n -> h t n"), writes=[GT[i2]])
        k.dma(kk[i2][:, :, :], c.MKd[tok, :].rearrange("p (h d) -> p h d", h=4), writes=[kk[i2]])
        k.dma(va[i2][:, :, :], c.MVd[tok, :].rearrange("p (h d) -> p h d", h=4), writes=[va[i2]])
        if full:
            k.dma(qT[i2][:, :, :], c.MQTd[:, :, tok].rearrange("h p t -> p h t"), writes=[qT[i2]])
            k.dma(kT[i2][:, :, :], c.MKTd[:, :, tok].rearrange("h p t -> p h t"), writes=[kT[i2]])

    def load_mo(ci):
        tok = slice(ci * 128, (ci + 1) * 128)
        k.dma(mo[ci % 2][:, :], c.MOd[tok, :], writes=[mo[ci % 2]])

    def gate_prep(d, gt, mprev, full, upd, cl=None, pT=None):
        mt, mc = mprev
        rb_, ra_, rg_, rng_, rin_, rgu_, rw_, rt_ = rows[d]
        pT = pT or pTd[d]
        cl = cl or cols[d]
        lf = lambda: gt[:, 1 + 2 * d, :]
        ig = lambda: gt[:, 2 * d, :]
        rv = (lambda ap: ap) if d == 0 else (lambda ap: ap[:, ::-1])
        last = 127 if d == 0 else 0
        k.op("dve", lambda e: e.tensor_tensor_scan(out=rv(rb_[:, :]), data0=rv(ones4[:, :]), data1=rv(lf()),
                                                   initial=0.0, op0=ALU.mult, op1=ALU.add),
             reads=[gt, ones4], writes=[rb_])
        yield
        k.op("dve", lambda e: e.tensor_tensor(out=ra_[:, :], in0=ig(), in1=rb_[:, :], op=ALU.subtract),
             reads=[gt, rb_], writes=[ra_])
        yield
        k.op("pe", lambda e: e.transpose(pT[:, 0:4], ra_[:, :], ident[0:4, 0:4]), reads=[ra_, ident], writes=[pT])
        k.op("dve", lambda e: e.tensor_tensor_scan(out=rv(rg_[:, :]), data0=rv(ra_[:, :]), data1=rv(ra_[:, :]),
                                                   initial=mt[:, mc:mc + 1], op0=ALU.max, op1=ALU.max),
             reads=[ra_, mt], writes=[rg_])
        yield
        k.op("dve", lambda e: e.tensor_scalar_mul(out=rng_[:, :], in0=rg_[:, :], scalar1=-1.0),
             reads=[rg_], writes=[rng_])
        if full:
            k.op("dve", lambda e: e.tensor_tensor(out=rt_[:, :], in0=rb_[:, :], in1=rg_[:, :], op=ALU.add),
                 reads=[rb_, rg_], writes=[rt_])
            yield
            bk0 = blk[d][0]
            k.op("dve", lambda e: e.tensor_tensor(
                out=bk0[:, :, :], in0=rng_[:, :].unsqueeze(1).to_broadcast([4, 4, 128]),
                in1=eye4[:, :].unsqueeze(2).to_broadcast([4, 4, 128]), op=ALU.mult),
                reads=[rng_, eye4], writes=[bk0])
            yield
            k.op("pe", lambda e: e.matmul(pGd[d][:, :], lhsT=ones4[:, :],
                                          rhs=bk0[:, :, :].rearrange("k h l -> k (h l)"),
                                          start=True, stop=False), reads=[ones4, bk0], writes=[pGd[d]], inc=False)
            k.op("pe", lambda e: e.matmul(pGd[d][:, :], lhsT=identb[:, :], rhs=maskneg[:, d, :],
                                          start=False, stop=True), reads=[identb, maskneg], writes=[pGd[d]])
        yield
        k.op("act", lambda e: e.activation(out=rin_[:, :], in_=rng_[:, :], func=AF.Exp, bias=mt[:, mc:mc + 1]),
             reads=[rng_, mt], writes=[rin_])
        if full:
            yield
            bk1 = blk[d][1]
            k.op("dve", lambda e: e.tensor_tensor(
                out=bk1[:, :, :], in0=rin_[:, :].unsqueeze(1).to_broadcast([4, 4, 128]),
                in1=eye4[:, :].unsqueeze(2).to_broadcast([4, 4, 128]), op=ALU.mult),
                reads=[rin_, eye4], writes=[bk1])
            yield
            k.op("pe", lambda e: e.matmul(pId[d][:, :], lhsT=ones4[:, :],
                                          rhs=bk1[:, :, :].rearrange("k h l -> k (h l)"),
                                          start=True, stop=True), reads=[ones4, bk1], writes=[pId[d]])
        if full:
            k.op("act", lambda e: e.activation(out=rgu_[:, :], in_=rt_[:, :], func=AF.Exp, scale=-1.0),
                 reads=[rt_], writes=[rgu_])
        if upd:
            k.op("act", lambda e: e.activation(out=rw_[:, :], in_=ra_[:, :], func=AF.Exp,
                                               bias=rng_[:, last:last + 1]), reads=[ra_, rng_], writes=[rw_])
        yield
        if full:
            k.op("pe", lambda e: e.transpose(pT[:, 4:8], rgu_[:, :], ident[0:4, 0:4]), reads=[rgu_, ident],
                 writes=[pT])
        if upd:
            k.op("pe", lambda e: e.transpose(pT[:, 8:12], rw_[:, :], ident[0:4, 0:4]), reads=[rw_, ident],
                 writes=[pT])
            k.op("dve", lambda e: e.tensor_scalar(out=dg[d][:, :], in0=eye4[:, :], scalar1=rin_[:, last:last + 1],
                                                  scalar2=None, op0=ALU.mult), reads=[eye4, rin_], writes=[dg[d]])
            yield
            k.op("pe", lambda e: e.matmul(pT[:, 12:16], lhsT=ones4[:, :], rhs=dg[d][:, :], start=True, stop=True),
                 reads=[ones4, dg[d]], writes=[pT])
        yield
        hi = 16 if upd else 8
        if not full and upd:
            k.op("dve", lambda e: e.tensor_copy(out=cl[:, 0:4], in_=pT[:, 0:4]), reads=[pT], writes=[cl])
            k.op("dve", lambda e: e.tensor_copy(out=cl[:, 8:16], in_=pT[:, 8:16]), reads=[pT], writes=[cl])
        else:
            k.op("dve", lambda e: e.tensor_copy(out=cl[:, 0:hi], in_=pT[:, 0:hi]), reads=[pT], writes=[cl])
        yield

    def new_m(d, mt_out, mc_out):
        rb_, rg_ = rows[d][0], rows[d][2]
        last = 127 if d == 0 else 0
        k.op("dve", lambda e: e.tensor_tensor(out=mt_out[:, mc_out:mc_out + 1], in0=rb_[:, last:last + 1],
                                              in1=rg_[:, last:last + 1], op=ALU.add),
             reads=[rb_, rg_], writes=[mt_out])

    def state_update(d, kk_, va_, banks=None, cl=None):
        banks = banks or ((pId[d], 0), (pTd[d], 128))
        Cs = Cst[d]
        cl = cl or cols[d]
        k.op("dve", lambda e: e.tensor_tensor(out=kw[:, :].rearrange("p (h d) -> p h d", h=4), in0=kk_[:, :, :],
                                              in1=cl[:, 8:12].unsqueeze(2).to_broadcast([128, 4, 128]),
                                              op=ALU.mult), reads=[kk_, cl], writes=[kw])
        yield
        for h in range(4):
            pd, off = banks[h % 2]
            k.op("pe", lambda e, h=h, pd=pd, off=off: e.matmul(pd[:, off:off + 129], lhsT=kw[:, h * 128:(h + 1) * 128],
                                                      rhs=va_[:, h, 0:129], start=True, stop=True),
                 reads=[kw, va_], writes=[pd])
            yield
            k.op("dve", lambda e, h=h, pd=pd, off=off: e.scalar_tensor_tensor(
                out=Cs[:, h, 0:129], in0=Cs[:, h, 0:129], scalar=cl[:, 12 + h:13 + h], in1=pd[:, off:off + 129],
                op0=ALU.mult, op1=ALU.add), reads=[Cs, cl, pd], writes=[Cs])
            yield

    def run(*gens):
        gens = list(gens)
        while gens:
            for g in list(gens):
                try:
                    next(g)
                except StopIteration:
                    gens.remove(g)

    def init_dir(seq, d):
        Cs = Cst[d]
        if seq == 0:
            k.op("pool", lambda e: e.memset(Cs[:, :, :], 0.0), writes=[Cs])
            k.op("dve", lambda e: e.memset(mst[d][:, :], 0.0), writes=[mst[d]])
            k.dma(Cs[:, :, 0:128], c.SC[d].rearrange("h p e -> p h e"), writes=[Cs])
            k.dma(Cs[:, :, 128], c.SN[d].rearrange("h p -> p h"), writes=[Cs], slow=True)
            k.dma(mst[d][:, 0:1], c.SM[d].rearrange("(h o) -> h o", o=1), writes=[mst[d]], slow=True)
        else:
            k.op("pool", lambda e: e.memset(Cs[:, :, :], 0.0), writes=[Cs])
            k.op("dve", lambda e: e.memset(mst[d][:, :], 0.0), writes=[mst[d]])

    def store_state(seq, d):
        p = seq - 1
        Cs = Cst[d]
        k.dma(c.NC_[p, d].rearrange("h p e -> p h e"), Cs[:, :, 0:128], reads=[Cs])
        k.dma(c.NN[p, d].rearrange("h p -> p h"), Cs[:, :, 128], reads=[Cs], slow=True)
        k.dma(c.NM[p, d].rearrange("(h o) -> h o", o=1), mst[d][:, 0:1], reads=[mst[d]], slow=True)

    def pre_G(ci):
        i2 = ci % 2
        load_chunk(ci, False)
        yield
        k.op("dve", lambda e: e.tensor_copy(out=mbprev[:, ci:ci + 1], in_=mst[1][:, 0:1]), reads=[mst[1]],
             writes=[mbprev])
        yield
        yield from gate_prep(1, GT[i2], (mst[1], 0), False, True, cl=cols1p[i2], pT=pX)
        new_m(1, mst[1], 0)
        yield

    def pre_U(ci):
        i2 = ci % 2
        k.op("act", lambda e: e.activation(out=Csnap[:, ci, :, :], in_=Cst[1][:, :, :], func=AF.Copy),
             reads=[Cst[1]], writes=[Csnap])
        yield
        yield from state_update(1, kk[i2], va[i2], banks=((pX, 128), (pX, 128)), cl=cols1p[i2])

    def zip2(g1, g2):
        gens = [g for g in (g1, g2) if g is not None]
        while gens:
            for g in list(gens):
                try:
                    next(g)
                except StopIteration:
                    gens.remove(g)
            yield

    seqs = [(0, list(range(32))), (1, [32, 33]), (2, [34, 35])]

    def pre_gen():
        for seq, chunks in seqs:
            init_dir(seq, 1)
            yield
            order = list(reversed(chunks))
            yield from pre_G(order[0])
            for n, ci in enumerate(order):
                nxt = pre_G(order[n + 1]) if n + 1 < len(order) else None
                yield from zip2(nxt, pre_U(ci))
            if seq > 0:
                store_state(seq, 1)
                yield

    def dir_chain(d, ci, q_, kk_, va_, gt):
        mprev = (mst[0], 0) if d == 0 else (mbprev, ci)
        rng_, rin_ = rows[d][3], rows[d][4]
        pG, pI, pT, cl = pGd[d], pId[d], pTd[d], cols[d]
        yield from gate_prep(d, gt, mprev, True, d == 0)
        if d == 0:
            new_m(0, mst[0], 0)
        for h in range(4):
            k.op("act", lambda e, h=h: e.activation(out=Dsb[d][:, h * 128:(h + 1) * 128],
                                                    in_=pG[:, h * 128:(h + 1) * 128], func=AF.Exp,
                                                    bias=cl[:, h:h + 1]), reads=[pG, cl], writes=[Dsb[d]])
        k.op("dve", lambda e: e.tensor_tensor(out=qI[d][:, :], in0=pI[:, :],
                                              in1=q_[:, :, :].rearrange("p h t -> p (h t)"), op=ALU.mult),
             reads=[pI, q_], writes=[qI[d]])
        yield
        k.op("dve", lambda e: e.tensor_tensor(out=swT[d][:, :], in0=pS[:, :], in1=Dsb[d][:, :], op=ALU.mult),
             reads=[pS, Dsb[d]], writes=[swT[d]])
        if d == 0:
            k.op("act", lambda e: e.activation(out=Cbf[:, :, :], in_=Cst[0][:, :, :], func=AF.Copy),
                 reads=[Cst[0]], writes=[Cbf])
        yield
        for h in range(4):
            cprev = (lambda h=h: Cbf[:, h, :]) if d == 0 else (lambda h=h: Csnap[:, ci, h, :])
            ctile = Cbf if d == 0 else Csnap
            k.op("pe", lambda e, h=h: e.matmul(pG[:, h * 128:(h + 1) * 128], lhsT=swT[d][:, h * 128:(h + 1) * 128],
                                               rhs=va_[:, h, 0:128], start=True, stop=False),
                 reads=[swT[d], va_], writes=[pG], inc=False)
            k.op("pe", lambda e, h=h, cprev=cprev: e.matmul(pG[:, h * 128:(h + 1) * 128],
                                                            lhsT=qI[d][:, h * 128:(h + 1) * 128],
                                                            rhs=cprev()[:, 0:128], start=False, stop=True),
                 reads=[qI[d], ctile], writes=[pG], inc=False)
            k.op("pe", lambda e, h=h: e.matmul(pT[:, 16 + h:17 + h], lhsT=swT[d][:, h * 128:(h + 1) * 128],
                                               rhs=va_[:, h, 128:129], start=True, stop=False),
                 reads=[swT[d], va_], writes=[pT], inc=False)
            k.op("pe", lambda e, h=h, cprev=cprev: e.matmul(pT[:, 16 + h:17 + h],
                                                            lhsT=qI[d][:, h * 128:(h + 1) * 128],
                                                            rhs=cprev()[:, 128:129], start=False, stop=True),
                 reads=[qI[d], ctile], writes=[pT])
            yield
        k.op("dve", lambda e: e.tensor_scalar(out=cl[:, 20:24], in0=pT[:, 16:20], scalar1=-1.0, scalar2=None,
                                              op0=ALU.mult), reads=[pT], writes=[cl])
        yield
        k.op("dve", lambda e: e.tensor_tensor(out=cl[:, 16:20], in0=pT[:, 16:20], in1=cl[:, 20:24], op=ALU.max),
             reads=[pT, cl], writes=[cl])
        yield
        k.op("dve", lambda e: e.tensor_tensor(out=cl[:, 16:20], in0=cl[:, 16:20], in1=cl[:, 4:8], op=ALU.max),
             reads=[cl], writes=[cl])
        yield
        k.op("dve", lambda e: e.reciprocal(out=cl[:, 16:20], in_=cl[:, 16:20]), reads=[cl], writes=[cl])
        yield
        hdt = hd[ci % 2][d]
        k.op("dve", lambda e: e.tensor_tensor(out=hdt[:, :].rearrange("p (h e) -> p h e", h=4),
                                              in0=pG[:, :].rearrange("p (h e) -> p h e", h=4),
                                              in1=cl[:, 16:20].unsqueeze(2).to_broadcast([128, 4, 128]),
                                              op=ALU.mult), reads=[pG, cl], writes=[hdt])
        yield
        if d == 0:
            yield from state_update(0, kk_, va_)

    def tail_chain(ci):
        i2 = ci % 2
        mo_ = mo[i2]
        h0, h1 = hd[i2]
        k.op("pool", lambda e: e.tensor_tensor(out=hm[:, :], in0=h0[:, :], in1=h1[:, :], op=ALU.add),
             reads=[h0, h1], writes=[hm])
        yield
        k.op("dve", lambda e: e.tensor_tensor(out=sqh[:, :], in0=hm[:, :], in1=hm[:, :], op=ALU.mult),
             reads=[hm], writes=[sqh])
        yield
        k.op("dve", lambda e: e.tensor_reduce(out=ss4[:, 0:4], in_=sqh[:, :].rearrange("p (h d) -> p h d", h=4),
                                              axis=AX.X, op=ALU.add), reads=[sqh], writes=[ss4])
        yield
        k.op("act", lambda e: e.activation(out=rs4[:, 0:4], in_=ss4[:, 0:4], func=AF.Ln, scale=1.0 / 128, bias=EPS),
             reads=[ss4], writes=[rs4])
        yield
        k.op("act", lambda e: e.activation(out=rs4[:, 0:4], in_=rs4[:, 0:4], func=AF.Exp, scale=-0.5),
             reads=[rs4], writes=[rs4])
        yield
        k.op("dve", lambda e: e.tensor_tensor(out=hm[:, :].rearrange("p (h d) -> p h d", h=4),
                                              in0=hm[:, :].rearrange("p (h d) -> p h d", h=4),
                                              in1=rs4[:, 0:4].unsqueeze(2).to_broadcast([128, 4, 128]),
                                              op=ALU.mult), reads=[hm, rs4], writes=[hm])
        yield
        k.op("pool", lambda e: e.tensor_tensor(out=hm[:, :], in0=hm[:, :], in1=mnw[:, :], op=ALU.mult),
             reads=[hm, mnw], writes=[hm])
        yield
        k.op("pool", lambda e: e.tensor_tensor(out=sqh[:, :], in0=hm[:, :], in1=mo_[:, :], op=ALU.mult),
             reads=[hm, mo_], writes=[sqh])
        yield
        for h in range(4):
            k.op("pe", lambda e, h=h: e.transpose(pX[:, h * 128:(h + 1) * 128], sqh[:, h * 128:(h + 1) * 128],
                                                  ident[:, :]), reads=[sqh, ident], writes=[pX], inc=(h == 3))
        yield
        ho = hmT[i2]
        k.op("act", lambda e: e.activation(out=ho[:, :, :], in_=pX[:, :].rearrange("p (h t) -> p h t", h=4),
                                           func=AF.Copy), reads=[pX], writes=[ho])
        yield
        k.dma(c.HMTd[:, :, ci * 128:(ci + 1) * 128].rearrange("h p t -> p h t"), ho[:, :, :], reads=[ho])

    def main_chunk(ci, prev, nxt):
        i2 = ci % 2
        if nxt is not None:
            load_chunk(nxt, True)
        q_, kT_, kk_, va_, gt = qT[i2], kT[i2], kk[i2], va[i2], GT[i2]
        for h in range(4):
            k.op("pe", lambda e, h=h: e.matmul(pS[:, h * 128:(h + 1) * 128], lhsT=kT_[:, h, :], rhs=q_[:, h, :],
                                               start=True, stop=True), reads=[kT_, q_], writes=[pS], inc=(h == 3))
        gens = [dir_chain(0, ci, q_, kk_, va_, gt), dir_chain(1, ci, q_, kk_, va_, gt)]
        if prev is not None:
            gens.append(tail_chain(prev))
        run(*gens)
        if nxt is not None:
            load_mo(nxt)

    def main():
        late_alloc()
        allc = [ci for _, chunks in seqs for ci in chunks]
        load_chunk(allc[0], True)
        load_mo(allc[0])
        pos = 0
        for seq, chunks in seqs:
            init_dir(seq, 0)
            prev = None
            for ci in chunks:
                nxt = allc[pos + 1] if pos + 1 < len(allc) else None
                main_chunk(ci, prev, nxt)
                prev = ci
                pos += 1
            run(tail_chain(prev))
            if seq > 0:
                store_state(seq, 0)
        k.barrier()
        k.sb_phase = ctx["sb_phase0"]
        k.phase_mem()

    return pre_gen, main


def phase_b2(ctx):
    ctx["b2_main"]()


def phase_c1(ctx):
    c = NS(ctx)
    k = c.k
    PS = c.PS
    Wout = k.sb([128, 8, D], BF16, "Wout")
    stg = [k.sb([128, D], F32, f"stgo{i}") for i in range(2)]
    g1b = k.sb([128, D], F32, "g1b")
    mixT = [k.sb([128, 8, 512], BF16, f"mixT{i}") for i in range(2)]
    xb = [k.sb([128, D], F32, f"xc{i}") for i in range(3)]
    x1 = [k.sb([128, D], F32, f"x1{i}") for i in range(2)]
    tmp = k.sb([128, D], F32, "tmpc1")
    junk = k.sb([128, D], BF16, "junkc")
    xs = [k.sb([128, D], F32, f"xsc{i}") for i in range(3)]
    deferred = []

    def flush(keep=0):
        while len(deferred) > keep:
            deferred.pop(0)()
    ss = k.sb([128, 8], F32, "ssc")
    rstd = k.sb([128, 8], F32, "rstdc")
    h2T = [k.sb([128, 8, 512], BF16, f"h2T{i}") for i in range(2)]
    for kc in range(8):
        st = stg[kc % 2]
        k.dma(st[:, :], c.WOUT[kc * 128:(kc + 1) * 128, :], writes=[st])
        k.op("pool", lambda e, kc=kc, st=st: e.tensor_copy(out=Wout[:, kc, :], in_=st[:, :]), reads=[st],
             writes=[Wout])

    def st_body(s):
        cj = 0 if s < 8 else 1
        tok = slice(s * 512, (s + 1) * 512)
        mx = mixT[s % 2]
        h2 = h2T[s % 2]
        if s == 0 or s == 8:
            k.dma(g1b[:, :], c.MODS[cj, 2 * D:3 * D].partition_broadcast(128), writes=[g1b])
        def load_mix(s_):
            tk = slice(s_ * 512, (s_ + 1) * 512)
            m_ = mixT[s_ % 2]
            k.dma(m_[:, 0:4, :], c.AOTd[:, :, tk].rearrange("c p t -> p c t"), writes=[m_])
            k.dma(m_[:, 4:8, :], c.HMTd[:, :, tk].rearrange("c p t -> p c t"), writes=[m_])

        def load_x(ti_):
            k.dma(xb[ti_ % 3][:, :], c.X[ti_ * 128:(ti_ + 1) * 128, :], writes=[xb[ti_ % 3]])

        if s == 0:
            load_mix(0)
            load_x(0)
        if s + 1 < 9:
            load_mix(s + 1)

        def tile_body(i):
            ti = s * 4 + i
            xt = xb[ti % 3]
            xo = x1[ti % 2]
            if ti + 1 < NT:
                load_x(ti + 1)
            for n in range(2):
                pb = PS[2 + (2 * ti + n) % 4]
                for kc in range(8):
                    k.op("pe", lambda e, pb=pb, kc=kc, n=n: e.matmul(
                        pb[:, :], lhsT=mx[:, kc, i * 128:(i + 1) * 128], rhs=Wout[:, kc, n * 512:(n + 1) * 512],
                        start=(kc == 0), stop=(kc == 7)), reads=[mx, Wout], writes=[pb], inc=(kc == 7))
                if n == 1:
                    flush(1)
                k.op("dve", lambda e, pb=pb, n=n: e.tensor_tensor(out=tmp[:, n * 512:(n + 1) * 512], in0=pb[:, :],
                                                                  in1=g1b[:, n * 512:(n + 1) * 512], op=ALU.mult),
                     reads=[pb, g1b], writes=[tmp])
            k.op("pool", lambda e: e.tensor_tensor(out=xo[:, :], in0=tmp[:, :], in1=xt[:, :], op=ALU.add),
                 reads=[tmp, xt], writes=[xo])
            k.dma(c.X1d[ti * 128:(ti + 1) * 128, :], xo[:, :], reads=[xo])
            norm_to_hT(k, c, xo, h2, i * 128,
                       lambda ch: c.G2[:, cj, ch:ch + 1], lambda ch: c.modc[:, cj, 3, ch:ch + 1],
                       (junk, ss, rstd, xs[ti % 3]), PS[0], PS[1], defer=deferred)

        for i in range(4):
            tile_body(i)
        flush()
        k.dma(c.H2Td[:, :, tok].rearrange("c p t -> p c t"), h2[:, :, :], reads=[h2])

    for s in range(9):
        st_body(s)
    k.barrier()
    k.phase_mem()


def phase_c2(ctx):
    c = NS(ctx)
    k = c.k
    PS = c.PS
    Wgb = [k.sb([128, 8, 256], BF16, f"Wgb{i}") for i in range(22)]
    Wdn = k.sb([128, NF, D], BF16, "Wdn")
    SW = 704
    stg = [k.sb([128, SW], F32, f"stgf{i}") for i in range(2)]
    g2b = k.sb([128, D], F32, "g2b")
    fnb = k.sb([128, D], F32, "fnb")
    h2 = k.sb([128, 8, 512], BF16, "h2c")
    actT = k.sb([128, NF, 512], BF16, "actT")
    sg = [k.sb([128, 512], F32, f"sg{i}") for i in range(2)]
    x1 = [k.sb([128, D], F32, f"x1c{i}") for i in range(2)]
    x2 = k.sb([128, D], F32, "x2c")
    yt = [k.sb([128, D], F32, f"yt{i}") for i in range(2)]
    junk = k.sb([128, D], BF16, "junkf")
    ss = k.sb([128, 8], F32, "ssf")
    rstd = k.sb([128, 8], F32, "rstdf")
    mhalf = k.sb([128, 1], F32, "mhalf")
    k.dma(fnb[:, :], c.FNW.partition_broadcast(128), writes=[fnb])
    k.dma(g2b[:, :], c.MODS[0, 5 * D:6 * D].partition_broadcast(128), writes=[g2b])
    k.dma(h2[:, :, :], c.H2Td[:, :, 0:512].rearrange("c p t -> p c t"), writes=[h2])
    it = 0
    for bi in range(11):
        for half in range(2):
            blkt = Wgb[2 * bi + half]
            c0 = half * DFF + bi * 256
            for kc0 in (0, 2, 4, 6):
                st = stg[it % 2]
                it += 1
                k.dma(st[:, 0:512].rearrange("p (a n) -> p a n", a=2),
                      c.WGU[kc0 * 128:(kc0 + 2) * 128, c0:c0 + 256].rearrange("(a p) n -> p a n", p=128), writes=[st])
                k.op("pool", lambda e, kc0=kc0, st=st, blkt=blkt: e.tensor_copy(
                    out=blkt[:, kc0:kc0 + 2, :], in_=st[:, 0:512].rearrange("p (a n) -> p a n", a=2)),
                    reads=[st], writes=[blkt])
    for f in range(NF):
        for j in range(2):
            st = stg[it % 2]
            it += 1
            k.dma(st[:, 0:512], c.WDN[f * 128:(f + 1) * 128, j * 512:(j + 1) * 512], writes=[st])
            k.op("pool", lambda e, f=f, j=j, st=st: e.tensor_copy(out=Wdn[:, f, j * 512:(j + 1) * 512],
                                                                  in_=st[:, 0:512]), reads=[st], writes=[Wdn])
    k.op("dve", lambda e: e.memset(mhalf[:, :], -0.5), writes=[mhalf])

    def st_body(s):
        cj = 0 if s < 8 else 1
        tok = slice(s * 512, (s + 1) * 512)
        if s == 8:
            k.dma(g2b[:, :], c.MODS[cj, 5 * D:6 * D].partition_broadcast(128), writes=[g2b])

        def up_body(f):
            pg, pu = PS[2 * (f % 2)], PS[2 * (f % 2) + 1]
            c0 = (f % 2) * 128
            for (pb, wt) in ((pg, Wgb[2 * (f // 2)]), (pu, Wgb[2 * (f // 2) + 1])):
                for kc in range(8):
                    k.op("pe", lambda e, pb=pb, wt=wt, kc=kc: e.matmul(
                        pb[:, :], lhsT=wt[:, kc, c0:c0 + 128], rhs=h2[:, kc, :], start=(kc == 0), stop=(kc == 7)),
                        reads=[wt, h2], writes=[pb], inc=(kc == 7))
            sgt = sg[f % 2]
            k.op("act", lambda e: e.activation(out=sgt[:, :], in_=pg[:, :], func=AF.Silu), reads=[pg], writes=[sgt])
            k.op("dve", lambda e: e.tensor_tensor(out=actT[:, f, :], in0=pu[:, :], in1=sgt[:, :], op=ALU.mult),
                 reads=[pu, sgt], writes=[actT])

        for f in range(NF):
            up_body(f)
        if s + 1 < 9:
            tokn = slice((s + 1) * 512, (s + 2) * 512)
            k.dma(h2[:, :, :], c.H2Td[:, :, tokn].rearrange("c p t -> p c t"), writes=[h2])

        def tile_body(i):
            ti = s * 4 + i
            xt = x1[ti % 2]
            y = yt[ti % 2]
            k.dma(xt[:, :], c.X1d[ti * 128:(ti + 1) * 128, :], writes=[xt])
            for n in range(2):
                pb = PS[4 + (2 * ti + n) % 4]
                for f in range(NF):
                    k.op("pe", lambda e, pb=pb, f=f, n=n: e.matmul(
                        pb[:, :], lhsT=actT[:, f, i * 128:(i + 1) * 128], rhs=Wdn[:, f, n * 512:(n + 1) * 512],
                        start=(f == 0), stop=(f == NF - 1)), reads=[actT, Wdn], writes=[pb], inc=(f == NF - 1))
                k.op("dve", lambda e, pb=pb, n=n: e.tensor_tensor(out=x2[:, n * 512:(n + 1) * 512], in0=pb[:, :],
                                                                  in1=g2b[:, n * 512:(n + 1) * 512], op=ALU.mult),
                     reads=[pb, g2b], writes=[x2])
            k.op("pool", lambda e: e.tensor_tensor(out=x2[:, :], in0=x2[:, :], in1=xt[:, :], op=ALU.add),
                 reads=[x2, xt], writes=[x2])
            k.op("dve", lambda e: e.scalar_tensor_tensor(out=junk[:, :], in0=x2[:, :], scalar=1.0, in1=x2[:, :],
                                                         op0=ALU.mult, op1=ALU.mult, accum_out=ss[:, 0:1]),
                 reads=[x2], writes=[junk, ss])
            k.op("dve", lambda e: e.tensor_scalar(out=ss[:, 1:2], in0=ss[:, 0:1], scalar1=1.0 / D, scalar2=EPS,
                                                  op0=ALU.mult, op1=ALU.add), reads=[ss], writes=[ss])
            k.op("pool", lambda e: e.tensor_tensor(out=rstd[:, 0:1], in0=ss[:, 1:2], in1=mhalf[:, 0:1], op=ALU.pow),
                 reads=[ss, mhalf], writes=[rstd])
            k.op("dve", lambda e: e.scalar_tensor_tensor(out=y[:, :], in0=x2[:, :], scalar=rstd[:, 0:1],
                                                         in1=fnb[:, :], op0=ALU.mult, op1=ALU.mult),
                 reads=[x2, rstd, fnb], writes=[y])
            k.dma(c.Y[ti * 128:(ti + 1) * 128, :], y[:, :], reads=[y])

        for i in range(4):
            tile_body(i)

    for s in range(9):
        st_body(s)
    k.barrier()
    k.phase_mem()
```

```python
import numpy as np
import concourse.bass as bass
import concourse.mybir as mybir
from concourse.bass_utils import run_bass_kernel_spmd

F32 = mybir.dt.float32
BF16 = mybir.dt.bfloat16
AF = mybir.ActivationFunctionType
ALU = mybir.AluOpType
AX = mybir.AxisListType

D = 1024
NT = 36
TT = NT * 128
NIN = 2832
DFF = 2816
NF = 22
EPS = 1e-6
KSC = 128.0 ** -0.5
NKEY = 38
import os as _os
NDS = 64
PRE_IN_ATT = int(_os.environ.get('PRE_IN_ATT', '1'))
STRICT = bool(int(_os.environ.get('KSTRICT', '1')))


class Tl:
    def __init__(s, h):
        s.h = h
        s.w = []
        s.r = []

    def __getitem__(s, i):
        return s.h[i]


class K:
    ENG = ("pe", "act", "dve", "pool", "sp")

    def __init__(s, nc):
        s.nc = nc
        s.ops = {e: [] for e in s.ENG}
        s.cnt = {e: 0 for e in s.ENG}
        s.seen = {e: {} for e in s.ENG}
        s.sem = {}
        s.dsems = [nc.alloc_semaphore(f"d_{i}") for i in range(NDS)]
        s.dcnt = [0] * NDS
        s.dptr = 0
        for e in s.ENG:
            s.sem[e] = nc.alloc_semaphore(f"s_{e}")
        s.sb_ptr = 0
        s.sb_phase = 0
        s.cap = None
        s.nalloc = 0

    def sb(s, shape, dt, name=None):
        esz = 4 if dt == F32 else 2
        n = 1
        for d_ in shape[1:]:
            n *= d_
        nbytes = (n * esz + 63) // 64 * 64
        off = s.sb_ptr
        s.sb_ptr += nbytes
        assert s.sb_ptr <= s.sb_top, (name, s.sb_ptr, s.sb_top)
        s.nalloc += 1
        h = s.nc.alloc_sbuf_tensor_at(f"{name or 't'}_{s.nalloc}", list(shape), dt, offset=off)
        return Tl(h)

    def phase_mem(s):
        s.sb_ptr = s.sb_phase

    def capture(s):
        s.cap = []
        return s.cap

    def capture_end(s):
        lst, s.cap = s.cap, None
        return lst

    def emit_interleaved(s, *lists):
        lists = [list(l) for l in lists if l]
        while lists:
            for l in list(lists):
                kind, args, kw = l.pop(0)
                (s.op if kind == "op" else s.dma)(*args, **kw)
                if not l:
                    lists.remove(l)

    def op(s, e, fn, reads=(), writes=(), inc=True):
        if s.cap is not None:
            s.cap.append(("op", (e, fn), dict(reads=list(reads), writes=list(writes), inc=inc)))
            return None
        deps = []
        for t in reads:
            for wt in t.w:
                if wt[0] != e or e != "pe":
                    deps.append(wt)
        for t in writes:
            for wt in t.w:
                if wt[0] != e or (STRICT and e != "pe"):
                    deps.append(wt)
            for rt in t.r:
                if rt[0] != e or (STRICT and e != "pe"):
                    deps.append(rt)
        need = {}
        for key, val in deps:
            if val > need.get(key, 0):
                need[key] = val
        waits = []
        for key, val in need.items():
            if s.seen[e].get(key, 0) >= val:
                continue
            s.seen[e][key] = val
            waits.append((key, val))
        tok = (e, s.cnt[e] + 1)
        if inc:
            s.cnt[e] += 1
        s.ops[e].append((waits, fn, inc, None))
        for t in writes:
            t.w = [tok]
            t.r = []
        for t in reads:
            if t not in writes:
                t.r.append(tok)
                if len(t.r) > 24:
                    t.r = s._compact(t.r)
        return tok

    @staticmethod
    def _compact(r):
        best = {}
        for key, val in r:
            if val > best.get(key, 0):
                best[key] = val
        return list(best.items())

    def dma(s, out, in_, reads=(), writes=(), q="sp", slow=False):
        if s.cap is not None:
            s.cap.append(("dma", (out, in_), dict(reads=list(reads), writes=list(writes), q=q, slow=slow)))
            return None
        deps = []
        for t in reads:
            deps.extend(t.w)
        for t in writes:
            for wt in t.w:
                if not isinstance(wt[0], int):
                    deps.append(wt)
            deps.extend(t.r)
        need = {}
        for key, val in deps:
            if val > need.get(key, 0):
                need[key] = val
        waits = []
        for key, val in need.items():
            if s.seen[q].get(key, 0) >= val:
                continue
            s.seen[q][key] = val
            waits.append((key, val))
        si = s.dptr
        s.dptr = (s.dptr + 1) % len(s.dsems)
        if s.dcnt[si] > s.seen[q].get(si, 0):
            s.seen[q][si] = s.dcnt[si]
            waits.append((si, s.dcnt[si]))
        s.dcnt[si] += 16
        tok = (si, s.dcnt[si])
        s.ops[q].append((waits, (out, in_, slow), False, si))
        for t in writes:
            if t.w and all(isinstance(wt[0], int) for wt in t.w):
                t.w = t.w + [tok]
            else:
                t.w = [tok]
            t.r = []
        for t in reads:
            t.r.append(tok)
        return tok

    def barrier(s):
        waits = []
        for e in s.ENG:
            if e != "sp" and s.cnt[e] > s.seen["sp"].get(e, 0):
                s.seen["sp"][e] = s.cnt[e]
                waits.append((e, s.cnt[e]))
        for si in range(len(s.dsems)):
            if s.dcnt[si] > s.seen["sp"].get(si, 0):
                s.seen["sp"][si] = s.dcnt[si]
                waits.append((si, s.dcnt[si]))
        s.cnt["sp"] += 1
        v = s.cnt["sp"]
        s.ops["sp"].append((waits, "inc", True, None))
        for e in s.ENG:
            if e != "sp":
                s.seen[e]["sp"] = v
                s.ops[e].append(([("sp", v)], None, False, None))
                for si in range(len(s.dsems)):
                    s.seen[e][si] = s.dcnt[si]
                for e2 in s.ENG:
                    s.seen[e][e2] = max(s.seen[e].get(e2, 0), s.cnt[e2])

    def semof(s, key):
        return s.dsems[key] if isinstance(key, int) else s.sem[key]

    def emit(s, e, eng):
        for waits, fn, inc, si in s.ops[e]:
            for key, val in waits:
                eng.wait_ge(s.semof(key), val)
            if fn is None:
                continue
            if fn == "inc":
                eng.sem_inc(s.sem[e], 1)
                continue
            if si is not None:
                out, in_, slow = fn
                if slow:
                    ins = eng.dma_start(out=out, in_=in_, allow_slow_non_contiguous=True)
                else:
                    ins = eng.dma_start(out=out, in_=in_)
                ins.then_inc(s.dsems[si], 16)
                continue
            ins = fn(eng)
            if inc:
                ins.then_inc(s.sem[e], 1)


def build_nc(debug=False, stop=99):
    import os
    stop = int(os.environ.get('KSTOP', stop))
    nc = bass.Bass("TRN2", target_bir_lowering=False)
    k = K(nc)
    k.sb_ptr = (nc.sbuf_base + 63) // 64 * 64
    k.sb_top = nc.sbuf_top

    def din(name, shape, dt=F32):
        return nc.dram_tensor(name, list(shape), dt, kind="ExternalInput").ap()

    def dout(name, shape, dt=F32):
        return nc.dram_tensor(name, list(shape), dt, kind="ExternalOutput").ap()

    def dscr(name, shape, dt=BF16):
        return nc.dram_tensor(name, list(shape), dt, kind="ExternalOutput" if debug else "Internal").ap()

    X = din("x", [TT, D])
    CK = din("cache_k", [256, 128])
    CV = din("cache_v", [256, 128])
    SC = din("state_C", [2, 4, 128, 128])
    SN = din("state_n", [2, 4, 128])
    SM = din("state_m", [2, 4])
    COND = din("cond", [2, D])
    WADA = din("w_ada", [D, 6 * D])
    BADA = din("b_ada", [6 * D])
    N1W = din("norm1_w", [D])
    WIN = din("w_in", [D, NIN])
    GB = din("gate_bias", [4, 4])
    QNW = din("q_norm_w", [64])
    KNW = din("k_norm_w", [64])
    MNW = din("mlstm_norm_w", [512])
    WOUT = din("w_out", [D, D])
    N2W = din("norm2_w", [D])
    WGU = din("w_gu", [D, 2 * DFF])
    WDN = din("w_down", [DFF, D])
    FNW = din("final_norm_w", [D])
    ROPE = din("rope", [128, 32, 2, 64])
    IDENT = din("ident", [128, 128])
    MASKS = din("masks", [128, 2, 512])

    Y = dout("y", [TT, D])
    NK = dout("nk", [512, 128])
    NV = dout("nv", [512, 128])
    NC_ = dout("nC", [2, 2, 4, 128, 128])
    NN = dout("nn", [2, 2, 4, 128])
    NM = dout("nm", [2, 2, 4])

    MODS = dscr("mods", [2, 6 * D], F32)
    QTd = dscr("QTd", [4, 128, TT])
    MQTd = dscr("MQTd", [4, 128, TT])
    MKTd = dscr("MKTd", [4, 128, TT])
    MKd = dscr("MKd", [TT, 512])
    MVd = dscr("MVd", [TT, 520])
    MOd = dscr("MOd", [TT, 512])
    GATd = dscr("GATd", [4, 4, TT], F32)
    AOTd = dscr("AOTd", [4, 128, TT])
    HMTd = dscr("HMTd", [4, 128, TT])
    H2Td = dscr("H2Td", [8, 128, TT])
    X1d = dscr("X1d", [TT, D], F32)

    PSALL = nc.alloc_psum_tensor("psall", [128, 4096], F32)
    PS = [Tl(PSALL[:, i * 512:(i + 1) * 512]) for i in range(8)]

    ident = k.sb([128, 128], F32, "ident")
    ones4 = k.sb([4, 128], F32, "ones4")
    eye4 = k.sb([4, 4], F32, "eye4")
    modc = k.sb([128, 2, 6, 8], F32, "modc")
    G1 = k.sb([128, 2, 8], F32, "G1")
    G2 = k.sb([128, 2, 8], F32, "G2")
    n1c = k.sb([128, 8], F32, "n1c")
    n2c = k.sb([128, 8], F32, "n2c")
    k.mhalf = k.sb([128, 8], F32, "mhalf")
    k.op("dve", lambda e: e.memset(k.mhalf[:, :], -0.5), writes=[k.mhalf])
    sb_phase0 = k.sb_ptr
    KT2 = k.sb([128, 2, TT + 256], BF16, "KT2")
    VA = k.sb([128, NKEY, 2, 192], BF16, "VA")
    sb_phase0_a = k.sb_ptr
    Win = k.sb([128, 8, NIN], BF16, "Win")
    Wg = k.sb([128, 8, 128], BF16, "Wg")
    stgw = [k.sb([128, NIN // 2], F32, f"stgw{i}") for i in range(2)]
    k.sb_phase = k.sb_ptr
    k.op("pool", lambda e: e.memset(Wg[:, :, :], 0.0), writes=[Wg])

    k.dma(ident[:, :], IDENT[:, :], writes=[ident])
    k.op("dve", lambda e: e.memset(ones4[:, :], 1.0), writes=[ones4])
    k.dma(eye4[:, :], IDENT[0:4, 0:4], writes=[eye4])

    condT = k.sb([128, 8, 2], F32, "condT")
    sil = k.sb([128, 8, 2], F32, "sil")
    tmpc = k.sb([128, 8, 2], F32, "tmpc")
    mods_sb = k.sb([2, 6 * D], F32, "mods_sb")
    bada = k.sb([2, 6 * D], F32, "bada")
    wa = [k.sb([128, 512], F32, f"wa{i}") for i in range(4)]
    for j in range(2):
        k.dma(condT[:, :, j], COND[j].rearrange("(c p) -> p c", p=128), writes=[condT], slow=True)
    for j in range(2):
        k.dma(bada[j:j + 1, :], BADA.rearrange("(o n) -> o n", o=1), writes=[bada])
    k.op("act", lambda e: e.activation(out=tmpc[:, :, :], in_=condT[:, :, :], func=AF.Exp, scale=-1.0),
         reads=[condT], writes=[tmpc])
    k.op("dve", lambda e: e.tensor_scalar_add(out=tmpc[:, :, :], in0=tmpc[:, :, :], scalar1=1.0),
         reads=[tmpc], writes=[tmpc])
    k.op("dve", lambda e: e.reciprocal(out=tmpc[:, :, :], in_=tmpc[:, :, :]), reads=[tmpc], writes=[tmpc])
    k.op("dve", lambda e: e.tensor_tensor(out=sil[:, :, :], in0=condT[:, :, :], in1=tmpc[:, :, :], op=ALU.mult),
         reads=[condT, tmpc], writes=[sil])
    win_steps = []
    HW = NIN // 2
    for kc in range(8):
        for hf in range(2):
            def step(kc=kc, hf=hf):
                st = stgw[hf]
                k.dma(st[:, :], WIN[kc * 128:(kc + 1) * 128, hf * HW:(hf + 1) * HW], writes=[st])
                k.op("pool", lambda e: e.tensor_copy(out=Win[:, kc, hf * HW:(hf + 1) * HW], in_=st[:, :]),
                     reads=[st], writes=[Win])
                if hf == 1:
                    k.op("pool", lambda e: e.tensor_copy(
                        out=Wg[:, kc, :].rearrange("p (j w) -> p j w", w=32)[:, :, 0:4],
                        in_=st[:, HW - 16:HW].rearrange("p (j w) -> p j w", w=4)), reads=[st], writes=[Wg])
            win_steps.append(step)
    it = 0
    for n in range(12):
        pb = PS[n % 2]
        for kc in range(8):
            w_t = wa[it % 4]
            if it % 6 == 0 and win_steps:
                win_steps.pop(0)()
            it += 1
            k.dma(w_t[:, :], WADA[kc * 128:(kc + 1) * 128, n * 512:(n + 1) * 512], writes=[w_t])
            k.op("pe", lambda e, w_t=w_t, kc=kc, pb=pb: e.matmul(pb[0:2, :], lhsT=sil[:, kc, :], rhs=w_t[:, :],
                                                                start=(kc == 0), stop=(kc == 7)),
                 reads=[w_t, sil], writes=[pb], inc=True)
        k.op("dve", lambda e, n=n, pb=pb: e.tensor_tensor(out=mods_sb[:, n * 512:(n + 1) * 512], in0=pb[0:2, :],
                                                          in1=bada[:, n * 512:(n + 1) * 512], op=ALU.add),
             reads=[pb, bada], writes=[mods_sb])
    while win_steps:
        win_steps.pop(0)()
    k.dma(MODS[:, :], mods_sb[:, :], reads=[mods_sb])
    k.barrier()
    for j in range(2):
        for s6 in range(6):
            k.dma(modc[:, j, s6, :], MODS[j, s6 * D:(s6 + 1) * D].rearrange("(c p) -> p c", p=128),
                  writes=[modc], slow=True)
    k.dma(n1c[:, :], N1W.rearrange("(c p) -> p c", p=128), writes=[n1c], slow=True)
    k.dma(n2c[:, :], N2W.rearrange("(c p) -> p c", p=128), writes=[n2c], slow=True)
    for j in range(2):
        k.op("dve", lambda e, j=j: e.scalar_tensor_tensor(out=G1[:, j, :], in0=modc[:, j, 1, :], scalar=1.0,
                                                          in1=n1c[:, :], op0=ALU.add, op1=ALU.mult),
             reads=[modc, n1c], writes=[G1])
        k.op("dve", lambda e, j=j: e.scalar_tensor_tensor(out=G2[:, j, :], in0=modc[:, j, 4, :], scalar=1.0,
                                                          in1=n2c[:, :], op0=ALU.add, op1=ALU.mult),
             reads=[modc, n2c], writes=[G2])
    k.barrier()
    k.phase_mem()
    ctx = dict(locals())
    if stop >= 1:
        phase_a(ctx)
    if stop >= 2:
        phase_b1(ctx)
    if stop >= 3:
        phase_b2(ctx)
    if stop >= 4:
        phase_c1(ctx)
    if stop >= 5:
        phase_c2(ctx)
    k.barrier()

    with nc.allow_low_precision(reason="bf16 matmul operands by design"), nc.Block() as block:
        names = {"pe": "tensor", "act": "scalar", "dve": "vector", "pool": "gpsimd", "sp": "sync"}
        for e in K.ENG:
            getattr(block, names[e])(lambda eng, e=e: k.emit(e, eng))
    return nc


class NS:
    def __init__(s, d):
        s.__dict__.update(d)


def rstd_from_ss(k, ss, out, n_inv, width):
    k.op("act", lambda e: e.activation(out=out[:, 0:width], in_=ss[:, 0:width], func=AF.Ln, scale=n_inv, bias=EPS),
         reads=[ss], writes=[out])
    k.op("act", lambda e: e.activation(out=out[:, 0:width], in_=out[:, 0:width], func=AF.Exp, scale=-0.5),
         reads=[out], writes=[out])


def norm_to_hT(k, c, xt, hT, col0, Gc, SHc, bufs, pA, pB, defer=None):
    junk, ss, rstd, xs = bufs
    k.op("dve", lambda e: e.scalar_tensor_tensor(out=junk[:, :], in0=xt[:, :], scalar=1.0, in1=xt[:, :],
                                                 op0=ALU.mult, op1=ALU.mult, accum_out=ss[:, 0:1]),
         reads=[xt], writes=[junk, ss])
    rstd_from_ss(k, ss, rstd, 1.0 / D, 1)
    k.op("pool", lambda e: e.tensor_scalar(out=xs[:, :], in0=xt[:, :], scalar1=rstd[:, 0:1], scalar2=1.0,
                                           op0=ALU.mult, op1=ALU.mult),
         reads=[xt, rstd], writes=[xs])

    def pe_part():
        for half, pb in ((0, pA), (1, pB)):
            for cc in range(4):
                ch = half * 4 + cc
                k.op("pe", lambda e, ch=ch, cc=cc, pb=pb: e.transpose(pb[:, cc * 128:(cc + 1) * 128],
                                                                      xs[:, ch * 128:(ch + 1) * 128], c.ident[:, :]),
                     reads=[xs, c.ident], writes=[pb], inc=(cc == 3))
            for cc in range(4):
                ch = half * 4 + cc
                k.op("act", lambda e, ch=ch, cc=cc, pb=pb: e.activation(
                    out=hT[:, ch, col0:col0 + 128], in_=pb[:, cc * 128:(cc + 1) * 128], func=AF.Identity,
                    scale=Gc(ch), bias=SHc(ch)), reads=[pb, c.G1, c.G2, c.modc], writes=[hT])

    if defer is None:
        pe_part()
    else:
        defer.append(pe_part)


def phase_a(ctx):
    c = NS(ctx)
    k = c.k
    PS = c.PS
    KT2, VA, Win, Wg = c.KT2, c.VA, c.Win, c.Wg
    k.op("pool", lambda e: e.memset(VA[:, :, :, :], 1.0), writes=[VA])
    xb = [k.sb([128, D], F32, f"xb{i}") for i in range(3)]
    junk = k.sb([128, D], BF16, "junk")
    xs = [k.sb([128, D], F32, f"xs{i}") for i in range(3)]
    ss = k.sb([128, 8], F32, "ss")
    rstd = k.sb([128, 8], F32, "rstd")
    hT = [k.sb([128, 8, 512], BF16, f"hT{i}") for i in range(2)]
    ropet = [k.sb([128, 4, 2, 64], F32, f"rope{i}") for i in range(2)]
    wq_bc = k.sb([128, 64], F32, "wq_bc")
    wk_bc = k.sb([128, 64], F32, "wk_bc")
    gbcol = k.sb([128, 1], F32, "gbcol")
    qf = k.sb([128, 512], F32, "qf")
    sq = k.sb([128, 512], F32, "sq")
    qn2 = [k.sb([128, 512], F32, f"qn{i}") for i in range(2)]
    t1 = k.sb([128, 512], F32, "t1")
    t2 = k.sb([128, 512], F32, "t2")
    qr2 = [k.sb([128, 512], F32, f"qr{i}") for i in range(2)]
    kvf = [k.sb([128, 256], F32, f"kvf{i}") for i in range(2)]
    kn = [k.sb([128, 128], F32, f"kn{i}") for i in range(2)]
    kt1 = k.sb([128, 128], F32, "kt1")
    kt2 = k.sb([128, 128], F32, "kt2")
    kr = k.sb([128, 128], F32, "kr")
    kdup2 = [k.sb([128, 2, 2, 64], F32, f"kdup{i}") for i in range(2)]
    ssq = k.sb([128, 8], F32, "ssq")
    rsq = k.sb([128, 8], F32, "rsq")
    ssk = k.sb([128, 8], F32, "ssk")
    rsk = k.sb([128, 8], F32, "rsk")
    mot = k.sb([128, 512], F32, "mot")
    gsb = k.sb([128, 512], F32, "gsb")
    gtmp = k.sb([128, 512], F32, "gtmp")
    cstg = k.sb([128, 2, 128], F32, "cstg")
    QTs = [k.sb([128, 4, 512], BF16, f"QTs{i}") for i in range(1)]
    MKs = [k.sb([128, 4, 512], BF16, f"MKs{i}") for i in range(1)]
    MVs = [k.sb([128, 4, 4, 130], BF16, f"MVs{i}") for i in range(1)]
    MOs = [k.sb([128, 4, 512], BF16, f"MOs{i}") for i in range(1)]
    MQTs = [k.sb([128, 4, 512], BF16, f"MQTs{i}") for i in range(1)]
    MKTs = [k.sb([128, 4, 512], BF16, f"MKTs{i}") for i in range(1)]

    k.dma(wq_bc[:, :], c.QNW.partition_broadcast(128), writes=[wq_bc])
    k.dma(wk_bc[:, :], c.KNW.partition_broadcast(128), writes=[wk_bc])
    k.op("dve", lambda e: e.tensor_scalar_mul(out=wq_bc[:, :], in0=wq_bc[:, :], scalar1=0.125),
         reads=[wq_bc], writes=[wq_bc])
    k.op("dve", lambda e: e.memset(gbcol[:, :], 0.0), writes=[gbcol])
    for j in range(4):
        k.dma(gbcol[32 * j:32 * j + 4, 0:1], c.GB[j].rearrange("(h o) -> h o", o=1), writes=[gbcol], slow=True)
    k.op("pool", lambda e: e.memset(MVs[0][:, :, :, :], 1.0), writes=[MVs[0]])
    kdup = kdup2[0]
    for blk in range(2):
        k.dma(cstg[:, 0, :], c.CK[blk * 128:(blk + 1) * 128, :], writes=[cstg])
        k.dma(cstg[:, 1, :], c.CV[blk * 128:(blk + 1) * 128, :], writes=[cstg])
        k.op("dve", lambda e: e.tensor_copy(out=kdup[:, :, :, :],
                                            in_=cstg[:, 0, :].rearrange("p (g o d) -> p g o d", g=2, o=1)
                                            .to_broadcast([128, 2, 2, 64])), reads=[cstg], writes=[kdup])
        pk = PS[2]
        for g in range(2):
            k.op("pe", lambda e, g=g: e.transpose(pk[:, g * 128:(g + 1) * 128],
                                                  kdup[:, g, :, :].rearrange("p a d -> p (a d)"), c.ident[:, :]),
                 reads=[kdup, c.ident], writes=[pk], inc=(g == 1))
        col = TT + blk * 128
        k.op("act", lambda e, col=col: e.activation(out=KT2[:, :, col:col + 128],
                                                    in_=pk[:, 0:256].rearrange("p (g t) -> p g t", g=2),
                                                    func=AF.Copy), reads=[pk], writes=[KT2])
        k.op("dve", lambda e, blk=blk: e.tensor_copy(out=VA[:, 36 + blk, :, 64:128],
                                                     in_=cstg[:, 1, :].rearrange("p (g d) -> p g d", g=2)),
             reads=[cstg], writes=[VA])

    class Deferred(list):
        cur = 0

        def append(self, fn):
            list.append(self, (self.cur, fn))

    deferred = Deferred()

    def flush(upto=10 ** 9):
        while deferred and deferred[0][0] <= upto:
            deferred.pop(0)[1]()

    def tile_a(s, i):
        cj = 0 if s < 8 else 1
        ti = s * 4 + i
        xt = xb[ti % 3]
        k.dma(xt[:, :], c.X[ti * 128:(ti + 1) * 128, :], writes=[xt])
        norm_to_hT(k, c, xt, hT[s % 2], i * 128,
                   lambda ch, cj=cj: c.G1[:, cj, ch:ch + 1], lambda ch, cj=cj: c.modc[:, cj, 0, ch:ch + 1],
                   (junk, ss, rstd, xs[ti % 3]), PS[0], PS[1], defer=deferred)

    def st_body(s):
        cj = 0 if s < 8 else 1
        smp = s < 8
        h = hT[s % 2]
        if smp:
            rp = ropet[s % 2]
            k.dma(rp[:, :, :, :], c.ROPE[:, s * 4:(s + 1) * 4, :, :], writes=[rp])
        QT_, MK_, MV_, MO_, MQT_, MKT_ = QTs[0], MKs[0], MVs[0], MOs[0], MQTs[0], MKTs[0]
        if s == 0:
            for i in range(4):
                tile_a(0, i)
                flush()

        def tile_b(i):
            ti = s * 4 + i
            tsl = slice(i * 128, (i + 1) * 128)
            pq, pkv, pmk, pmv, pmo = PS[2], PS[3], PS[4], PS[5], PS[6]
            for (pb, c0, c1) in ((pmk, 1280, 1792), (pmv, 1792, 2304), (pmo, 2304, 2816), (pq, 0, 512),
                                 (pkv, 512, 768)):
                for kc in range(8):
                    k.op("pe", lambda e, pb=pb, c0=c0, c1=c1, kc=kc, tsl=tsl: e.matmul(
                        pb[:, 0:c1 - c0], lhsT=h[:, kc, tsl], rhs=Win[:, kc, c0:c1], start=(kc == 0), stop=(kc == 7)),
                        reads=[h, Win], writes=[pb], inc=(kc == 7))
            flush(ti - 2)
            deferred.cur = ti
            qn, qr, kdup = qn2[ti % 2], qr2[ti % 2], kdup2[ti % 2]
            k.capture()
            k.op("act", lambda e, i=i: e.activation(out=MK_[:, i, :], in_=pmk[:, :], func=AF.Copy, scale=KSC),
                 reads=[pmk], writes=[MK_])
            k.op("dve", lambda e, i=i: e.tensor_copy(out=MV_[:, i, :, 0:128],
                                                     in_=pmv[:, :].rearrange("p (h d) -> p h d", h=4)),
                 reads=[pmv], writes=[MV_])
            k.op("act", lambda e: e.activation(out=mot[:, :], in_=pmo[:, :], func=AF.Exp, scale=-1.0),
                 reads=[pmo], writes=[mot])
            k.op("act", lambda e: e.activation(out=mot[:, :], in_=mot[:, :], func=AF.Ln, bias=1.0),
                 reads=[mot], writes=[mot])
            k.op("act", lambda e, i=i: e.activation(out=MO_[:, i, :], in_=mot[:, :], func=AF.Exp, scale=-1.0),
                 reads=[mot], writes=[MO_])
            Lm = k.capture_end()
            kv = kvf[ti % 2]
            kk = kn[ti % 2]
            k.capture()
            k.op("act", lambda e: e.activation(out=qf[:, :], in_=pq[:, :], func=AF.Copy), reads=[pq], writes=[qf])
            k.op("dve", lambda e: e.tensor_tensor(out=sq[:, :], in0=qf[:, :], in1=qf[:, :], op=ALU.mult),
                 reads=[qf], writes=[sq])
            k.op("dve", lambda e: e.tensor_reduce(out=ssq[:, 0:8], in_=sq[:, :].rearrange("p (h d) -> p h d", d=64),
                                                  axis=AX.X, op=ALU.add), reads=[sq], writes=[ssq])
            rstd_from_ss(k, ssq, rsq, 1.0 / 64, 8)
            k.op("dve", lambda e: e.tensor_tensor(out=qn[:, :].rearrange("p (h d) -> p h d", d=64),
                                                  in0=qf[:, :].rearrange("p (h d) -> p h d", d=64),
                                                  in1=rsq[:, 0:8].unsqueeze(2).to_broadcast([128, 8, 64]),
                                                  op=ALU.mult), reads=[qf, rsq], writes=[qn])
            k.op("dve", lambda e: e.tensor_tensor(out=qn[:, :].rearrange("p (h d) -> p h d", d=64),
                                                  in0=qn[:, :].rearrange("p (h d) -> p h d", d=64),
                                                  in1=wq_bc[:, :].unsqueeze(1).to_broadcast([128, 8, 64]),
                                                  op=ALU.mult), reads=[qn, wq_bc], writes=[qn])
            if smp:
                k.op("pool", lambda e, i=i: e.tensor_tensor(
                    out=t1[:, :].rearrange("p (h d) -> p h d", d=64),
                    in0=qn[:, :].rearrange("p (h d) -> p h d", d=64),
                    in1=rp[:, i, 0, :].unsqueeze(1).to_broadcast([128, 8, 64]), op=ALU.mult),
                    reads=[qn, rp], writes=[t1])
                for jj in range(2):
                    k.op("pool", lambda e, i=i, jj=jj: e.tensor_tensor(
                        out=t2[:, :].rearrange("p (h a j w) -> p h a j w", h=8, a=2, j=2)[:, :, :, jj, :],
                        in0=qn[:, :].rearrange("p (h a j w) -> p h a j w", h=8, a=2, j=2)[:, :, :, 1 - jj, :],
                        in1=rp[:, i, 1, :].rearrange("p (a j w) -> p a j w", a=2, j=2)[:, :, jj, :].unsqueeze(1)
                        .to_broadcast([128, 8, 2, 16]), op=ALU.mult),
                        reads=[qn, rp], writes=[t2])
                k.op("pool", lambda e: e.tensor_tensor(out=qr[:, :], in0=t1[:, :], in1=t2[:, :], op=ALU.add),
                     reads=[t1, t2], writes=[qr])
                qsrc = qr
            else:
                qsrc = qn
            def q_tr(qsrc=qsrc, tsl=tsl):
                pt = PS[7]
                for j in range(4):
                    k.op("pe", lambda e, j=j: e.transpose(pt[:, j * 128:(j + 1) * 128],
                                                          qsrc[:, j * 128:(j + 1) * 128], c.ident[:, :]),
                         reads=[qsrc, c.ident], writes=[pt], inc=(j == 3))
                k.op("act", lambda e: e.activation(out=QT_[:, :, tsl],
                                                   in_=pt[:, :].rearrange("p (j t) -> p j t", j=4),
                                                   func=AF.Copy), reads=[pt], writes=[QT_])
            deferred.append(q_tr)
            Lq = k.capture_end()
            k.capture()
            k.op("act", lambda e, kv=kv: e.activation(out=kv[:, :], in_=pkv[:, 0:256], func=AF.Copy),
                 reads=[pkv], writes=[kv])
            k.op("dve", lambda e, kv=kv: e.tensor_tensor(out=kt1[:, :], in0=kv[:, 0:128], in1=kv[:, 0:128],
                                                         op=ALU.mult), reads=[kv], writes=[kt1])
            k.op("dve", lambda e: e.tensor_reduce(out=ssk[:, 0:2], in_=kt1[:, :].rearrange("p (h d) -> p h d", d=64),
                                                  axis=AX.X, op=ALU.add), reads=[kt1], writes=[ssk])
            rstd_from_ss(k, ssk, rsk, 1.0 / 64, 2)
            k.op("dve", lambda e, kv=kv, kk=kk: e.tensor_tensor(
                out=kk[:, :].rearrange("p (h d) -> p h d", d=64),
                in0=kv[:, 0:128].rearrange("p (h d) -> p h d", d=64),
                in1=rsk[:, 0:2].unsqueeze(2).to_broadcast([128, 2, 64]), op=ALU.mult),
                reads=[kv, rsk], writes=[kk])
            k.op("dve", lambda e, kk=kk: e.tensor_tensor(
                out=kk[:, :].rearrange("p (h d) -> p h d", d=64),
                in0=kk[:, :].rearrange("p (h d) -> p h d", d=64),
                in1=wk_bc[:, :].unsqueeze(1).to_broadcast([128, 2, 64]), op=ALU.mult),
                reads=[kk, wk_bc], writes=[kk])
            if smp:
                k.op("pool", lambda e, i=i, kk=kk: e.tensor_tensor(
                    out=kt1[:, :].rearrange("p (h d) -> p h d", d=64),
                    in0=kk[:, :].rearrange("p (h d) -> p h d", d=64),
                    in1=rp[:, i, 0, :].unsqueeze(1).to_broadcast([128, 2, 64]), op=ALU.mult),
                    reads=[kk, rp], writes=[kt1])
                for jj in range(2):
                    k.op("pool", lambda e, i=i, kk=kk, jj=jj: e.tensor_tensor(
                        out=kt2[:, :].rearrange("p (h a j w) -> p h a j w", h=2, a=2, j=2)[:, :, :, jj, :],
                        in0=kk[:, :].rearrange("p (h a j w) -> p h a j w", h=2, a=2, j=2)[:, :, :, 1 - jj, :],
                        in1=rp[:, i, 1, :].rearrange("p (a j w) -> p a j w", a=2, j=2)[:, :, jj, :].unsqueeze(1)
                        .to_broadcast([128, 2, 2, 16]), op=ALU.mult),
                        reads=[kk, rp], writes=[kt2])
                k.op("pool", lambda e: e.tensor_tensor(out=kr[:, :], in0=kt1[:, :], in1=kt2[:, :], op=ALU.add),
                     reads=[kt1, kt2], writes=[kr])
                ksrc = kr
            else:
                ksrc = kk
                pt0 = (ti - 32) * 128
                k.dma(c.NK[pt0:pt0 + 128, :], kk[:, :], reads=[kk])
                k.dma(c.NV[pt0:pt0 + 128, :], kv[:, 128:256], reads=[kv])
            k.op("dve", lambda e, ksrc=ksrc: e.tensor_copy(
                out=kdup[:, :, :, :], in_=ksrc[:, :].rearrange("p (g o d) -> p g o d", g=2, o=1)
                .to_broadcast([128, 2, 2, 64])), reads=[ksrc], writes=[kdup])
            def k_tr(ti=ti):
                pk = PS[7]
                for g in range(2):
                    k.op("pe", lambda e, g=g: e.transpose(pk[:, g * 128:(g + 1) * 128],
                                                          kdup[:, g, :, :].rearrange("p a d -> p (a d)"),
                                                          c.ident[:, :]),
                         reads=[kdup, c.ident], writes=[pk], inc=(g == 1))
                k.op("act", lambda e: e.activation(out=KT2[:, :, ti * 128:(ti + 1) * 128],
                                                   in_=pk[:, 0:256].rearrange("p (g t) -> p g t", g=2),
                                                   func=AF.Copy), reads=[pk], writes=[KT2])
            deferred.append(k_tr)
            k.op("dve", lambda e, ti=ti, kv=kv: e.tensor_copy(out=VA[:, ti, :, 64:128],
                                                              in_=kv[:, 128:256].rearrange("p (g d) -> p g d", g=2)),
                 reads=[kv], writes=[VA])
            Lk = k.capture_end()
            k.emit_interleaved(Lm, Lq, Lk)

        for i in range(4):
            tile_b(i)
            if s + 1 < 9:
                tile_a(s + 1, i)
        for hh in range(8):
            pb = PS[2 + hh % 4]
            c0 = 768 + hh * 128
            for kc in range(8):
                k.op("pe", lambda e, pb=pb, c0=c0, kc=kc: e.matmul(
                    pb[:, :], lhsT=Win[:, kc, c0:c0 + 128], rhs=h[:, kc, :], start=(kc == 0), stop=(kc == 7)),
                    reads=[h, Win], writes=[pb], inc=(kc == 7))
            if hh == 3:
                flush()
            if hh < 4:
                k.op("dve", lambda e, pb=pb, hh=hh: e.tensor_copy(out=MQT_[:, hh, :], in_=pb[:, :]),
                     reads=[pb], writes=[MQT_])
            else:
                k.op("act", lambda e, pb=pb, hh=hh: e.activation(out=MKT_[:, hh - 4, :], in_=pb[:, :], func=AF.Copy,
                                                                 scale=KSC), reads=[pb], writes=[MKT_])
        pg = PS[6]
        for kc in range(8):
            k.op("pe", lambda e, kc=kc: e.matmul(pg[:, :], lhsT=Wg[:, kc, :], rhs=h[:, kc, :], start=(kc == 0),
                                                 stop=(kc == 7)), reads=[h, Wg], writes=[pg], inc=(kc == 7))
        k.op("act", lambda e: e.activation(out=gsb[:, :], in_=pg[:, :], func=AF.Identity, bias=gbcol[:, 0:1]),
             reads=[pg, gbcol], writes=[gsb])
        for r0 in (32, 96):
            k.op("act", lambda e, r0=r0: e.activation(out=gtmp[r0:r0 + 4, :], in_=gsb[r0:r0 + 4, :], func=AF.Exp,
                                                      scale=-1.0), reads=[gsb], writes=[gtmp])
            k.op("act", lambda e, r0=r0: e.activation(out=gtmp[r0:r0 + 4, :], in_=gtmp[r0:r0 + 4, :], func=AF.Ln,
                                                      bias=1.0), reads=[gtmp], writes=[gtmp])
            k.op("dve", lambda e, r0=r0: e.tensor_scalar_mul(out=gsb[r0:r0 + 4, :], in0=gtmp[r0:r0 + 4, :],
                                                             scalar1=-1.0), reads=[gtmp], writes=[gsb])
        flush()
        tok = slice(s * 512, (s + 1) * 512)
        k.dma(c.QTd[:, :, tok].rearrange("c p t -> p c t"), QT_[:, :, :], reads=[QT_])
        k.dma(c.MQTd[:, :, tok].rearrange("c p t -> p c t"), MQT_[:, :, :], reads=[MQT_])
        k.dma(c.MKTd[:, :, tok].rearrange("c p t -> p c t"), MKT_[:, :, :], reads=[MKT_])
        k.dma(c.MKd[tok, :].rearrange("(i p) f -> p i f", p=128), MK_[:, :, :], reads=[MK_])
        k.dma(c.MVd[tok, :].rearrange("(i p) f -> p i f", p=128), MV_[:, :, :, :].rearrange("p i h d -> p i (h d)"),
              reads=[MV_])
        k.dma(c.MOd[tok, :].rearrange("(i p) f -> p i f", p=128), MO_[:, :, :], reads=[MO_])
        for j in range(4):
            k.dma(c.GATd[j, :, tok], gsb[32 * j:32 * j + 4, :], reads=[gsb])
    for s in range(9):
        st_body(s)
    k.barrier()
    k.sb_phase = c.sb_phase0_a
    k.phase_mem()


def _consts():
    f = np.float32
    tok = np.arange(4096)
    row = (tok // 64).astype(f)
    col = (tok % 64).astype(f)
    inv = (f(10000.0) ** (-np.arange(0, 32, 2, dtype=f) / f(32))).astype(f)
    ang = np.stack([row[:, None] * inv[None, :], col[:, None] * inv[None, :]], axis=1).astype(f)
    cs, sn = np.cos(ang).astype(f), np.sin(ang).astype(f)
    Cf = np.stack([cs, cs], axis=2)
    Ss = np.stack([-sn, sn], axis=2)
    tab = np.stack([Cf.reshape(4096, 64), Ss.reshape(4096, 64)], axis=1)
    rope = np.ascontiguousarray(tab.reshape(32, 128, 2, 64).transpose(1, 0, 2, 3))
    ident = np.eye(128, dtype=f)
    sidx = np.arange(128)[:, None]
    lidx = np.arange(128)[None, :]
    mf = (lidx >= sidx).astype(f)
    mb = (lidx <= sidx).astype(f)
    masks = np.stack([np.tile(mf, (1, 4)), np.tile(mb, (1, 4))], axis=1)
    return rope, ident, np.ascontiguousarray(masks)


def make_in_maps(inp):
    rope, ident, masks = _consts()
    f = np.float32
    c = lambda a: np.ascontiguousarray(np.asarray(a), dtype=f)
    maps = []
    for b in range(8):
        x = np.concatenate([inp["x_sample"][b], inp["x_prompt"][2 * b], inp["x_prompt"][2 * b + 1]], axis=0)
        m = {
            "x": c(x),
            "cache_k": c(np.asarray(inp["cache_k"])[b, 0].reshape(256, 128)),
            "cache_v": c(np.asarray(inp["cache_v"])[b, 0].reshape(256, 128)),
            "state_C": c(np.asarray(inp["state_C"])[b, 0]),
            "state_n": c(np.asarray(inp["state_n"])[b, 0]),
            "state_m": c(np.asarray(inp["state_m"])[b, 0]),
            "cond": c(np.stack([np.asarray(inp["c"])[b], np.asarray(inp["c_ctx"])], axis=0)),
            "w_ada": c(np.asarray(inp["w_ada"])[0]), "b_ada": c(np.asarray(inp["b_ada"])[0]),
            "norm1_w": c(np.asarray(inp["norm1_w"])[0]), "w_in": c(np.asarray(inp["w_in"])[0]),
            "gate_bias": c(np.asarray(inp["gate_bias"])[0]), "q_norm_w": c(np.asarray(inp["q_norm_w"])[0]),
            "k_norm_w": c(np.asarray(inp["k_norm_w"])[0]), "mlstm_norm_w": c(np.asarray(inp["mlstm_norm_w"])[0]),
            "w_out": c(np.asarray(inp["w_out"])[0]), "norm2_w": c(np.asarray(inp["norm2_w"])[0]),
            "w_gu": c(np.asarray(inp["w_gu"])[0]), "w_down": c(np.asarray(inp["w_down"])[0]),
            "final_norm_w": c(inp["final_norm_w"]),
            "rope": rope, "ident": ident, "masks": masks,
        }
        maps.append(m)
    return maps


_NC = None


def kernel(**inp):
    global _NC
    if _NC is None:
        _NC = build_nc()
    maps = make_in_maps(inp)
    res = run_bass_kernel_spmd(_NC, maps, core_ids=list(range(8)))
    R = res.results
    f = np.float32
    y_s = np.stack([R[b]["y"][0:4096] for b in range(8)], axis=0).astype(f)
    y_p = np.stack([R[b]["y"][4096 + 256 * j:4096 + 256 * (j + 1)] for b in range(8) for j in range(2)], axis=0).astype(f)
    nk = np.stack([R[b]["nk"][256 * j:256 * (j + 1)].reshape(256, 2, 64) for b in range(8) for j in range(2)], axis=0)
    nv = np.stack([R[b]["nv"][256 * j:256 * (j + 1)].reshape(256, 2, 64) for b in range(8) for j in range(2)], axis=0)
    nC = np.stack([R[b]["nC"][j] for b in range(8) for j in range(2)], axis=0)
    nn = np.stack([R[b]["nn"][j] for b in range(8) for j in range(2)], axis=0)
    nm = np.stack([R[b]["nm"][j] for b in range(8) for j in range(2)], axis=0)
    return (y_p, y_s, nk[:, None].astype(f), nv[:, None].astype(f), nC[:, None].astype(f), nn[:, None].astype(f),
            nm[:, None].astype(f))


def phase_b1(ctx):
    c = NS(ctx)
    k = c.k
    PS = c.PS
    KT2, VA = c.KT2, c.VA
    pre_gen, b2_main = make_b2(ctx)
    ctx["b2_main"] = b2_main
    k.sb_phase = k.sb_ptr
    pre = pre_gen()
    SP = [Tl(c.PSALL[:, b * 1024:(b + 1) * 1024]) for b in range(2)]
    QTg = [k.sb([128, 4, 512], BF16, f"QTg{i}") for i in range(2)]
    PTP = [k.sb([128, 1024], BF16, f"PT{i}") for i in range(3)]
    AO = [k.sb([128, 4, 512], BF16, f"AO{i}") for i in range(2)]
    rden = [k.sb([128, 512], F32, f"rden{i}") for i in range(2)]
    obs = k.sb([128, 512], F32, "obs")
    groups = [(g * 512, 512, list(range(32)) + [36, 37]) for g in range(8)]
    groups += [(4096, 256, [32, 33]), (4352, 256, [34, 35])]
    cnt = [0]

    def group_body(gi, t0, nq, kbs):
        qt = QTg[gi % 2]
        ao = AO[gi % 2]
        if gi == 0:
            k.dma(qt[:, :, 0:nq], c.QTd[:, :, t0:t0 + nq].rearrange("c p t -> p c t"), writes=[qt])
        if gi + 1 < len(groups):
            t0n, nqn, _ = groups[gi + 1]
            qtn = QTg[(gi + 1) % 2]
            k.dma(qtn[:, :, 0:nqn], c.QTd[:, :, t0n:t0n + nqn].rearrange("c p t -> p c t"), writes=[qtn])
        iters = [(j, idx, kb) for j in range(4) for idx, kb in enumerate(kbs)]
        base = cnt[0]
        cnt[0] += len(iters)

        def emit_s(n):
            j, idx, kb = iters[n]
            g = j // 2
            kcol = kb * 128 if kb < 36 else TT + (kb - 36) * 128
            sp = SP[(base + n) % 2]
            k.op("pe", lambda e: e.matmul(sp[:, 0:nq], lhsT=KT2[0:64, g, kcol:kcol + 128], rhs=qt[0:64, j, 0:nq],
                                          start=True, stop=True), reads=[KT2, qt], writes=[sp], inc=False)
            k.op("pe", lambda e: e.matmul(sp[:, 512:512 + nq], lhsT=KT2[64:128, g, kcol:kcol + 128],
                                          rhs=qt[64:128, j, 0:nq], start=True, stop=True),
                 reads=[KT2, qt], writes=[sp])

        def emit_rest(n):
            j, idx, kb = iters[n]
            g = j // 2
            sp = SP[(base + n) % 2]
            pt = PTP[(base + n) % 3]
            oa, ob = (PS[4], PS[6])[j % 2], PS[5]
            k.op("act", lambda e: e.activation(out=pt[:, :].rearrange("p (a t) -> p a t", a=2)[:, :, 0:nq],
                                               in_=sp[:, :].rearrange("p (a t) -> p a t", a=2)[:, :, 0:nq],
                                               func=AF.Exp), reads=[sp], writes=[pt])
            first, last = idx == 0, idx == len(kbs) - 1
            k.op("pe", lambda e: e.matmul(oa[:, 0:nq], lhsT=VA[:, kb, g, 64:192], rhs=pt[:, 0:nq], start=first,
                                          stop=last), reads=[VA, pt], writes=[oa], inc=False)
            k.op("pe", lambda e: e.matmul(ob[:, 0:nq], lhsT=VA[:, kb, g, 0:128], rhs=pt[:, 512:512 + nq],
                                          start=first, stop=last), reads=[VA, pt], writes=[ob])
            if last:
                ra, rb = rden[0], rden[1]
                k.op("dve", lambda e: e.tensor_copy(out=obs[:, 0:nq], in_=ob[:, 0:nq]), reads=[ob], writes=[obs])
                k.op("dve", lambda e: e.reciprocal(out=rb[64:128, 0:nq], in_=obs[0:64, 0:nq]), reads=[obs],
                     writes=[rb])
                k.op("dve", lambda e: e.tensor_tensor(out=ao[64:128, j, 0:nq], in0=obs[64:128, 0:nq],
                                                      in1=rb[64:128, 0:nq], op=ALU.mult), reads=[obs, rb],
                     writes=[ao])
                k.op("dve", lambda e: e.reciprocal(out=ra[0:64, 0:nq], in_=oa[64:128, 0:nq]), reads=[oa], writes=[ra])
                k.op("dve", lambda e: e.tensor_tensor(out=ao[0:64, j, 0:nq], in0=oa[0:64, 0:nq], in1=ra[0:64, 0:nq],
                                                      op=ALU.mult), reads=[oa, ra], writes=[ao])

        emit_s(0)
        for n in range(len(iters)):
            if n + 1 < len(iters):
                emit_s(n + 1)
            emit_rest(n)
            if PRE_IN_ATT:
                next(pre, None)
        k.dma(c.AOTd[:, :, t0:t0 + nq].rearrange("c p t -> p c t"), ao[:, :, 0:nq], reads=[ao])

    for gi, (t0, nq, kbs) in enumerate(groups):
        group_body(gi, t0, nq, kbs)
    for _ in pre:
        pass
    k.barrier()
    k.phase_mem()


def make_b2(ctx):
    c = NS(ctx)
    k = c.k
    PS = c.PS
    pS = PS[0]
    pGd, pId, pTd = (PS[1], PS[4]), (PS[2], PS[5]), (PS[3], PS[6])
    pX = PS[7]
    Cst = [k.sb([128, 4, 130], F32, f"Cst{i}") for i in range(2)]
    Csnap = k.sb([128, NT, 4, 130], BF16, "Csnap")
    mst = [k.sb([4, 2], F32, f"mst{i}") for i in range(2)]
    mbprev = k.sb([4, NT + 4], F32, "mbprev")
    GT = [k.sb([4, 4, 128], F32, f"GT{i}") for i in range(2)]
    kk = [k.sb([128, 4, 128], BF16, f"kk{i}") for i in range(2)]
    va = [k.sb([128, 4, 130], BF16, f"va{i}") for i in range(2)]
    rows = [[k.sb([4, 128], F32, f"row{d}_{i}") for i in range(8)] for d in range(2)]
    dg = [k.sb([4, 4], F32, f"dg{d}") for d in range(2)]
    cols = [k.sb([128, 24], F32, f"cols{d}") for d in range(2)]
    kw = k.sb([128, 512], BF16, "kw")
    cols1p = [cols[1], k.sb([128, 24], F32, "cols1b")]
    ident, ones4, eye4 = c.ident, c.ones4, c.eye4
    maskneg = identb = None
    masks = mnw = Cbf = qT = kT = mo = blk = Dsb = swT = qI = hd = hm = sqh = ss4 = rs4 = hmT = None

    def late_alloc():
        nonlocal maskneg, identb
        nonlocal masks, mnw, Cbf, qT, kT, mo, blk, Dsb, swT, qI, hd, hm, sqh, ss4, rs4, hmT
        masks = k.sb([128, 2, 512], F32, "masks")
        mnw = k.sb([128, 512], F32, "mnw")
        Cbf = k.sb([128, 4, 130], BF16, "Cbf")
        qT = [k.sb([128, 4, 128], BF16, f"qT{i}") for i in range(2)]
        kT = [k.sb([128, 4, 128], BF16, f"kT{i}") for i in range(2)]
        mo = [k.sb([128, 512], BF16, f"mo{i}") for i in range(2)]
        blk = [[k.sb([4, 4, 128], F32, f"blk{d}_{i}") for i in range(2)] for d in range(2)]
        Dsb = [k.sb([128, 512], F32, f"Dsb{d}") for d in range(2)]
        swT = [k.sb([128, 512], BF16, f"swT{d}") for d in range(2)]
        qI = [k.sb([128, 512], BF16, f"qI{d}") for d in range(2)]
        hd = [[k.sb([128, 512], F32, f"hd{p}_{d}") for d in range(2)] for p in range(2)]
        hm = k.sb([128, 512], F32, "hm")
        sqh = k.sb([128, 512], F32, "sqh")
        ss4 = k.sb([128, 8], F32, "ss4")
        rs4 = k.sb([128, 8], F32, "rs4")
        hmT = [k.sb([128, 4, 128], BF16, f"hmT{i}") for i in range(2)]
        maskneg = k.sb([128, 2, 512], BF16, "maskneg")
        identb = k.sb([128, 128], BF16, "identb")
        k.dma(masks[:, :, :], c.MASKS[:, :, :], writes=[masks])
        k.dma(mnw[:, :], c.MNW.partition_broadcast(128), writes=[mnw])
        k.op("dve", lambda e: e.tensor_scalar(out=maskneg[:, :, :], in0=masks[:, :, :], scalar1=-1.0, scalar2=30000.0,
                                              op0=ALU.add, op1=ALU.mult), reads=[masks], writes=[maskneg])
        k.op("dve", lambda e: e.tensor_copy(out=identb[:, :], in_=ident[:, :]), reads=[ident], writes=[identb])

    def load_chunk(ci, full):
        tok = slice(ci * 128, (ci + 1) * 128)
        i2 = ci % 2
        k.dma(GT[i2][:, :, :], c.GATd[:, :, tok].rearrange("t h n -> h t n"), writes=[GT[i2]])
        k.dma(kk[i2][:, :, :], c.MKd[tok, :].rearrange("p (h d) -> p h d", h=4), writes=[kk[i2]])
        k.dma(va[i2][:, :, :], c.MVd[tok, :].rearrange("p (h d) -> p h d", h=4), writes=[va[i2]])
        if full:
            k.dma(qT[i2][:, :, :], c.MQTd[:, :, tok].rearrange("h p t -> p h t"), writes=[qT[i2]])
            k.dma(kT[i2][:, :, :], c.MKTd[:, :, tok].rearrange("h p t -> p h t"), writes=[kT[i2]])

    def load_mo(ci):
        tok = slice(ci * 128, (ci + 1) * 128)
        k.dma(mo[ci % 2][:, :], c.MOd[tok, :], writes=[mo[ci % 2]])

    def gate_prep(d, gt, mprev, full, upd, cl=None, pT=None):
        mt, mc = mprev
        rb_, ra_, rg_, rng_, rin_, rgu_, rw_, rt_ = rows[d]
        pT = pT or pTd[d]
        cl = cl or cols[d]
        lf = lambda: gt[:, 1 + 2 * d, :]
        ig = lambda: gt[:, 2 * d, :]
        rv = (lambda ap: ap) if d == 0 else (lambda ap: ap[:, ::-1])
        last = 127 if d == 0 else 0
        k.op("dve", lambda e: e.tensor_tensor_scan(out=rv(rb_[:, :]), data0=rv(ones4[:, :]), data1=rv(lf()),
                                                   initial=0.0, op0=ALU.mult, op1=ALU.add),
             reads=[gt, ones4], writes=[rb_])
        yield
        k.op("dve", lambda e: e.tensor_tensor(out=ra_[:, :], in0=ig(), in1=rb_[:, :], op=ALU.subtract),
             reads=[gt, rb_], writes=[ra_])
        yield
        k.op("pe", lambda e: e.transpose(pT[:, 0:4], ra_[:, :], ident[0:4, 0:4]), reads=[ra_, ident], writes=[pT])
        k.op("dve", lambda e: e.tensor_tensor_scan(out=rv(rg_[:, :]), data0=rv(ra_[:, :]), data1=rv(ra_[:, :]),
                                                   initial=mt[:, mc:mc + 1], op0=ALU.max, op1=ALU.max),
             reads=[ra_, mt], writes=[rg_])
        yield
        k.op("dve", lambda e: e.tensor_scalar_mul(out=rng_[:, :], in0=rg_[:, :], scalar1=-1.0),
             reads=[rg_], writes=[rng_])
        if full:
            k.op("dve", lambda e: e.tensor_tensor(out=rt_[:, :], in0=rb_[:, :], in1=rg_[:, :], op=ALU.add),
                 reads=[rb_, rg_], writes=[rt_])
            yield
            bk0 = blk[d][0]
            k.op("dve", lambda e: e.tensor_tensor(
                out=bk0[:, :, :], in0=rng_[:, :].unsqueeze(1).to_broadcast([4, 4, 128]),
                in1=eye4[:, :].unsqueeze(2).to_broadcast([4, 4, 128]), op=ALU.mult),
                reads=[rng_, eye4], writes=[bk0])
            yield
            k.op("pe", lambda e: e.matmul(pGd[d][:, :], lhsT=ones4[:, :],
                                          rhs=bk0[:, :, :].rearrange("k h l -> k (h l)"),
                                          start=True, stop=False), reads=[ones4, bk0], writes=[pGd[d]], inc=False)
            k.op("pe", lambda e: e.matmul(pGd[d][:, :], lhsT=identb[:, :], rhs=maskneg[:, d, :],
                                          start=False, stop=True), reads=[identb, maskneg], writes=[pGd[d]])
        yield
        k.op("act", lambda e: e.activation(out=rin_[:, :], in_=rng_[:, :], func=AF.Exp, bias=mt[:, mc:mc + 1]),
             reads=[rng_, mt], writes=[rin_])
        if full:
            yield
            bk1 = blk[d][1]
            k.op("dve", lambda e: e.tensor_tensor(
                out=bk1[:, :, :], in0=rin_[:, :].unsqueeze(1).to_broadcast([4, 4, 128]),
                in1=eye4[:, :].unsqueeze(2).to_broadcast([4, 4, 128]), op=ALU.mult),
                reads=[rin_, eye4], writes=[bk1])
            yield
            k.op("pe", lambda e: e.matmul(pId[d][:, :], lhsT=ones4[:, :],
                                          rhs=bk1[:, :, :].rearrange("k h l -> k (h l)"),
                                          start=True, stop=True), reads=[ones4, bk1], writes=[pId[d]])
        if full:
            k.op("act", lambda e: e.activation(out=rgu_[:, :], in_=rt_[:, :], func=AF.Exp, scale=-1.0),
                 reads=[rt_], writes=[rgu_])
        if upd:
            k.op("act", lambda e: e.activation(out=rw_[:, :], in_=ra_[:, :], func=AF.Exp,
                                               bias=rng_[:, last:last + 1]), reads=[ra_, rng_], writes=[rw_])
        yield
        if full:
            k.op("pe", lambda e: e.transpose(pT[:, 4:8], rgu_[:, :], ident[0:4, 0:4]), reads=[rgu_, ident],
                 writes=[pT])
        if upd:
            k.op("pe", lambda e: e.transpose(pT[:, 8:12], rw_[:, :], ident[0:4, 0:4]), reads=[rw_, ident],
                 writes=[pT])
            k.op("dve", lambda e: e.tensor_scalar(out=dg[d][:, :], in0=eye4[:, :], scalar1=rin_[:, last:last + 1],
                                                  scalar2=None, op0=ALU.mult), reads=[eye4, rin_], writes=[dg[d]])
            yield
            k.op("pe", lambda e: e.matmul(pT[:, 12:16], lhsT=ones4[:, :], rhs=dg[d][:, :], start=True, stop=True),
                 reads=[ones4, dg[d]], writes=[pT])
        yield
        hi = 16 if upd else 8
        if not full and upd:
            k.op("dve", lambda e: e.tensor_copy(out=cl[:, 0:4], in_=pT[:, 0:4]), reads=[pT], writes=[cl])
            k.op("dve", lambda e: e.tensor_copy(out=cl[:, 8:16], in_=pT[:, 8:16]), reads=[pT], writes=[cl])
        else:
            k.op("dve", lambda e: e.tensor_copy(out=cl[:, 0:hi], in_=pT[:, 0:hi]), reads=[pT], writes=[cl])
        yield

    def new_m(d, mt_out, mc_out):
        rb_, rg_ = rows[d][0], rows[d][2]
        last = 127 if d == 0 else 0
        k.op("dve", lambda e: e.tensor_tensor(out=mt_out[:, mc_out:mc_out + 1], in0=rb_[:, last:last + 1],
                                              in1=rg_[:, last:last + 1], op=ALU.add),
             reads=[rb_, rg_], writes=[mt_out])

    def state_update(d, kk_, va_, banks=None, cl=None):
        banks = banks or ((pId[d], 0), (pTd[d], 128))
        Cs = Cst[d]
        cl = cl or cols[d]
        k.op("dve", lambda e: e.tensor_tensor(out=kw[:, :].rearrange("p (h d) -> p h d", h=4), in0=kk_[:, :, :],
                                              in1=cl[:, 8:12].unsqueeze(2).to_broadcast([128, 4, 128]),
                                              op=ALU.mult), reads=[kk_, cl], writes=[kw])
        yield
        for h in range(4):
            pd, off = banks[h % 2]
            k.op("pe", lambda e, h=h, pd=pd, off=off: e.matmul(pd[:, off:off + 129], lhsT=kw[:, h * 128:(h + 1) * 128],
                                                      rhs=va_[:, h, 0:129], start=True, stop=True),
                 reads=[kw, va_], writes=[pd])
            yield
            k.op("dve", lambda e, h=h, pd=pd, off=off: e.scalar_tensor_tensor(
                out=Cs[:, h, 0:129], in0=Cs[:, h, 0:129], scalar=cl[:, 12 + h:13 + h], in1=pd[:, off:off + 129],
                op0=ALU.mult, op1=ALU.add), reads=[Cs, cl, pd], writes=[Cs])
            yield

    def run(*gens):
        gens = list(gens)
        while gens:
            for g in list(gens):
                try:
                    next(g)
                except StopIteration:
                    gens.remove(g)

    def init_dir(seq, d):
        Cs = Cst[d]
        if seq == 0:
            k.op("pool", lambda e: e.memset(Cs[:, :, :], 0.0), writes=[Cs])
            k.op("dve", lambda e: e.memset(mst[d][:, :], 0.0), writes=[mst[d]])
            k.dma(Cs[:, :, 0:128], c.SC[d].rearrange("h p e -> p h e"), writes=[Cs])
            k.dma(Cs[:, :, 128], c.SN[d].rearrange("h p -> p h"), writes=[Cs], slow=True)
            k.dma(mst[d][:, 0:1], c.SM[d].rearrange("(h o) -> h o", o=1), writes=[mst[d]], slow=True)
        else:
            k.op("pool", lambda e: e.memset(Cs[:, :, :], 0.0), writes=[Cs])
            k.op("dve", lambda e: e.memset(mst[d][:, :], 0.0), writes=[mst[d]])

    def store_state(seq, d):
        p = seq - 1
        Cs = Cst[d]
        k.dma(c.NC_[p, d].rearrange("h p e -> p h e"), Cs[:, :, 0:128], reads=[Cs])
        k.dma(c.NN[p, d].rearrange("h p -> p h"), Cs[:, :, 128], reads=[Cs], slow=True)
        k.dma(c.NM[p, d].rearrange("(h o) -> h o", o=1), mst[d][:, 0:1], reads=[mst[d]], slow=True)

    def pre_G(ci):
        i2 = ci % 2
        load_chunk(ci, False)
        yield
        k.op("dve", lambda e: e.tensor_copy(out=mbprev[:, ci:ci + 1], in_=mst[1][:, 0:1]), reads=[mst[1]],
             writes=[mbprev])
        yield
        yield from gate_prep(1, GT[i2], (mst[1], 0), False, True, cl=cols1p[i2], pT=pX)
        new_m(1, mst[1], 0)
        yield

    def pre_U(ci):
        i2 = ci % 2
        k.op("act", lambda e: e.activation(out=Csnap[:, ci, :, :], in_=Cst[1][:, :, :], func=AF.Copy),
             reads=[Cst[1]], writes=[Csnap])
        yield
        yield from state_update(1, kk[i2], va[i2], banks=((pX, 128), (pX, 128)), cl=cols1p[i2])

    def zip2(g1, g2):
        gens = [g for g in (g1, g2) if g is not None]
        while gens:
            for g in list(gens):
                try:
                    next(g)
                except StopIteration:
                    gens.remove(g)
            yield

    seqs = [(0, list(range(32))), (1, [32, 33]), (2, [34, 35])]

    def pre_gen():
        for seq, chunks in seqs:
            init_dir(seq, 1)
            yield
            order = list(reversed(chunks))
            yield from pre_G(order[0])
            for n, ci in enumerate(order):
                nxt = pre_G(order[n + 1]) if n + 1 < len(order) else None
                yield from zip2(nxt, pre_U(ci))
            if seq > 0:
                store_state(seq, 1)
                yield

    def dir_chain(d, ci, q_, kk_, va_, gt):
        mprev = (mst[0], 0) if d == 0 else (mbprev, ci)
        rng_, rin_ = rows[d][3], rows[d][4]
        pG, pI, pT, cl = pGd[d], pId[d], pTd[d], cols[d]
        yield from gate_prep(d, gt, mprev, True, d == 0)
        if d == 0:
            new_m(0, mst[0], 0)
        for h in range(4):
            k.op("act", lambda e, h=h: e.activation(out=Dsb[d][:, h * 128:(h + 1) * 128],
                                                    in_=pG[:, h * 128:(h + 1) * 128], func=AF.Exp,
                                                    bias=cl[:, h:h + 1]), reads=[pG, cl], writes=[Dsb[d]])
        k.op("dve", lambda e: e.tensor_tensor(out=qI[d][:, :], in0=pI[:, :],
                                              in1=q_[:, :, :].rearrange("p h t -> p (h t)"), op=ALU.mult),
             reads=[pI, q_], writes=[qI[d]])
        yield
        k.op("dve", lambda e: e.tensor_tensor(out=swT[d][:, :], in0=pS[:, :], in1=Dsb[d][:, :], op=ALU.mult),
             reads=[pS, Dsb[d]], writes=[swT[d]])
        if d == 0:
            k.op("act", lambda e: e.activation(out=Cbf[:, :, :], in_=Cst[0][:, :, :], func=AF.Copy),
                 reads=[Cst[0]], writes=[Cbf])
        yield
        for h in range(4):
            cprev = (lambda h=h: Cbf[:, h, :]) if d == 0 else (lambda h=h: Csnap[:, ci, h, :])
            ctile = Cbf if d == 0 else Csnap
            k.op("pe", lambda e, h=h: e.matmul(pG[:, h * 128:(h + 1) * 128], lhsT=swT[d][:, h * 128:(h + 1) * 128],
                                               rhs=va_[:, h, 0:128], start=True, stop=False),
                 reads=[swT[d], va_], writes=[pG], inc=False)
            k.op("pe", lambda e, h=h, cprev=cprev: e.matmul(pG[:, h * 128:(h + 1) * 128],
                                                            lhsT=qI[d][:, h * 128:(h + 1) * 128],
                                                            rhs=cprev()[:, 0:128], start=False, stop=True),
                 reads=[qI[d], ctile], writes=[pG], inc=False)
            k.op("pe", lambda e, h=h: e.matmul(pT[:, 16 + h:17 + h], lhsT=swT[d][:, h * 128:(h + 1) * 128],
                                               rhs=va_[:, h, 128:129], start=True, stop=False),
                 reads=[swT[d], va_], writes=[pT], inc=False)
            k.op("pe", lambda e, h=h, cprev=cprev: e.matmul(pT[:, 16 + h:17 + h],
                                                            lhsT=qI[d][:, h * 128:(h + 1) * 128],
                                                            rhs=cprev()[:, 128:129], start=False, stop=True),
                 reads=[qI[d], ctile], writes=[pT])
            yield
        k.op("dve", lambda e: e.tensor_scalar(out=cl[:, 20:24], in0=pT[:, 16:20], scalar1=-1.0, scalar2=None,
                                              op0=ALU.mult), reads=[pT], writes=[cl])
        yield
        k.op("dve", lambda e: e.tensor_tensor(out=cl[:, 16:20], in0=pT[:, 16:20], in1=cl[:, 20:24], op=ALU.max),
             reads=[pT, cl], writes=[cl])
        yield
        k.op("dve", lambda e: e.tensor_tensor(out=cl[:, 16:20], in0=cl[:, 16:20], in1=cl[:, 4:8], op=ALU.max),
             reads=[cl], writes=[cl])
        yield
        k.op("dve", lambda e: e.reciprocal(out=cl[:, 16:20], in_=cl[:, 16:20]), reads=[cl], writes=[cl])
        yield
        hdt = hd[ci % 2][d]
        k.op("dve", lambda e: e.tensor_tensor(out=hdt[:, :].rearrange("p (h e) -> p h e", h=4),
                                              in0=pG[:, :].rearrange("p (h e) -> p h e", h=4),
                                              in1=cl[:, 16:20].unsqueeze(2).to_broadcast([128, 4, 128]),
                                              op=ALU.mult), reads=[pG, cl], writes=[hdt])
        yield
        if d == 0:
            yield from state_update(0, kk_, va_)

    def tail_chain(ci):
        i2 = ci % 2
        mo_ = mo[i2]
        h0, h1 = hd[i2]
        k.op("pool", lambda e: e.tensor_tensor(out=hm[:, :], in0=h0[:, :], in1=h1[:, :], op=ALU.add),
             reads=[h0, h1], writes=[hm])
        yield
        k.op("dve", lambda e: e.tensor_tensor(out=sqh[:, :], in0=hm[:, :], in1=hm[:, :], op=ALU.mult),
             reads=[hm], writes=[sqh])
        yield
        k.op("dve", lambda e: e.tensor_reduce(out=ss4[:, 0:4], in_=sqh[:, :].rearrange("p (h d) -> p h d", h=4),
                                              axis=AX.X, op=ALU.add), reads=[sqh], writes=[ss4])
        yield
        k.op("act", lambda e: e.activation(out=rs4[:, 0:4], in_=ss4[:, 0:4], func=AF.Ln, scale=1.0 / 128, bias=EPS),
             reads=[ss4], writes=[rs4])
        yield
        k.op("act", lambda e: e.activation(out=rs4[:, 0:4], in_=rs4[:, 0:4], func=AF.Exp, scale=-0.5),
             reads=[rs4], writes=[rs4])
        yield
        k.op("dve", lambda e: e.tensor_tensor(out=hm[:, :].rearrange("p (h d) -> p h d", h=4),
                                              in0=hm[:, :].rearrange("p (h d) -> p h d", h=4),
                                              in1=rs4[:, 0:4].unsqueeze(2).to_broadcast([128, 4, 128]),
                                              op=ALU.mult), reads=[hm, rs4], writes=[hm])
        yield
        k.op("pool", lambda e: e.tensor_tensor(out=hm[:, :], in0=hm[:, :], in1=mnw[:, :], op=ALU.mult),
             reads=[hm, mnw], writes=[hm])
        yield
        k.op("pool", lambda e: e.tensor_tensor(out=sqh[:, :], in0=hm[:, :], in1=mo_[:, :], op=ALU.mult),
             reads=[hm, mo_], writes=[sqh])
        yield
        for h in range(4):
            k.op("pe", lambda e, h=h: e.transpose(pX[:, h * 128:(h + 1) * 128], sqh[:, h * 128:(h + 1) * 128],
                                                  ident[:, :]), reads=[sqh, ident], writes=[pX], inc=(h == 3))
        yield
        ho = hmT[i2]
        k.op("act", lambda e: e.activation(out=ho[:, :, :], in_=pX[:, :].rearrange("p (h t) -> p h t", h=4),
                                           func=AF.Copy), reads=[pX], writes=[ho])
        yield
        k.dma(c.HMTd[:, :, ci * 128:(ci + 1) * 128].rearrange("h p t -> p h t"), ho[:, :, :], reads=[ho])

    def main_chunk(ci, prev, nxt):
        i2 = ci % 2
        if nxt is not None:
            load_chunk(nxt, True)
        q_, kT_, kk_, va_, gt = qT[i2], kT[i2], kk[i2], va[i2], GT[i2]
        for h in range(4):
            k.op("pe", lambda e, h=h: e.matmul(pS[:, h * 128:(h + 1) * 128], lhsT=kT_[:, h, :], rhs=q_[:, h, :],
                                               start=True, stop=True), reads=[kT_, q_], writes=[pS], inc=(h == 3))
        gens = [dir_chain(0, ci, q_, kk_, va_, gt), dir_chain(1, ci, q_, kk_, va_, gt)]
        if prev is not None:
            gens.append(tail_chain(prev))
        run(*gens)
        if nxt is not None:
            load_mo(nxt)

    def main():
        late_alloc()
        allc = [ci for _, chunks in seqs for ci in chunks]
        load_chunk(allc[0], True)
        load_mo(allc[0])
        pos = 0
        for seq, chunks in seqs:
            init_dir(seq, 0)
            prev = None
            for ci in chunks:
                nxt = allc[pos + 1] if pos + 1 < len(allc) else None
                main_chunk(ci, prev, nxt)
                prev = ci
                pos += 1
            run(tail_chain(prev))
            if seq > 0:
                store_state(seq, 0)
        k.barrier()
        k.sb_phase = ctx["sb_phase0"]
        k.phase_mem()

    return pre_gen, main


def phase_b2(ctx):
    ctx["b2_main"]()


def phase_c1(ctx):
    c = NS(ctx)
    k = c.k
    PS = c.PS
    Wout = k.sb([128, 8, D], BF16, "Wout")
    stg = [k.sb([128, D], F32, f"stgo{i}") for i in range(2)]
    g1b = k.sb([128, D], F32, "g1b")
    mixT = [k.sb([128, 8, 512], BF16, f"mixT{i}") for i in range(2)]
    xb = [k.sb([128, D], F32, f"xc{i}") for i in range(3)]
    x1 = [k.sb([128, D], F32, f"x1{i}") for i in range(2)]
    tmp = k.sb([128, D], F32, "tmpc1")
    junk = k.sb([128, D], BF16, "junkc")
    xs = [k.sb([128, D], F32, f"xsc{i}") for i in range(3)]
    deferred = []

    def flush(keep=0):
        while len(deferred) > keep:
            deferred.pop(0)()
    ss = k.sb([128, 8], F32, "ssc")
    rstd = k.sb([128, 8], F32, "rstdc")
    h2T = [k.sb([128, 8, 512], BF16, f"h2T{i}") for i in range(2)]
    for kc in range(8):
        st = stg[kc % 2]
        k.dma(st[:, :], c.WOUT[kc * 128:(kc + 1) * 128, :], writes=[st])
        k.op("pool", lambda e, kc=kc, st=st: e.tensor_copy(out=Wout[:, kc, :], in_=st[:, :]), reads=[st],
             writes=[Wout])

    def st_body(s):
        cj = 0 if s < 8 else 1
        tok = slice(s * 512, (s + 1) * 512)
        mx = mixT[s % 2]
        h2 = h2T[s % 2]
        if s == 0 or s == 8:
            k.dma(g1b[:, :], c.MODS[cj, 2 * D:3 * D].partition_broadcast(128), writes=[g1b])
        def load_mix(s_):
            tk = slice(s_ * 512, (s_ + 1) * 512)
            m_ = mixT[s_ % 2]
            k.dma(m_[:, 0:4, :], c.AOTd[:, :, tk].rearrange("c p t -> p c t"), writes=[m_])
            k.dma(m_[:, 4:8, :], c.HMTd[:, :, tk].rearrange("c p t -> p c t"), writes=[m_])

        def load_x(ti_):
            k.dma(xb[ti_ % 3][:, :], c.X[ti_ * 128:(ti_ + 1) * 128, :], writes=[xb[ti_ % 3]])

        if s == 0:
            load_mix(0)
            load_x(0)
        if s + 1 < 9:
            load_mix(s + 1)

        def tile_body(i):
            ti = s * 4 + i
            xt = xb[ti % 3]
            xo = x1[ti % 2]
            if ti + 1 < NT:
                load_x(ti + 1)
            for n in range(2):
                pb = PS[2 + (2 * ti + n) % 4]
                for kc in range(8):
                    k.op("pe", lambda e, pb=pb, kc=kc, n=n: e.matmul(
                        pb[:, :], lhsT=mx[:, kc, i * 128:(i + 1) * 128], rhs=Wout[:, kc, n * 512:(n + 1) * 512],
                        start=(kc == 0), stop=(kc == 7)), reads=[mx, Wout], writes=[pb], inc=(kc == 7))
                if n == 1:
                    flush(1)
                k.op("dve", lambda e, pb=pb, n=n: e.tensor_tensor(out=tmp[:, n * 512:(n + 1) * 512], in0=pb[:, :],
                                                                  in1=g1b[:, n * 512:(n + 1) * 512], op=ALU.mult),
                     reads=[pb, g1b], writes=[tmp])
            k.op("pool", lambda e: e.tensor_tensor(out=xo[:, :], in0=tmp[:, :], in1=xt[:, :], op=ALU.add),
                 reads=[tmp, xt], writes=[xo])
            k.dma(c.X1d[ti * 128:(ti + 1) * 128, :], xo[:, :], reads=[xo])
            norm_to_hT(k, c, xo, h2, i * 128,
                       lambda ch: c.G2[:, cj, ch:ch + 1], lambda ch: c.modc[:, cj, 3, ch:ch + 1],
                       (junk, ss, rstd, xs[ti % 3]), PS[0], PS[1], defer=deferred)

        for i in range(4):
            tile_body(i)
        flush()
        k.dma(c.H2Td[:, :, tok].rearrange("c p t -> p c t"), h2[:, :, :], reads=[h2])

    for s in range(9):
        st_body(s)
    k.barrier()
    k.phase_mem()


def phase_c2(ctx):
    c = NS(ctx)
    k = c.k
    PS = c.PS
    Wgb = [k.sb([128, 8, 256], BF16, f"Wgb{i}") for i in range(22)]
    Wdn = k.sb([128, NF, D], BF16, "Wdn")
    SW = 704
    stg = [k.sb([128, SW], F32, f"stgf{i}") for i in range(2)]
    g2b = k.sb([128, D], F32, "g2b")
    fnb = k.sb([128, D], F32, "fnb")
    h2 = k.sb([128, 8, 512], BF16, "h2c")
    actT = k.sb([128, NF, 512], BF16, "actT")
    sg = [k.sb([128, 512], F32, f"sg{i}") for i in range(2)]
    x1 = [k.sb([128, D], F32, f"x1c{i}") for i in range(2)]
    x2 = k.sb([128, D], F32, "x2c")
    yt = [k.sb([128, D], F32, f"yt{i}") for i in range(2)]
    junk = k.sb([128, D], BF16, "junkf")
    ss = k.sb([128, 8], F32, "ssf")
    rstd = k.sb([128, 8], F32, "rstdf")
    mhalf = k.sb([128, 1], F32, "mhalf")
    k.dma(fnb[:, :], c.FNW.partition_broadcast(128), writes=[fnb])
    k.dma(g2b[:, :], c.MODS[0, 5 * D:6 * D].partition_broadcast(128), writes=[g2b])
    k.dma(h2[:, :, :], c.H2Td[:, :, 0:512].rearrange("c p t -> p c t"), writes=[h2])
    it = 0
    for bi in range(11):
        for half in range(2):
            blkt = Wgb[2 * bi + half]
            c0 = half * DFF + bi * 256
            for kc0 in (0, 2, 4, 6):
                st = stg[it % 2]
                it += 1
                k.dma(st[:, 0:512].rearrange("p (a n) -> p a n", a=2),
                      c.WGU[kc0 * 128:(kc0 + 2) * 128, c0:c0 + 256].rearrange("(a p) n -> p a n", p=128), writes=[st])
                k.op("pool", lambda e, kc0=kc0, st=st, blkt=blkt: e.tensor_copy(
                    out=blkt[:, kc0:kc0 + 2, :], in_=st[:, 0:512].rearrange("p (a n) -> p a n", a=2)),
                    reads=[st], writes=[blkt])
    for f in range(NF):
        for j in range(2):
            st = stg[it % 2]
            it += 1
            k.dma(st[:, 0:512], c.WDN[f * 128:(f + 1) * 128, j * 512:(j + 1) * 512], writes=[st])
            k.op("pool", lambda e, f=f, j=j, st=st: e.tensor_copy(out=Wdn[:, f, j * 512:(j + 1) * 512],
                                                                  in_=st[:, 0:512]), reads=[st], writes=[Wdn])
    k.op("dve", lambda e: e.memset(mhalf[:, :], -0.5), writes=[mhalf])

    def st_body(s):
        cj = 0 if s < 8 else 1
        tok = slice(s * 512, (s + 1) * 512)
        if s == 8:
            k.dma(g2b[:, :], c.MODS[cj, 5 * D:6 * D].partition_broadcast(128), writes=[g2b])

        def up_body(f):
            pg, pu = PS[2 * (f % 2)], PS[2 * (f % 2) + 1]
            c0 = (f % 2) * 128
            for (pb, wt) in ((pg, Wgb[2 * (f // 2)]), (pu, Wgb[2 * (f // 2) + 1])):
                for kc in range(8):
                    k.op("pe", lambda e, pb=pb, wt=wt, kc=kc: e.matmul(
                        pb[:, :], lhsT=wt[:, kc, c0:c0 + 128], rhs=h2[:, kc, :], start=(kc == 0), stop=(kc == 7)),
                        reads=[wt, h2], writes=[pb], inc=(kc == 7))
            sgt = sg[f % 2]
            k.op("act", lambda e: e.activation(out=sgt[:, :], in_=pg[:, :], func=AF.Silu), reads=[pg], writes=[sgt])
            k.op("dve", lambda e: e.tensor_tensor(out=actT[:, f, :], in0=pu[:, :], in1=sgt[:, :], op=ALU.mult),
                 reads=[pu, sgt], writes=[actT])

        for f in range(NF):
            up_body(f)
        if s + 1 < 9:
            tokn = slice((s + 1) * 512, (s + 2) * 512)
            k.dma(h2[:, :, :], c.H2Td[:, :, tokn].rearrange("c p t -> p c t"), writes=[h2])

        def tile_body(i):
            ti = s * 4 + i
            xt = x1[ti % 2]
            y = yt[ti % 2]
            if ti == 0:
                k.dma(xt[:, :], c.X1d[0:128, :], writes=[xt])
            if ti + 1 < NT:
                xn = x1[(ti + 1) % 2]
                k.dma(xn[:, :], c.X1d[(ti + 1) * 128:(ti + 2) * 128, :], writes=[xn])
            for n in range(2):
                pb = PS[4 + (2 * ti + n) % 4]
                for f in range(NF):
                    k.op("pe", lambda e, pb=pb, f=f, n=n: e.matmul(
                        pb[:, :], lhsT=actT[:, f, i * 128:(i + 1) * 128], rhs=Wdn[:, f, n * 512:(n + 1) * 512],
                        start=(f == 0), stop=(f == NF - 1)), reads=[actT, Wdn], writes=[pb], inc=(f == NF - 1))
                k.op("dve", lambda e, pb=pb, n=n: e.tensor_tensor(out=x2[:, n * 512:(n + 1) * 512], in0=pb[:, :],
                                                                  in1=g2b[:, n * 512:(n + 1) * 512], op=ALU.mult),
                     reads=[pb, g2b], writes=[x2])
            k.op("pool", lambda e: e.tensor_tensor(out=x2[:, :], in0=x2[:, :], in1=xt[:, :], op=ALU.add),
                 reads=[x2, xt], writes=[x2])
            k.op("dve", lambda e: e.scalar_tensor_tensor(out=junk[:, :], in0=x2[:, :], scalar=1.0, in1=x2[:, :],
                                                         op0=ALU.mult, op1=ALU.mult, accum_out=ss[:, 0:1]),
                 reads=[x2], writes=[junk, ss])
            k.op("dve", lambda e: e.tensor_scalar(out=ss[:, 1:2], in0=ss[:, 0:1], scalar1=1.0 / D, scalar2=EPS,
                                                  op0=ALU.mult, op1=ALU.add), reads=[ss], writes=[ss])
            k.op("pool", lambda e: e.tensor_tensor(out=rstd[:, 0:1], in0=ss[:, 1:2], in1=mhalf[:, 0:1], op=ALU.pow),
                 reads=[ss, mhalf], writes=[rstd])
            k.op("dve", lambda e: e.scalar_tensor_tensor(out=y[:, :], in0=x2[:, :], scalar=rstd[:, 0:1],
                                                         in1=fnb[:, :], op0=ALU.mult, op1=ALU.mult),
                 reads=[x2, rstd, fnb], writes=[y])
            k.dma(c.Y[ti * 128:(ti + 1) * 128, :], y[:, :], reads=[y])

        for i in range(4):
            tile_body(i)

    for s in range(9):
        st_body(s)
    k.barrier()
    k.phase_mem()
```
